# Optimizing a Trainium2 kernel written in Bass

```python
import math
import jax
import jax.numpy as jnp
from jax import lax
import numpy as np

D_MODEL = 1024
BATCH = 2
SEQ = 16384
DEPTH = 2

D_SSM = 512
SSM_GROUP = 16
N_SSM_GROUPS = D_SSM // SSM_GROUP
SSM_STATE = 64
D_ATTN = D_MODEL - D_SSM
HEAD_DIM = 64
N_Q_HEADS = D_ATTN // HEAD_DIM
N_KV_HEADS = 2
Q_PER_KV = N_Q_HEADS // N_KV_HEADS
KV_DIM = N_KV_HEADS * HEAD_DIM
D_IN = D_SSM + D_ATTN + 2 * KV_DIM
WINDOW = 128
BLOCK = 128
N_BUCKETS = 32
MAX_DISTANCE = 128
D_FF = 2816
CONV_WIDTH = 3
N_MOD = 6
EPS = 1e-6
NEG_INF = -1e30

kernel_name = "hymba_s5_swa_sink_convffn_adaln"


def _rms(x, w):
    xf = x.astype(jnp.float32)
    y = xf * lax.rsqrt(jnp.mean(xf * xf, axis=-1, keepdims=True) + EPS)
    return (y * w.astype(jnp.float32)).astype(x.dtype)


def _t5_bucket(n):
    n = np.maximum(n, 0)
    max_exact = N_BUCKETS // 2
    log_part = np.log(np.maximum(n, 1) / max_exact) / math.log(MAX_DISTANCE / max_exact)
    large = max_exact + (log_part * (N_BUCKETS - max_exact)).astype(np.int32)
    large = np.minimum(large, N_BUCKETS - 1)
    return np.where(n < max_exact, n, large).astype(np.int32)


def _s5(u, lam_re, lam_im, log_dt, b_re, b_im, c_re, c_im, d_skip, w_glu, b_glu):
    bsz, seq = u.shape[0], u.shape[1]
    f32 = jnp.float32
    ug = u.astype(f32).reshape(bsz, seq, N_SSM_GROUPS, SSM_GROUP)
    lr = jnp.minimum(lam_re.astype(f32), -1e-4)
    li = lam_im.astype(f32)
    dt = jnp.exp(log_dt.astype(f32))[:, None]
    mag = jnp.exp(dt * lr)
    a_re = mag * jnp.cos(dt * li)
    a_im = mag * jnp.sin(dt * li)
    den = lr * lr + li * li
    f_re = ((a_re - 1.0) * lr + a_im * li) / den
    f_im = (a_im * lr - (a_re - 1.0) * li) / den
    br = b_re.astype(f32)
    bi = b_im.astype(f32)
    bb_re = f_re[..., None] * br - f_im[..., None] * bi
    bb_im = f_re[..., None] * bi + f_im[..., None] * br
    bu_re = jnp.einsum("blgh,gph->blgp", ug, bb_re)
    bu_im = jnp.einsum("blgh,gph->blgp", ug, bb_im)
    shape_a = (1, seq, N_SSM_GROUPS, SSM_STATE)
    a_re_t = jnp.broadcast_to(a_re, shape_a)
    a_im_t = jnp.broadcast_to(a_im, shape_a)

    def combine(e1, e2):
        a1r, a1i, b1r, b1i = e1
        a2r, a2i, b2r, b2i = e2
        return (a2r * a1r - a2i * a1i,
                a2r * a1i + a2i * a1r,
                a2r * b1r - a2i * b1i + b2r,
                a2r * b1i + a2i * b1r + b2i)

    _, _, s_re, s_im = lax.associative_scan(combine, (a_re_t, a_im_t, bu_re, bu_im), axis=1)
    y = (jnp.einsum("blgp,ghp->blgh", s_re, c_re.astype(f32))
         - jnp.einsum("blgp,ghp->blgh", s_im, c_im.astype(f32))
         + d_skip.astype(f32).reshape(N_SSM_GROUPS, SSM_GROUP) * ug)
    z = jax.nn.gelu(y.reshape(bsz, seq, D_SSM))
    out = z * jax.nn.sigmoid(z @ w_glu.astype(f32) + b_glu.astype(f32))
    return out.astype(u.dtype)


def _swa_sink(q, k, v, rel_bias, sinks):
    bsz, seq = q.shape[0], q.shape[1]
    nb = seq // BLOCK
    f32 = jnp.float32
    qb = q.astype(f32).reshape(bsz, nb, BLOCK, N_KV_HEADS, Q_PER_KV, HEAD_DIM)
    pad = ((0, 0), (BLOCK, 0), (0, 0), (0, 0))
    kb = jnp.pad(k.astype(f32), pad).reshape(bsz, nb + 1, BLOCK, N_KV_HEADS, HEAD_DIM)
    vb = jnp.pad(v.astype(f32), pad).reshape(bsz, nb + 1, BLOCK, N_KV_HEADS, HEAD_DIM)
    k_band = jnp.concatenate([kb[:, :-1], kb[:, 1:]], axis=2)
    v_band = jnp.concatenate([vb[:, :-1], vb[:, 1:]], axis=2)
    logits = jnp.einsum("bnqhgd,bnkhd->bnhgqk", qb, k_band) * (HEAD_DIM ** -0.5)
    q_off = np.arange(BLOCK)[:, None] + BLOCK
    k_off = np.arange(2 * BLOCK)[None, :]
    dist = q_off - k_off
    bias = jnp.take(rel_bias.astype(f32), _t5_bucket(dist), axis=0)
    bias = bias.transpose(2, 0, 1).reshape(N_KV_HEADS, Q_PER_KV, BLOCK, 2 * BLOCK)
    key_pos = np.arange(nb)[:, None, None] * BLOCK + k_off[None] - BLOCK
    mask = (dist[None] >= 0) & (dist[None] < WINDOW) & (key_pos >= 0)
    logits = jnp.where(mask[None, :, None, None], logits + bias, NEG_INF)
    sink = sinks.astype(f32).reshape(N_KV_HEADS, Q_PER_KV)[None, None, :, :, None, None]
    m = jnp.maximum(jnp.max(logits, axis=-1, keepdims=True), sink)
    p = jnp.exp(logits - m)
    probs = p / (jnp.sum(p, axis=-1, keepdims=True) + jnp.exp(sink - m))
    out = jnp.einsum("bnhgqk,bnkhd->bnqhgd", probs, v_band)
    return out.reshape(bsz, seq, D_ATTN).astype(q.dtype)


def _conv_ffn(h, w_up, conv_w, conv_b, w_down):
    up = h @ w_up
    rhs = conv_w.reshape(CONV_WIDTH, 1, 2 * D_FF).astype(up.dtype)
    up = lax.conv_general_dilated(up, rhs, window_strides=(1,),
                                  padding=[(CONV_WIDTH - 1, 0)],
                                  dimension_numbers=("NWC", "WIO", "NWC"),
                                  feature_group_count=2 * D_FF) + conv_b
    val, gate = jnp.split(up, 2, axis=-1)
    return (jax.nn.silu(gate) * val) @ w_down


def setup_inputs(seed: int = 0) -> dict:
    key = jax.random.key(seed)
    ks = jax.random.split(key, 28)
    f32 = jnp.float32

    def nrm(k, shape, scale):
        return jax.random.normal(k, shape, f32) * scale

    def gain(k, shape):
        return 1.0 + 0.05 * jax.random.normal(k, shape, f32)

    G, P, H = N_SSM_GROUPS, SSM_STATE, SSM_GROUP
    lam_im0 = math.pi * jnp.arange(P, dtype=f32)
    return {
        "x": nrm(ks[0], (BATCH, SEQ, D_MODEL), 1.0),
        "c": nrm(ks[1], (BATCH, D_MODEL), 1.0),
        "w_mod": nrm(ks[2], (DEPTH, D_MODEL, N_MOD * D_MODEL), 0.5 * D_MODEL ** -0.5),
        "b_mod": nrm(ks[3], (DEPTH, N_MOD * D_MODEL), 0.02),
        "norm1_w": gain(ks[4], (DEPTH, D_MODEL)),
        "w_in": nrm(ks[5], (DEPTH, D_MODEL, D_IN), D_MODEL ** -0.5),
        "lam_re": -0.5 * jnp.exp(0.1 * jax.random.normal(ks[6], (DEPTH, G, P), f32)),
        "lam_im": lam_im0 + 0.05 * jax.random.normal(ks[7], (DEPTH, G, P), f32),
        "log_dt": jax.random.uniform(ks[8], (DEPTH, G), f32, math.log(1e-3), math.log(1e-1)),
        "ssm_b_re": nrm(ks[9], (DEPTH, G, P, H), (2 * H) ** -0.5),
        "ssm_b_im": nrm(ks[10], (DEPTH, G, P, H), (2 * H) ** -0.5),
        "ssm_c_re": nrm(ks[11], (DEPTH, G, H, P), (2 * P) ** -0.5),
        "ssm_c_im": nrm(ks[12], (DEPTH, G, H, P), (2 * P) ** -0.5),
        "ssm_d": nrm(ks[13], (DEPTH, D_SSM), 1.0),
        "w_glu": nrm(ks[14], (DEPTH, D_SSM, D_SSM), D_SSM ** -0.5),
        "b_glu": nrm(ks[15], (DEPTH, D_SSM), 0.02),
        "q_norm_w": gain(ks[16], (DEPTH, HEAD_DIM)),
        "k_norm_w": gain(ks[17], (DEPTH, HEAD_DIM)),
        "rel_bias": nrm(ks[18], (N_BUCKETS, N_Q_HEADS), 0.5),
        "sinks": nrm(ks[19], (DEPTH, N_Q_HEADS), 0.5),
        "out_norm_ssm": gain(ks[20], (DEPTH, D_SSM)),
        "out_norm_attn": gain(ks[21], (DEPTH, D_ATTN)),
        "w_out": nrm(ks[22], (DEPTH, D_MODEL, D_MODEL), D_MODEL ** -0.5),
        "norm2_w": gain(ks[23], (DEPTH, D_MODEL)),
        "w_up": nrm(ks[24], (DEPTH, D_MODEL, 2 * D_FF), D_MODEL ** -0.5),
        "conv_w": nrm(ks[25], (DEPTH, CONV_WIDTH, 2 * D_FF), CONV_WIDTH ** -0.5),
        "conv_b": nrm(ks[26], (DEPTH, 2 * D_FF), 0.02),
        "w_down": nrm(ks[27], (DEPTH, D_FF, D_MODEL), D_FF ** -0.5),
    }


def reference(x, c, w_mod, b_mod, norm1_w, w_in, lam_re, lam_im, log_dt,
              ssm_b_re, ssm_b_im, ssm_c_re, ssm_c_im, ssm_d, w_glu, b_glu,
              q_norm_w, k_norm_w, rel_bias, sinks, out_norm_ssm, out_norm_attn,
              w_out, norm2_w, w_up, conv_w, conv_b, w_down):
    bsz, seq = x.shape[0], x.shape[1]
    c_act = jax.nn.silu(c)
    for l in range(DEPTH):
        mod = (c_act @ w_mod[l] + b_mod[l])[:, None, :]
        sh1, sc1, g1, sh2, sc2, g2 = jnp.split(mod, N_MOD, axis=-1)

        h = _rms(x, norm1_w[l]) * (1 + sc1) + sh1
        proj = h @ w_in[l]
        u, q, k, v = jnp.split(proj, [D_SSM, D_SSM + D_ATTN, D_SSM + D_ATTN + KV_DIM], axis=-1)
        y_ssm = _s5(u, lam_re[l], lam_im[l], log_dt[l], ssm_b_re[l], ssm_b_im[l],
                    ssm_c_re[l], ssm_c_im[l], ssm_d[l], w_glu[l], b_glu[l])
        q = _rms(q.reshape(bsz, seq, N_Q_HEADS, HEAD_DIM), q_norm_w[l])
        k = _rms(k.reshape(bsz, seq, N_KV_HEADS, HEAD_DIM), k_norm_w[l])
        v = v.reshape(bsz, seq, N_KV_HEADS, HEAD_DIM)
        y_attn = _swa_sink(q, k, v, rel_bias, sinks[l])
        mixed = jnp.concatenate([_rms(y_ssm, out_norm_ssm[l]),
                                 _rms(y_attn, out_norm_attn[l])], axis=-1)
        x = x + g1 * (mixed @ w_out[l])

        h2 = _rms(x, norm2_w[l]) * (1 + sc2) + sh2
        x = x + g2 * _conv_ffn(h2, w_up[l], conv_w[l], conv_b[l], w_down[l])
    return x
```

```python
import numpy as np
from contextlib import ExitStack
import concourse.bass as bass
import concourse.mybir as mybir
from concourse.bass_utils import run_bass_kernel_spmd

F32 = mybir.dt.float32
BF = mybir.dt.bfloat16
AF = mybir.ActivationFunctionType
ALU = mybir.AluOpType
AX = mybir.AxisListType

D = 1024
TT = 512
NB = 4
TC = 8
NCH = TT // TC
DFF = 2816
NV = 22
EPS = 1e-6
NEG = -30000.0
NT6 = 6
STAGE = 99


def nr6(t):
    return 96 if t < 5 else 32


def np6(t):
    return 3 if t < 5 else 1


class Sched:
    ROT = 30000

    def __init__(s, nc, es):
        s.nc = nc
        s.es = es
        s.prog = {e: [] for e in ('pe', 'act', 'dve', 'pool', 'sp')}
        s.cnt = {e: 0 for e in s.prog}
        s.sem = {}
        s.allsems = []
        s.nsem = 0
        for e in s.prog:
            s._newsem(e)
        s.waited = {e: {} for e in s.prog}
        s.res = {}
        s.dsem = {}

    def _newsem(s, e):
        s.nsem += 1
        sm = s.es.enter_context(s.nc.semaphore(f"s_{e}_{s.nsem}"))
        s.sem[e] = sm
        s.cnt[e] = 0
        s.allsems.append([sm, 0])
        s.cur = getattr(s, 'cur', {})
        s.cur[e] = s.allsems[-1]

    def _deps(s, reads, writes):
        deps = {}

        def add(tok):
            if tok is None:
                return
            sem, val = tok
            k = id(sem)
            if k not in deps or deps[k][1] < val:
                deps[k] = (sem, val)
        for k in reads:
            r = s.res.get(k)
            if r:
                add(r[0])
        for k in writes:
            r = s.res.get(k)
            if r:
                add(r[0])
                for t in r[1].values():
                    add(t)
        return deps

    def _emit_waits(s, e, deps):
        for k, (sem, val) in deps.items():
            if s.waited[e].get(k, 0) < val:
                s.waited[e][k] = val
                s.prog[e].append(('w', sem, val))

    def _record(s, tok, reads, writes):
        kk = id(tok[0])
        for k in reads:
            r = s.res.setdefault(k, [None, {}])
            if kk not in r[1] or r[1][kk][1] < tok[1]:
                r[1][kk] = tok
        for k in writes:
            s.res[k] = [tok, {}]

    def op(s, e, fn, reads=(), writes=()):
        deps = s._deps(reads, writes)
        if e == 'pe':
            deps.pop(id(s.sem['pe']), None)
        s._emit_waits(e, deps)
        if s.cnt[e] >= s.ROT:
            s._newsem(e)
        s.cnt[e] += 1
        s.cur[e][1] = s.cnt[e]
        tok = (s.sem[e], s.cnt[e])
        s.prog[e].append(('i', fn, tok[0]))
        s._record(tok, reads, writes)

    def dma(s, e, out, in_, reads, writes, semkey, **kw):
        d = s.dsem.get(semkey)
        if d is None:
            sm = s.es.enter_context(s.nc.semaphore(f"d_{len(s.dsem)}"))
            d = [sm, 0]
            s.dsem[semkey] = d
            s.allsems.append(d)
        deps = s._deps(reads, writes)
        s._emit_waits(e, deps)
        d[1] += 16
        tok = (d[0], d[1])
        s.prog[e].append(('d', out, in_, tok[0], kw))
        s._record(tok, reads, writes)

    def barrier(s):
        for e in s.prog:
            deps = {id(sm): (sm, v) for sm, v in s.allsems if v > 0}
            if e == 'pe':
                deps.pop(id(s.sem['pe']), None)
            s._emit_waits(e, deps)
        s.res = {}

    def wait_all(s, e):
        deps = {id(sm): (sm, v) for sm, v in s.allsems if v > 0}
        s._emit_waits(e, deps)

    def emit(s, block):
        def run(eng, lst):
            for it in lst:
                if it[0] == 'w':
                    eng.wait_ge(it[1], it[2])
                elif it[0] == 'i':
                    it[1](eng).then_inc(it[2], 1)
                else:
                    eng.dma_start(out=it[1], in_=it[2], allow_slow_non_contiguous=True, **it[4]).then_inc(it[3], 16)

        @block.tensor
        def _(e):
            run(e, s.prog['pe'])

        @block.scalar
        def _(e):
            run(e, s.prog['act'])

        @block.vector
        def _(e):
            run(e, s.prog['dve'])

        @block.gpsimd
        def _(e):
            run(e, s.prog['pool'])

        @block.sync
        def _(e):
            run(e, s.prog['sp'])


def build(seq, dbg=False):
    nt = seq // TT
    nc = bass.Bass("TRN2", target_bir_lowering=False)

    def din(name, shape, dt=F32):
        return nc.dram_tensor(name, list(shape), dt, kind="ExternalInput").ap()

    def dscr(name, shape, dt=BF):
        return nc.dram_tensor(name, list(shape), dt, kind="Internal").ap()

    x_d = din("x", [seq, D])
    cT_d = din("cT", [128, 8])
    wmod_d = din("w_mod", [2, D, 6 * D])
    bmod_d = din("b_mod", [2, 6 * D])
    n1T_d = din("n1T", [2, 128, 8])
    n2T_d = din("n2T", [2, 128, 8])
    win_d = din("w_in", [2, D, 1280])
    wout_d = din("w_out", [2, D, D])
    wup_d = din("w_up", [2, D, 2 * DFF])
    wdn_d = din("w_down", [2, DFF, D])
    wglu_d = din("w_glu", [2, 512, 512])
    lamre_d = din("lamre", [2, 128, 16])
    lamim_d = din("lamim", [2, 128, 16])
    ldt_d = din("ldt", [2, 128, 16])
    bre_d = din("bre", [2, 128, 16, 16])
    bim_d = din("bim", [2, 128, 16, 16])
    cre_d = din("cre", [2, 128, 16, 16])
    cim_d = din("cim", [2, 128, 16, 16])
    dT_d = din("dT", [2, 128, 6])
    bgluT_d = din("bgluT", [2, 128, 6])
    qnw_d = din("qnw", [2, 128, 1])
    knw_d = din("knw", [2, 128, 1])
    relb_d = din("relb", [32, 8])
    oneh_d = din("oneh", [33, 384])
    sinks_d = din("sinks", [2, 8])
    onsT_d = din("onsT", [2, 128, 6])
    onaT_d = din("onaT", [2, 128, 4])
    cwT_d = din("cwT", [2, 128, 44, 3])
    cbT_d = din("cbT", [2, 128, 44])
    ident_d = din("ident", [128, 128])
    anti_d = din("antiI", [128, 128])
    y_d = nc.dram_tensor("y", [seq, D], F32, kind="ExternalOutput").ap()

    winu_s = dscr("winu_s", [2, 6, 128, 8 * 96])
    winq_s = dscr("winq_s", [2, 128, 8, 768])
    wout_s = dscr("wout_s", [2, 2, 128, 10 * 512])
    wglu_s = dscr("wglu_s", [2, 128, 6, 512])
    wup_s = dscr("wup_s", [2, 44, 128, 8 * 128])
    wdn_s = dscr("wdn_s", [2, 4, 128, NV * 256])
    kt_s = dscr("kt_s", [2, 128, 8, 6, 128])
    bf_s = dscr("bf_s", [2, 128, 6, 8, 2, 128])
    cs_s = dscr("cs_s", [2, 128, 8, 2, 16, 32])
    bd_s = dscr("bd_s", [8, 384], F32)
    ph_s = dscr("ph_s", [2, 128, 2, NCH + 1, 16], F32)

    with ExitStack() as es:
        S = Sched(nc, es)

        def sb(name, shape, dt=F32):
            return es.enter_context(nc.sbuf_tensor(name, list(shape), dt))

        ident = sb("ident_sb", [128, 128])
        identb = sb("identb", [128, 128], BF)
        ones_f = sb("ones_f", [128, 1])
        nhalf = sb("nhalf", [128, 10])
        epsc = sb("epsc", [128, 1])
        gA = sb("gA", [128, 2, 8]); shA = sb("shA", [128, 2, 8])
        gB = sb("gB", [128, 2, 8]); shB = sb("shB", [128, 2, 8])
        dT = sb("dTt", [128, 2, 6]); bglu = sb("bglu", [128, 2, 6])
        qsc = sb("qsc", [128, 2]); ksc = sb("ksc", [128, 2])
        ons = sb("ons", [128, 2, 6]); ona = sb("ona", [128, 2, 4])
        cw = sb("cw", [128, 2, 44, 3]); cb = sb("cb", [128, 2, 44])
        esink = sb("esink", [128, 2, 8])
        AT1 = sb("AT1", [128, 2, 32]); AT2 = sb("AT2", [128, 2, 32])
        biasT = sb("biasT", [128, 2, 8, 128])
        R8t = sb("R8t", [128, 2, 16])
        SC = sb("SC", [128, 2, 32])
        kT = sb("kT", [128, 2, 128 + TT], BF)
        Vaug = sb("Vaug", [128, 2, NB + 1, 2, 65], BF)
        HB = sb("HB", [128, 2, NV, 2, 2], BF)

        ps_t = [es.enter_context(nc.psum_tensor(f"ps{i}", [128, 1024], F32)) for i in range(4)]
        state = {'bank': 0}

        def bank():
            b = state['bank']
            state['bank'] = (b + 1) % 8
            return b

        def bank2():
            b = state['bank']
            if b % 2:
                b = (b + 1) % 8
            state['bank'] = (b + 2) % 8
            return b

        def pb(b, lo=0, hi=512):
            return ps_t[b // 2][:, (b % 2) * 512 + lo:(b % 2) * 512 + hi]

        def pk(b):
            return ('ps', b)

        pes = ExitStack()

        def psb(name, shape, dt=F32):
            return pes.enter_context(nc.sbuf_tensor(name, list(shape), dt))

        S.dma('sp', ident[:], ident_d[:, :], [], ['ident'], 'c0')
        S.op('dve', lambda e: e.tensor_copy(out=identb[:], in_=ident[:]), ['ident'], ['identb'])
        S.op('dve', lambda e: e.memset(ones_f[:], 1.0), [], ['ones_f'])
        S.op('dve', lambda e: e.memset(nhalf[:], -0.5), [], ['nhalf'])
        S.op('dve', lambda e: e.memset(epsc[:], EPS), [], ['epsc'])
        S.op('pool', lambda e: e.memset(SC[:], 0.0), [], ['SC0', 'SC1'])
        S.op('pool', lambda e: e.memset(kT[:], 0.0), [], ['kT0', 'kT1'])
        S.op('pool', lambda e: e.memset(Vaug[:], 0.0), [], ['Vaug'])
        S.op('pool', lambda e: e.memset(Vaug[:, :, :, :, 64:65], 1.0), [], ['Vaug'])
        S.op('pool', lambda e: e.memset(HB[:], 0.0), [], ['HB0', 'HB1'])

        small = [(dT, dT_d, 'dT'), (bglu, bgluT_d, 'bglu'), (ons, onsT_d, 'ons'), (ona, onaT_d, 'ona'),
                 (cb, cbT_d, 'cb')]
        for i, (t, d_, k) in enumerate(small):
            S.dma('sp', t[:], d_.rearrange("l p a -> p l a"), [], [k], f'c{i + 1}')
        S.dma('sp', cw[:], cwT_d.rearrange("l p a b -> p l a b"), [], ['cw'], 'c6')
        S.op('dve', lambda e: e.tensor_scalar(out=bglu[:], in0=bglu[:], scalar1=0.5, scalar2=None, op0=ALU.mult), ['bglu'], ['bglu'])
        S.op('dve', lambda e: e.tensor_scalar(out=ons[:], in0=ons[:], scalar1=0.25, scalar2=None, op0=ALU.mult), ['ons'], ['ons'])
        qn_t = psb("qn_t", [128, 2, 1]); kn_t = psb("kn_t", [128, 2, 1])
        S.dma('sp', qn_t[:], qnw_d.rearrange("l p a -> p l a"), [], ['qn_t'], 'c7')
        S.dma('sp', kn_t[:], knw_d.rearrange("l p a -> p l a"), [], ['kn_t'], 'c8')
        S.op('dve', lambda e: e.tensor_scalar(out=qsc[:], in0=qn_t[:, :, 0], scalar1=0.125, scalar2=None, op0=ALU.mult),
             ['qn_t'], ['qsc'])
        S.op('dve', lambda e: e.tensor_copy(out=ksc[:], in_=kn_t[:, :, 0]), ['kn_t'], ['ksc'])
        sk_t = psb("sk_t", [128, 2, 8])
        S.dma('sp', sk_t[:].rearrange("p l h -> p (l h)"),
              sinks_d.rearrange("l h -> (l h)").unsqueeze(0).to_broadcast([128, 16]), [], ['sk_t'], 'c9')
        S.op('act', lambda e: e.activation(out=esink[:], in_=sk_t[:], func=AF.Exp), ['sk_t'], ['esink'])

        rb = psb("rb", [33, 8]); oh = psb("oh", [33, 384])
        S.op('dve', lambda e: e.memset(rb[:], NEG), [], ['rb'])
        S.dma('sp', rb[0:32, :], relb_d[:, :], [], ['rb'], 'c10')
        S.dma('sp', oh[:], oneh_d[:, :], [], ['oh'], 'c11')
        b0 = bank()
        S.op('pe', lambda e: e.matmul(pb(b0)[0:8, 0:384], lhsT=rb[:, :], rhs=oh[:, :], start=True, stop=True),
             ['rb', 'oh'], [pk(b0)])
        bd_sb = psb("bd_sb", [8, 384])
        S.op('dve', lambda e: e.tensor_copy(out=bd_sb[:], in_=pb(b0)[0:8, 0:384]), [pk(b0)], ['bd_sb'])
        S.dma('sp', bd_s[:, :], bd_sb[:], ['bd_sb'], ['bd_s'], 'c12')
        antiI = psb("antiI_sb", [128, 128])
        S.dma('sp', antiI[:], anti_d[:, :], [], ['antiI'], 'cai')
        tmpb = psb("tmpb", [128, 2, 8, 128])
        for tl in range(2):
            src = bass.AP(tensor=bd_s.tensor, offset=128 * (1 - tl), ap=[[1, 128], [384, 8], [1, 128]])
            S.dma('sp', tmpb[:, tl], src, ['bd_s'], ['tmpb'], f'cb{tl}')
            for hh in range(2):
                b_ = bank()
                S.op('pe', lambda e, b_=b_, tl=tl, hh=hh: e.matmul(
                    pb(b_), lhsT=antiI[:, :], rhs=tmpb[:, tl, 4 * hh:4 * hh + 4, :].rearrange("p h q -> p (h q)"),
                    start=True, stop=True), ['antiI', 'tmpb'], [pk(b_)])
                S.op('dve', lambda e, b_=b_, tl=tl, hh=hh: e.tensor_copy(
                    out=biasT[:, tl, 4 * hh:4 * hh + 4, :].rearrange("p h q -> p (h q)"), in_=pb(b_)), [pk(b_)], ['biasT'])

        cT = psb("cTt", [128, 8]); cact = psb("cact", [128, 8]); cbc = psb("cbc", [128, 8, 128])
        S.dma('sp', cT[:], cT_d[:, :], [], ['cT'], 'c13')
        S.op('act', lambda e: e.activation(out=cact[:], in_=cT[:], func=AF.Silu), ['cT'], ['cact'])
        S.op('dve', lambda e: e.tensor_copy(out=cbc[:], in_=cact[:].unsqueeze(2).to_broadcast([128, 8, 128])),
             ['cact'], ['cbc'])
        grow = psb("grow", [128, 2, 2, D])
        wm = [psb(f"wm{i}", [128, 8, 512]) for i in range(2)]
        bmr = [psb(f"bmr{i}", [128, 512]) for i in range(2)]
        modt = [psb(f"modt{i}", [128, 512]) for i in range(2)]
        n1 = psb("n1", [128, 2, 8]); n2 = psb("n2", [128, 2, 8])
        S.dma('sp', n1[:], n1T_d.rearrange("l p a -> p l a"), [], ['n1'], 'c15')
        S.dma('sp', n2[:], n2T_d.rearrange("l p a -> p l a"), [], ['n2'], 'c16')
        dtmp = psb("dtmp", [128, 128]); sct = psb("sct", [128, 2, 8])
        cnt = 0
        for l in range(2):
            for cc in range(12):
                i_ = cnt % 2
                w_ = wm[i_]; wk = f'wm{i_}'; bm_ = bmr[i_]; bk = f'bmr{i_}'; mt_ = modt[i_]; mk = f'modt{i_}'
                cnt += 1
                S.dma('sp', w_[:], wmod_d[l].rearrange("(kt p) n -> p kt n", p=128)[:, :, cc * 512:(cc + 1) * 512],
                      [], [wk], wk)
                S.dma('sp', bm_[:], bmod_d[l:l + 1, cc * 512:(cc + 1) * 512].to_broadcast([128, 512]), [], [bk], bk)
                b_ = bank()
                for kt in range(8):
                    S.op('pe', lambda e, b_=b_, kt=kt, w_=w_: e.matmul(pb(b_), lhsT=cbc[:, kt, :], rhs=w_[:, kt, :],
                                                                       start=(kt == 0), stop=(kt == 7)),
                         ['cbc', wk], [pk(b_)])
                slot, hf = cc // 2, cc % 2
                if slot in (2, 5):
                    S.op('dve', lambda e, b_=b_, l=l, slot=slot, hf=hf, bm_=bm_: e.tensor_tensor(
                        out=grow[:, l, 0 if slot == 2 else 1, hf * 512:(hf + 1) * 512], in0=pb(b_), in1=bm_[:], op=ALU.add),
                        [pk(b_), bk], ['grow'])
                else:
                    S.op('dve', lambda e, b_=b_, bm_=bm_, mt_=mt_: e.tensor_tensor(out=mt_[:], in0=pb(b_), in1=bm_[:], op=ALU.add),
                         [pk(b_), bk], [mk])
                    dst = {0: shA, 3: shB}.get(slot)
                    for k4 in range(4):
                        kt = hf * 4 + k4
                        tgt = (dst[:, l, kt:kt + 1] if dst is not None else sct[:, 0 if slot == 1 else 1, kt:kt + 1])
                        S.op('dve', lambda e, mt_=mt_, k4=k4: e.tensor_tensor(
                            out=dtmp[:], in0=mt_[:, k4 * 128:(k4 + 1) * 128], in1=ident[:], op=ALU.mult),
                            [mk, 'ident'], ['dtmp'])
                        S.op('dve', lambda e, tgt=tgt: e.tensor_reduce(out=tgt, in_=dtmp[:], axis=AX.X, op=ALU.add),
                             ['dtmp'], ['sct', 'shA', 'shB'])
            S.op('dve', lambda e, l=l: e.scalar_tensor_tensor(out=gA[:, l, :], in0=sct[:, 0, :], scalar=1.0, in1=n1[:, l, :],
                                                              op0=ALU.add, op1=ALU.mult), ['sct', 'n1'], ['gA'])
            S.op('dve', lambda e, l=l: e.scalar_tensor_tensor(out=gB[:, l, :], in0=sct[:, 1, :], scalar=1.0, in1=n2[:, l, :],
                                                              op0=ALU.add, op1=ALU.mult), ['sct', 'n2'], ['gB'])

        NWS = 4
        wst = [psb(f"wst{i}", [128, 2048]) for i in range(NWS)]
        wsb = [psb(f"wsb{i}", [128, 2048], BF) for i in range(NWS)]
        cnt = 0

        def cast_piece(src_ap, stores, n, gate=None, rows=128):
            nonlocal cnt
            i = cnt % NWS
            cnt += 1
            S.dma('sp', wst[i][0:rows, 0:n], src_ap, [], [f'wst{i}'], f'wst{i}')
            if gate is None:
                S.op('dve', lambda e: e.tensor_copy(out=wsb[i][0:rows, 0:n], in_=wst[i][0:rows, 0:n]), [f'wst{i}'], [f'wsb{i}'])
            else:
                S.op('dve', lambda e: e.tensor_tensor(out=wsb[i][0:rows, 0:n], in0=wst[i][0:rows, 0:n], in1=gate[0:rows], op=ALU.mult),
                     [f'wst{i}', 'grow'], [f'wsb{i}'])
            for dst_ap, vf in stores:
                S.dma('act', dst_ap, vf(wsb[i]), [f'wsb{i}'], ['wscr'], f'wsbst{i}')

        def cast_all():
            for l in range(2):
                for kt in range(8):
                    cast_piece(win_d[l, kt * 128:(kt + 1) * 128, :], [
                        (winu_s[l, 0:5].rearrange("t p c -> p t c")[:, :, kt * 96:(kt + 1) * 96],
                         lambda w: w[:, 0:480].rearrange("p (t c) -> p t c", c=96)),
                        (winu_s[l, 5, :, kt * 96:kt * 96 + 32], lambda w: w[:, 480:512]),
                        (winq_s[l, :, kt, :], lambda w: w[:, 512:1280])], 1280)
                    yield
                    for c3 in range(4):
                        cast_piece(wup_d[l, kt * 128:(kt + 1) * 128, c3 * 1408:(c3 + 1) * 1408], [
                            (wup_s[l, c3 * 11:(c3 + 1) * 11].rearrange("c p x -> p c x")[:, :, kt * 128:(kt + 1) * 128],
                             lambda w: w[:, 0:1408].rearrange("p (c x) -> p c x", x=128))], 1408)
                        yield
                for t6 in range(NT6):
                    n_ = nr6(t6)
                    cast_piece(wglu_d[l, t6 * 96:t6 * 96 + n_, :], [(wglu_s[l, 0:n_, t6, :], lambda w, n_=n_: w[0:n_, 0:512])], 512, rows=n_)
                    yield
                    cast_piece(wout_d[l, t6 * 96:t6 * 96 + n_, :], [
                        (wout_s[l, :, 0:n_, t6 * 512:(t6 + 1) * 512].rearrange("h p c -> p h c"),
                         lambda w, n_=n_: w[0:n_, 0:1024].rearrange("p (h c) -> p h c", h=2))], 1024, gate=grow[:, l, 0, :], rows=n_)
                    yield
                for kt in range(4):
                    cast_piece(wout_d[l, 512 + kt * 128:512 + (kt + 1) * 128, :], [
                        (wout_s[l, :, :, (6 + kt) * 512:(7 + kt) * 512].rearrange("h p c -> p h c"),
                         lambda w: w[:, 0:1024].rearrange("p (h c) -> p h c", h=2))], 1024, gate=grow[:, l, 0, :])
                    yield
                for v in range(NV):
                    cast_piece(wdn_d[l, v * 128:(v + 1) * 128, :], [
                        (wdn_s[l, :, :, v * 256:(v + 1) * 256].rearrange("q p c -> p q c"),
                         lambda w: w[:, 0:1024].rearrange("p (q c) -> p q c", q=4))], 1024, gate=grow[:, l, 1, :])
                    yield


        castgen = cast_all()
        dvc = {'n': 0}

        def V(name, shape=(128, 16)):
            return psb(name, list(shape))
        lre = V("lre", (128, 2, 16)); lim = V("lim", (128, 2, 16)); ldt = V("ldtt", (128, 2, 16))
        S.dma('sp', lre[:], lamre_d.rearrange("l p a -> p l a"), [], ['lre'], 'c17')
        S.dma('sp', lim[:], lamim_d.rearrange("l p a -> p l a"), [], ['lim'], 'c18')
        S.dma('sp', ldt[:], ldt_d.rearrange("l p a -> p l a"), [], ['ldt'], 'c19')
        Bre = V("Bre", (128, 2, 16, 16)); Bim = V("Bim", (128, 2, 16, 16))
        Cre = V("Cre", (128, 2, 16, 16)); Cim = V("Cim", (128, 2, 16, 16))
        for i, (t, d_) in enumerate(((Bre, bre_d), (Bim, bim_d), (Cre, cre_d), (Cim, cim_d))):
            S.dma('sp', t[:], d_.rearrange("l p a b -> p l a b"), [], [f'BC{i}'], f'c2{i}')
        tnames = ['dt', 'zr', 'th', 'mag', 'sn', 'cs', 't1', 't2', 't3', 'ar', 'ai', 'fr', 'fi', 'pr', 'pi', 'qr', 'qi']
        tv = {n: V("v_" + n) for n in tnames}
        Er = V("Er", (128, 16, 16)); Ei = V("Ei", (128, 16, 16)); Gr = V("Gr", (128, 16, 16)); Gi = V("Gi", (128, 16, 16))
        X1 = V("X1", (128, 16, 16)); X2 = V("X2", (128, 16, 16)); X3 = V("X3", (128, 16, 16))
        Eblk = psb("Eblk", [128, 16, 8, 2, 32], BF)
        Gblk = psb("Gblk", [128, 8, 2, 16, 32], BF)
        Cw = psb("Cw", [128, 16, 2, 128], BF)
        tabev = psb("tabev", [128, 768], BF)
        PHsb = psb("PHsb", [128, 2, NCH + 1, 16])
        phu = psb("phu", [128, 2, 16]); pht = psb("pht", [128, 4, 16])

        def dv(fn, r, w):
            S.op('dve', fn, r, w)
            dvc['n'] += 1
            if dvc['n'] % 5 == 0:
                next(castgen, None)

        def tt(o, a, b, op, r=('ssmv',), w=('ssmv',)):
            dv(lambda e: e.tensor_tensor(out=o, in0=a, in1=b, op=op), list(r), list(w))

        def ts(o, a, s1, s2, op0, op1=None, r=('ssmv',), w=('ssmv',)):
            if op1 is None:
                dv(lambda e: e.tensor_scalar(out=o, in0=a, scalar1=s1, scalar2=None, op0=op0), list(r), list(w))
            else:
                dv(lambda e: e.tensor_scalar(out=o, in0=a, scalar1=s1, scalar2=s2, op0=op0, op1=op1), list(r), list(w))

        def bc(a):
            return a.unsqueeze(2).to_broadcast([128, 16, 16])

        def cmul_b(orr, oi, sr, si, xr, xi):
            tt(X1[:], xr, bc(sr), ALU.mult); tt(X2[:], xi, bc(si), ALU.mult)
            tt(X3[:], X1[:], X2[:], ALU.subtract)
            tt(X1[:], xr, bc(si), ALU.mult); tt(X2[:], xi, bc(sr), ALU.mult)
            tt(oi, X1[:], X2[:], ALU.add)
            dv(lambda e: e.tensor_copy(out=orr, in_=X3[:]), ['ssmv'], ['ssmv'])

        for l in range(2):
            rk = ['ssmv', 'lre', 'lim', 'ldt', 'BC0', 'BC1', 'BC2', 'BC3']
            t = {k: v[:] for k, v in tv.items()}
            S.op('act', lambda e, l=l: e.activation(out=tv['dt'][:], in_=ldt[:, l, :], func=AF.Exp), ['ldt', 'ssmv'], ['ssmv'])
            ts(t['t1'], lre[:, l, :], -1e-4, None, ALU.min, r=rk)
            tt(t['zr'], t['t1'], t['dt'], ALU.mult)
            tt(t['th'], lim[:, l, :], t['dt'], ALU.mult, r=rk)
            ts(t['mag'], t['zr'], 1.0 / 720, 1.0 / 120, ALU.mult, ALU.add)
            for cf in (1.0 / 24, 1.0 / 6, 0.5, 1.0, 1.0):
                tt(t['mag'], t['mag'], t['zr'], ALU.mult)
                ts(t['mag'], t['mag'], cf, None, ALU.add)
            ts(t['t2'], t['th'], 1.0 / 32, None, ALU.mult)
            tt(t['t3'], t['t2'], t['t2'], ALU.mult)
            ts(t['sn'], t['t3'], 1.0 / 362880, -1.0 / 5040, ALU.mult, ALU.add)
            for cf in (1.0 / 120, -1.0 / 6, 1.0):
                tt(t['sn'], t['sn'], t['t3'], ALU.mult)
                ts(t['sn'], t['sn'], cf, None, ALU.add)
            tt(t['sn'], t['sn'], t['t2'], ALU.mult)
            ts(t['cs'], t['t3'], -1.0 / 3628800, 1.0 / 40320, ALU.mult, ALU.add)
            for cf in (-1.0 / 720, 1.0 / 24, -0.5, 1.0):
                tt(t['cs'], t['cs'], t['t3'], ALU.mult)
                ts(t['cs'], t['cs'], cf, None, ALU.add)
            for _ in range(5):
                tt(t['pr'], t['cs'], t['cs'], ALU.mult); tt(t['pi'], t['sn'], t['sn'], ALU.mult)
                tt(t['qr'], t['sn'], t['cs'], ALU.mult)
                tt(t['cs'], t['pr'], t['pi'], ALU.subtract)
                ts(t['sn'], t['qr'], 2.0, None, ALU.mult)
            tt(t['ar'], t['mag'], t['cs'], ALU.mult); tt(t['ai'], t['mag'], t['sn'], ALU.mult)
            tt(t['pr'], t['t1'], t['t1'], ALU.mult); tt(t['pi'], lim[:, l, :], lim[:, l, :], ALU.mult, r=rk)
            tt(t['pr'], t['pr'], t['pi'], ALU.add)
            dv(lambda e: e.reciprocal(out=tv['pr'][:], in_=tv['pr'][:]), ['ssmv'], ['ssmv'])
            ts(t['qr'], t['ar'], -1.0, None, ALU.add)
            tt(t['t2'], t['qr'], t['t1'], ALU.mult); tt(t['t3'], t['ai'], lim[:, l, :], ALU.mult, r=rk)
            tt(t['t2'], t['t2'], t['t3'], ALU.add); tt(t['fr'], t['t2'], t['pr'], ALU.mult)
            tt(t['t2'], t['ai'], t['t1'], ALU.mult); tt(t['t3'], t['qr'], lim[:, l, :], ALU.mult, r=rk)
            tt(t['t2'], t['t2'], t['t3'], ALU.subtract); tt(t['fi'], t['t2'], t['pr'], ALU.mult)
            cmul_b(Er[:], Ei[:], t['fr'], t['fi'], Bre[:, l], Bim[:, l])
            cmul_b(Gr[:], Gi[:], t['ar'], t['ai'], Cre[:, l], Cim[:, l])
            S.op('pool', lambda e: e.memset(Eblk[:], 0.0), ['Eblk'], ['Eblk'])
            S.op('pool', lambda e: e.memset(Gblk[:], 0.0), ['Gblk'], ['Gblk'])
            S.op('pool', lambda e: e.memset(Cw[:], 0.0), ['Cw'], ['Cw'])
            for two in range(2):
                ps_ = slice(64 * two, 64 * two + 64)
                for q in range(3):
                    prs = slice(q, 16, 3)
                    S.op('dve', lambda e, ps_=ps_, two=two, q=q, prs=prs, l=l: e.tensor_copy(
                        out=Cw[ps_, prs, 0, 32 * q + 16 * two:32 * q + 16 * two + 16], in_=Cre[ps_, l, prs, :]),
                        ['BC2', 'Cw'], ['Cw'])
                    S.op('dve', lambda e, ps_=ps_, two=two, q=q, prs=prs, l=l: e.tensor_scalar(
                        out=Cw[ps_, prs, 1, 32 * q + 16 * two:32 * q + 16 * two + 16], in0=Cim[ps_, l, prs, :],
                        scalar1=-1.0, scalar2=None, op0=ALU.mult), ['BC3', 'Cw'], ['Cw'])
            for d_ in range(8):
                if d_ > 0:
                    cmul_b(Er[:], Ei[:], t['ar'], t['ai'], Er[:], Ei[:])
                    cmul_b(Gr[:], Gi[:], t['ar'], t['ai'], Gr[:], Gi[:])
                for two in range(2):
                    ps_ = slice(64 * two, 64 * two + 64)
                    cs_ = slice(16 * two, 16 * two + 16)
                    for part, (E_, G_) in enumerate(((Er, Gr), (Ei, Gi))):
                        S.op('dve', lambda e, ps_=ps_, cs_=cs_, d_=d_, part=part, E_=E_: e.tensor_copy(
                            out=Eblk[ps_, :, d_, part, cs_], in_=E_[ps_, :, :]), ['ssmv', 'Eblk'], ['Eblk'])
                        if part == 0:
                            S.op('dve', lambda e, ps_=ps_, cs_=cs_, d_=d_, G_=G_: e.tensor_copy(
                                out=Gblk[ps_, d_, 0, :, cs_], in_=G_[ps_, :, :]), ['ssmv', 'Gblk'], ['Gblk'])
                        else:
                            S.op('dve', lambda e, ps_=ps_, cs_=cs_, d_=d_, G_=G_: e.tensor_scalar(
                                out=Gblk[ps_, d_, 1, :, cs_], in0=G_[ps_, :, :], scalar1=-1.0, scalar2=None, op0=ALU.mult),
                                ['ssmv', 'Gblk'], ['Gblk'])
            tt(t['t2'], t['mag'], t['mag'], ALU.mult); tt(t['t3'], t['t2'], t['t2'], ALU.mult)
            dv(lambda e, l=l: e.tensor_tensor(out=R8t[:, l, :], in0=tv['t3'][:], in1=tv['t3'][:], op=ALU.mult), ['ssmv'], ['R8t'])

            def csq(orr, oi, xr, xi):
                tt(t['t2'], xr, xr, ALU.mult); tt(t['t3'], xi, xi, ALU.mult)
                tt(t['mag'], xr, xi, ALU.mult)
                tt(orr, t['t2'], t['t3'], ALU.subtract)
                ts(oi, t['mag'], 2.0, None, ALU.mult)
            csq(t['pr'], t['pi'], t['ar'], t['ai'])
            csq(t['qr'], t['qi'], t['pr'], t['pi'])
            csq(t['pr'], t['pi'], t['qr'], t['qi'])
            dv(lambda e, l=l: e.tensor_copy(out=AT1[:, l, 0:16], in_=tv['pr'][:]), ['ssmv'], ['AT'])
            dv(lambda e, l=l: e.tensor_copy(out=AT1[:, l, 16:32], in_=tv['pr'][:]), ['ssmv'], ['AT'])
            dv(lambda e, l=l: e.tensor_copy(out=AT2[:, l, 16:32], in_=tv['pi'][:]), ['ssmv'], ['AT'])
            dv(lambda e, l=l: e.tensor_scalar(out=AT2[:, l, 0:16], in0=tv['pi'][:], scalar1=-1.0, scalar2=None, op0=ALU.mult),
               ['ssmv'], ['AT'])
            dv(lambda e, l=l: e.reciprocal(out=tv['t1'][:], in_=R8t[:, l, :]), ['R8t', 'ssmv'], ['ssmv'])
            tt(t['qr'], t['pr'], t['t1'], ALU.mult); tt(t['qi'], t['pi'], t['t1'], ALU.mult)
            dv(lambda e: e.tensor_copy(out=phu[:, 0, :], in_=tv['qr'][:]), ['ssmv', 'phu'], ['phu'])
            dv(lambda e: e.tensor_copy(out=phu[:, 1, :], in_=tv['qi'][:]), ['ssmv', 'phu'], ['phu'])
            S.op('pool', lambda e: e.memset(PHsb[:, 0, 0, :], 1.0), ['PHsb'], ['PHsb'])
            S.op('pool', lambda e: e.memset(PHsb[:, 1, 0, :], 0.0), ['PHsb'], ['PHsb'])

            def ptt(o, a_, b_, op, r, w):
                S.op('pool', lambda e: e.tensor_tensor(out=o, in0=a_, in1=b_, op=op), r, w)
            for c in range(NCH):
                cr, ci = PHsb[:, 0, c, :], PHsb[:, 1, c, :]
                ptt(pht[:, 0, :], cr, phu[:, 0, :], ALU.mult, ['phu', 'PHsb', 'pht0'], ['pht0'])
                ptt(pht[:, 1, :], ci, phu[:, 1, :], ALU.mult, ['phu', 'PHsb', 'pht1'], ['pht1'])
                ptt(PHsb[:, 0, c + 1, :], pht[:, 0, :], pht[:, 1, :], ALU.subtract, ['pht0', 'pht1'], ['PHsb'])
                ptt(pht[:, 2, :], cr, phu[:, 1, :], ALU.mult, ['phu', 'PHsb', 'pht2'], ['pht2'])
                ptt(pht[:, 3, :], ci, phu[:, 0, :], ALU.mult, ['phu', 'PHsb', 'pht3'], ['pht3'])
                ptt(PHsb[:, 1, c + 1, :], pht[:, 2, :], pht[:, 3, :], ALU.add, ['pht2', 'pht3'], ['PHsb'])
            S.dma('sp', ph_s[l], PHsb[:], ['PHsb'], ['ph_s'], 'tb3')
            S.dma('sp', cs_s[l], Gblk[:], ['Gblk'], ['cs_s'], 'tb0')
            for d_ in range(8):
                b_ = bank2()
                S.op('dve', lambda e, b_=b_: e.memset(ps_t[b_ // 2][:, 0:768], 0.0), [], [pk(b_), pk(b_ + 1)])
                for pr in range(16):
                    t6, q = pr // 3, pr % 3
                    for part in range(2):
                        S.op('pe', lambda e, b_=b_, t6=t6, q=q, pr=pr, part=part, d_=d_: e.matmul(
                            ps_t[b_ // 2][32 * q:32 * q + 32, t6 * 128:(t6 + 1) * 128], lhsT=Eblk[:, pr, d_, part, :],
                            rhs=Cw[:, pr, part, :], start=(part == 0), stop=(part == 1)),
                            ['Eblk', 'Cw'], [pk(b_), pk(b_ + 1)])
                S.op('dve', lambda e, b_=b_: e.tensor_copy(out=tabev[:], in_=ps_t[b_ // 2][:, 0:768]), [pk(b_), pk(b_ + 1)], ['tabev'])
                S.dma('sp', kt_s[l, :, d_].rearrange("p g c -> p (g c)"), tabev[:], ['tabev'], ['kt_s'], 'tb1')
            for j in range(8):
                for part in range(2):
                    b_ = bank2()
                    S.op('dve', lambda e, b_=b_: e.memset(ps_t[b_ // 2][:, 0:768], 0.0), [], [pk(b_), pk(b_ + 1)])
                    for pr in range(16):
                        t6, q = pr // 3, pr % 3
                        S.op('pe', lambda e, b_=b_, t6=t6, q=q, pr=pr, part=part, j=j: e.matmul(
                            ps_t[b_ // 2][32 * q:32 * q + 32, t6 * 128:(t6 + 1) * 128], lhsT=Eblk[:, pr, 7 - j, part, :],
                            rhs=identb[:, :], start=True, stop=True), ['Eblk', 'identb'], [pk(b_), pk(b_ + 1)])
                    S.op('dve', lambda e, b_=b_: e.tensor_copy(out=tabev[:], in_=ps_t[b_ // 2][:, 0:768]), [pk(b_), pk(b_ + 1)], ['tabev'])
                    S.dma('sp', bf_s[l, :, :, j, part, :], tabev[:].rearrange("p (g c) -> p g c", g=6), ['tabev'], ['bf_s'], 'tb2')

        for _ in castgen:
            pass
        S.barrier()
        pes.close()

        xt = sb("xt", [128, NB, D])
        tok = [sb(f"tok{i}", [128, D]) for i in range(2)]
        sqj = sb("sqj", [128, D], BF)
        rst = sb("rst", [128, 8])
        featT = sb("featT", [128, 8, TT], BF)
        uT = sb("uT", [128, NT6, TC, NCH], BF)
        qkv_sq = sb("qkv_sq", [128, 640])
        qn = sb("qn", [128, 640])
        qT = sb("qT", [128, NB, 512], BF)
        SAw = sb("SAw", [128, NCH + 1, 32])
        Sprev = sb("Sprev", [128, 32, NCH], BF)
        f32all = sb("f32all", [128, 6, TT])
        f32b = [f32all[:, i, :] for i in range(6)]
        PHt = sb("PHt", [128, 2, NCH + 1, 16])
        ysb = f32b[0:2]; ytmp = f32b[2:4]; S_sb = f32b[4:6]; sgate = f32b[0:2]
        YSK = ["f32b0", "f32b1"]; YTK = ["f32b2", "f32b3"]; SSK = ["f32b4", "f32b5"]
        zT = sb("zT", [128, NT6, TT], BF)
        ssmT = sb("ssmT", [128, NT6, TT], BF)
        attnT = sb("attnT", [128, 4, TT], BF)
        PT = [sb(f"PT{i}", [128, 2, 512], BF) for i in range(2)]
        actT = sb("actT", [128, NV, TT], BF)
        U = [sb(f"U{i}", [128, 2, TT + 2], BF) for i in range(2)]
        dg = [sb(f"dg{i}", [128, 6, 128], BF) for i in range(2)]
        NWCH = 4
        wch = [sb(f"wch{i}", [128, 8, 128], BF) for i in range(NWCH)]
        wsl = [sb(f"wsl{i}", [128, 6144], BF) for i in range(2)]
        wslc = {"n": 0}

        def wslot():
            i = wslc["n"] % 2
            wslc["n"] += 1
            return wsl[i], f"wsl{i}"
        tab = sb("tab", [128, 14336], BF)
        BFt = tab[:, 0:12288].rearrange("p (a b c d) -> p a b c d", a=6, b=8, c=2)
        KTt = tab[:, 0:6144].rearrange("p (a b c) -> p a b c", a=8, b=6)
        CSt = tab[:, 6144:14336].rearrange("p (a b c d) -> p a b c d", a=8, b=2, c=16)
        st1 = sb("st1", [128, 16])
        rs_s = sb("rs_s", [128, NB]); rs_a = sb("rs_a", [128, NB])
        ctmp = [sb(f"ctmp{i}", [128, 32]) for i in range(2)]
        wcnt = {'n': 0}

        def load_chunk(src):
            i = wcnt['n'] % NWCH
            wcnt['n'] += 1
            S.dma('sp', wch[i][:], src, [], [f'wch{i}'], f'wch{i}')
            return wch[i], f'wch{i}'

        XK = [f'xt{b}' for b in range(NB)]

        def rstd_pow(out_ap, in_ap, scale, n, rkeys, wkeys):
            S.op('pool', lambda e: e.tensor_scalar(out=out_ap, in0=in_ap, scalar1=scale, scalar2=EPS, op0=ALU.mult, op1=ALU.add),
                 list(rkeys), list(wkeys))
            S.op('pool', lambda e: e.tensor_tensor(out=out_ap, in0=out_ap, in1=nhalf[:, 0:n], op=ALU.pow),
                 list(wkeys) + ['nhalf'], list(wkeys))

        def rms_to_featT(l, g_t, sh_t):
            for b in range(NB):
                S.op('act', lambda e, b=b: e.activation(out=sqj[:], in_=xt[:, b, :], func=AF.Square,
                                                        accum_out=rst[:, b:b + 1]), [XK[b]], ['sqj', f'rst{b}'])
            for b in range(NB):
                rstd_pow(rst[:, 4 + b:5 + b], rst[:, b:b + 1], 1.0 / D, 1, [f'rst{b}'], [f'rstd{b}'])
            for b in range(NB):
                tk = tok[b % 2]
                tkk = f'tok{b % 2}'
                S.op('dve', lambda e, b=b, tk=tk: e.tensor_scalar(out=tk[:], in0=xt[:, b, :], scalar1=rst[:, 4 + b:5 + b], scalar2=None,
                                                                   op0=ALU.mult), [XK[b], f'rstd{b}', tkk], [tkk])
                for half in range(2):
                    b_ = bank()
                    for j in range(4):
                        kt = half * 4 + j
                        S.op('pe', lambda e, b_=b_, j=j, kt=kt, tk=tk: e.transpose(
                            out=pb(b_, j * 128, (j + 1) * 128), in_=tk[:, kt * 128:(kt + 1) * 128], identity=ident[:]),
                            [tkk, 'ident'], [pk(b_)])
                    for j in range(4):
                        kt = half * 4 + j
                        S.op('dve', lambda e, b_=b_, j=j, kt=kt, b=b: e.tensor_scalar(
                            out=featT[:, kt, b * 128:(b + 1) * 128], in0=pb(b_, j * 128, (j + 1) * 128),
                            scalar1=g_t[:, l, kt:kt + 1], scalar2=sh_t[:, l, kt:kt + 1], op0=ALU.mult, op1=ALU.add),
                            [pk(b_), 'gA', 'gB', 'shA', 'shB'], ['featT'])

        def tile_layer(ti, l):
            if True:
                SAk = 'SAw'; SCk = f'SC{l}'; kTk = f'kT{l}'; HBk = f'HB{l}'
                if ti == 0 and l == 0:
                    S.dma('sp', tab[:, 0:12288], bf_s[l].rearrange("p a b c d -> p (a b c d)"), [], ['tab'], 'tab')
                S.dma('sp', PHt[:], ph_s[l], [], ['PHt'], 'PHt')
                wq_t, wqk = wslot()
                wqkv = wq_t[:, :].rearrange("p (k c) -> p k c", k=8)
                S.dma('sp', wq_t[:, :], winq_s[l].rearrange("p k c -> p (k c)"), [], [wqk], wqk)
                wg_t, wgk = wslot()
                wgl = wg_t[:, 0:3072].rearrange("p (k c) -> p k c", k=NT6)
                S.dma('sp', wg_t[:, 0:3072], wglu_s[l].rearrange("p k c -> p (k c)"), [], [wgk], wgk)
                rms_to_featT(l, gA, shA)
                for t6 in range(NT6):
                    n_ = nr6(t6)
                    i_ = wcnt['n'] % NWCH
                    wcnt['n'] += 1
                    wc, wk = wch[i_], f'wch{i_}'
                    S.dma('sp', wc[:, :, 0:n_], winu_s[l, t6].rearrange("p (k c) -> p k c", c=96)[:, :, 0:n_], [], [wk], wk)
                    b_ = bank()
                    for kt in range(8):
                        S.op('pe', lambda e, b_=b_, kt=kt, wc=wc, n_=n_: e.matmul(pb(b_)[0:n_, :], lhsT=wc[:, kt, 0:n_], rhs=featT[:, kt, :],
                                                                                 start=(kt == 0), stop=(kt == 7)),
                             [wk, 'featT'], [pk(b_)])
                    S.op('act', lambda e, b_=b_, t6=t6, n_=n_: e.activation(
                        out=uT[0:n_, t6].rearrange("p j c -> p c j"), in_=pb(b_)[0:n_, :].rearrange("p (c j) -> p c j", j=TC),
                        func=AF.Copy), [pk(b_)], ['uT'])
                for q in range(3):
                    combos = [(part, pr) for part in range(2) for pr in range(16) if pr % 3 == q]
                    b2 = bank2()
                    for sl, (part, pr) in enumerate(combos):
                        t6 = pr // 3
                        bb = b2 + sl // 8
                        for j in range(TC):
                            S.op('pe', lambda e, bb=bb, sl=sl, t6=t6, q=q, j=j, part=part: e.matmul(
                                pb(bb, (sl % 8) * 64, (sl % 8) * 64 + 64), lhsT=BFt[32 * q:32 * q + 32, t6, j, part, :],
                                rhs=uT[32 * q:32 * q + 32, t6, j, :], start=(j == 0), stop=(j == TC - 1)),
                                ['tab', 'uT'], [pk(bb)])
                    for sl, (part, pr) in enumerate(combos):
                        bb = b2 + sl // 8
                        S.op('dve', lambda e, bb=bb, sl=sl, part=part, pr=pr: e.tensor_copy(
                            out=SAw[:, 1:NCH + 1, part * 16 + pr], in_=pb(bb, (sl % 8) * 64, (sl % 8) * 64 + 64)), [pk(bb)], [SAk])
                S.dma('sp', tab[:, 0:6144], kt_s[l].rearrange("p a b c -> p (a b c)"), [], ['tab'], 'tab')
                S.dma('sp', tab[:, 6144:14336], cs_s[l].rearrange("p a b c d -> p (a b c d)"), [], ['tab'], 'tab2')
                def chain_pre():
                    S.op('pool', lambda e: e.tensor_copy(out=SAw[:, 0, :], in_=SC[:, l, :]), [SCk, SAk], [SAk])
                    T2 = f32all[:, 0:4, :].rearrange("p a (c t q) -> p (a c) t q", t=2, q=16)
                    T2K = ['f32b0', 'f32b1', 'f32b2', 'f32b3']
                    Fv = SAw[:, 1:NCH + 1, :].rearrange("p c (t q) -> p c t q", t=2)
                    cosf, sinf = PHt[:, 0, 1:NCH + 1, :], PHt[:, 1, 1:NCH + 1, :]
                    S.op('pool', lambda e: e.tensor_tensor(out=T2[:, :, 0, :], in0=Fv[:, :, 1, :], in1=sinf, op=ALU.mult), [SAk, 'PHt'] + T2K, T2K)
                    S.op('pool', lambda e: e.tensor_tensor(out=T2[:, :, 1, :], in0=Fv[:, :, 0, :], in1=sinf, op=ALU.mult), [SAk, 'PHt'] + T2K, T2K)
                    S.op('pool', lambda e: e.tensor_tensor(out=Fv, in0=Fv, in1=cosf.unsqueeze(2).to_broadcast([128, NCH, 2, 16]), op=ALU.mult),
                         [SAk, 'PHt'], [SAk])
                    S.op('pool', lambda e: e.tensor_tensor(out=Fv[:, :, 0, :], in0=Fv[:, :, 0, :], in1=T2[:, :, 0, :], op=ALU.add), [SAk] + T2K, [SAk])
                    S.op('pool', lambda e: e.tensor_tensor(out=Fv[:, :, 1, :], in0=Fv[:, :, 1, :], in1=T2[:, :, 1, :], op=ALU.subtract), [SAk] + T2K, [SAk])
                    return T2, T2K

                def chain_scan():
                    for s_ in range(32):
                        q_ = s_ % 16
                        S.op('dve', lambda e, s_=s_, q_=q_: e.tensor_tensor_scan(
                            out=SAw[:, 1:NCH + 1, s_], data0=R8t[:, l, q_:q_ + 1].to_broadcast([128, NCH]), data1=SAw[:, 1:NCH + 1, s_],
                            initial=SAw[:, 0, s_:s_ + 1], op0=ALU.mult, op1=ALU.add), [SAk, 'R8t'], [SAk])

                def chain_post(T2, T2K):
                    Wv = SAw[:, 0:NCH, :].rearrange("p c (t q) -> p c t q", t=2)
                    cosb, sinb = PHt[:, 0, 0:NCH, :], PHt[:, 1, 0:NCH, :]
                    Spv = Sprev[:].rearrange("p (t q) c -> p c t q", t=2)
                    W64 = SAw[:, NCH, :]
                    S.op('pool', lambda e: e.tensor_tensor(out=ctmp[0][:, 0:16], in0=W64[:, 16:32], in1=PHt[:, 1, NCH, :], op=ALU.mult), [SAk, 'PHt'], ['ct0'])
                    S.op('pool', lambda e: e.tensor_tensor(out=ctmp[0][:, 16:32], in0=W64[:, 0:16], in1=PHt[:, 1, NCH, :], op=ALU.mult), [SAk, 'PHt', 'ct0'], ['ct0'])
                    S.op('pool', lambda e: e.tensor_tensor(out=ctmp[1][:, 0:16], in0=W64[:, 0:16], in1=PHt[:, 0, NCH, :], op=ALU.mult), [SAk, 'PHt'], ['ct1'])
                    S.op('pool', lambda e: e.tensor_tensor(out=ctmp[1][:, 16:32], in0=W64[:, 16:32], in1=PHt[:, 0, NCH, :], op=ALU.mult), [SAk, 'PHt', 'ct1'], ['ct1'])
                    S.op('pool', lambda e: e.tensor_tensor(out=SC[:, l, 0:16], in0=ctmp[1][:, 0:16], in1=ctmp[0][:, 0:16], op=ALU.subtract), ['ct0', 'ct1', SCk], [SCk])
                    S.op('pool', lambda e: e.tensor_tensor(out=SC[:, l, 16:32], in0=ctmp[1][:, 16:32], in1=ctmp[0][:, 16:32], op=ALU.add), ['ct0', 'ct1', SCk], [SCk])
                    S.op('pool', lambda e: e.tensor_tensor(out=T2[:, :, 0, :], in0=Wv[:, :, 1, :], in1=sinb, op=ALU.mult), [SAk, 'PHt'] + T2K, T2K)
                    S.op('pool', lambda e: e.tensor_tensor(out=T2[:, :, 1, :], in0=Wv[:, :, 0, :], in1=sinb, op=ALU.mult), [SAk, 'PHt'] + T2K, T2K)
                    S.op('pool', lambda e: e.tensor_tensor(out=Wv, in0=Wv, in1=cosb.unsqueeze(2).to_broadcast([128, NCH, 2, 16]), op=ALU.mult),
                         [SAk, 'PHt'], [SAk])
                    S.op('pool', lambda e: e.tensor_tensor(out=Spv[:, :, 0, :], in0=Wv[:, :, 0, :], in1=T2[:, :, 0, :], op=ALU.subtract), [SAk] + T2K, ['Sprev'])
                    S.op('pool', lambda e: e.tensor_tensor(out=Spv[:, :, 1, :], in0=Wv[:, :, 1, :], in1=T2[:, :, 1, :], op=ALU.add), [SAk] + T2K + ['Sprev'], ['Sprev'])

                def att_A(b):
                    b2 = bank2()
                    for kt in range(8):
                        S.op('pe', lambda e, b2=b2, kt=kt, b=b: e.matmul(pb(b2), lhsT=featT[:, kt, b * 128:(b + 1) * 128],
                                                                        rhs=wqkv[:, kt, 0:512], start=(kt == 0), stop=(kt == 7)),
                             ['featT', wqk], [pk(b2)])
                    for kt in range(8):
                        S.op('pe', lambda e, b2=b2, kt=kt, b=b: e.matmul(pb(b2 + 1, 0, 256), lhsT=featT[:, kt, b * 128:(b + 1) * 128],
                                                                        rhs=wqkv[:, kt, 512:768], start=(kt == 0), stop=(kt == 7)),
                             ['featT', wqk], [pk(b2 + 1)])
                    qk_ps = ps_t[b2 // 2][:, 0:640]
                    S.op('act', lambda e, qk_ps=qk_ps: e.activation(out=qkv_sq[:], in_=qk_ps, func=AF.Square),
                         [pk(b2), pk(b2 + 1)], ['qkv_sq'])
                    S.op('dve', lambda e: e.tensor_reduce(out=st1[:, 4:14], in_=qkv_sq[:].rearrange("p (h d) -> p h d", d=64),
                                                          axis=AX.X, op=ALU.add), ['qkv_sq'], ['st1'])
                    S.op('act', lambda e: e.activation(out=st1[:, 4:14], in_=st1[:, 4:14], func=AF.Sqrt, bias=epsc[:, 0:1],
                                                       scale=1.0 / 64), ['st1', 'epsc'], ['st1'])
                    S.op('dve', lambda e: e.reciprocal(out=st1[:, 4:14], in_=st1[:, 4:14]), ['st1'], ['st1'])
                    S.op('dve', lambda e, qk_ps=qk_ps: e.tensor_tensor(
                        out=qn[:, 0:512].rearrange("p (m t d) -> p t m d", m=4, t=2),
                        in0=qk_ps[:, 0:512].rearrange("p (t m d) -> p t m d", t=2, m=4),
                        in1=st1[:, 4:12].rearrange("p (t m) -> p t m", t=2).unsqueeze(3).to_broadcast([128, 2, 4, 64]), op=ALU.mult),
                        [pk(b2), pk(b2 + 1), 'st1'], ['qn'])
                    S.op('dve', lambda e, qk_ps=qk_ps: e.tensor_tensor(
                        out=qn[:, 512:640].rearrange("p (h d) -> p h d", d=64), in0=qk_ps[:, 512:640].rearrange("p (h d) -> p h d", d=64),
                        in1=st1[:, 12:14].unsqueeze(2).to_broadcast([128, 2, 64]), op=ALU.mult),
                        [pk(b2), pk(b2 + 1), 'st1', 'qn'], ['qn'])
                    S.op('act', lambda e, b2=b2, b=b: e.activation(
                        out=Vaug[:, l, b + 1, :, 0:64], in_=pb(b2 + 1, 128, 256).rearrange("p (g d) -> p g d", g=2),
                        func=AF.Copy), [pk(b2 + 1)], ['Vaug'])
                    tb = bank2()
                    for m in range(4):
                        S.op('pe', lambda e, tb=tb, m=m: e.transpose(
                            out=pb(tb, m * 128, (m + 1) * 128),
                            in_=qn[:, m * 128:(m + 1) * 128], identity=ident[:]),
                            ['qn', 'ident'], [pk(tb)])
                    S.op('pe', lambda e, tb=tb: e.transpose(out=pb(tb + 1, 0, 128), in_=qn[:, 512:640], identity=ident[:]),
                         ['qn', 'ident'], [pk(tb + 1)])
                    S.op('act', lambda e, tb=tb, b=b: e.activation(
                        out=qT[:, b, :], in_=pb(tb), func=AF.Identity,
                        scale=qsc[:, l:l + 1]), [pk(tb), 'qsc'], ['qT'])
                    S.op('act', lambda e, tb=tb, b=b: e.activation(
                        out=kT[:, l, 128 + b * 128:128 + (b + 1) * 128], in_=pb(tb + 1, 0, 128), func=AF.Identity,
                        scale=ksc[:, l:l + 1]), [pk(tb + 1), 'ksc'], [kTk])

                BST = {}

                def att_B1(b):
                    first = (ti == 0 and b == 0)
                    tiles = [1] if first else [0, 1]
                    BST[b] = tiles
                    for g in range(2):
                        gs = slice(64 * g, 64 * g + 64)
                        pt = PT[g]; ptk = f'PT{g}'
                        for tl in tiles:
                            sb_ = bank()
                            kcol = b * 128 + tl * 128
                            S.op('pe', lambda e, sb_=sb_, gs=gs, kcol=kcol, b=b: e.matmul(
                                pb(sb_), lhsT=kT[gs, l, kcol:kcol + 128], rhs=qT[gs, b, :],
                                start=True, stop=True), [kTk, 'qT'], [pk(sb_)])
                            ssb_ = S_sb[tl]; ssk = SSK[tl]
                            S.op('dve', lambda e, sb_=sb_, ssb_=ssb_, tl=tl, g=g: e.tensor_tensor(
                                out=ssb_[:].rearrange("p (m q) -> p m q", m=4), in0=pb(sb_).rearrange("p (m q) -> p m q", m=4),
                                in1=biasT[:, tl, 4 * g:4 * g + 4, :], op=ALU.add), [pk(sb_), 'biasT'], [ssk])
                            S.op('act', lambda e, ssb_=ssb_, pt=pt, tl=tl: e.activation(out=pt[:, tl, :], in_=ssb_[:], func=AF.Exp),
                                 [ssk], [ptk])

                def att_B2(b):
                    tiles = BST[b]
                    ob = bank2()
                    for g in range(2):
                        pt = PT[g]; ptk = f'PT{g}'
                        for m in range(4):
                            for ii, tl in enumerate(tiles):
                                S.op('pe', lambda e, ob=ob, g=g, m=m, tl=tl, ii=ii, pt=pt, b=b, n_=len(tiles): e.matmul(
                                    pb(ob + g, m * 65, m * 65 + 65), lhsT=pt[:, tl, m * 128:(m + 1) * 128],
                                    rhs=Vaug[:, l, b + tl, g, :], start=(ii == 0), stop=(ii == n_ - 1)),
                                    [ptk, 'Vaug'], [pk(ob + g)])
                    at = tok[b % 2]; atk = f'tok{b % 2}'
                    for g in range(2):
                        o3 = pb(ob + g, 0, 260).rearrange("p (m d) -> p m d", m=4)
                        S.op('dve', lambda e, o3=o3, g=g: e.tensor_tensor(out=st1[:, 0:4], in0=o3[:, :, 64], in1=esink[:, l, 4 * g:4 * g + 4],
                                                                          op=ALU.add), [pk(ob + g), 'esink', 'st1'], ['st1'])
                        S.op('dve', lambda e: e.reciprocal(out=st1[:, 0:4], in_=st1[:, 0:4]), ['st1'], ['st1'])
                        S.op('dve', lambda e, o3=o3, g=g, at=at: e.tensor_tensor(
                            out=at[:, 256 * g:256 * g + 256].rearrange("p (m d) -> p m d", m=4), in0=o3[:, :, 0:64],
                            in1=st1[:, 0:4].unsqueeze(2).to_broadcast([128, 4, 64]), op=ALU.mult),
                            [pk(ob + g), 'st1', atk], [atk])
                    S.op('act', lambda e, at=at, b=b: e.activation(out=qkv_sq[:, 0:512], in_=at[:, 0:512], func=AF.Square,
                                                                   accum_out=rs_a[:, b:b + 1]), [atk, 'qkv_sq'], ['qkv_sq', 'rs_a'])
                    rstd_pow(rs_a[:, b:b + 1], rs_a[:, b:b + 1], 1.0 / 512, 1, ['rs_a'], ['rs_a'])

                def att_B3(b):
                    at = tok[b % 2]; atk = f'tok{b % 2}'
                    tb = bank()
                    for m in range(4):
                        S.op('pe', lambda e, tb=tb, m=m, at=at: e.transpose(out=pb(tb, m * 128, (m + 1) * 128),
                                                                           in_=at[:, m * 128:(m + 1) * 128], identity=ident[:]),
                             [atk, 'ident'], [pk(tb)])
                    for m in range(4):
                        S.op('act', lambda e, tb=tb, m=m, b=b: e.activation(
                            out=attnT[:, m, b * 128:(b + 1) * 128], in_=pb(tb, m * 128, (m + 1) * 128), func=AF.Identity,
                            scale=ona[:, l, m:m + 1]), [pk(tb), 'ona'], ['attnT'])

                def Y_tile(t6):
                    n_ = nr6(t6)
                    b_ = bank()
                    for i in range(TC):
                        for j in range(i + 1):
                            S.op('pe', lambda e, b_=b_, i=i, j=j, t6=t6, n_=n_: e.matmul(
                                pb(b_, i * 64, i * 64 + 64)[0:n_, :], lhsT=KTt[0:n_, i - j, t6, 0:n_], rhs=uT[0:n_, t6, j, :],
                                start=(j == 0), stop=False, skip_group_check=True),
                                ['tab', 'uT'], [pk(b_)])
                        for q in range(np6(t6)):
                            pr = t6 * 3 + q
                            for part in range(2):
                                last = (q == np6(t6) - 1 and part == 1)
                                S.op('pe', lambda e, b_=b_, i=i, q=q, pr=pr, part=part, last=last: e.matmul(
                                    pb(b_, i * 64, i * 64 + 64)[32 * q:32 * q + 32, :], lhsT=CSt[:, i, part, pr, :],
                                    rhs=Sprev[:, part * 16 + pr, :], start=False, stop=last, skip_group_check=True),
                                    ['tab', 'Sprev'], [pk(b_)])
                    yb = ysb[t6 % 2]; yk = YSK[t6 % 2]; yt_ = ytmp[t6 % 2]; ytk = YTK[t6 % 2]
                    S.op('dve', lambda e, b_=b_, t6=t6, yb=yb, n_=n_: e.scalar_tensor_tensor(
                        out=yb[0:n_, :], in0=uT[0:n_, t6].rearrange("p j c -> p (j c)"), scalar=dT[0:n_, l, t6:t6 + 1], in1=pb(b_)[0:n_, :],
                        op0=ALU.mult, op1=ALU.add), ['uT', 'dT', pk(b_)], [yk])
                    S.op('act', lambda e, yb=yb, yt_=yt_, n_=n_: e.activation(out=yt_[0:n_, :], in_=yb[0:n_, :], func=AF.Square), [yk], [ytk])
                    S.op('dve', lambda e, yt_=yt_, n_=n_: e.tensor_scalar(out=yt_[0:n_, :], in0=yt_[0:n_, :], scalar1=0.044715, scalar2=1.0,
                                                                           op0=ALU.mult, op1=ALU.add), [ytk], [ytk])
                    S.op('dve', lambda e, yt_=yt_, yb=yb, n_=n_: e.tensor_tensor(out=yt_[0:n_, :], in0=yt_[0:n_, :], in1=yb[0:n_, :], op=ALU.mult),
                         [ytk, yk], [ytk])
                    S.op('act', lambda e, yt_=yt_, n_=n_: e.activation(out=yt_[0:n_, :], in_=yt_[0:n_, :], func=AF.Tanh, scale=0.7978845608),
                         [ytk], [ytk])
                    S.op('dve', lambda e, yt_=yt_, yb=yb, t6=t6, n_=n_: e.scalar_tensor_tensor(
                        out=zT[0:n_, t6, :].rearrange("p (c j) -> p j c", j=TC), in0=yt_[0:n_, :].rearrange("p (j c) -> p j c", j=TC),
                        scalar=1.0, in1=yb[0:n_, :].rearrange("p (j c) -> p j c", j=TC), op0=ALU.add, op1=ALU.mult), [ytk, yk], ['zT'])
                T2, T2K = chain_pre()
                att_A(0)
                att_A(1)
                chain_scan()
                att_A(2)
                att_A(3)
                chain_post(T2, T2K)
                att_B1(0); Y_tile(0); att_B2(0); Y_tile(1); att_B3(0)
                att_B1(1); Y_tile(2); att_B2(1); Y_tile(3); att_B3(1)
                att_B1(2); Y_tile(4); att_B2(2); Y_tile(5); att_B3(2)
                att_B1(3); att_B2(3); att_B3(3)
                S.op('pool', lambda e: e.tensor_copy(out=kT[:, l, 0:128], in_=kT[:, l, TT:TT + 128]), [kTk], [kTk])
                S.op('pool', lambda e: e.tensor_copy(out=Vaug[:, l, 0, :, :], in_=Vaug[:, l, NB, :, :]), ['Vaug'], ['Vaug'])
                if not (ti == nt - 1 and l == 1):
                    S.dma('sp', tab[:, 0:12288], bf_s[1 - l].rearrange("p a b c d -> p (a b c d)"), [], ['tab'], 'tab')
                ssb = bank()
                for m in range(NT6):
                    no = nr6(m)
                    b_ = bank()
                    if b_ == ssb:
                        b_ = bank()
                    for t6 in range(NT6):
                        n_ = nr6(t6)
                        S.op('pe', lambda e, b_=b_, m=m, t6=t6, n_=n_, no=no: e.matmul(
                            pb(b_)[0:no, :], lhsT=wgl[0:n_, t6, m * 96:m * 96 + no], rhs=zT[0:n_, t6, :],
                            start=(t6 == 0), stop=(t6 == NT6 - 1)), [wgk, 'zT'], [pk(b_)])
                    yb = ysb[m % 2]; yk = YSK[m % 2]; yt_ = ytmp[m % 2]; ytk = YTK[m % 2]
                    S.op('act', lambda e, b_=b_, m=m, yb=yb, no=no: e.activation(out=yb[0:no, :], in_=pb(b_)[0:no, :], func=AF.Tanh,
                                                                                bias=bglu[0:no, l, m:m + 1], scale=0.25),
                         [pk(b_), 'bglu'], [yk])
                    S.op('dve', lambda e, m=m, yb=yb, no=no: e.scalar_tensor_tensor(out=yb[0:no, :], in0=yb[0:no, :], scalar=1.0, in1=zT[0:no, m, :],
                                                                                    op0=ALU.add, op1=ALU.mult), [yk, 'zT'], [yk])
                    S.op('act', lambda e, yb=yb, yt_=yt_, no=no: e.activation(out=yt_[0:no, :], in_=yb[0:no, :], func=AF.Square), [yk], [ytk])
                    S.op('dve', lambda e, m=m, yb=yb, no=no: e.tensor_scalar(out=ssmT[0:no, m, :], in0=yb[0:no, :], scalar1=ons[0:no, l, m:m + 1],
                                                                              scalar2=None, op0=ALU.mult), [yk, 'ons'], ['ssmT'])
                    for b in range(NB):
                        S.op('pe', lambda e, m=m, b=b, yt_=yt_, ssb=ssb, no=no: e.matmul(
                            pb(ssb)[:, m * NB + b:m * NB + b + 1], lhsT=yt_[0:no, b * 128:(b + 1) * 128], rhs=ones_f[0:no, 0:1],
                            start=True, stop=True), [ytk, 'ones_f'], [pk(ssb)])
                S.op('dve', lambda e, ssb=ssb: e.tensor_reduce(out=rs_s[:], in_=pb(ssb)[:, 0:NT6 * NB].rearrange("p (m b) -> p b m", b=NB),
                                                              axis=AX.X, op=ALU.add), [pk(ssb)], ['rs_s'])
                rstd_pow(rs_s[:], rs_s[:], 1.0 / (16 * 512), NB, ['rs_s'], ['rs_s'])
                for half in range(2):
                    wo_t, wok = wslot()
                    woh = wo_t[:, 0:5120].rearrange("p (k c) -> p k c", k=10)
                    S.dma('sp', wo_t[:, 0:5120], wout_s[l, half], [], [wok], wok)
                    for b in range(NB):
                        ba = bank(); bb_ = bank()
                        for t6 in range(NT6):
                            n_ = nr6(t6)
                            S.op('pe', lambda e, ba=ba, t6=t6, b=b, n_=n_, woh=woh: e.matmul(pb(ba), lhsT=ssmT[0:n_, t6, b * 128:(b + 1) * 128],
                                                                                   rhs=woh[0:n_, t6, :], start=(t6 == 0), stop=(t6 == NT6 - 1)),
                                 ['ssmT', wok], [pk(ba)])
                        for ft in range(4):
                            S.op('pe', lambda e, bb_=bb_, ft=ft, b=b, woh=woh: e.matmul(pb(bb_), lhsT=attnT[:, ft, b * 128:(b + 1) * 128],
                                                                              rhs=woh[:, 6 + ft, :], start=(ft == 0), stop=(ft == 3)),
                                 ['attnT', wok], [pk(bb_)])
                        xs = xt[:, b, half * 512:(half + 1) * 512]
                        S.op('dve', lambda e, ba=ba, b=b, xs=xs: e.scalar_tensor_tensor(
                            out=xs, in0=pb(ba), scalar=rs_s[:, b:b + 1], in1=xs, op0=ALU.mult, op1=ALU.add),
                            [pk(ba), 'rs_s', XK[b]], [XK[b]])
                        S.op('dve', lambda e, bb_=bb_, b=b, xs=xs: e.scalar_tensor_tensor(
                            out=xs, in0=pb(bb_), scalar=rs_a[:, b:b + 1], in1=xs, op0=ALU.mult, op1=ALU.add),
                            [pk(bb_), 'rs_a', XK[b]], [XK[b]])
                rms_to_featT(l, gB, shB)
                ups_all = {}

                def stage_up(v):
                    dgv = dg[v % 2]; dgk = f'dg{v % 2}'
                    ups = []
                    for vg in range(2):
                        wc, wk = load_chunk(wup_s[l, vg * NV + v].rearrange("p (k c) -> p k c", k=8))
                        b_ = bank()
                        for kt in range(8):
                            S.op('pe', lambda e, b_=b_, kt=kt, wc=wc: e.matmul(pb(b_), lhsT=wc[:, kt, :], rhs=featT[:, kt, :],
                                                                               start=(kt == 0), stop=(kt == 7)),
                                 [wk, 'featT'], [pk(b_)])
                        ups.append(b_)
                    ups_all[v] = ups
                    S.op('pool', lambda e, dgv=dgv, v=v: e.tensor_tensor(
                        out=dgv[:, :, :].rearrange("p (g j) c -> p g j c", g=2),
                        in0=identb[:].unsqueeze(1).unsqueeze(1).to_broadcast([128, 2, 3, 128]),
                        in1=cw[:, l].rearrange("p (g v) j -> p g v j", g=2)[:, :, v, :].unsqueeze(3).to_broadcast([128, 2, 3, 128]),
                        op=ALU.mult), ['identb', 'cw', dgk], [dgk])

                def stage_mid(v):
                    Uv = U[v % 2]; Uk = f'U{v % 2}'; dgv = dg[v % 2]; dgk = f'dg{v % 2}'
                    sg = sgate[v % 2]; sgk = YSK[v % 2]
                    ups = ups_all.pop(v)
                    S.op('pool', lambda e, Uv=Uv, v=v: e.tensor_copy(out=Uv[:, :, 0:2], in_=HB[:, l, v, :, :]), [HBk, Uk], [Uk])
                    for vg in range(2):
                        S.op('act', lambda e, Uv=Uv, vg=vg, b_=ups[vg]: e.activation(out=Uv[:, vg, 2:TT + 2], in_=pb(b_), func=AF.Copy),
                             [pk(ups[vg]), Uk], [Uk])
                    S.op('pool', lambda e, Uv=Uv, v=v: e.tensor_copy(out=HB[:, l, v, :, :], in_=Uv[:, :, TT:TT + 2]), [Uk, HBk], [HBk])
                    cps = []
                    for vg in range(2):
                        b_ = bank()
                        for j in range(3):
                            S.op('pe', lambda e, b_=b_, vg=vg, j=j, dgv=dgv, Uv=Uv: e.matmul(
                                pb(b_), lhsT=dgv[:, vg * 3 + j, :], rhs=Uv[:, vg, j:j + TT], start=(j == 0), stop=(j == 2)),
                                [dgk, Uk], [pk(b_)])
                        cps.append(b_)
                    S.op('act', lambda e, sg=sg, b_=cps[1], v=v: e.activation(out=sg[:], in_=pb(b_), func=AF.Silu,
                                                                             bias=cb[:, l, NV + v:NV + v + 1], scale=1.0),
                         [pk(cps[1]), 'cb'], [sgk])
                    S.op('dve', lambda e, sg=sg, b_=cps[0], v=v: e.scalar_tensor_tensor(
                        out=actT[:, v, :], in0=pb(b_), scalar=cb[:, l, v:v + 1], in1=sg[:], op0=ALU.add, op1=ALU.mult),
                        [pk(cps[0]), 'cb', sgk], ['actT'])

                stage_up(0)
                for v in range(NV):
                    if v + 1 < NV:
                        stage_up(v + 1)
                    stage_mid(v)
                for qt in range(4):
                    wd_t, wdk = wslot()
                    wdh = wd_t[:, 0:5632].rearrange("p (k c) -> p k c", k=NV)
                    S.dma('sp', wd_t[:, 0:5632], wdn_s[l, qt], [], [wdk], wdk)
                    for b in range(NB):
                        b_ = bank()
                        for v in range(NV):
                            S.op('pe', lambda e, b_=b_, v=v, b=b, wdh=wdh: e.matmul(pb(b_, 0, 256), lhsT=actT[:, v, b * 128:(b + 1) * 128],
                                                                          rhs=wdh[:, v, :], start=(v == 0), stop=(v == NV - 1)),
                                 ['actT', wdk], [pk(b_)])
                        xs = xt[:, b, qt * 256:(qt + 1) * 256]
                        S.op('dve', lambda e, b_=b_, xs=xs: e.tensor_tensor(out=xs, in0=pb(b_, 0, 256), in1=xs, op=ALU.add),
                             [pk(b_), XK[b]], [XK[b]])
                        if l == 1 and qt == 3:
                            tk = tok[b % 2]; tkk = f'tok{b % 2}'
                            S.op('act', lambda e, tk=tk, b=b: e.activation(out=tk[:], in_=xt[:, b, :], func=AF.Copy), [XK[b], tkk], [tkk])
                            S.dma('sp', y_d[ti * TT + b * 128: ti * TT + (b + 1) * 128, :], tk[:], [tkk], ['y'], f'yst{b % 2}')
                            if ti + 1 < nt:
                                S.dma('sp', xt[:, b, :], x_d[(ti + 1) * TT + b * 128:(ti + 1) * TT + (b + 1) * 128, :], [], [XK[b]], f'xld{b}')

        S.dma('sp', xt[:], x_d[0:TT, :].rearrange("(b p) f -> p b f", p=128), [], XK, 'xt')
        for ti in range(nt):
            for l in range(2):
                tile_layer(ti, l)
        S.wait_all('sp')
        block = es.enter_context(nc.Block())
        S.emit(block)
    return nc


def _bucket_onehot():
    nb, md = 32, 128
    idx = np.arange(384)
    dist = idx - 127
    n = np.maximum(dist, 0)
    max_exact = nb // 2
    log_part = np.log(np.maximum(n, 1) / max_exact) / np.log(md / max_exact)
    large = max_exact + (log_part * (nb - max_exact)).astype(np.int32)
    large = np.minimum(large, nb - 1)
    bucket = np.where(n < max_exact, n, large).astype(np.int32)
    valid = (dist >= 0) & (dist < 128)
    oh = np.zeros((33, 384), np.float32)
    for i in range(384):
        if valid[i]:
            oh[bucket[i], i] = 1.0
        else:
            oh[32, i] = 1.0
    return oh


def prep_inputs(b, x, c, w_mod, b_mod, norm1_w, w_in, lam_re, lam_im, log_dt, ssm_b_re, ssm_b_im, ssm_c_re, ssm_c_im,
                ssm_d, w_glu, b_glu, q_norm_w, k_norm_w, rel_bias, sinks, out_norm_ssm, out_norm_attn, w_out, norm2_w,
                w_up, conv_w, conv_b, w_down):
    f = lambda a: np.ascontiguousarray(a, dtype=np.float32)

    def fm(a, n):
        return f(np.asarray(a).reshape(2, n, 128).transpose(0, 2, 1))

    def pairlay(a):
        a = np.asarray(a)
        sh = a.shape
        a = a.reshape(2, 16, 2, 64, *sh[3:])
        a = np.moveaxis(a, 1, 3)
        return f(a.reshape(2, 128, 16, *sh[3:]))
    m = {}
    m["x"] = f(x[b])
    m["cT"] = f(np.asarray(c[b]).reshape(8, 128).T)
    m["w_mod"] = f(w_mod); m["b_mod"] = f(b_mod)
    m["n1T"] = fm(norm1_w, 8); m["n2T"] = fm(norm2_w, 8)
    m["w_in"] = f(w_in); m["w_out"] = f(w_out); m["w_up"] = f(w_up); m["w_down"] = f(w_down); m["w_glu"] = f(w_glu)
    m["lamre"] = pairlay(lam_re); m["lamim"] = pairlay(lam_im)
    m["ldt"] = pairlay(np.repeat(np.asarray(log_dt)[:, :, None], 64, axis=2))
    m["bre"] = pairlay(ssm_b_re); m["bim"] = pairlay(ssm_b_im)
    m["cre"] = pairlay(np.asarray(ssm_c_re).transpose(0, 1, 3, 2)); m["cim"] = pairlay(np.asarray(ssm_c_im).transpose(0, 1, 3, 2))
    def fm96(a):
        a = np.asarray(a, dtype=np.float32)
        o = np.zeros((2, 6, 128), np.float32)
        pad = np.zeros((2, 576), np.float32)
        pad[:, :512] = a
        o[:, :, :96] = pad.reshape(2, 6, 96)
        return f(o.transpose(0, 2, 1))
    m["dT"] = fm96(ssm_d); m["bgluT"] = fm96(b_glu)
    m["qnw"] = f(np.tile(np.asarray(q_norm_w), (1, 2))[:, :, None]); m["knw"] = f(np.tile(np.asarray(k_norm_w), (1, 2))[:, :, None])
    m["relb"] = f(rel_bias); m["oneh"] = _bucket_onehot(); m["sinks"] = f(sinks)
    m["onsT"] = fm96(out_norm_ssm); m["onaT"] = fm(out_norm_attn, 4)
    m["cwT"] = f(np.asarray(conv_w).reshape(2, 3, 44, 128).transpose(0, 3, 2, 1))
    m["cbT"] = fm(conv_b, 44)
    m["ident"] = np.eye(128, dtype=np.float32)
    m["antiI"] = np.ascontiguousarray(np.eye(128, dtype=np.float32)[::-1])
    return m


def kernel(**inputs):
    x = np.asarray(inputs["x"])
    B, seq, _ = x.shape
    nc = build(seq)
    maps = [prep_inputs(ci % B, **inputs) for ci in range(8)]
    res = run_bass_kernel_spmd(nc, maps, core_ids=list(range(8)))
    out = np.stack([np.asarray(res.results[b]["y"]) for b in range(B)], axis=0)
    return out.astype(np.float32)
```

```python
import numpy as np
from contextlib import ExitStack
import concourse.bass as bass
import concourse.mybir as mybir
from concourse.bass_utils import run_bass_kernel_spmd

F32 = mybir.dt.float32
BF = mybir.dt.bfloat16
AF = mybir.ActivationFunctionType
ALU = mybir.AluOpType
AX = mybir.AxisListType

D = 1024
TT = 512
NB = 4
TC = 8
NCH = TT // TC
DFF = 2816
NV = 22
EPS = 1e-6
NEG = -30000.0
NT6 = 6
STAGE = 99


def nr6(t):
    return 96 if t < 5 else 32


def np6(t):
    return 3 if t < 5 else 1


class Sched:
    ROT = 30000

    def __init__(s, nc, es):
        s.nc = nc
        s.es = es
        s.prog = {e: [] for e in ('pe', 'act', 'dve', 'pool', 'sp')}
        s.cnt = {e: 0 for e in s.prog}
        s.sem = {}
        s.allsems = []
        s.nsem = 0
        for e in s.prog:
            s._newsem(e)
        s.waited = {e: {} for e in s.prog}
        s.res = {}
        s.dsem = {}

    def _newsem(s, e):
        s.nsem += 1
        sm = s.es.enter_context(s.nc.semaphore(f"s_{e}_{s.nsem}"))
        s.sem[e] = sm
        s.cnt[e] = 0
        s.allsems.append([sm, 0])
        s.cur = getattr(s, 'cur', {})
        s.cur[e] = s.allsems[-1]

    def _deps(s, reads, writes):
        deps = {}

        def add(tok):
            if tok is None:
                return
            sem, val = tok
            k = id(sem)
            if k not in deps or deps[k][1] < val:
                deps[k] = (sem, val)
        for k in reads:
            r = s.res.get(k)
            if r:
                add(r[0])
        for k in writes:
            r = s.res.get(k)
            if r:
                add(r[0])
                for t in r[1].values():
                    add(t)
        return deps

    def _emit_waits(s, e, deps):
        for k, (sem, val) in deps.items():
            if s.waited[e].get(k, 0) < val:
                s.waited[e][k] = val
                s.prog[e].append(('w', sem, val))

    def _record(s, tok, reads, writes):
        kk = id(tok[0])
        for k in reads:
            r = s.res.setdefault(k, [None, {}])
            if kk not in r[1] or r[1][kk][1] < tok[1]:
                r[1][kk] = tok
        for k in writes:
            s.res[k] = [tok, {}]

    def op(s, e, fn, reads=(), writes=()):
        deps = s._deps(reads, writes)
        if e == 'pe':
            deps.pop(id(s.sem['pe']), None)
        s._emit_waits(e, deps)
        if s.cnt[e] >= s.ROT:
            s._newsem(e)
        s.cnt[e] += 1
        s.cur[e][1] = s.cnt[e]
        tok = (s.sem[e], s.cnt[e])
        s.prog[e].append(('i', fn, tok[0]))
        s._record(tok, reads, writes)

    def dma(s, e, out, in_, reads, writes, semkey, **kw):
        d = s.dsem.get(semkey)
        if d is None:
            sm = s.es.enter_context(s.nc.semaphore(f"d_{len(s.dsem)}"))
            d = [sm, 0]
            s.dsem[semkey] = d
            s.allsems.append(d)
        deps = s._deps(reads, writes)
        s._emit_waits(e, deps)
        d[1] += 16
        tok = (d[0], d[1])
        s.prog[e].append(('d', out, in_, tok[0], kw))
        s._record(tok, reads, writes)

    def barrier(s):
        for e in s.prog:
            deps = {id(sm): (sm, v) for sm, v in s.allsems if v > 0}
            if e == 'pe':
                deps.pop(id(s.sem['pe']), None)
            s._emit_waits(e, deps)
        s.res = {}

    def wait_all(s, e):
        deps = {id(sm): (sm, v) for sm, v in s.allsems if v > 0}
        s._emit_waits(e, deps)

    def emit(s, block):
        def run(eng, lst):
            for it in lst:
                if it[0] == 'w':
                    eng.wait_ge(it[1], it[2])
                elif it[0] == 'i':
                    it[1](eng).then_inc(it[2], 1)
                else:
                    eng.dma_start(out=it[1], in_=it[2], allow_slow_non_contiguous=True, **it[4]).then_inc(it[3], 16)

        @block.tensor
        def _(e):
            run(e, s.prog['pe'])

        @block.scalar
        def _(e):
            run(e, s.prog['act'])

        @block.vector
        def _(e):
            run(e, s.prog['dve'])

        @block.gpsimd
        def _(e):
            run(e, s.prog['pool'])

        @block.sync
        def _(e):
            run(e, s.prog['sp'])


def build(seq, dbg=False):
    nt = seq // TT
    nc = bass.Bass("TRN2", target_bir_lowering=False)

    def din(name, shape, dt=F32):
        return nc.dram_tensor(name, list(shape), dt, kind="ExternalInput").ap()

    def dscr(name, shape, dt=BF):
        return nc.dram_tensor(name, list(shape), dt, kind="Internal").ap()

    x_d = din("x", [seq, D])
    cT_d = din("cT", [128, 8])
    wmod_d = din("w_mod", [2, D, 6 * D])
    bmod_d = din("b_mod", [2, 6 * D])
    n1T_d = din("n1T", [2, 128, 8])
    n2T_d = din("n2T", [2, 128, 8])
    win_d = din("w_in", [2, D, 1280])
    wout_d = din("w_out", [2, D, D])
    wup_d = din("w_up", [2, D, 2 * DFF])
    wdn_d = din("w_down", [2, DFF, D])
    wglu_d = din("w_glu", [2, 512, 512])
    lamre_d = din("lamre", [2, 128, 16])
    lamim_d = din("lamim", [2, 128, 16])
    ldt_d = din("ldt", [2, 128, 16])
    bre_d = din("bre", [2, 128, 16, 16])
    bim_d = din("bim", [2, 128, 16, 16])
    cre_d = din("cre", [2, 128, 16, 16])
    cim_d = din("cim", [2, 128, 16, 16])
    dT_d = din("dT", [2, 128, 6])
    bgluT_d = din("bgluT", [2, 128, 6])
    qnw_d = din("qnw", [2, 128, 1])
    knw_d = din("knw", [2, 128, 1])
    relb_d = din("relb", [32, 8])
    oneh_d = din("oneh", [33, 384])
    sinks_d = din("sinks", [2, 8])
    onsT_d = din("onsT", [2, 128, 6])
    onaT_d = din("onaT", [2, 128, 4])
    cwT_d = din("cwT", [2, 128, 44, 3])
    cbT_d = din("cbT", [2, 128, 44])
    ident_d = din("ident", [128, 128])
    anti_d = din("antiI", [128, 128])
    y_d = nc.dram_tensor("y", [seq, D], F32, kind="ExternalOutput").ap()

    winu_s = dscr("winu_s", [2, 6, 128, 8 * 96])
    winq_s = dscr("winq_s", [2, 128, 8, 768])
    wout_s = dscr("wout_s", [2, 2, 128, 10 * 512])
    wglu_s = dscr("wglu_s", [2, 128, 6, 512])
    wup_s = dscr("wup_s", [2, 44, 128, 8 * 128])
    wdn_s = dscr("wdn_s", [2, 4, 128, NV * 256])
    kt_s = dscr("kt_s", [2, 128, 8, 6, 128])
    bf_s = dscr("bf_s", [2, 128, 6, 8, 2, 128])
    cs_s = dscr("cs_s", [2, 128, 8, 2, 16, 32])
    bd_s = dscr("bd_s", [8, 384], F32)
    ph_s = dscr("ph_s", [2, 128, 2, NCH + 1, 16], F32)

    with ExitStack() as es:
        S = Sched(nc, es)

        def sb(name, shape, dt=F32):
            return es.enter_context(nc.sbuf_tensor(name, list(shape), dt))

        ident = sb("ident_sb", [128, 128])
        identb = sb("identb", [128, 128], BF)
        ones_f = sb("ones_f", [128, 1])
        nhalf = sb("nhalf", [128, 10])
        epsc = sb("epsc", [128, 1])
        gA = sb("gA", [128, 2, 8]); shA = sb("shA", [128, 2, 8])
        gB = sb("gB", [128, 2, 8]); shB = sb("shB", [128, 2, 8])
        dT = sb("dTt", [128, 2, 6]); bglu = sb("bglu", [128, 2, 6])
        qsc = sb("qsc", [128, 2]); ksc = sb("ksc", [128, 2])
        ons = sb("ons", [128, 2, 6]); ona = sb("ona", [128, 2, 4])
        cw = sb("cw", [128, 2, 44, 3]); cb = sb("cb", [128, 2, 44])
        esink = sb("esink", [128, 2, 8])
        AT1 = sb("AT1", [128, 2, 32]); AT2 = sb("AT2", [128, 2, 32])
        biasT = sb("biasT", [128, 2, 8, 128])
        R8t = sb("R8t", [128, 2, 16])
        SC = sb("SC", [128, 2, 32])
        kT = sb("kT", [128, 2, 128 + TT], BF)
        Vaug = sb("Vaug", [128, 2, NB + 1, 2, 65], BF)
        HB = sb("HB", [128, 2, NV, 2, 2], BF)

        ps_t = [es.enter_context(nc.psum_tensor(f"ps{i}", [128, 1024], F32)) for i in range(4)]
        state = {'bank': 0}

        def bank():
            b = state['bank']
            state['bank'] = (b + 1) % 8
            return b

        def bank2():
            b = state['bank']
            if b % 2:
                b = (b + 1) % 8
            state['bank'] = (b + 2) % 8
            return b

        def pb(b, lo=0, hi=512):
            return ps_t[b // 2][:, (b % 2) * 512 + lo:(b % 2) * 512 + hi]

        def pk(b):
            return ('ps', b)

        pes = ExitStack()

        def psb(name, shape, dt=F32):
            return pes.enter_context(nc.sbuf_tensor(name, list(shape), dt))

        S.dma('sp', ident[:], ident_d[:, :], [], ['ident'], 'c0')
        S.op('dve', lambda e: e.tensor_copy(out=identb[:], in_=ident[:]), ['ident'], ['identb'])
        S.op('dve', lambda e: e.memset(ones_f[:], 1.0), [], ['ones_f'])
        S.op('dve', lambda e: e.memset(nhalf[:], -0.5), [], ['nhalf'])
        S.op('dve', lambda e: e.memset(epsc[:], EPS), [], ['epsc'])
        S.op('pool', lambda e: e.memset(SC[:], 0.0), [], ['SC0', 'SC1'])
        S.op('pool', lambda e: e.memset(kT[:], 0.0), [], ['kT0', 'kT1'])
        S.op('pool', lambda e: e.memset(Vaug[:], 0.0), [], ['Vaug'])
        S.op('pool', lambda e: e.memset(Vaug[:, :, :, :, 64:65], 1.0), [], ['Vaug'])
        S.op('pool', lambda e: e.memset(HB[:], 0.0), [], ['HB0', 'HB1'])

        small = [(dT, dT_d, 'dT'), (bglu, bgluT_d, 'bglu'), (ons, onsT_d, 'ons'), (ona, onaT_d, 'ona'),
                 (cb, cbT_d, 'cb')]
        for i, (t, d_, k) in enumerate(small):
            S.dma('sp', t[:], d_.rearrange("l p a -> p l a"), [], [k], f'c{i + 1}')
        S.dma('sp', cw[:], cwT_d.rearrange("l p a b -> p l a b"), [], ['cw'], 'c6')
        S.op('dve', lambda e: e.tensor_scalar(out=bglu[:], in0=bglu[:], scalar1=0.5, scalar2=None, op0=ALU.mult), ['bglu'], ['bglu'])
        S.op('dve', lambda e: e.tensor_scalar(out=ons[:], in0=ons[:], scalar1=0.25, scalar2=None, op0=ALU.mult), ['ons'], ['ons'])
        qn_t = psb("qn_t", [128, 2, 1]); kn_t = psb("kn_t", [128, 2, 1])
        S.dma('sp', qn_t[:], qnw_d.rearrange("l p a -> p l a"), [], ['qn_t'], 'c7')
        S.dma('sp', kn_t[:], knw_d.rearrange("l p a -> p l a"), [], ['kn_t'], 'c8')
        S.op('dve', lambda e: e.tensor_scalar(out=qsc[:], in0=qn_t[:, :, 0], scalar1=0.125, scalar2=None, op0=ALU.mult),
             ['qn_t'], ['qsc'])
        S.op('dve', lambda e: e.tensor_copy(out=ksc[:], in_=kn_t[:, :, 0]), ['kn_t'], ['ksc'])
        sk_t = psb("sk_t", [128, 2, 8])
        S.dma('sp', sk_t[:].rearrange("p l h -> p (l h)"),
              sinks_d.rearrange("l h -> (l h)").unsqueeze(0).to_broadcast([128, 16]), [], ['sk_t'], 'c9')
        S.op('act', lambda e: e.activation(out=esink[:], in_=sk_t[:], func=AF.Exp), ['sk_t'], ['esink'])

        rb = psb("rb", [33, 8]); oh = psb("oh", [33, 384])
        S.op('dve', lambda e: e.memset(rb[:], NEG), [], ['rb'])
        S.dma('sp', rb[0:32, :], relb_d[:, :], [], ['rb'], 'c10')
        S.dma('sp', oh[:], oneh_d[:, :], [], ['oh'], 'c11')
        b0 = bank()
        S.op('pe', lambda e: e.matmul(pb(b0)[0:8, 0:384], lhsT=rb[:, :], rhs=oh[:, :], start=True, stop=True),
             ['rb', 'oh'], [pk(b0)])
        bd_sb = psb("bd_sb", [8, 384])
        S.op('dve', lambda e: e.tensor_copy(out=bd_sb[:], in_=pb(b0)[0:8, 0:384]), [pk(b0)], ['bd_sb'])
        S.dma('sp', bd_s[:, :], bd_sb[:], ['bd_sb'], ['bd_s'], 'c12')
        antiI = psb("antiI_sb", [128, 128])
        S.dma('sp', antiI[:], anti_d[:, :], [], ['antiI'], 'cai')
        tmpb = psb("tmpb", [128, 2, 8, 128])
        for tl in range(2):
            src = bass.AP(tensor=bd_s.tensor, offset=128 * (1 - tl), ap=[[1, 128], [384, 8], [1, 128]])
            S.dma('sp', tmpb[:, tl], src, ['bd_s'], ['tmpb'], f'cb{tl}')
            for hh in range(2):
                b_ = bank()
                S.op('pe', lambda e, b_=b_, tl=tl, hh=hh: e.matmul(
                    pb(b_), lhsT=antiI[:, :], rhs=tmpb[:, tl, 4 * hh:4 * hh + 4, :].rearrange("p h q -> p (h q)"),
                    start=True, stop=True), ['antiI', 'tmpb'], [pk(b_)])
                S.op('dve', lambda e, b_=b_, tl=tl, hh=hh: e.tensor_copy(
                    out=biasT[:, tl, 4 * hh:4 * hh + 4, :].rearrange("p h q -> p (h q)"), in_=pb(b_)), [pk(b_)], ['biasT'])

        cT = psb("cTt", [128, 8]); cact = psb("cact", [128, 8]); cbc = psb("cbc", [128, 8, 128])
        S.dma('sp', cT[:], cT_d[:, :], [], ['cT'], 'c13')
        S.op('act', lambda e: e.activation(out=cact[:], in_=cT[:], func=AF.Silu), ['cT'], ['cact'])
        S.op('dve', lambda e: e.tensor_copy(out=cbc[:], in_=cact[:].unsqueeze(2).to_broadcast([128, 8, 128])),
             ['cact'], ['cbc'])
        grow = psb("grow", [128, 2, 2, D])
        wm = [psb(f"wm{i}", [128, 8, 512]) for i in range(2)]
        bmr = [psb(f"bmr{i}", [128, 512]) for i in range(2)]
        modt = [psb(f"modt{i}", [128, 512]) for i in range(2)]
        n1 = psb("n1", [128, 2, 8]); n2 = psb("n2", [128, 2, 8])
        S.dma('sp', n1[:], n1T_d.rearrange("l p a -> p l a"), [], ['n1'], 'c15')
        S.dma('sp', n2[:], n2T_d.rearrange("l p a -> p l a"), [], ['n2'], 'c16')
        dtmp = psb("dtmp", [128, 128]); sct = psb("sct", [128, 2, 8])
        cnt = 0
        for l in range(2):
            for cc in range(12):
                i_ = cnt % 2
                w_ = wm[i_]; wk = f'wm{i_}'; bm_ = bmr[i_]; bk = f'bmr{i_}'; mt_ = modt[i_]; mk = f'modt{i_}'
                cnt += 1
                S.dma('sp', w_[:], wmod_d[l].rearrange("(kt p) n -> p kt n", p=128)[:, :, cc * 512:(cc + 1) * 512],
                      [], [wk], wk)
                S.dma('sp', bm_[:], bmod_d[l:l + 1, cc * 512:(cc + 1) * 512].to_broadcast([128, 512]), [], [bk], bk)
                b_ = bank()
                for kt in range(8):
                    S.op('pe', lambda e, b_=b_, kt=kt, w_=w_: e.matmul(pb(b_), lhsT=cbc[:, kt, :], rhs=w_[:, kt, :],
                                                                       start=(kt == 0), stop=(kt == 7)),
                         ['cbc', wk], [pk(b_)])
                slot, hf = cc // 2, cc % 2
                if slot in (2, 5):
                    S.op('dve', lambda e, b_=b_, l=l, slot=slot, hf=hf, bm_=bm_: e.tensor_tensor(
                        out=grow[:, l, 0 if slot == 2 else 1, hf * 512:(hf + 1) * 512], in0=pb(b_), in1=bm_[:], op=ALU.add),
                        [pk(b_), bk], ['grow'])
                else:
                    S.op('dve', lambda e, b_=b_, bm_=bm_, mt_=mt_: e.tensor_tensor(out=mt_[:], in0=pb(b_), in1=bm_[:], op=ALU.add),
                         [pk(b_), bk], [mk])
                    dst = {0: shA, 3: shB}.get(slot)
                    for k4 in range(4):
                        kt = hf * 4 + k4
                        tgt = (dst[:, l, kt:kt + 1] if dst is not None else sct[:, 0 if slot == 1 else 1, kt:kt + 1])
                        S.op('dve', lambda e, mt_=mt_, k4=k4: e.tensor_tensor(
                            out=dtmp[:], in0=mt_[:, k4 * 128:(k4 + 1) * 128], in1=ident[:], op=ALU.mult),
                            [mk, 'ident'], ['dtmp'])
                        S.op('dve', lambda e, tgt=tgt: e.tensor_reduce(out=tgt, in_=dtmp[:], axis=AX.X, op=ALU.add),
                             ['dtmp'], ['sct', 'shA', 'shB'])
            S.op('dve', lambda e, l=l: e.scalar_tensor_tensor(out=gA[:, l, :], in0=sct[:, 0, :], scalar=1.0, in1=n1[:, l, :],
                                                              op0=ALU.add, op1=ALU.mult), ['sct', 'n1'], ['gA'])
            S.op('dve', lambda e, l=l: e.scalar_tensor_tensor(out=gB[:, l, :], in0=sct[:, 1, :], scalar=1.0, in1=n2[:, l, :],
                                                              op0=ALU.add, op1=ALU.mult), ['sct', 'n2'], ['gB'])

        NWS = 4
        wst = [psb(f"wst{i}", [128, 2048]) for i in range(NWS)]
        wsb = [psb(f"wsb{i}", [128, 2048], BF) for i in range(NWS)]
        cnt = 0

        def cast_piece(src_ap, stores, n, gate=None, rows=128):
            nonlocal cnt
            i = cnt % NWS
            cnt += 1
            S.dma('sp', wst[i][0:rows, 0:n], src_ap, [], [f'wst{i}'], f'wst{i}')
            if gate is None:
                S.op('dve', lambda e: e.tensor_copy(out=wsb[i][0:rows, 0:n], in_=wst[i][0:rows, 0:n]), [f'wst{i}'], [f'wsb{i}'])
            else:
                S.op('dve', lambda e: e.tensor_tensor(out=wsb[i][0:rows, 0:n], in0=wst[i][0:rows, 0:n], in1=gate[0:rows], op=ALU.mult),
                     [f'wst{i}', 'grow'], [f'wsb{i}'])
            for dst_ap, vf in stores:
                S.dma('act', dst_ap, vf(wsb[i]), [f'wsb{i}'], ['wscr'], f'wsbst{i}')

        def cast_all():
            for l in range(2):
                for kt in range(8):
                    cast_piece(win_d[l, kt * 128:(kt + 1) * 128, :], [
                        (winu_s[l, 0:5].rearrange("t p c -> p t c")[:, :, kt * 96:(kt + 1) * 96],
                         lambda w: w[:, 0:480].rearrange("p (t c) -> p t c", c=96)),
                        (winu_s[l, 5, :, kt * 96:kt * 96 + 32], lambda w: w[:, 480:512]),
                        (winq_s[l, :, kt, :], lambda w: w[:, 512:1280])], 1280)
                    yield
                    for c3 in range(4):
                        cast_piece(wup_d[l, kt * 128:(kt + 1) * 128, c3 * 1408:(c3 + 1) * 1408], [
                            (wup_s[l, c3 * 11:(c3 + 1) * 11].rearrange("c p x -> p c x")[:, :, kt * 128:(kt + 1) * 128],
                             lambda w: w[:, 0:1408].rearrange("p (c x) -> p c x", x=128))], 1408)
                        yield
                for t6 in range(NT6):
                    n_ = nr6(t6)
                    cast_piece(wglu_d[l, t6 * 96:t6 * 96 + n_, :], [(wglu_s[l, 0:n_, t6, :], lambda w, n_=n_: w[0:n_, 0:512])], 512, rows=n_)
                    yield
                    cast_piece(wout_d[l, t6 * 96:t6 * 96 + n_, :], [
                        (wout_s[l, :, 0:n_, t6 * 512:(t6 + 1) * 512].rearrange("h p c -> p h c"),
                         lambda w, n_=n_: w[0:n_, 0:1024].rearrange("p (h c) -> p h c", h=2))], 1024, gate=grow[:, l, 0, :], rows=n_)
                    yield
                for kt in range(4):
                    cast_piece(wout_d[l, 512 + kt * 128:512 + (kt + 1) * 128, :], [
                        (wout_s[l, :, :, (6 + kt) * 512:(7 + kt) * 512].rearrange("h p c -> p h c"),
                         lambda w: w[:, 0:1024].rearrange("p (h c) -> p h c", h=2))], 1024, gate=grow[:, l, 0, :])
                    yield
                for v in range(NV):
                    cast_piece(wdn_d[l, v * 128:(v + 1) * 128, :], [
                        (wdn_s[l, :, :, v * 256:(v + 1) * 256].rearrange("q p c -> p q c"),
                         lambda w: w[:, 0:1024].rearrange("p (q c) -> p q c", q=4))], 1024, gate=grow[:, l, 1, :])
                    yield


        castgen = cast_all()
        dvc = {'n': 0}

        def V(name, shape=(128, 16)):
            return psb(name, list(shape))
        lre = V("lre", (128, 2, 16)); lim = V("lim", (128, 2, 16)); ldt = V("ldtt", (128, 2, 16))
        S.dma('sp', lre[:], lamre_d.rearrange("l p a -> p l a"), [], ['lre'], 'c17')
        S.dma('sp', lim[:], lamim_d.rearrange("l p a -> p l a"), [], ['lim'], 'c18')
        S.dma('sp', ldt[:], ldt_d.rearrange("l p a -> p l a"), [], ['ldt'], 'c19')
        Bre = V("Bre", (128, 2, 16, 16)); Bim = V("Bim", (128, 2, 16, 16))
        Cre = V("Cre", (128, 2, 16, 16)); Cim = V("Cim", (128, 2, 16, 16))
        for i, (t, d_) in enumerate(((Bre, bre_d), (Bim, bim_d), (Cre, cre_d), (Cim, cim_d))):
            S.dma('sp', t[:], d_.rearrange("l p a b -> p l a b"), [], [f'BC{i}'], f'c2{i}')
        tnames = ['dt', 'zr', 'th', 'mag', 'sn', 'cs', 't1', 't2', 't3', 'ar', 'ai', 'fr', 'fi', 'pr', 'pi', 'qr', 'qi']
        tv = {n: V("v_" + n) for n in tnames}
        Er = V("Er", (128, 16, 16)); Ei = V("Ei", (128, 16, 16)); Gr = V("Gr", (128, 16, 16)); Gi = V("Gi", (128, 16, 16))
        X1 = V("X1", (128, 16, 16)); X2 = V("X2", (128, 16, 16)); X3 = V("X3", (128, 16, 16))
        Eblk = psb("Eblk", [128, 16, 8, 2, 32], BF)
        Gblk = psb("Gblk", [128, 8, 2, 16, 32], BF)
        Cw = psb("Cw", [128, 16, 2, 128], BF)
        tabev = psb("tabev", [128, 768], BF)
        PHsb = psb("PHsb", [128, 2, NCH + 1, 16])
        phu = psb("phu", [128, 2, 16]); pht = psb("pht", [128, 4, 16])

        def dv(fn, r, w):
            S.op('dve', fn, r, w)
            dvc['n'] += 1
            if dvc['n'] % 5 == 0:
                next(castgen, None)

        def tt(o, a, b, op, r=('ssmv',), w=('ssmv',)):
            dv(lambda e: e.tensor_tensor(out=o, in0=a, in1=b, op=op), list(r), list(w))

        def ts(o, a, s1, s2, op0, op1=None, r=('ssmv',), w=('ssmv',)):
            if op1 is None:
                dv(lambda e: e.tensor_scalar(out=o, in0=a, scalar1=s1, scalar2=None, op0=op0), list(r), list(w))
            else:
                dv(lambda e: e.tensor_scalar(out=o, in0=a, scalar1=s1, scalar2=s2, op0=op0, op1=op1), list(r), list(w))

        def bc(a):
            return a.unsqueeze(2).to_broadcast([128, 16, 16])

        def cmul_b(orr, oi, sr, si, xr, xi):
            tt(X1[:], xr, bc(sr), ALU.mult); tt(X2[:], xi, bc(si), ALU.mult)
            tt(X3[:], X1[:], X2[:], ALU.subtract)
            tt(X1[:], xr, bc(si), ALU.mult); tt(X2[:], xi, bc(sr), ALU.mult)
            tt(oi, X1[:], X2[:], ALU.add)
            dv(lambda e: e.tensor_copy(out=orr, in_=X3[:]), ['ssmv'], ['ssmv'])

        for l in range(2):
            rk = ['ssmv', 'lre', 'lim', 'ldt', 'BC0', 'BC1', 'BC2', 'BC3']
            t = {k: v[:] for k, v in tv.items()}
            S.op('act', lambda e, l=l: e.activation(out=tv['dt'][:], in_=ldt[:, l, :], func=AF.Exp), ['ldt', 'ssmv'], ['ssmv'])
            ts(t['t1'], lre[:, l, :], -1e-4, None, ALU.min, r=rk)
            tt(t['zr'], t['t1'], t['dt'], ALU.mult)
            tt(t['th'], lim[:, l, :], t['dt'], ALU.mult, r=rk)
            ts(t['mag'], t['zr'], 1.0 / 720, 1.0 / 120, ALU.mult, ALU.add)
            for cf in (1.0 / 24, 1.0 / 6, 0.5, 1.0, 1.0):
                tt(t['mag'], t['mag'], t['zr'], ALU.mult)
                ts(t['mag'], t['mag'], cf, None, ALU.add)
            ts(t['t2'], t['th'], 1.0 / 32, None, ALU.mult)
            tt(t['t3'], t['t2'], t['t2'], ALU.mult)
            ts(t['sn'], t['t3'], 1.0 / 362880, -1.0 / 5040, ALU.mult, ALU.add)
            for cf in (1.0 / 120, -1.0 / 6, 1.0):
                tt(t['sn'], t['sn'], t['t3'], ALU.mult)
                ts(t['sn'], t['sn'], cf, None, ALU.add)
            tt(t['sn'], t['sn'], t['t2'], ALU.mult)
            ts(t['cs'], t['t3'], -1.0 / 3628800, 1.0 / 40320, ALU.mult, ALU.add)
            for cf in (-1.0 / 720, 1.0 / 24, -0.5, 1.0):
                tt(t['cs'], t['cs'], t['t3'], ALU.mult)
                ts(t['cs'], t['cs'], cf, None, ALU.add)
            for _ in range(5):
                tt(t['pr'], t['cs'], t['cs'], ALU.mult); tt(t['pi'], t['sn'], t['sn'], ALU.mult)
                tt(t['qr'], t['sn'], t['cs'], ALU.mult)
                tt(t['cs'], t['pr'], t['pi'], ALU.subtract)
                ts(t['sn'], t['qr'], 2.0, None, ALU.mult)
            tt(t['ar'], t['mag'], t['cs'], ALU.mult); tt(t['ai'], t['mag'], t['sn'], ALU.mult)
            tt(t['pr'], t['t1'], t['t1'], ALU.mult); tt(t['pi'], lim[:, l, :], lim[:, l, :], ALU.mult, r=rk)
            tt(t['pr'], t['pr'], t['pi'], ALU.add)
            dv(lambda e: e.reciprocal(out=tv['pr'][:], in_=tv['pr'][:]), ['ssmv'], ['ssmv'])
            ts(t['qr'], t['ar'], -1.0, None, ALU.add)
            tt(t['t2'], t['qr'], t['t1'], ALU.mult); tt(t['t3'], t['ai'], lim[:, l, :], ALU.mult, r=rk)
            tt(t['t2'], t['t2'], t['t3'], ALU.add); tt(t['fr'], t['t2'], t['pr'], ALU.mult)
            tt(t['t2'], t['ai'], t['t1'], ALU.mult); tt(t['t3'], t['qr'], lim[:, l, :], ALU.mult, r=rk)
            tt(t['t2'], t['t2'], t['t3'], ALU.subtract); tt(t['fi'], t['t2'], t['pr'], ALU.mult)
            cmul_b(Er[:], Ei[:], t['fr'], t['fi'], Bre[:, l], Bim[:, l])
            cmul_b(Gr[:], Gi[:], t['ar'], t['ai'], Cre[:, l], Cim[:, l])
            S.op('pool', lambda e: e.memset(Eblk[:], 0.0), ['Eblk'], ['Eblk'])
            S.op('pool', lambda e: e.memset(Gblk[:], 0.0), ['Gblk'], ['Gblk'])
            S.op('pool', lambda e: e.memset(Cw[:], 0.0), ['Cw'], ['Cw'])
            for two in range(2):
                ps_ = slice(64 * two, 64 * two + 64)
                for q in range(3):
                    prs = slice(q, 16, 3)
                    S.op('dve', lambda e, ps_=ps_, two=two, q=q, prs=prs, l=l: e.tensor_copy(
                        out=Cw[ps_, prs, 0, 32 * q + 16 * two:32 * q + 16 * two + 16], in_=Cre[ps_, l, prs, :]),
                        ['BC2', 'Cw'], ['Cw'])
                    S.op('dve', lambda e, ps_=ps_, two=two, q=q, prs=prs, l=l: e.tensor_scalar(
                        out=Cw[ps_, prs, 1, 32 * q + 16 * two:32 * q + 16 * two + 16], in0=Cim[ps_, l, prs, :],
                        scalar1=-1.0, scalar2=None, op0=ALU.mult), ['BC3', 'Cw'], ['Cw'])
            for d_ in range(8):
                if d_ > 0:
                    cmul_b(Er[:], Ei[:], t['ar'], t['ai'], Er[:], Ei[:])
                    cmul_b(Gr[:], Gi[:], t['ar'], t['ai'], Gr[:], Gi[:])
                for two in range(2):
                    ps_ = slice(64 * two, 64 * two + 64)
                    cs_ = slice(16 * two, 16 * two + 16)
                    for part, (E_, G_) in enumerate(((Er, Gr), (Ei, Gi))):
                        S.op('dve', lambda e, ps_=ps_, cs_=cs_, d_=d_, part=part, E_=E_: e.tensor_copy(
                            out=Eblk[ps_, :, d_, part, cs_], in_=E_[ps_, :, :]), ['ssmv', 'Eblk'], ['Eblk'])
                        if part == 0:
                            S.op('dve', lambda e, ps_=ps_, cs_=cs_, d_=d_, G_=G_: e.tensor_copy(
                                out=Gblk[ps_, d_, 0, :, cs_], in_=G_[ps_, :, :]), ['ssmv', 'Gblk'], ['Gblk'])
                        else:
                            S.op('dve', lambda e, ps_=ps_, cs_=cs_, d_=d_, G_=G_: e.tensor_scalar(
                                out=Gblk[ps_, d_, 1, :, cs_], in0=G_[ps_, :, :], scalar1=-1.0, scalar2=None, op0=ALU.mult),
                                ['ssmv', 'Gblk'], ['Gblk'])
            tt(t['t2'], t['mag'], t['mag'], ALU.mult); tt(t['t3'], t['t2'], t['t2'], ALU.mult)
            dv(lambda e, l=l: e.tensor_tensor(out=R8t[:, l, :], in0=tv['t3'][:], in1=tv['t3'][:], op=ALU.mult), ['ssmv'], ['R8t'])

            def csq(orr, oi, xr, xi):
                tt(t['t2'], xr, xr, ALU.mult); tt(t['t3'], xi, xi, ALU.mult)
                tt(t['mag'], xr, xi, ALU.mult)
                tt(orr, t['t2'], t['t3'], ALU.subtract)
                ts(oi, t['mag'], 2.0, None, ALU.mult)
            csq(t['pr'], t['pi'], t['ar'], t['ai'])
            csq(t['qr'], t['qi'], t['pr'], t['pi'])
            csq(t['pr'], t['pi'], t['qr'], t['qi'])
            dv(lambda e, l=l: e.tensor_copy(out=AT1[:, l, 0:16], in_=tv['pr'][:]), ['ssmv'], ['AT'])
            dv(lambda e, l=l: e.tensor_copy(out=AT1[:, l, 16:32], in_=tv['pr'][:]), ['ssmv'], ['AT'])
            dv(lambda e, l=l: e.tensor_copy(out=AT2[:, l, 16:32], in_=tv['pi'][:]), ['ssmv'], ['AT'])
            dv(lambda e, l=l: e.tensor_scalar(out=AT2[:, l, 0:16], in0=tv['pi'][:], scalar1=-1.0, scalar2=None, op0=ALU.mult),
               ['ssmv'], ['AT'])
            dv(lambda e, l=l: e.reciprocal(out=tv['t1'][:], in_=R8t[:, l, :]), ['R8t', 'ssmv'], ['ssmv'])
            tt(t['qr'], t['pr'], t['t1'], ALU.mult); tt(t['qi'], t['pi'], t['t1'], ALU.mult)
            dv(lambda e: e.tensor_copy(out=phu[:, 0, :], in_=tv['qr'][:]), ['ssmv', 'phu'], ['phu'])
            dv(lambda e: e.tensor_copy(out=phu[:, 1, :], in_=tv['qi'][:]), ['ssmv', 'phu'], ['phu'])
            S.op('pool', lambda e: e.memset(PHsb[:, 0, 0, :], 1.0), ['PHsb'], ['PHsb'])
            S.op('pool', lambda e: e.memset(PHsb[:, 1, 0, :], 0.0), ['PHsb'], ['PHsb'])

            def ptt(o, a_, b_, op, r, w):
                S.op('pool', lambda e: e.tensor_tensor(out=o, in0=a_, in1=b_, op=op), r, w)
            for c in range(NCH):
                cr, ci = PHsb[:, 0, c, :], PHsb[:, 1, c, :]
                ptt(pht[:, 0, :], cr, phu[:, 0, :], ALU.mult, ['phu', 'PHsb', 'pht0'], ['pht0'])
                ptt(pht[:, 1, :], ci, phu[:, 1, :], ALU.mult, ['phu', 'PHsb', 'pht1'], ['pht1'])
                ptt(PHsb[:, 0, c + 1, :], pht[:, 0, :], pht[:, 1, :], ALU.subtract, ['pht0', 'pht1'], ['PHsb'])
                ptt(pht[:, 2, :], cr, phu[:, 1, :], ALU.mult, ['phu', 'PHsb', 'pht2'], ['pht2'])
                ptt(pht[:, 3, :], ci, phu[:, 0, :], ALU.mult, ['phu', 'PHsb', 'pht3'], ['pht3'])
                ptt(PHsb[:, 1, c + 1, :], pht[:, 2, :], pht[:, 3, :], ALU.add, ['pht2', 'pht3'], ['PHsb'])
            S.dma('sp', ph_s[l], PHsb[:], ['PHsb'], ['ph_s'], 'tb3')
            S.dma('sp', cs_s[l], Gblk[:], ['Gblk'], ['cs_s'], 'tb0')
            for d_ in range(8):
                b_ = bank2()
                S.op('dve', lambda e, b_=b_: e.memset(ps_t[b_ // 2][:, 0:768], 0.0), [], [pk(b_), pk(b_ + 1)])
                for pr in range(16):
                    t6, q = pr // 3, pr % 3
                    for part in range(2):
                        S.op('pe', lambda e, b_=b_, t6=t6, q=q, pr=pr, part=part, d_=d_: e.matmul(
                            ps_t[b_ // 2][32 * q:32 * q + 32, t6 * 128:(t6 + 1) * 128], lhsT=Eblk[:, pr, d_, part, :],
                            rhs=Cw[:, pr, part, :], start=(part == 0), stop=(part == 1)),
                            ['Eblk', 'Cw'], [pk(b_), pk(b_ + 1)])
                S.op('dve', lambda e, b_=b_: e.tensor_copy(out=tabev[:], in_=ps_t[b_ // 2][:, 0:768]), [pk(b_), pk(b_ + 1)], ['tabev'])
                S.dma('sp', kt_s[l, :, d_].rearrange("p g c -> p (g c)"), tabev[:], ['tabev'], ['kt_s'], 'tb1')
            for j in range(8):
                for part in range(2):
                    b_ = bank2()
                    S.op('dve', lambda e, b_=b_: e.memset(ps_t[b_ // 2][:, 0:768], 0.0), [], [pk(b_), pk(b_ + 1)])
                    for pr in range(16):
                        t6, q = pr // 3, pr % 3
                        S.op('pe', lambda e, b_=b_, t6=t6, q=q, pr=pr, part=part, j=j: e.matmul(
                            ps_t[b_ // 2][32 * q:32 * q + 32, t6 * 128:(t6 + 1) * 128], lhsT=Eblk[:, pr, 7 - j, part, :],
                            rhs=identb[:, :], start=True, stop=True), ['Eblk', 'identb'], [pk(b_), pk(b_ + 1)])
                    S.op('dve', lambda e, b_=b_: e.tensor_copy(out=tabev[:], in_=ps_t[b_ // 2][:, 0:768]), [pk(b_), pk(b_ + 1)], ['tabev'])
                    S.dma('sp', bf_s[l, :, :, j, part, :], tabev[:].rearrange("p (g c) -> p g c", g=6), ['tabev'], ['bf_s'], 'tb2')

        for _ in castgen:
            pass
        S.barrier()
        pes.close()

        xt = sb("xt", [128, NB, D])
        tok = [sb(f"tok{i}", [128, D]) for i in range(2)]
        sqj = sb("sqj", [128, D], BF)
        rst = sb("rst", [128, 8])
        featT = sb("featT", [128, 8, TT], BF)
        uT = sb("uT", [128, NT6, TC, NCH], BF)
        qkv_sq = sb("qkv_sq", [128, 640])
        qn = sb("qn", [128, 640])
        qnb = [qn, qkv_sq]; QNK = ['qn', 'qkv_sq']
        qT = sb("qT", [128, NB, 512], BF)
        SAw = sb("SAw", [128, NCH + 1, 32])
        Sprev = sb("Sprev", [128, 32, NCH], BF)
        f32all = sb("f32all", [128, 6, TT])
        f32b = [f32all[:, i, :] for i in range(6)]
        PHt = sb("PHt", [128, 2, NCH + 1, 16])
        ysb = f32b[0:2]; ytmp = f32b[2:4]; S_sb = f32b[4:6]; sgate = f32b[0:2]
        YSK = ["f32b0", "f32b1"]; YTK = ["f32b2", "f32b3"]; SSK = ["f32b4", "f32b5"]
        zT = sb("zT", [128, NT6, TT], BF)
        ssmT = sb("ssmT", [128, NT6, TT], BF)
        attnT = sb("attnT", [128, 4, TT], BF)
        PT = [sb(f"PT{i}", [128, 2, 512], BF) for i in range(2)]
        actT = sb("actT", [128, NV, TT], BF)
        U = [sb(f"U{i}", [128, 2, TT + 2], BF) for i in range(2)]
        dg = [sb(f"dg{i}", [128, 6, 128], BF) for i in range(2)]
        NWCH = 4
        wch = [sb(f"wch{i}", [128, 8, 128], BF) for i in range(NWCH)]
        wsl = [sb(f"wsl{i}", [128, 6144], BF) for i in range(2)]
        wslc = {"n": 0}

        def wslot():
            i = wslc["n"] % 2
            wslc["n"] += 1
            return wsl[i], f"wsl{i}"
        tab = sb("tab", [128, 14336], BF)
        BFt = tab[:, 0:12288].rearrange("p (a b c d) -> p a b c d", a=6, b=8, c=2)
        KTt = tab[:, 0:6144].rearrange("p (a b c) -> p a b c", a=8, b=6)
        CSt = tab[:, 6144:14336].rearrange("p (a b c d) -> p a b c d", a=8, b=2, c=16)
        st1 = sb("st1", [128, 16])
        rs_s = sb("rs_s", [128, NB]); rs_a = sb("rs_a", [128, NB])
        ctmp = [sb(f"ctmp{i}", [128, 32]) for i in range(2)]
        wcnt = {'n': 0}

        def load_chunk(src):
            i = wcnt['n'] % NWCH
            wcnt['n'] += 1
            S.dma('sp', wch[i][:], src, [], [f'wch{i}'], f'wch{i}')
            return wch[i], f'wch{i}'

        XK = [f'xt{b}' for b in range(NB)]

        def rstd_pow(out_ap, in_ap, scale, n, rkeys, wkeys):
            S.op('pool', lambda e: e.tensor_scalar(out=out_ap, in0=in_ap, scalar1=scale, scalar2=EPS, op0=ALU.mult, op1=ALU.add),
                 list(rkeys), list(wkeys))
            S.op('pool', lambda e: e.tensor_tensor(out=out_ap, in0=out_ap, in1=nhalf[:, 0:n], op=ALU.pow),
                 list(wkeys) + ['nhalf'], list(wkeys))

        def rms_to_featT(l, g_t, sh_t):
            for b in range(NB):
                S.op('act', lambda e, b=b: e.activation(out=sqj[:], in_=xt[:, b, :], func=AF.Square,
                                                        accum_out=rst[:, b:b + 1]), [XK[b]], ['sqj', f'rst{b}'])
            for b in range(NB):
                rstd_pow(rst[:, 4 + b:5 + b], rst[:, b:b + 1], 1.0 / D, 1, [f'rst{b}'], [f'rstd{b}'])
            for b in range(NB):
                tk = tok[b % 2]
                tkk = f'tok{b % 2}'
                S.op('dve', lambda e, b=b, tk=tk: e.tensor_scalar(out=tk[:], in0=xt[:, b, :], scalar1=rst[:, 4 + b:5 + b], scalar2=None,
                                                                   op0=ALU.mult), [XK[b], f'rstd{b}', tkk], [tkk])
                for half in range(2):
                    b_ = bank()
                    for j in range(4):
                        kt = half * 4 + j
                        S.op('pe', lambda e, b_=b_, j=j, kt=kt, tk=tk: e.transpose(
                            out=pb(b_, j * 128, (j + 1) * 128), in_=tk[:, kt * 128:(kt + 1) * 128], identity=ident[:]),
                            [tkk, 'ident'], [pk(b_)])
                    for j in range(4):
                        kt = half * 4 + j
                        if kt < 3:
                            S.op('dve', lambda e, b_=b_, j=j, kt=kt, b=b: e.tensor_scalar(
                                out=featT[:, kt, b * 128:(b + 1) * 128], in0=pb(b_, j * 128, (j + 1) * 128),
                                scalar1=g_t[:, l, kt:kt + 1], scalar2=sh_t[:, l, kt:kt + 1], op0=ALU.mult, op1=ALU.add),
                                [pk(b_), 'gA', 'gB', 'shA', 'shB'], ['featT'])
                        else:
                            S.op('act', lambda e, b_=b_, j=j, kt=kt, b=b: e.activation(
                                out=featT[:, kt, b * 128:(b + 1) * 128], in_=pb(b_, j * 128, (j + 1) * 128), func=AF.Identity,
                                bias=sh_t[:, l, kt:kt + 1], scale=g_t[:, l, kt:kt + 1]), [pk(b_), 'gA', 'gB', 'shA', 'shB'], ['featT'])

        def tile_layer(ti, l):
            if True:
                SAk = 'SAw'; SCk = f'SC{l}'; kTk = f'kT{l}'; HBk = f'HB{l}'
                if ti == 0 and l == 0:
                    S.dma('sp', tab[:, 0:12288], bf_s[l].rearrange("p a b c d -> p (a b c d)"), [], ['tab'], 'tab')
                S.dma('sp', PHt[:], ph_s[l], [], ['PHt'], 'PHt')
                wq_t, wqk = wslot()
                wqkv = wq_t[:, :].rearrange("p (k c) -> p k c", k=8)
                S.dma('sp', wq_t[:, :], winq_s[l].rearrange("p k c -> p (k c)"), [], [wqk], wqk)
                wg_t, wgk = wslot()
                wgl = wg_t[:, 0:3072].rearrange("p (k c) -> p k c", k=NT6)
                S.dma('sp', wg_t[:, 0:3072], wglu_s[l].rearrange("p k c -> p (k c)"), [], [wgk], wgk)
                rms_to_featT(l, gA, shA)
                for t6 in range(NT6):
                    n_ = nr6(t6)
                    i_ = wcnt['n'] % NWCH
                    wcnt['n'] += 1
                    wc, wk = wch[i_], f'wch{i_}'
                    S.dma('sp', wc[:, :, 0:n_], winu_s[l, t6].rearrange("p (k c) -> p k c", c=96)[:, :, 0:n_], [], [wk], wk)
                    b_ = bank()
                    for kt in range(8):
                        S.op('pe', lambda e, b_=b_, kt=kt, wc=wc, n_=n_: e.matmul(pb(b_)[0:n_, :], lhsT=wc[:, kt, 0:n_], rhs=featT[:, kt, :],
                                                                                 start=(kt == 0), stop=(kt == 7)),
                             [wk, 'featT'], [pk(b_)])
                    S.op('act', lambda e, b_=b_, t6=t6, n_=n_: e.activation(
                        out=uT[0:n_, t6].rearrange("p j c -> p c j"), in_=pb(b_)[0:n_, :].rearrange("p (c j) -> p c j", j=TC),
                        func=AF.Copy), [pk(b_)], ['uT'])
                for q in range(3):
                    combos = [(part, pr) for part in range(2) for pr in range(16) if pr % 3 == q]
                    b2 = bank2()
                    for sl, (part, pr) in enumerate(combos):
                        t6 = pr // 3
                        bb = b2 + sl // 8
                        for j in range(TC):
                            S.op('pe', lambda e, bb=bb, sl=sl, t6=t6, q=q, j=j, part=part: e.matmul(
                                pb(bb, (sl % 8) * 64, (sl % 8) * 64 + 64), lhsT=BFt[32 * q:32 * q + 32, t6, j, part, :],
                                rhs=uT[32 * q:32 * q + 32, t6, j, :], start=(j == 0), stop=(j == TC - 1)),
                                ['tab', 'uT'], [pk(bb)])
                    for sl, (part, pr) in enumerate(combos):
                        bb = b2 + sl // 8
                        S.op('dve', lambda e, bb=bb, sl=sl, part=part, pr=pr: e.tensor_copy(
                            out=SAw[:, 1:NCH + 1, part * 16 + pr], in_=pb(bb, (sl % 8) * 64, (sl % 8) * 64 + 64)), [pk(bb)], [SAk])
                S.dma('sp', tab[:, 0:6144], kt_s[l].rearrange("p a b c -> p (a b c)"), [], ['tab'], 'tab')
                S.dma('sp', tab[:, 6144:14336], cs_s[l].rearrange("p a b c d -> p (a b c d)"), [], ['tab'], 'tab2')
                def chain_pre():
                    S.op('pool', lambda e: e.tensor_copy(out=SAw[:, 0, :], in_=SC[:, l, :]), [SCk, SAk], [SAk])
                    T2 = f32all[:, 0:4, :].rearrange("p a (c t q) -> p (a c) t q", t=2, q=16)
                    T2K = ['f32b0', 'f32b1', 'f32b2', 'f32b3']
                    Fv = SAw[:, 1:NCH + 1, :].rearrange("p c (t q) -> p c t q", t=2)
                    cosf, sinf = PHt[:, 0, 1:NCH + 1, :], PHt[:, 1, 1:NCH + 1, :]
                    S.op('pool', lambda e: e.tensor_tensor(out=T2[:, :, 0, :], in0=Fv[:, :, 1, :], in1=sinf, op=ALU.mult), [SAk, 'PHt'] + T2K, T2K)
                    S.op('pool', lambda e: e.tensor_tensor(out=T2[:, :, 1, :], in0=Fv[:, :, 0, :], in1=sinf, op=ALU.mult), [SAk, 'PHt'] + T2K, T2K)
                    S.op('pool', lambda e: e.tensor_tensor(out=Fv, in0=Fv, in1=cosf.unsqueeze(2).to_broadcast([128, NCH, 2, 16]), op=ALU.mult),
                         [SAk, 'PHt'], [SAk])
                    S.op('pool', lambda e: e.tensor_tensor(out=Fv[:, :, 0, :], in0=Fv[:, :, 0, :], in1=T2[:, :, 0, :], op=ALU.add), [SAk] + T2K, [SAk])
                    S.op('pool', lambda e: e.tensor_tensor(out=Fv[:, :, 1, :], in0=Fv[:, :, 1, :], in1=T2[:, :, 1, :], op=ALU.subtract), [SAk] + T2K, [SAk])
                    return T2, T2K

                def chain_scan():
                    for s_ in range(32):
                        q_ = s_ % 16
                        S.op('dve', lambda e, s_=s_, q_=q_: e.tensor_tensor_scan(
                            out=SAw[:, 1:NCH + 1, s_], data0=R8t[:, l, q_:q_ + 1].to_broadcast([128, NCH]), data1=SAw[:, 1:NCH + 1, s_],
                            initial=SAw[:, 0, s_:s_ + 1], op0=ALU.mult, op1=ALU.add), [SAk, 'R8t'], [SAk])

                def chain_post(T2, T2K):
                    Wv = SAw[:, 0:NCH, :].rearrange("p c (t q) -> p c t q", t=2)
                    cosb, sinb = PHt[:, 0, 0:NCH, :], PHt[:, 1, 0:NCH, :]
                    Spv = Sprev[:].rearrange("p (t q) c -> p c t q", t=2)
                    W64 = SAw[:, NCH, :]
                    S.op('pool', lambda e: e.tensor_tensor(out=ctmp[0][:, 0:16], in0=W64[:, 16:32], in1=PHt[:, 1, NCH, :], op=ALU.mult), [SAk, 'PHt'], ['ct0'])
                    S.op('pool', lambda e: e.tensor_tensor(out=ctmp[0][:, 16:32], in0=W64[:, 0:16], in1=PHt[:, 1, NCH, :], op=ALU.mult), [SAk, 'PHt', 'ct0'], ['ct0'])
                    S.op('pool', lambda e: e.tensor_tensor(out=ctmp[1][:, 0:16], in0=W64[:, 0:16], in1=PHt[:, 0, NCH, :], op=ALU.mult), [SAk, 'PHt'], ['ct1'])
                    S.op('pool', lambda e: e.tensor_tensor(out=ctmp[1][:, 16:32], in0=W64[:, 16:32], in1=PHt[:, 0, NCH, :], op=ALU.mult), [SAk, 'PHt', 'ct1'], ['ct1'])
                    S.op('pool', lambda e: e.tensor_tensor(out=SC[:, l, 0:16], in0=ctmp[1][:, 0:16], in1=ctmp[0][:, 0:16], op=ALU.subtract), ['ct0', 'ct1', SCk], [SCk])
                    S.op('pool', lambda e: e.tensor_tensor(out=SC[:, l, 16:32], in0=ctmp[1][:, 16:32], in1=ctmp[0][:, 16:32], op=ALU.add), ['ct0', 'ct1', SCk], [SCk])
                    S.op('pool', lambda e: e.tensor_tensor(out=T2[:, :, 0, :], in0=Wv[:, :, 1, :], in1=sinb, op=ALU.mult), [SAk, 'PHt'] + T2K, T2K)
                    S.op('pool', lambda e: e.tensor_tensor(out=T2[:, :, 1, :], in0=Wv[:, :, 0, :], in1=sinb, op=ALU.mult), [SAk, 'PHt'] + T2K, T2K)
                    S.op('pool', lambda e: e.tensor_tensor(out=Wv, in0=Wv, in1=cosb.unsqueeze(2).to_broadcast([128, NCH, 2, 16]), op=ALU.mult),
                         [SAk, 'PHt'], [SAk])
                    S.op('pool', lambda e: e.tensor_tensor(out=Spv[:, :, 0, :], in0=Wv[:, :, 0, :], in1=T2[:, :, 0, :], op=ALU.subtract), [SAk] + T2K, ['Sprev'])
                    S.op('pool', lambda e: e.tensor_tensor(out=Spv[:, :, 1, :], in0=Wv[:, :, 1, :], in1=T2[:, :, 1, :], op=ALU.add), [SAk] + T2K + ['Sprev'], ['Sprev'])

                def att_A1(b):
                    Q = qnb[b % 2]; QK = QNK[b % 2]
                    b2 = bank2()
                    for kt in range(8):
                        S.op('pe', lambda e, b2=b2, kt=kt, b=b: e.matmul(pb(b2), lhsT=featT[:, kt, b * 128:(b + 1) * 128],
                                                                        rhs=wqkv[:, kt, 0:512], start=(kt == 0), stop=(kt == 7)),
                             ['featT', wqk], [pk(b2)])
                    for kt in range(8):
                        S.op('pe', lambda e, b2=b2, kt=kt, b=b: e.matmul(pb(b2 + 1, 0, 256), lhsT=featT[:, kt, b * 128:(b + 1) * 128],
                                                                        rhs=wqkv[:, kt, 512:768], start=(kt == 0), stop=(kt == 7)),
                             ['featT', wqk], [pk(b2 + 1)])
                    qk_ps = ps_t[b2 // 2][:, 0:640]
                    S.op('act', lambda e, qk_ps=qk_ps, Q=Q: e.activation(out=Q[:], in_=qk_ps, func=AF.Square),
                         [pk(b2), pk(b2 + 1)], [QK])
                    S.op('dve', lambda e, Q=Q: e.tensor_reduce(out=st1[:, 4:14], in_=Q[:].rearrange("p (h d) -> p h d", d=64),
                                                          axis=AX.X, op=ALU.add), [QK], ['st1'])
                    S.op('act', lambda e: e.activation(out=st1[:, 4:14], in_=st1[:, 4:14], func=AF.Sqrt, bias=epsc[:, 0:1],
                                                       scale=1.0 / 64), ['st1', 'epsc'], ['st1'])
                    S.op('dve', lambda e: e.reciprocal(out=st1[:, 4:14], in_=st1[:, 4:14]), ['st1'], ['st1'])
                    S.op('dve', lambda e, qk_ps=qk_ps, Q=Q: e.tensor_tensor(
                        out=Q[:, 0:512].rearrange("p (m t d) -> p t m d", m=4, t=2),
                        in0=qk_ps[:, 0:512].rearrange("p (t m d) -> p t m d", t=2, m=4),
                        in1=st1[:, 4:12].rearrange("p (t m) -> p t m", t=2).unsqueeze(3).to_broadcast([128, 2, 4, 64]), op=ALU.mult),
                        [pk(b2), pk(b2 + 1), 'st1', QK], [QK])
                    S.op('dve', lambda e, qk_ps=qk_ps, Q=Q: e.tensor_tensor(
                        out=Q[:, 512:640].rearrange("p (h d) -> p h d", d=64), in0=qk_ps[:, 512:640].rearrange("p (h d) -> p h d", d=64),
                        in1=st1[:, 12:14].unsqueeze(2).to_broadcast([128, 2, 64]), op=ALU.mult),
                        [pk(b2), pk(b2 + 1), 'st1', QK], [QK])
                    S.op('act', lambda e, b2=b2, b=b: e.activation(
                        out=Vaug[:, l, b + 1, :, 0:64], in_=pb(b2 + 1, 128, 256).rearrange("p (g d) -> p g d", g=2),
                        func=AF.Copy), [pk(b2 + 1)], ['Vaug'])

                def att_A2(b):
                    Q = qnb[b % 2]; QK = QNK[b % 2]
                    tb = bank2()
                    for m in range(4):
                        S.op('pe', lambda e, tb=tb, m=m, Q=Q: e.transpose(
                            out=pb(tb, m * 128, (m + 1) * 128),
                            in_=Q[:, m * 128:(m + 1) * 128], identity=ident[:]),
                            [QK, 'ident'], [pk(tb)])
                    S.op('pe', lambda e, tb=tb, Q=Q: e.transpose(out=pb(tb + 1, 0, 128), in_=Q[:, 512:640], identity=ident[:]),
                         [QK, 'ident'], [pk(tb + 1)])
                    S.op('act', lambda e, tb=tb, b=b: e.activation(
                        out=qT[:, b, :], in_=pb(tb), func=AF.Identity,
                        scale=qsc[:, l:l + 1]), [pk(tb), 'qsc'], ['qT'])
                    S.op('act', lambda e, tb=tb, b=b: e.activation(
                        out=kT[:, l, 128 + b * 128:128 + (b + 1) * 128], in_=pb(tb + 1, 0, 128), func=AF.Identity,
                        scale=ksc[:, l:l + 1]), [pk(tb + 1), 'ksc'], [kTk])

                BST = {}

                def att_B1(b):
                    first = (ti == 0 and b == 0)
                    tiles = [1] if first else [0, 1]
                    BST[b] = tiles
                    for g in range(2):
                        gs = slice(64 * g, 64 * g + 64)
                        pt = PT[g]; ptk = f'PT{g}'
                        for tl in tiles:
                            sb_ = bank()
                            kcol = b * 128 + tl * 128
                            S.op('pe', lambda e, sb_=sb_, gs=gs, kcol=kcol, b=b: e.matmul(
                                pb(sb_), lhsT=kT[gs, l, kcol:kcol + 128], rhs=qT[gs, b, :],
                                start=True, stop=True), [kTk, 'qT'], [pk(sb_)])
                            ssb_ = S_sb[tl]; ssk = SSK[tl]
                            S.op('dve', lambda e, sb_=sb_, ssb_=ssb_, tl=tl, g=g: e.tensor_tensor(
                                out=ssb_[:].rearrange("p (m q) -> p m q", m=4), in0=pb(sb_).rearrange("p (m q) -> p m q", m=4),
                                in1=biasT[:, tl, 4 * g:4 * g + 4, :], op=ALU.add), [pk(sb_), 'biasT'], [ssk])
                            S.op('act', lambda e, ssb_=ssb_, pt=pt, tl=tl: e.activation(out=pt[:, tl, :], in_=ssb_[:], func=AF.Exp),
                                 [ssk], [ptk])

                def att_B2(b):
                    tiles = BST[b]
                    ob = bank2()
                    for g in range(2):
                        pt = PT[g]; ptk = f'PT{g}'
                        for m in range(4):
                            for ii, tl in enumerate(tiles):
                                S.op('pe', lambda e, ob=ob, g=g, m=m, tl=tl, ii=ii, pt=pt, b=b, n_=len(tiles): e.matmul(
                                    pb(ob + g, m * 65, m * 65 + 65), lhsT=pt[:, tl, m * 128:(m + 1) * 128],
                                    rhs=Vaug[:, l, b + tl, g, :], start=(ii == 0), stop=(ii == n_ - 1)),
                                    [ptk, 'Vaug'], [pk(ob + g)])
                    at = tok[b % 2]; atk = f'tok{b % 2}'
                    for g in range(2):
                        o3 = pb(ob + g, 0, 260).rearrange("p (m d) -> p m d", m=4)
                        S.op('dve', lambda e, o3=o3, g=g: e.tensor_tensor(out=st1[:, 0:4], in0=o3[:, :, 64], in1=esink[:, l, 4 * g:4 * g + 4],
                                                                          op=ALU.add), [pk(ob + g), 'esink', 'st1'], ['st1'])
                        S.op('dve', lambda e: e.reciprocal(out=st1[:, 0:4], in_=st1[:, 0:4]), ['st1'], ['st1'])
                        S.op('dve', lambda e, o3=o3, g=g, at=at: e.tensor_tensor(
                            out=at[:, 256 * g:256 * g + 256].rearrange("p (m d) -> p m d", m=4), in0=o3[:, :, 0:64],
                            in1=st1[:, 0:4].unsqueeze(2).to_broadcast([128, 4, 64]), op=ALU.mult),
                            [pk(ob + g), 'st1', atk], [atk])
                    S.op('act', lambda e, at=at, b=b: e.activation(out=qkv_sq[:, 0:512], in_=at[:, 0:512], func=AF.Square,
                                                                   accum_out=rs_a[:, b:b + 1]), [atk, 'qkv_sq'], ['qkv_sq', 'rs_a'])
                    rstd_pow(rs_a[:, b:b + 1], rs_a[:, b:b + 1], 1.0 / 512, 1, ['rs_a'], ['rs_a'])

                def att_B3(b):
                    at = tok[b % 2]; atk = f'tok{b % 2}'
                    tb = bank()
                    for m in range(4):
                        S.op('pe', lambda e, tb=tb, m=m, at=at: e.transpose(out=pb(tb, m * 128, (m + 1) * 128),
                                                                           in_=at[:, m * 128:(m + 1) * 128], identity=ident[:]),
                             [atk, 'ident'], [pk(tb)])
                    for m in range(4):
                        S.op('act', lambda e, tb=tb, m=m, b=b: e.activation(
                            out=attnT[:, m, b * 128:(b + 1) * 128], in_=pb(tb, m * 128, (m + 1) * 128), func=AF.Identity,
                            scale=ona[:, l, m:m + 1]), [pk(tb), 'ona'], ['attnT'])

                def Y_tile(t6):
                    n_ = nr6(t6)
                    b_ = bank()
                    for i in range(TC):
                        for j in range(i + 1):
                            S.op('pe', lambda e, b_=b_, i=i, j=j, t6=t6, n_=n_: e.matmul(
                                pb(b_, i * 64, i * 64 + 64)[0:n_, :], lhsT=KTt[0:n_, i - j, t6, 0:n_], rhs=uT[0:n_, t6, j, :],
                                start=(j == 0), stop=False, skip_group_check=True),
                                ['tab', 'uT'], [pk(b_)])
                        for q in range(np6(t6)):
                            pr = t6 * 3 + q
                            for part in range(2):
                                last = (q == np6(t6) - 1 and part == 1)
                                S.op('pe', lambda e, b_=b_, i=i, q=q, pr=pr, part=part, last=last: e.matmul(
                                    pb(b_, i * 64, i * 64 + 64)[32 * q:32 * q + 32, :], lhsT=CSt[:, i, part, pr, :],
                                    rhs=Sprev[:, part * 16 + pr, :], start=False, stop=last, skip_group_check=True),
                                    ['tab', 'Sprev'], [pk(b_)])
                    yb = ysb[t6 % 2]; yk = YSK[t6 % 2]; yt_ = ytmp[t6 % 2]; ytk = YTK[t6 % 2]
                    S.op('dve', lambda e, b_=b_, t6=t6, yb=yb, n_=n_: e.scalar_tensor_tensor(
                        out=yb[0:n_, :], in0=uT[0:n_, t6].rearrange("p j c -> p (j c)"), scalar=dT[0:n_, l, t6:t6 + 1], in1=pb(b_)[0:n_, :],
                        op0=ALU.mult, op1=ALU.add), ['uT', 'dT', pk(b_)], [yk])
                    S.op('act', lambda e, yb=yb, yt_=yt_, n_=n_: e.activation(out=yt_[0:n_, :], in_=yb[0:n_, :], func=AF.Square), [yk], [ytk])
                    S.op('dve', lambda e, yt_=yt_, n_=n_: e.tensor_scalar(out=yt_[0:n_, :], in0=yt_[0:n_, :], scalar1=0.044715, scalar2=1.0,
                                                                           op0=ALU.mult, op1=ALU.add), [ytk], [ytk])
                    S.op('dve', lambda e, yt_=yt_, yb=yb, n_=n_: e.tensor_tensor(out=yt_[0:n_, :], in0=yt_[0:n_, :], in1=yb[0:n_, :], op=ALU.mult),
                         [ytk, yk], [ytk])
                    S.op('act', lambda e, yt_=yt_, n_=n_: e.activation(out=yt_[0:n_, :], in_=yt_[0:n_, :], func=AF.Tanh, scale=0.7978845608),
                         [ytk], [ytk])
                    S.op('dve', lambda e, yt_=yt_, yb=yb, t6=t6, n_=n_: e.scalar_tensor_tensor(
                        out=zT[0:n_, t6, :].rearrange("p (c j) -> p j c", j=TC), in0=yt_[0:n_, :].rearrange("p (j c) -> p j c", j=TC),
                        scalar=1.0, in1=yb[0:n_, :].rearrange("p (j c) -> p j c", j=TC), op0=ALU.add, op1=ALU.mult), [ytk, yk], ['zT'])
                T2, T2K = chain_pre()
                att_A1(0)
                att_A1(1)
                att_A2(0)
                chain_scan()
                att_A1(2)
                att_A2(1)
                att_A1(3)
                att_A2(2)
                chain_post(T2, T2K)
                att_A2(3)
                att_B1(0); Y_tile(0); att_B2(0); Y_tile(1); att_B3(0)
                att_B1(1); Y_tile(2); att_B2(1); Y_tile(3); att_B3(1)
                att_B1(2); Y_tile(4); att_B2(2); Y_tile(5); att_B3(2)
                att_B1(3); att_B2(3); att_B3(3)
                S.op('pool', lambda e: e.tensor_copy(out=kT[:, l, 0:128], in_=kT[:, l, TT:TT + 128]), [kTk], [kTk])
                S.op('pool', lambda e: e.tensor_copy(out=Vaug[:, l, 0, :, :], in_=Vaug[:, l, NB, :, :]), ['Vaug'], ['Vaug'])
                if not (ti == nt - 1 and l == 1):
                    S.dma('sp', tab[:, 0:12288], bf_s[1 - l].rearrange("p a b c d -> p (a b c d)"), [], ['tab'], 'tab')
                ssb = bank()
                for m in range(NT6):
                    no = nr6(m)
                    b_ = bank()
                    if b_ == ssb:
                        b_ = bank()
                    for t6 in range(NT6):
                        n_ = nr6(t6)
                        S.op('pe', lambda e, b_=b_, m=m, t6=t6, n_=n_, no=no: e.matmul(
                            pb(b_)[0:no, :], lhsT=wgl[0:n_, t6, m * 96:m * 96 + no], rhs=zT[0:n_, t6, :],
                            start=(t6 == 0), stop=(t6 == NT6 - 1)), [wgk, 'zT'], [pk(b_)])
                    yb = ysb[m % 2]; yk = YSK[m % 2]; yt_ = ytmp[m % 2]; ytk = YTK[m % 2]
                    S.op('act', lambda e, b_=b_, m=m, yb=yb, no=no: e.activation(out=yb[0:no, :], in_=pb(b_)[0:no, :], func=AF.Tanh,
                                                                                bias=bglu[0:no, l, m:m + 1], scale=0.25),
                         [pk(b_), 'bglu'], [yk])
                    S.op('dve', lambda e, m=m, yb=yb, no=no: e.scalar_tensor_tensor(out=yb[0:no, :], in0=yb[0:no, :], scalar=1.0, in1=zT[0:no, m, :],
                                                                                    op0=ALU.add, op1=ALU.mult), [yk, 'zT'], [yk])
                    S.op('act', lambda e, yb=yb, yt_=yt_, no=no: e.activation(out=yt_[0:no, :], in_=yb[0:no, :], func=AF.Square), [yk], [ytk])
                    S.op('dve', lambda e, m=m, yb=yb, no=no: e.tensor_scalar(out=ssmT[0:no, m, :], in0=yb[0:no, :], scalar1=ons[0:no, l, m:m + 1],
                                                                              scalar2=None, op0=ALU.mult), [yk, 'ons'], ['ssmT'])
                    for b in range(NB):
                        S.op('pe', lambda e, m=m, b=b, yt_=yt_, ssb=ssb, no=no: e.matmul(
                            pb(ssb)[:, m * NB + b:m * NB + b + 1], lhsT=yt_[0:no, b * 128:(b + 1) * 128], rhs=ones_f[0:no, 0:1],
                            start=True, stop=True), [ytk, 'ones_f'], [pk(ssb)])
                S.op('dve', lambda e, ssb=ssb: e.tensor_reduce(out=rs_s[:], in_=pb(ssb)[:, 0:NT6 * NB].rearrange("p (m b) -> p b m", b=NB),
                                                              axis=AX.X, op=ALU.add), [pk(ssb)], ['rs_s'])
                rstd_pow(rs_s[:], rs_s[:], 1.0 / (16 * 512), NB, ['rs_s'], ['rs_s'])
                for half in range(2):
                    wo_t, wok = wslot()
                    woh = wo_t[:, 0:5120].rearrange("p (k c) -> p k c", k=10)
                    S.dma('sp', wo_t[:, 0:5120], wout_s[l, half], [], [wok], wok)
                    for b in range(NB):
                        ba = bank(); bb_ = bank()
                        for t6 in range(NT6):
                            n_ = nr6(t6)
                            S.op('pe', lambda e, ba=ba, t6=t6, b=b, n_=n_, woh=woh: e.matmul(pb(ba), lhsT=ssmT[0:n_, t6, b * 128:(b + 1) * 128],
                                                                                   rhs=woh[0:n_, t6, :], start=(t6 == 0), stop=(t6 == NT6 - 1)),
                                 ['ssmT', wok], [pk(ba)])
                        for ft in range(4):
                            S.op('pe', lambda e, bb_=bb_, ft=ft, b=b, woh=woh: e.matmul(pb(bb_), lhsT=attnT[:, ft, b * 128:(b + 1) * 128],
                                                                              rhs=woh[:, 6 + ft, :], start=(ft == 0), stop=(ft == 3)),
                                 ['attnT', wok], [pk(bb_)])
                        xs = xt[:, b, half * 512:(half + 1) * 512]
                        S.op('dve', lambda e, ba=ba, b=b, xs=xs: e.scalar_tensor_tensor(
                            out=xs, in0=pb(ba), scalar=rs_s[:, b:b + 1], in1=xs, op0=ALU.mult, op1=ALU.add),
                            [pk(ba), 'rs_s', XK[b]], [XK[b]])
                        S.op('dve', lambda e, bb_=bb_, b=b, xs=xs: e.scalar_tensor_tensor(
                            out=xs, in0=pb(bb_), scalar=rs_a[:, b:b + 1], in1=xs, op0=ALU.mult, op1=ALU.add),
                            [pk(bb_), 'rs_a', XK[b]], [XK[b]])
                rms_to_featT(l, gB, shB)
                ups_all = {}

                def stage_up(v):
                    dgv = dg[v % 2]; dgk = f'dg{v % 2}'
                    ups = []
                    for vg in range(2):
                        wc, wk = load_chunk(wup_s[l, vg * NV + v].rearrange("p (k c) -> p k c", k=8))
                        b_ = bank()
                        for kt in range(8):
                            S.op('pe', lambda e, b_=b_, kt=kt, wc=wc: e.matmul(pb(b_), lhsT=wc[:, kt, :], rhs=featT[:, kt, :],
                                                                               start=(kt == 0), stop=(kt == 7)),
                                 [wk, 'featT'], [pk(b_)])
                        ups.append(b_)
                    ups_all[v] = ups
                    S.op('pool', lambda e, dgv=dgv, v=v: e.tensor_tensor(
                        out=dgv[:, :, :].rearrange("p (g j) c -> p g j c", g=2),
                        in0=identb[:].unsqueeze(1).unsqueeze(1).to_broadcast([128, 2, 3, 128]),
                        in1=cw[:, l].rearrange("p (g v) j -> p g v j", g=2)[:, :, v, :].unsqueeze(3).to_broadcast([128, 2, 3, 128]),
                        op=ALU.mult), ['identb', 'cw', dgk], [dgk])

                def stage_mid(v):
                    Uv = U[v % 2]; Uk = f'U{v % 2}'; dgv = dg[v % 2]; dgk = f'dg{v % 2}'
                    sg = sgate[v % 2]; sgk = YSK[v % 2]
                    ups = ups_all.pop(v)
                    S.op('pool', lambda e, Uv=Uv, v=v: e.tensor_copy(out=Uv[:, :, 0:2], in_=HB[:, l, v, :, :]), [HBk, Uk], [Uk])
                    for vg in range(2):
                        S.op('act', lambda e, Uv=Uv, vg=vg, b_=ups[vg]: e.activation(out=Uv[:, vg, 2:TT + 2], in_=pb(b_), func=AF.Copy),
                             [pk(ups[vg]), Uk], [Uk])
                    S.op('pool', lambda e, Uv=Uv, v=v: e.tensor_copy(out=HB[:, l, v, :, :], in_=Uv[:, :, TT:TT + 2]), [Uk, HBk], [HBk])
                    cps = []
                    for vg in range(2):
                        b_ = bank()
                        for j in range(3):
                            S.op('pe', lambda e, b_=b_, vg=vg, j=j, dgv=dgv, Uv=Uv: e.matmul(
                                pb(b_), lhsT=dgv[:, vg * 3 + j, :], rhs=Uv[:, vg, j:j + TT], start=(j == 0), stop=(j == 2)),
                                [dgk, Uk], [pk(b_)])
                        cps.append(b_)
                    S.op('act', lambda e, sg=sg, b_=cps[1], v=v: e.activation(out=sg[:], in_=pb(b_), func=AF.Silu,
                                                                             bias=cb[:, l, NV + v:NV + v + 1], scale=1.0),
                         [pk(cps[1]), 'cb'], [sgk])
                    S.op('dve', lambda e, sg=sg, b_=cps[0], v=v: e.scalar_tensor_tensor(
                        out=actT[:, v, :], in0=pb(b_), scalar=cb[:, l, v:v + 1], in1=sg[:], op0=ALU.add, op1=ALU.mult),
                        [pk(cps[0]), 'cb', sgk], ['actT'])

                stage_up(0)
                for v in range(NV):
                    if v + 1 < NV:
                        stage_up(v + 1)
                    stage_mid(v)
                for qt in range(4):
                    wd_t, wdk = wslot()
                    wdh = wd_t[:, 0:5632].rearrange("p (k c) -> p k c", k=NV)
                    S.dma('sp', wd_t[:, 0:5632], wdn_s[l, qt], [], [wdk], wdk)
                    for b in range(NB):
                        b_ = bank()
                        for v in range(NV):
                            S.op('pe', lambda e, b_=b_, v=v, b=b, wdh=wdh: e.matmul(pb(b_, 0, 256), lhsT=actT[:, v, b * 128:(b + 1) * 128],
                                                                          rhs=wdh[:, v, :], start=(v == 0), stop=(v == NV - 1)),
                                 ['actT', wdk], [pk(b_)])
                        xs = xt[:, b, qt * 256:(qt + 1) * 256]
                        S.op('dve', lambda e, b_=b_, xs=xs: e.tensor_tensor(out=xs, in0=pb(b_, 0, 256), in1=xs, op=ALU.add),
                             [pk(b_), XK[b]], [XK[b]])
                        if l == 1 and qt == 3:
                            tk = tok[b % 2]; tkk = f'tok{b % 2}'
                            S.op('act', lambda e, tk=tk, b=b: e.activation(out=tk[:], in_=xt[:, b, :], func=AF.Copy), [XK[b], tkk], [tkk])
                            S.dma('sp', y_d[ti * TT + b * 128: ti * TT + (b + 1) * 128, :], tk[:], [tkk], ['y'], f'yst{b % 2}')
                            if ti + 1 < nt:
                                S.dma('sp', xt[:, b, :], x_d[(ti + 1) * TT + b * 128:(ti + 1) * TT + (b + 1) * 128, :], [], [XK[b]], f'xld{b}')

        S.dma('sp', xt[:], x_d[0:TT, :].rearrange("(b p) f -> p b f", p=128), [], XK, 'xt')
        for ti in range(nt):
            for l in range(2):
                tile_layer(ti, l)
        S.wait_all('sp')
        block = es.enter_context(nc.Block())
        S.emit(block)
    return nc


def _bucket_onehot():
    nb, md = 32, 128
    idx = np.arange(384)
    dist = idx - 127
    n = np.maximum(dist, 0)
    max_exact = nb // 2
    log_part = np.log(np.maximum(n, 1) / max_exact) / np.log(md / max_exact)
    large = max_exact + (log_part * (nb - max_exact)).astype(np.int32)
    large = np.minimum(large, nb - 1)
    bucket = np.where(n < max_exact, n, large).astype(np.int32)
    valid = (dist >= 0) & (dist < 128)
    oh = np.zeros((33, 384), np.float32)
    for i in range(384):
        if valid[i]:
            oh[bucket[i], i] = 1.0
        else:
            oh[32, i] = 1.0
    return oh


def prep_inputs(b, x, c, w_mod, b_mod, norm1_w, w_in, lam_re, lam_im, log_dt, ssm_b_re, ssm_b_im, ssm_c_re, ssm_c_im,
                ssm_d, w_glu, b_glu, q_norm_w, k_norm_w, rel_bias, sinks, out_norm_ssm, out_norm_attn, w_out, norm2_w,
                w_up, conv_w, conv_b, w_down):
    f = lambda a: np.ascontiguousarray(a, dtype=np.float32)

    def fm(a, n):
        return f(np.asarray(a).reshape(2, n, 128).transpose(0, 2, 1))

    def pairlay(a):
        a = np.asarray(a)
        sh = a.shape
        a = a.reshape(2, 16, 2, 64, *sh[3:])
        a = np.moveaxis(a, 1, 3)
        return f(a.reshape(2, 128, 16, *sh[3:]))
    m = {}
    m["x"] = f(x[b])
    m["cT"] = f(np.asarray(c[b]).reshape(8, 128).T)
    m["w_mod"] = f(w_mod); m["b_mod"] = f(b_mod)
    m["n1T"] = fm(norm1_w, 8); m["n2T"] = fm(norm2_w, 8)
    m["w_in"] = f(w_in); m["w_out"] = f(w_out); m["w_up"] = f(w_up); m["w_down"] = f(w_down); m["w_glu"] = f(w_glu)
    m["lamre"] = pairlay(lam_re); m["lamim"] = pairlay(lam_im)
    m["ldt"] = pairlay(np.repeat(np.asarray(log_dt)[:, :, None], 64, axis=2))
    m["bre"] = pairlay(ssm_b_re); m["bim"] = pairlay(ssm_b_im)
    m["cre"] = pairlay(np.asarray(ssm_c_re).transpose(0, 1, 3, 2)); m["cim"] = pairlay(np.asarray(ssm_c_im).transpose(0, 1, 3, 2))
    def fm96(a):
        a = np.asarray(a, dtype=np.float32)
        o = np.zeros((2, 6, 128), np.float32)
        pad = np.zeros((2, 576), np.float32)
        pad[:, :512] = a
        o[:, :, :96] = pad.reshape(2, 6, 96)
        return f(o.transpose(0, 2, 1))
    m["dT"] = fm96(ssm_d); m["bgluT"] = fm96(b_glu)
    m["qnw"] = f(np.tile(np.asarray(q_norm_w), (1, 2))[:, :, None]); m["knw"] = f(np.tile(np.asarray(k_norm_w), (1, 2))[:, :, None])
    m["relb"] = f(rel_bias); m["oneh"] = _bucket_onehot(); m["sinks"] = f(sinks)
    m["onsT"] = fm96(out_norm_ssm); m["onaT"] = fm(out_norm_attn, 4)
    m["cwT"] = f(np.asarray(conv_w).reshape(2, 3, 44, 128).transpose(0, 3, 2, 1))
    m["cbT"] = fm(conv_b, 44)
    m["ident"] = np.eye(128, dtype=np.float32)
    m["antiI"] = np.ascontiguousarray(np.eye(128, dtype=np.float32)[::-1])
    return m


def kernel(**inputs):
    x = np.asarray(inputs["x"])
    B, seq, _ = x.shape
    nc = build(seq)
    maps = [prep_inputs(ci % B, **inputs) for ci in range(8)]
    res = run_bass_kernel_spmd(nc, maps, core_ids=list(range(8)))
    out = np.stack([np.asarray(res.results[b]["y"]) for b in range(B)], axis=0)
    return out.astype(np.float32)
```

```python
import numpy as np
from contextlib import ExitStack
import concourse.bass as bass
import concourse.mybir as mybir
from concourse.bass_utils import run_bass_kernel_spmd

F32 = mybir.dt.float32
BF = mybir.dt.bfloat16
AF = mybir.ActivationFunctionType
ALU = mybir.AluOpType
AX = mybir.AxisListType

D = 1024
TT = 512
NB = 4
TC = 8
NCH = TT // TC
DFF = 2816
NV = 22
EPS = 1e-6
NEG = -30000.0
NT6 = 6
STAGE = 99


def nr6(t):
    return 96 if t < 5 else 32


def np6(t):
    return 3 if t < 5 else 1


class Sched:
    ROT = 30000

    def __init__(s, nc, es):
        s.nc = nc
        s.es = es
        s.prog = {e: [] for e in ('pe', 'act', 'dve', 'pool', 'sp')}
        s.cnt = {e: 0 for e in s.prog}
        s.sem = {}
        s.allsems = []
        s.nsem = 0
        for e in s.prog:
            s._newsem(e)
        s.waited = {e: {} for e in s.prog}
        s.res = {}
        s.dsem = {}

    def _newsem(s, e):
        s.nsem += 1
        sm = s.es.enter_context(s.nc.semaphore(f"s_{e}_{s.nsem}"))
        s.sem[e] = sm
        s.cnt[e] = 0
        s.allsems.append([sm, 0])
        s.cur = getattr(s, 'cur', {})
        s.cur[e] = s.allsems[-1]

    def _deps(s, reads, writes):
        deps = {}

        def add(tok):
            if tok is None:
                return
            sem, val = tok
            k = id(sem)
            if k not in deps or deps[k][1] < val:
                deps[k] = (sem, val)
        for k in reads:
            r = s.res.get(k)
            if r:
                add(r[0])
        for k in writes:
            r = s.res.get(k)
            if r:
                add(r[0])
                for t in r[1].values():
                    add(t)
        return deps

    def _emit_waits(s, e, deps):
        for k, (sem, val) in deps.items():
            if s.waited[e].get(k, 0) < val:
                s.waited[e][k] = val
                s.prog[e].append(('w', sem, val))

    def _record(s, tok, reads, writes):
        kk = id(tok[0])
        for k in reads:
            r = s.res.setdefault(k, [None, {}])
            if kk not in r[1] or r[1][kk][1] < tok[1]:
                r[1][kk] = tok
        for k in writes:
            s.res[k] = [tok, {}]

    def op(s, e, fn, reads=(), writes=()):
        deps = s._deps(reads, writes)
        if e == 'pe':
            deps.pop(id(s.sem['pe']), None)
        s._emit_waits(e, deps)
        if s.cnt[e] >= s.ROT:
            s._newsem(e)
        s.cnt[e] += 1
        s.cur[e][1] = s.cnt[e]
        tok = (s.sem[e], s.cnt[e])
        s.prog[e].append(('i', fn, tok[0]))
        s._record(tok, reads, writes)

    def dma(s, e, out, in_, reads, writes, semkey, **kw):
        d = s.dsem.get(semkey)
        if d is None:
            sm = s.es.enter_context(s.nc.semaphore(f"d_{len(s.dsem)}"))
            d = [sm, 0]
            s.dsem[semkey] = d
            s.allsems.append(d)
        deps = s._deps(reads, writes)
        s._emit_waits(e, deps)
        d[1] += 16
        tok = (d[0], d[1])
        s.prog[e].append(('d', out, in_, tok[0], kw))
        s._record(tok, reads, writes)

    def barrier(s):
        for e in s.prog:
            deps = {id(sm): (sm, v) for sm, v in s.allsems if v > 0}
            if e == 'pe':
                deps.pop(id(s.sem['pe']), None)
            s._emit_waits(e, deps)
        s.res = {}

    def wait_all(s, e):
        deps = {id(sm): (sm, v) for sm, v in s.allsems if v > 0}
        s._emit_waits(e, deps)

    def emit(s, block):
        def run(eng, lst):
            for it in lst:
                if it[0] == 'w':
                    eng.wait_ge(it[1], it[2])
                elif it[0] == 'i':
                    it[1](eng).then_inc(it[2], 1)
                else:
                    eng.dma_start(out=it[1], in_=it[2], allow_slow_non_contiguous=True, **it[4]).then_inc(it[3], 16)

        @block.tensor
        def _(e):
            run(e, s.prog['pe'])

        @block.scalar
        def _(e):
            run(e, s.prog['act'])

        @block.vector
        def _(e):
            run(e, s.prog['dve'])

        @block.gpsimd
        def _(e):
            run(e, s.prog['pool'])

        @block.sync
        def _(e):
            run(e, s.prog['sp'])


def build(seq, dbg=False):
    nt = seq // TT
    nc = bass.Bass("TRN2", target_bir_lowering=False)

    def din(name, shape, dt=F32):
        return nc.dram_tensor(name, list(shape), dt, kind="ExternalInput").ap()

    def dscr(name, shape, dt=BF):
        return nc.dram_tensor(name, list(shape), dt, kind="Internal").ap()

    x_d = din("x", [seq, D])
    cT_d = din("cT", [128, 8])
    wmod_d = din("w_mod", [2, D, 6 * D])
    bmod_d = din("b_mod", [2, 6 * D])
    n1T_d = din("n1T", [2, 128, 8])
    n2T_d = din("n2T", [2, 128, 8])
    win_d = din("w_in", [2, D, 1280])
    wout_d = din("w_out", [2, D, D])
    wup_d = din("w_up", [2, D, 2 * DFF])
    wdn_d = din("w_down", [2, DFF, D])
    wglu_d = din("w_glu", [2, 512, 512])
    lamre_d = din("lamre", [2, 128, 16])
    lamim_d = din("lamim", [2, 128, 16])
    ldt_d = din("ldt", [2, 128, 16])
    bre_d = din("bre", [2, 128, 16, 16])
    bim_d = din("bim", [2, 128, 16, 16])
    cre_d = din("cre", [2, 128, 16, 16])
    cim_d = din("cim", [2, 128, 16, 16])
    dT_d = din("dT", [2, 128, 6])
    bgluT_d = din("bgluT", [2, 128, 6])
    qnw_d = din("qnw", [2, 128, 1])
    knw_d = din("knw", [2, 128, 1])
    relb_d = din("relb", [32, 8])
    oneh_d = din("oneh", [33, 384])
    sinks_d = din("sinks", [2, 8])
    onsT_d = din("onsT", [2, 128, 6])
    onaT_d = din("onaT", [2, 128, 4])
    cwT_d = din("cwT", [2, 128, 44, 3])
    cbT_d = din("cbT", [2, 128, 44])
    ident_d = din("ident", [128, 128])
    anti_d = din("antiI", [128, 128])
    y_d = nc.dram_tensor("y", [seq, D], F32, kind="ExternalOutput").ap()

    winu_s = dscr("winu_s", [2, 6, 128, 8 * 96])
    winq_s = dscr("winq_s", [2, 128, 8, 768])
    wout_s = dscr("wout_s", [2, 2, 128, 10 * 512])
    wglu_s = dscr("wglu_s", [2, 128, 6, 512])
    wup_s = dscr("wup_s", [2, 44, 128, 8 * 128])
    wdn_s = dscr("wdn_s", [2, 4, 128, NV * 256])
    kt_s = dscr("kt_s", [2, 128, 8, 6, 128])
    bf_s = dscr("bf_s", [2, 128, 6, 8, 2, 128])
    cs_s = dscr("cs_s", [2, 128, 8, 2, 16, 32])
    bd_s = dscr("bd_s", [8, 384], F32)
    ph_s = dscr("ph_s", [2, 128, 2, NCH + 1, 16], F32)

    with ExitStack() as es:
        S = Sched(nc, es)

        def sb(name, shape, dt=F32):
            return es.enter_context(nc.sbuf_tensor(name, list(shape), dt))

        ident = sb("ident_sb", [128, 128])
        identb = sb("identb", [128, 128], BF)
        ones_f = sb("ones_f", [128, 1])
        nhalf = sb("nhalf", [128, 10])
        epsc = sb("epsc", [128, 1])
        gA = sb("gA", [128, 2, 8]); shA = sb("shA", [128, 2, 8])
        gB = sb("gB", [128, 2, 8]); shB = sb("shB", [128, 2, 8])
        dT = sb("dTt", [128, 2, 6]); bglu = sb("bglu", [128, 2, 6])
        qsc = sb("qsc", [128, 2]); ksc = sb("ksc", [128, 2])
        ons = sb("ons", [128, 2, 6]); ona = sb("ona", [128, 2, 4])
        cw = sb("cw", [128, 2, 44, 3]); cb = sb("cb", [128, 2, 44])
        esink = sb("esink", [128, 2, 8])
        AT1 = sb("AT1", [128, 2, 32]); AT2 = sb("AT2", [128, 2, 32])
        biasT = sb("biasT", [128, 2, 8, 128])
        R8t = sb("R8t", [128, 2, 16])
        SC = sb("SC", [128, 2, 32])
        kT = sb("kT", [128, 2, 128 + TT], BF)
        Vaug = sb("Vaug", [128, 2, NB + 1, 2, 65], BF)
        HB = sb("HB", [128, 2, NV, 2, 2], BF)

        ps_t = [es.enter_context(nc.psum_tensor(f"ps{i}", [128, 1024], F32)) for i in range(4)]
        state = {'bank': 0}

        def bank():
            b = state['bank']
            state['bank'] = (b + 1) % 8
            return b

        def bank2():
            b = state['bank']
            if b % 2:
                b = (b + 1) % 8
            state['bank'] = (b + 2) % 8
            return b

        def pb(b, lo=0, hi=512):
            return ps_t[b // 2][:, (b % 2) * 512 + lo:(b % 2) * 512 + hi]

        def pk(b):
            return ('ps', b)

        pes = ExitStack()

        def psb(name, shape, dt=F32):
            return pes.enter_context(nc.sbuf_tensor(name, list(shape), dt))

        S.dma('sp', ident[:], ident_d[:, :], [], ['ident'], 'c0')
        S.op('dve', lambda e: e.tensor_copy(out=identb[:], in_=ident[:]), ['ident'], ['identb'])
        S.op('dve', lambda e: e.memset(ones_f[:], 1.0), [], ['ones_f'])
        S.op('dve', lambda e: e.memset(nhalf[:], -0.5), [], ['nhalf'])
        S.op('dve', lambda e: e.memset(epsc[:], EPS), [], ['epsc'])
        S.op('pool', lambda e: e.memset(SC[:], 0.0), [], ['SC0', 'SC1'])
        S.op('pool', lambda e: e.memset(kT[:], 0.0), [], ['kT0', 'kT1'])
        S.op('pool', lambda e: e.memset(Vaug[:], 0.0), [], ['Vaug'])
        S.op('pool', lambda e: e.memset(Vaug[:, :, :, :, 64:65], 1.0), [], ['Vaug'])
        S.op('pool', lambda e: e.memset(HB[:], 0.0), [], ['HB0', 'HB1'])

        small = [(dT, dT_d, 'dT'), (bglu, bgluT_d, 'bglu'), (ons, onsT_d, 'ons'), (ona, onaT_d, 'ona'),
                 (cb, cbT_d, 'cb')]
        for i, (t, d_, k) in enumerate(small):
            S.dma('sp', t[:], d_.rearrange("l p a -> p l a"), [], [k], f'c{i + 1}')
        S.dma('sp', cw[:], cwT_d.rearrange("l p a b -> p l a b"), [], ['cw'], 'c6')
        S.op('dve', lambda e: e.tensor_scalar(out=bglu[:], in0=bglu[:], scalar1=0.5, scalar2=None, op0=ALU.mult), ['bglu'], ['bglu'])
        S.op('dve', lambda e: e.tensor_scalar(out=ons[:], in0=ons[:], scalar1=0.25, scalar2=None, op0=ALU.mult), ['ons'], ['ons'])
        qn_t = psb("qn_t", [128, 2, 1]); kn_t = psb("kn_t", [128, 2, 1])
        S.dma('sp', qn_t[:], qnw_d.rearrange("l p a -> p l a"), [], ['qn_t'], 'c7')
        S.dma('sp', kn_t[:], knw_d.rearrange("l p a -> p l a"), [], ['kn_t'], 'c8')
        S.op('dve', lambda e: e.tensor_scalar(out=qsc[:], in0=qn_t[:, :, 0], scalar1=0.125, scalar2=None, op0=ALU.mult),
             ['qn_t'], ['qsc'])
        S.op('dve', lambda e: e.tensor_copy(out=ksc[:], in_=kn_t[:, :, 0]), ['kn_t'], ['ksc'])
        sk_t = psb("sk_t", [128, 2, 8])
        S.dma('sp', sk_t[:].rearrange("p l h -> p (l h)"),
              sinks_d.rearrange("l h -> (l h)").unsqueeze(0).to_broadcast([128, 16]), [], ['sk_t'], 'c9')
        S.op('act', lambda e: e.activation(out=esink[:], in_=sk_t[:], func=AF.Exp), ['sk_t'], ['esink'])

        rb = psb("rb", [33, 8]); oh = psb("oh", [33, 384])
        S.op('dve', lambda e: e.memset(rb[:], NEG), [], ['rb'])
        S.dma('sp', rb[0:32, :], relb_d[:, :], [], ['rb'], 'c10')
        S.dma('sp', oh[:], oneh_d[:, :], [], ['oh'], 'c11')
        b0 = bank()
        S.op('pe', lambda e: e.matmul(pb(b0)[0:8, 0:384], lhsT=rb[:, :], rhs=oh[:, :], start=True, stop=True),
             ['rb', 'oh'], [pk(b0)])
        bd_sb = psb("bd_sb", [8, 384])
        S.op('dve', lambda e: e.tensor_copy(out=bd_sb[:], in_=pb(b0)[0:8, 0:384]), [pk(b0)], ['bd_sb'])
        S.dma('sp', bd_s[:, :], bd_sb[:], ['bd_sb'], ['bd_s'], 'c12')
        antiI = psb("antiI_sb", [128, 128])
        S.dma('sp', antiI[:], anti_d[:, :], [], ['antiI'], 'cai')
        tmpb = psb("tmpb", [128, 2, 8, 128])
        for tl in range(2):
            src = bass.AP(tensor=bd_s.tensor, offset=128 * (1 - tl), ap=[[1, 128], [384, 8], [1, 128]])
            S.dma('sp', tmpb[:, tl], src, ['bd_s'], ['tmpb'], f'cb{tl}')
            for hh in range(2):
                b_ = bank()
                S.op('pe', lambda e, b_=b_, tl=tl, hh=hh: e.matmul(
                    pb(b_), lhsT=antiI[:, :], rhs=tmpb[:, tl, 4 * hh:4 * hh + 4, :].rearrange("p h q -> p (h q)"),
                    start=True, stop=True), ['antiI', 'tmpb'], [pk(b_)])
                S.op('dve', lambda e, b_=b_, tl=tl, hh=hh: e.tensor_copy(
                    out=biasT[:, tl, 4 * hh:4 * hh + 4, :].rearrange("p h q -> p (h q)"), in_=pb(b_)), [pk(b_)], ['biasT'])

        cT = psb("cTt", [128, 8]); cact = psb("cact", [128, 8]); cbc = psb("cbc", [128, 8, 128])
        S.dma('sp', cT[:], cT_d[:, :], [], ['cT'], 'c13')
        S.op('act', lambda e: e.activation(out=cact[:], in_=cT[:], func=AF.Silu), ['cT'], ['cact'])
        S.op('dve', lambda e: e.tensor_copy(out=cbc[:], in_=cact[:].unsqueeze(2).to_broadcast([128, 8, 128])),
             ['cact'], ['cbc'])
        grow = psb("grow", [128, 2, 2, D])
        wm = [psb(f"wm{i}", [128, 8, 512]) for i in range(2)]
        bmr = [psb(f"bmr{i}", [128, 512]) for i in range(2)]
        modt = [psb(f"modt{i}", [128, 512]) for i in range(2)]
        n1 = psb("n1", [128, 2, 8]); n2 = psb("n2", [128, 2, 8])
        S.dma('sp', n1[:], n1T_d.rearrange("l p a -> p l a"), [], ['n1'], 'c15')
        S.dma('sp', n2[:], n2T_d.rearrange("l p a -> p l a"), [], ['n2'], 'c16')
        dtmp = psb("dtmp", [128, 128]); sct = psb("sct", [128, 2, 8])
        cnt = 0
        for l in range(2):
            for cc in range(12):
                i_ = cnt % 2
                w_ = wm[i_]; wk = f'wm{i_}'; bm_ = bmr[i_]; bk = f'bmr{i_}'; mt_ = modt[i_]; mk = f'modt{i_}'
                cnt += 1
                S.dma('sp', w_[:], wmod_d[l].rearrange("(kt p) n -> p kt n", p=128)[:, :, cc * 512:(cc + 1) * 512],
                      [], [wk], wk)
                S.dma('sp', bm_[:], bmod_d[l:l + 1, cc * 512:(cc + 1) * 512].to_broadcast([128, 512]), [], [bk], bk)
                b_ = bank()
                for kt in range(8):
                    S.op('pe', lambda e, b_=b_, kt=kt, w_=w_: e.matmul(pb(b_), lhsT=cbc[:, kt, :], rhs=w_[:, kt, :],
                                                                       start=(kt == 0), stop=(kt == 7)),
                         ['cbc', wk], [pk(b_)])
                slot, hf = cc // 2, cc % 2
                if slot in (2, 5):
                    S.op('dve', lambda e, b_=b_, l=l, slot=slot, hf=hf, bm_=bm_: e.tensor_tensor(
                        out=grow[:, l, 0 if slot == 2 else 1, hf * 512:(hf + 1) * 512], in0=pb(b_), in1=bm_[:], op=ALU.add),
                        [pk(b_), bk], ['grow'])
                else:
                    S.op('dve', lambda e, b_=b_, bm_=bm_, mt_=mt_: e.tensor_tensor(out=mt_[:], in0=pb(b_), in1=bm_[:], op=ALU.add),
                         [pk(b_), bk], [mk])
                    dst = {0: shA, 3: shB}.get(slot)
                    for k4 in range(4):
                        kt = hf * 4 + k4
                        tgt = (dst[:, l, kt:kt + 1] if dst is not None else sct[:, 0 if slot == 1 else 1, kt:kt + 1])
                        S.op('dve', lambda e, mt_=mt_, k4=k4: e.tensor_tensor(
                            out=dtmp[:], in0=mt_[:, k4 * 128:(k4 + 1) * 128], in1=ident[:], op=ALU.mult),
                            [mk, 'ident'], ['dtmp'])
                        S.op('dve', lambda e, tgt=tgt: e.tensor_reduce(out=tgt, in_=dtmp[:], axis=AX.X, op=ALU.add),
                             ['dtmp'], ['sct', 'shA', 'shB'])
            S.op('dve', lambda e, l=l: e.scalar_tensor_tensor(out=gA[:, l, :], in0=sct[:, 0, :], scalar=1.0, in1=n1[:, l, :],
                                                              op0=ALU.add, op1=ALU.mult), ['sct', 'n1'], ['gA'])
            S.op('dve', lambda e, l=l: e.scalar_tensor_tensor(out=gB[:, l, :], in0=sct[:, 1, :], scalar=1.0, in1=n2[:, l, :],
                                                              op0=ALU.add, op1=ALU.mult), ['sct', 'n2'], ['gB'])

        NWS = 4
        wst = [psb(f"wst{i}", [128, 2048]) for i in range(NWS)]
        wsb = [psb(f"wsb{i}", [128, 2048], BF) for i in range(NWS)]
        cnt = 0

        def cast_piece(src_ap, stores, n, gate=None, rows=128):
            nonlocal cnt
            i = cnt % NWS
            cnt += 1
            S.dma('sp', wst[i][0:rows, 0:n], src_ap, [], [f'wst{i}'], f'wst{i}')
            if gate is None:
                S.op('dve', lambda e: e.tensor_copy(out=wsb[i][0:rows, 0:n], in_=wst[i][0:rows, 0:n]), [f'wst{i}'], [f'wsb{i}'])
            else:
                S.op('dve', lambda e: e.tensor_tensor(out=wsb[i][0:rows, 0:n], in0=wst[i][0:rows, 0:n], in1=gate[0:rows], op=ALU.mult),
                     [f'wst{i}', 'grow'], [f'wsb{i}'])
            for dst_ap, vf in stores:
                S.dma('act', dst_ap, vf(wsb[i]), [f'wsb{i}'], ['wscr'], f'wsbst{i}')

        def cast_all():
            for l in range(2):
                for kt in range(8):
                    cast_piece(win_d[l, kt * 128:(kt + 1) * 128, :], [
                        (winu_s[l, 0:5].rearrange("t p c -> p t c")[:, :, kt * 96:(kt + 1) * 96],
                         lambda w: w[:, 0:480].rearrange("p (t c) -> p t c", c=96)),
                        (winu_s[l, 5, :, kt * 96:kt * 96 + 32], lambda w: w[:, 480:512]),
                        (winq_s[l, :, kt, :], lambda w: w[:, 512:1280])], 1280)
                    yield
                    for c3 in range(4):
                        cast_piece(wup_d[l, kt * 128:(kt + 1) * 128, c3 * 1408:(c3 + 1) * 1408], [
                            (wup_s[l, c3 * 11:(c3 + 1) * 11].rearrange("c p x -> p c x")[:, :, kt * 128:(kt + 1) * 128],
                             lambda w: w[:, 0:1408].rearrange("p (c x) -> p c x", x=128))], 1408)
                        yield
                for t6 in range(NT6):
                    n_ = nr6(t6)
                    cast_piece(wglu_d[l, t6 * 96:t6 * 96 + n_, :], [(wglu_s[l, 0:n_, t6, :], lambda w, n_=n_: w[0:n_, 0:512])], 512, rows=n_)
                    yield
                    cast_piece(wout_d[l, t6 * 96:t6 * 96 + n_, :], [
                        (wout_s[l, :, 0:n_, t6 * 512:(t6 + 1) * 512].rearrange("h p c -> p h c"),
                         lambda w, n_=n_: w[0:n_, 0:1024].rearrange("p (h c) -> p h c", h=2))], 1024, gate=grow[:, l, 0, :], rows=n_)
                    yield
                for kt in range(4):
                    cast_piece(wout_d[l, 512 + kt * 128:512 + (kt + 1) * 128, :], [
                        (wout_s[l, :, :, (6 + kt) * 512:(7 + kt) * 512].rearrange("h p c -> p h c"),
                         lambda w: w[:, 0:1024].rearrange("p (h c) -> p h c", h=2))], 1024, gate=grow[:, l, 0, :])
                    yield
                for v in range(NV):
                    cast_piece(wdn_d[l, v * 128:(v + 1) * 128, :], [
                        (wdn_s[l, :, :, v * 256:(v + 1) * 256].rearrange("q p c -> p q c"),
                         lambda w: w[:, 0:1024].rearrange("p (q c) -> p q c", q=4))], 1024, gate=grow[:, l, 1, :])
                    yield


        castgen = cast_all()
        dvc = {'n': 0}

        def V(name, shape=(128, 16)):
            return psb(name, list(shape))
        lre = V("lre", (128, 2, 16)); lim = V("lim", (128, 2, 16)); ldt = V("ldtt", (128, 2, 16))
        S.dma('sp', lre[:], lamre_d.rearrange("l p a -> p l a"), [], ['lre'], 'c17')
        S.dma('sp', lim[:], lamim_d.rearrange("l p a -> p l a"), [], ['lim'], 'c18')
        S.dma('sp', ldt[:], ldt_d.rearrange("l p a -> p l a"), [], ['ldt'], 'c19')
        Bre = V("Bre", (128, 2, 16, 16)); Bim = V("Bim", (128, 2, 16, 16))
        Cre = V("Cre", (128, 2, 16, 16)); Cim = V("Cim", (128, 2, 16, 16))
        for i, (t, d_) in enumerate(((Bre, bre_d), (Bim, bim_d), (Cre, cre_d), (Cim, cim_d))):
            S.dma('sp', t[:], d_.rearrange("l p a b -> p l a b"), [], [f'BC{i}'], f'c2{i}')
        tnames = ['dt', 'zr', 'th', 'mag', 'sn', 'cs', 't1', 't2', 't3', 'ar', 'ai', 'fr', 'fi', 'pr', 'pi', 'qr', 'qi']
        tv = {n: V("v_" + n) for n in tnames}
        Er = V("Er", (128, 16, 16)); Ei = V("Ei", (128, 16, 16)); Gr = V("Gr", (128, 16, 16)); Gi = V("Gi", (128, 16, 16))
        X1 = V("X1", (128, 16, 16)); X2 = V("X2", (128, 16, 16)); X3 = V("X3", (128, 16, 16))
        Eblk = psb("Eblk", [128, 16, 8, 2, 32], BF)
        Gblk = psb("Gblk", [128, 8, 2, 16, 32], BF)
        Cw = psb("Cw", [128, 16, 2, 128], BF)
        tabev = psb("tabev", [128, 768], BF)
        PHsb = psb("PHsb", [128, 2, NCH + 1, 16])
        phu = psb("phu", [128, 2, 16]); pht = psb("pht", [128, 4, 16])

        def dv(fn, r, w):
            S.op('dve', fn, r, w)
            dvc['n'] += 1
            if dvc['n'] % 5 == 0:
                next(castgen, None)

        def tt(o, a, b, op, r=('ssmv',), w=('ssmv',)):
            dv(lambda e: e.tensor_tensor(out=o, in0=a, in1=b, op=op), list(r), list(w))

        def ts(o, a, s1, s2, op0, op1=None, r=('ssmv',), w=('ssmv',)):
            if op1 is None:
                dv(lambda e: e.tensor_scalar(out=o, in0=a, scalar1=s1, scalar2=None, op0=op0), list(r), list(w))
            else:
                dv(lambda e: e.tensor_scalar(out=o, in0=a, scalar1=s1, scalar2=s2, op0=op0, op1=op1), list(r), list(w))

        def bc(a):
            return a.unsqueeze(2).to_broadcast([128, 16, 16])

        def cmul_b(orr, oi, sr, si, xr, xi):
            tt(X1[:], xr, bc(sr), ALU.mult); tt(X2[:], xi, bc(si), ALU.mult)
            tt(X3[:], X1[:], X2[:], ALU.subtract)
            tt(X1[:], xr, bc(si), ALU.mult); tt(X2[:], xi, bc(sr), ALU.mult)
            tt(oi, X1[:], X2[:], ALU.add)
            dv(lambda e: e.tensor_copy(out=orr, in_=X3[:]), ['ssmv'], ['ssmv'])

        for l in range(2):
            rk = ['ssmv', 'lre', 'lim', 'ldt', 'BC0', 'BC1', 'BC2', 'BC3']
            t = {k: v[:] for k, v in tv.items()}
            S.op('act', lambda e, l=l: e.activation(out=tv['dt'][:], in_=ldt[:, l, :], func=AF.Exp), ['ldt', 'ssmv'], ['ssmv'])
            ts(t['t1'], lre[:, l, :], -1e-4, None, ALU.min, r=rk)
            tt(t['zr'], t['t1'], t['dt'], ALU.mult)
            tt(t['th'], lim[:, l, :], t['dt'], ALU.mult, r=rk)
            ts(t['mag'], t['zr'], 1.0 / 720, 1.0 / 120, ALU.mult, ALU.add)
            for cf in (1.0 / 24, 1.0 / 6, 0.5, 1.0, 1.0):
                tt(t['mag'], t['mag'], t['zr'], ALU.mult)
                ts(t['mag'], t['mag'], cf, None, ALU.add)
            ts(t['t2'], t['th'], 1.0 / 32, None, ALU.mult)
            tt(t['t3'], t['t2'], t['t2'], ALU.mult)
            ts(t['sn'], t['t3'], 1.0 / 362880, -1.0 / 5040, ALU.mult, ALU.add)
            for cf in (1.0 / 120, -1.0 / 6, 1.0):
                tt(t['sn'], t['sn'], t['t3'], ALU.mult)
                ts(t['sn'], t['sn'], cf, None, ALU.add)
            tt(t['sn'], t['sn'], t['t2'], ALU.mult)
            ts(t['cs'], t['t3'], -1.0 / 3628800, 1.0 / 40320, ALU.mult, ALU.add)
            for cf in (-1.0 / 720, 1.0 / 24, -0.5, 1.0):
                tt(t['cs'], t['cs'], t['t3'], ALU.mult)
                ts(t['cs'], t['cs'], cf, None, ALU.add)
            for _ in range(5):
                tt(t['pr'], t['cs'], t['cs'], ALU.mult); tt(t['pi'], t['sn'], t['sn'], ALU.mult)
                tt(t['qr'], t['sn'], t['cs'], ALU.mult)
                tt(t['cs'], t['pr'], t['pi'], ALU.subtract)
                ts(t['sn'], t['qr'], 2.0, None, ALU.mult)
            tt(t['ar'], t['mag'], t['cs'], ALU.mult); tt(t['ai'], t['mag'], t['sn'], ALU.mult)
            tt(t['pr'], t['t1'], t['t1'], ALU.mult); tt(t['pi'], lim[:, l, :], lim[:, l, :], ALU.mult, r=rk)
            tt(t['pr'], t['pr'], t['pi'], ALU.add)
            dv(lambda e: e.reciprocal(out=tv['pr'][:], in_=tv['pr'][:]), ['ssmv'], ['ssmv'])
            ts(t['qr'], t['ar'], -1.0, None, ALU.add)
            tt(t['t2'], t['qr'], t['t1'], ALU.mult); tt(t['t3'], t['ai'], lim[:, l, :], ALU.mult, r=rk)
            tt(t['t2'], t['t2'], t['t3'], ALU.add); tt(t['fr'], t['t2'], t['pr'], ALU.mult)
            tt(t['t2'], t['ai'], t['t1'], ALU.mult); tt(t['t3'], t['qr'], lim[:, l, :], ALU.mult, r=rk)
            tt(t['t2'], t['t2'], t['t3'], ALU.subtract); tt(t['fi'], t['t2'], t['pr'], ALU.mult)
            cmul_b(Er[:], Ei[:], t['fr'], t['fi'], Bre[:, l], Bim[:, l])
            cmul_b(Gr[:], Gi[:], t['ar'], t['ai'], Cre[:, l], Cim[:, l])
            S.op('pool', lambda e: e.memset(Eblk[:], 0.0), ['Eblk'], ['Eblk'])
            S.op('pool', lambda e: e.memset(Gblk[:], 0.0), ['Gblk'], ['Gblk'])
            S.op('pool', lambda e: e.memset(Cw[:], 0.0), ['Cw'], ['Cw'])
            for two in range(2):
                ps_ = slice(64 * two, 64 * two + 64)
                for q in range(3):
                    prs = slice(q, 16, 3)
                    S.op('dve', lambda e, ps_=ps_, two=two, q=q, prs=prs, l=l: e.tensor_copy(
                        out=Cw[ps_, prs, 0, 32 * q + 16 * two:32 * q + 16 * two + 16], in_=Cre[ps_, l, prs, :]),
                        ['BC2', 'Cw'], ['Cw'])
                    S.op('dve', lambda e, ps_=ps_, two=two, q=q, prs=prs, l=l: e.tensor_scalar(
                        out=Cw[ps_, prs, 1, 32 * q + 16 * two:32 * q + 16 * two + 16], in0=Cim[ps_, l, prs, :],
                        scalar1=-1.0, scalar2=None, op0=ALU.mult), ['BC3', 'Cw'], ['Cw'])
            for d_ in range(8):
                if d_ > 0:
                    cmul_b(Er[:], Ei[:], t['ar'], t['ai'], Er[:], Ei[:])
                    cmul_b(Gr[:], Gi[:], t['ar'], t['ai'], Gr[:], Gi[:])
                for two in range(2):
                    ps_ = slice(64 * two, 64 * two + 64)
                    cs_ = slice(16 * two, 16 * two + 16)
                    for part, (E_, G_) in enumerate(((Er, Gr), (Ei, Gi))):
                        S.op('dve', lambda e, ps_=ps_, cs_=cs_, d_=d_, part=part, E_=E_: e.tensor_copy(
                            out=Eblk[ps_, :, d_, part, cs_], in_=E_[ps_, :, :]), ['ssmv', 'Eblk'], ['Eblk'])
                        if part == 0:
                            S.op('dve', lambda e, ps_=ps_, cs_=cs_, d_=d_, G_=G_: e.tensor_copy(
                                out=Gblk[ps_, d_, 0, :, cs_], in_=G_[ps_, :, :]), ['ssmv', 'Gblk'], ['Gblk'])
                        else:
                            S.op('dve', lambda e, ps_=ps_, cs_=cs_, d_=d_, G_=G_: e.tensor_scalar(
                                out=Gblk[ps_, d_, 1, :, cs_], in0=G_[ps_, :, :], scalar1=-1.0, scalar2=None, op0=ALU.mult),
                                ['ssmv', 'Gblk'], ['Gblk'])
            tt(t['t2'], t['mag'], t['mag'], ALU.mult); tt(t['t3'], t['t2'], t['t2'], ALU.mult)
            dv(lambda e, l=l: e.tensor_tensor(out=R8t[:, l, :], in0=tv['t3'][:], in1=tv['t3'][:], op=ALU.mult), ['ssmv'], ['R8t'])

            def csq(orr, oi, xr, xi):
                tt(t['t2'], xr, xr, ALU.mult); tt(t['t3'], xi, xi, ALU.mult)
                tt(t['mag'], xr, xi, ALU.mult)
                tt(orr, t['t2'], t['t3'], ALU.subtract)
                ts(oi, t['mag'], 2.0, None, ALU.mult)
            csq(t['pr'], t['pi'], t['ar'], t['ai'])
            csq(t['qr'], t['qi'], t['pr'], t['pi'])
            csq(t['pr'], t['pi'], t['qr'], t['qi'])
            dv(lambda e, l=l: e.tensor_copy(out=AT1[:, l, 0:16], in_=tv['pr'][:]), ['ssmv'], ['AT'])
            dv(lambda e, l=l: e.tensor_copy(out=AT1[:, l, 16:32], in_=tv['pr'][:]), ['ssmv'], ['AT'])
            dv(lambda e, l=l: e.tensor_copy(out=AT2[:, l, 16:32], in_=tv['pi'][:]), ['ssmv'], ['AT'])
            dv(lambda e, l=l: e.tensor_scalar(out=AT2[:, l, 0:16], in0=tv['pi'][:], scalar1=-1.0, scalar2=None, op0=ALU.mult),
               ['ssmv'], ['AT'])
            dv(lambda e, l=l: e.reciprocal(out=tv['t1'][:], in_=R8t[:, l, :]), ['R8t', 'ssmv'], ['ssmv'])
            tt(t['qr'], t['pr'], t['t1'], ALU.mult); tt(t['qi'], t['pi'], t['t1'], ALU.mult)
            dv(lambda e: e.tensor_copy(out=phu[:, 0, :], in_=tv['qr'][:]), ['ssmv', 'phu'], ['phu'])
            dv(lambda e: e.tensor_copy(out=phu[:, 1, :], in_=tv['qi'][:]), ['ssmv', 'phu'], ['phu'])
            S.op('pool', lambda e: e.memset(PHsb[:, 0, 0, :], 1.0), ['PHsb'], ['PHsb'])
            S.op('pool', lambda e: e.memset(PHsb[:, 1, 0, :], 0.0), ['PHsb'], ['PHsb'])

            def ptt(o, a_, b_, op, r, w):
                S.op('pool', lambda e: e.tensor_tensor(out=o, in0=a_, in1=b_, op=op), r, w)
            for c in range(NCH):
                cr, ci = PHsb[:, 0, c, :], PHsb[:, 1, c, :]
                ptt(pht[:, 0, :], cr, phu[:, 0, :], ALU.mult, ['phu', 'PHsb', 'pht0'], ['pht0'])
                ptt(pht[:, 1, :], ci, phu[:, 1, :], ALU.mult, ['phu', 'PHsb', 'pht1'], ['pht1'])
                ptt(PHsb[:, 0, c + 1, :], pht[:, 0, :], pht[:, 1, :], ALU.subtract, ['pht0', 'pht1'], ['PHsb'])
                ptt(pht[:, 2, :], cr, phu[:, 1, :], ALU.mult, ['phu', 'PHsb', 'pht2'], ['pht2'])
                ptt(pht[:, 3, :], ci, phu[:, 0, :], ALU.mult, ['phu', 'PHsb', 'pht3'], ['pht3'])
                ptt(PHsb[:, 1, c + 1, :], pht[:, 2, :], pht[:, 3, :], ALU.add, ['pht2', 'pht3'], ['PHsb'])
            S.dma('sp', ph_s[l], PHsb[:], ['PHsb'], ['ph_s'], 'tb3')
            S.dma('sp', cs_s[l], Gblk[:], ['Gblk'], ['cs_s'], 'tb0')
            for d_ in range(8):
                b_ = bank2()
                S.op('dve', lambda e, b_=b_: e.memset(ps_t[b_ // 2][:, 0:768], 0.0), [], [pk(b_), pk(b_ + 1)])
                for pr in range(16):
                    t6, q = pr // 3, pr % 3
                    for part in range(2):
                        S.op('pe', lambda e, b_=b_, t6=t6, q=q, pr=pr, part=part, d_=d_: e.matmul(
                            ps_t[b_ // 2][32 * q:32 * q + 32, t6 * 128:(t6 + 1) * 128], lhsT=Eblk[:, pr, d_, part, :],
                            rhs=Cw[:, pr, part, :], start=(part == 0), stop=(part == 1)),
                            ['Eblk', 'Cw'], [pk(b_), pk(b_ + 1)])
                S.op('dve', lambda e, b_=b_: e.tensor_copy(out=tabev[:], in_=ps_t[b_ // 2][:, 0:768]), [pk(b_), pk(b_ + 1)], ['tabev'])
                S.dma('sp', kt_s[l, :, d_].rearrange("p g c -> p (g c)"), tabev[:], ['tabev'], ['kt_s'], 'tb1')
            for j in range(8):
                for part in range(2):
                    b_ = bank2()
                    S.op('dve', lambda e, b_=b_: e.memset(ps_t[b_ // 2][:, 0:768], 0.0), [], [pk(b_), pk(b_ + 1)])
                    for pr in range(16):
                        t6, q = pr // 3, pr % 3
                        S.op('pe', lambda e, b_=b_, t6=t6, q=q, pr=pr, part=part, j=j: e.matmul(
                            ps_t[b_ // 2][32 * q:32 * q + 32, t6 * 128:(t6 + 1) * 128], lhsT=Eblk[:, pr, 7 - j, part, :],
                            rhs=identb[:, :], start=True, stop=True), ['Eblk', 'identb'], [pk(b_), pk(b_ + 1)])
                    S.op('dve', lambda e, b_=b_: e.tensor_copy(out=tabev[:], in_=ps_t[b_ // 2][:, 0:768]), [pk(b_), pk(b_ + 1)], ['tabev'])
                    S.dma('sp', bf_s[l, :, :, j, part, :], tabev[:].rearrange("p (g c) -> p g c", g=6), ['tabev'], ['bf_s'], 'tb2')

        for _ in castgen:
            pass
        S.barrier()
        pes.close()

        xt = sb("xt", [128, NB, D])
        tok = [sb(f"tok{i}", [128, D]) for i in range(2)]
        sqj = sb("sqj", [128, D], BF)
        rst = sb("rst", [128, 8])
        featT = sb("featT", [128, 8, TT], BF)
        uT = sb("uT", [128, NT6, TC, NCH], BF)
        qkv_sq = sb("qkv_sq", [128, 640])
        qn = sb("qn", [128, 640])
        qnb = [qn, qkv_sq]; QNK = ['qn', 'qkv_sq']
        qT = sb("qT", [128, NB, 512], BF)
        SAw = sb("SAw", [128, NCH + 1, 32])
        Sprev = sb("Sprev", [128, 32, NCH], BF)
        f32all = sb("f32all", [128, 6, TT])
        f32b = [f32all[:, i, :] for i in range(6)]
        PHt = sb("PHt", [128, 2, NCH + 1, 16])
        ysb = f32b[0:2]; ytmp = f32b[2:4]; S_sb = f32b[4:6]; sgate = f32b[0:2]
        YSK = ["f32b0", "f32b1"]; YTK = ["f32b2", "f32b3"]; SSK = ["f32b4", "f32b5"]
        zT = sb("zT", [128, NT6, TT], BF)
        ssmT = sb("ssmT", [128, NT6, TT], BF)
        attnT = sb("attnT", [128, 4, TT], BF)
        PT = [sb(f"PT{i}", [128, 2, 512], BF) for i in range(2)]
        actT = sb("actT", [128, NV, TT], BF)
        U = [sb(f"U{i}", [128, 2, TT + 2], BF) for i in range(2)]
        dg = [sb(f"dg{i}", [128, 6, 128], BF) for i in range(2)]
        NWCH = 4
        wch = [sb(f"wch{i}", [128, 8, 128], BF) for i in range(NWCH)]
        wsl = [sb(f"wsl{i}", [128, 6144], BF) for i in range(2)]
        wslc = {"n": 0}

        def wslot():
            i = wslc["n"] % 2
            wslc["n"] += 1
            return wsl[i], f"wsl{i}"
        tab = sb("tab", [128, 14336], BF)
        BFt = tab[:, 0:12288].rearrange("p (a b c d) -> p a b c d", a=6, b=8, c=2)
        KTt = tab[:, 0:6144].rearrange("p (a b c) -> p a b c", a=8, b=6)
        CSt = tab[:, 6144:14336].rearrange("p (a b c d) -> p a b c d", a=8, b=2, c=16)
        st1 = sb("st1", [128, 16])
        rs_s = sb("rs_s", [128, NB]); rs_a = sb("rs_a", [128, NB])
        ctmp = [sb(f"ctmp{i}", [128, 32]) for i in range(2)]
        wcnt = {'n': 0}

        def load_chunk(src):
            i = wcnt['n'] % NWCH
            wcnt['n'] += 1
            S.dma('sp', wch[i][:], src, [], [f'wch{i}'], f'wch{i}')
            return wch[i], f'wch{i}'

        XK = [f'xt{b}' for b in range(NB)]

        def rstd_pow(out_ap, in_ap, scale, n, rkeys, wkeys):
            S.op('pool', lambda e: e.tensor_scalar(out=out_ap, in0=in_ap, scalar1=scale, scalar2=EPS, op0=ALU.mult, op1=ALU.add),
                 list(rkeys), list(wkeys))
            S.op('pool', lambda e: e.tensor_tensor(out=out_ap, in0=out_ap, in1=nhalf[:, 0:n], op=ALU.pow),
                 list(wkeys) + ['nhalf'], list(wkeys))

        def rms_to_featT(l, g_t, sh_t):
            for b in range(NB):
                S.op('act', lambda e, b=b: e.activation(out=sqj[:], in_=xt[:, b, :], func=AF.Square,
                                                        accum_out=rst[:, b:b + 1]), [XK[b]], ['sqj', f'rst{b}'])
            for b in range(NB):
                rstd_pow(rst[:, 4 + b:5 + b], rst[:, b:b + 1], 1.0 / D, 1, [f'rst{b}'], [f'rstd{b}'])
            for b in range(NB):
                tk = tok[b % 2]
                tkk = f'tok{b % 2}'
                S.op('dve', lambda e, b=b, tk=tk: e.tensor_scalar(out=tk[:], in0=xt[:, b, :], scalar1=rst[:, 4 + b:5 + b], scalar2=None,
                                                                   op0=ALU.mult), [XK[b], f'rstd{b}', tkk], [tkk])
                for half in range(2):
                    b_ = bank()
                    for j in range(4):
                        kt = half * 4 + j
                        S.op('pe', lambda e, b_=b_, j=j, kt=kt, tk=tk: e.transpose(
                            out=pb(b_, j * 128, (j + 1) * 128), in_=tk[:, kt * 128:(kt + 1) * 128], identity=ident[:]),
                            [tkk, 'ident'], [pk(b_)])
                    for j in range(4):
                        kt = half * 4 + j
                        if kt < 3:
                            S.op('dve', lambda e, b_=b_, j=j, kt=kt, b=b: e.tensor_scalar(
                                out=featT[:, kt, b * 128:(b + 1) * 128], in0=pb(b_, j * 128, (j + 1) * 128),
                                scalar1=g_t[:, l, kt:kt + 1], scalar2=sh_t[:, l, kt:kt + 1], op0=ALU.mult, op1=ALU.add),
                                [pk(b_), 'gA', 'gB', 'shA', 'shB'], ['featT'])
                        else:
                            S.op('act', lambda e, b_=b_, j=j, kt=kt, b=b: e.activation(
                                out=featT[:, kt, b * 128:(b + 1) * 128], in_=pb(b_, j * 128, (j + 1) * 128), func=AF.Identity,
                                bias=sh_t[:, l, kt:kt + 1], scale=g_t[:, l, kt:kt + 1]), [pk(b_), 'gA', 'gB', 'shA', 'shB'], ['featT'])

        def tile_layer(ti, l):
            if True:
                SAk = 'SAw'; SCk = f'SC{l}'; kTk = f'kT{l}'; HBk = f'HB{l}'
                if ti == 0 and l == 0:
                    S.dma('sp', tab[:, 0:12288], bf_s[l].rearrange("p a b c d -> p (a b c d)"), [], ['tab'], 'tab')
                S.dma('sp', PHt[:], ph_s[l], [], ['PHt'], 'PHt')
                wq_t, wqk = wslot()
                wqkv = wq_t[:, :].rearrange("p (k c) -> p k c", k=8)
                S.dma('sp', wq_t[:, :], winq_s[l].rearrange("p k c -> p (k c)"), [], [wqk], wqk)
                wg_t, wgk = wslot()
                wgl = wg_t[:, 0:3072].rearrange("p (k c) -> p k c", k=NT6)
                S.dma('sp', wg_t[:, 0:3072], wglu_s[l].rearrange("p k c -> p (k c)"), [], [wgk], wgk)
                rms_to_featT(l, gA, shA)
                for t6 in range(NT6):
                    n_ = nr6(t6)
                    i_ = wcnt['n'] % NWCH
                    wcnt['n'] += 1
                    wc, wk = wch[i_], f'wch{i_}'
                    S.dma('sp', wc[:, :, 0:n_], winu_s[l, t6].rearrange("p (k c) -> p k c", c=96)[:, :, 0:n_], [], [wk], wk)
                    b_ = bank()
                    for kt in range(8):
                        S.op('pe', lambda e, b_=b_, kt=kt, wc=wc, n_=n_: e.matmul(pb(b_)[0:n_, :], lhsT=wc[:, kt, 0:n_], rhs=featT[:, kt, :],
                                                                                 start=(kt == 0), stop=(kt == 7)),
                             [wk, 'featT'], [pk(b_)])
                    S.op('act', lambda e, b_=b_, t6=t6, n_=n_: e.activation(
                        out=uT[0:n_, t6].rearrange("p j c -> p c j"), in_=pb(b_)[0:n_, :].rearrange("p (c j) -> p c j", j=TC),
                        func=AF.Copy), [pk(b_)], ['uT'])
                for q in range(3):
                    combos = [(part, pr) for part in range(2) for pr in range(16) if pr % 3 == q]
                    b2 = bank2()
                    for sl, (part, pr) in enumerate(combos):
                        t6 = pr // 3
                        bb = b2 + sl // 8
                        for j in range(TC):
                            S.op('pe', lambda e, bb=bb, sl=sl, t6=t6, q=q, j=j, part=part: e.matmul(
                                pb(bb, (sl % 8) * 64, (sl % 8) * 64 + 64), lhsT=BFt[32 * q:32 * q + 32, t6, j, part, :],
                                rhs=uT[32 * q:32 * q + 32, t6, j, :], start=(j == 0), stop=(j == TC - 1)),
                                ['tab', 'uT'], [pk(bb)])
                    for sl, (part, pr) in enumerate(combos):
                        bb = b2 + sl // 8
                        S.op('dve', lambda e, bb=bb, sl=sl, part=part, pr=pr: e.tensor_copy(
                            out=SAw[:, 1:NCH + 1, part * 16 + pr], in_=pb(bb, (sl % 8) * 64, (sl % 8) * 64 + 64)), [pk(bb)], [SAk])
                S.dma('sp', tab[:, 0:6144], kt_s[l].rearrange("p a b c -> p (a b c)"), [], ['tab'], 'tab')
                S.dma('sp', tab[:, 6144:14336], cs_s[l].rearrange("p a b c d -> p (a b c d)"), [], ['tab'], 'tab2')
                def chain_pre():
                    S.op('pool', lambda e: e.tensor_copy(out=SAw[:, 0, :], in_=SC[:, l, :]), [SCk, SAk], [SAk])
                    T2 = f32all[:, 0:4, :].rearrange("p a (c t q) -> p (a c) t q", t=2, q=16)
                    T2K = ['f32b0', 'f32b1', 'f32b2', 'f32b3']
                    Fv = SAw[:, 1:NCH + 1, :].rearrange("p c (t q) -> p c t q", t=2)
                    cosf, sinf = PHt[:, 0, 1:NCH + 1, :], PHt[:, 1, 1:NCH + 1, :]
                    S.op('pool', lambda e: e.tensor_tensor(out=T2[:, :, 0, :], in0=Fv[:, :, 1, :], in1=sinf, op=ALU.mult), [SAk, 'PHt'] + T2K, T2K)
                    S.op('pool', lambda e: e.tensor_tensor(out=T2[:, :, 1, :], in0=Fv[:, :, 0, :], in1=sinf, op=ALU.mult), [SAk, 'PHt'] + T2K, T2K)
                    S.op('pool', lambda e: e.tensor_tensor(out=Fv, in0=Fv, in1=cosf.unsqueeze(2).to_broadcast([128, NCH, 2, 16]), op=ALU.mult),
                         [SAk, 'PHt'], [SAk])
                    S.op('pool', lambda e: e.tensor_tensor(out=Fv[:, :, 0, :], in0=Fv[:, :, 0, :], in1=T2[:, :, 0, :], op=ALU.add), [SAk] + T2K, [SAk])
                    S.op('pool', lambda e: e.tensor_tensor(out=Fv[:, :, 1, :], in0=Fv[:, :, 1, :], in1=T2[:, :, 1, :], op=ALU.subtract), [SAk] + T2K, [SAk])
                    return T2, T2K

                def chain_scan():
                    for s_ in range(32):
                        q_ = s_ % 16
                        S.op('dve', lambda e, s_=s_, q_=q_: e.tensor_tensor_scan(
                            out=SAw[:, 1:NCH + 1, s_], data0=R8t[:, l, q_:q_ + 1].to_broadcast([128, NCH]), data1=SAw[:, 1:NCH + 1, s_],
                            initial=SAw[:, 0, s_:s_ + 1], op0=ALU.mult, op1=ALU.add), [SAk, 'R8t'], [SAk])

                def chain_post(T2, T2K):
                    Wv = SAw[:, 0:NCH, :].rearrange("p c (t q) -> p c t q", t=2)
                    cosb, sinb = PHt[:, 0, 0:NCH, :], PHt[:, 1, 0:NCH, :]
                    Spv = Sprev[:].rearrange("p (t q) c -> p c t q", t=2)
                    W64 = SAw[:, NCH, :]
                    S.op('pool', lambda e: e.tensor_tensor(out=ctmp[0][:, 0:16], in0=W64[:, 16:32], in1=PHt[:, 1, NCH, :], op=ALU.mult), [SAk, 'PHt'], ['ct0'])
                    S.op('pool', lambda e: e.tensor_tensor(out=ctmp[0][:, 16:32], in0=W64[:, 0:16], in1=PHt[:, 1, NCH, :], op=ALU.mult), [SAk, 'PHt', 'ct0'], ['ct0'])
                    S.op('pool', lambda e: e.tensor_tensor(out=ctmp[1][:, 0:16], in0=W64[:, 0:16], in1=PHt[:, 0, NCH, :], op=ALU.mult), [SAk, 'PHt'], ['ct1'])
                    S.op('pool', lambda e: e.tensor_tensor(out=ctmp[1][:, 16:32], in0=W64[:, 16:32], in1=PHt[:, 0, NCH, :], op=ALU.mult), [SAk, 'PHt', 'ct1'], ['ct1'])
                    S.op('pool', lambda e: e.tensor_tensor(out=SC[:, l, 0:16], in0=ctmp[1][:, 0:16], in1=ctmp[0][:, 0:16], op=ALU.subtract), ['ct0', 'ct1', SCk], [SCk])
                    S.op('pool', lambda e: e.tensor_tensor(out=SC[:, l, 16:32], in0=ctmp[1][:, 16:32], in1=ctmp[0][:, 16:32], op=ALU.add), ['ct0', 'ct1', SCk], [SCk])
                    S.op('pool', lambda e: e.tensor_tensor(out=T2[:, :, 0, :], in0=Wv[:, :, 1, :], in1=sinb, op=ALU.mult), [SAk, 'PHt'] + T2K, T2K)
                    S.op('pool', lambda e: e.tensor_tensor(out=T2[:, :, 1, :], in0=Wv[:, :, 0, :], in1=sinb, op=ALU.mult), [SAk, 'PHt'] + T2K, T2K)
                    S.op('pool', lambda e: e.tensor_tensor(out=Wv, in0=Wv, in1=cosb.unsqueeze(2).to_broadcast([128, NCH, 2, 16]), op=ALU.mult),
                         [SAk, 'PHt'], [SAk])
                    S.op('pool', lambda e: e.tensor_tensor(out=Spv[:, :, 0, :], in0=Wv[:, :, 0, :], in1=T2[:, :, 0, :], op=ALU.subtract), [SAk] + T2K, ['Sprev'])
                    S.op('pool', lambda e: e.tensor_tensor(out=Spv[:, :, 1, :], in0=Wv[:, :, 1, :], in1=T2[:, :, 1, :], op=ALU.add), [SAk] + T2K + ['Sprev'], ['Sprev'])

                def att_A1(b):
                    Q = qnb[b % 2]; QK = QNK[b % 2]
                    b2 = bank2()
                    for kt in range(8):
                        S.op('pe', lambda e, b2=b2, kt=kt, b=b: e.matmul(pb(b2), lhsT=featT[:, kt, b * 128:(b + 1) * 128],
                                                                        rhs=wqkv[:, kt, 0:512], start=(kt == 0), stop=(kt == 7)),
                             ['featT', wqk], [pk(b2)])
                    for kt in range(8):
                        S.op('pe', lambda e, b2=b2, kt=kt, b=b: e.matmul(pb(b2 + 1, 0, 256), lhsT=featT[:, kt, b * 128:(b + 1) * 128],
                                                                        rhs=wqkv[:, kt, 512:768], start=(kt == 0), stop=(kt == 7)),
                             ['featT', wqk], [pk(b2 + 1)])
                    qk_ps = ps_t[b2 // 2][:, 0:640]
                    S.op('act', lambda e, qk_ps=qk_ps, Q=Q: e.activation(out=Q[:], in_=qk_ps, func=AF.Square),
                         [pk(b2), pk(b2 + 1)], [QK])
                    S.op('dve', lambda e, Q=Q: e.tensor_reduce(out=st1[:, 4:14], in_=Q[:].rearrange("p (h d) -> p h d", d=64),
                                                          axis=AX.X, op=ALU.add), [QK], ['st1'])
                    S.op('act', lambda e: e.activation(out=st1[:, 4:14], in_=st1[:, 4:14], func=AF.Sqrt, bias=epsc[:, 0:1],
                                                       scale=1.0 / 64), ['st1', 'epsc'], ['st1'])
                    S.op('dve', lambda e: e.reciprocal(out=st1[:, 4:14], in_=st1[:, 4:14]), ['st1'], ['st1'])
                    S.op('dve', lambda e, qk_ps=qk_ps, Q=Q: e.tensor_tensor(
                        out=Q[:, 0:512].rearrange("p (m t d) -> p t m d", m=4, t=2),
                        in0=qk_ps[:, 0:512].rearrange("p (t m d) -> p t m d", t=2, m=4),
                        in1=st1[:, 4:12].rearrange("p (t m) -> p t m", t=2).unsqueeze(3).to_broadcast([128, 2, 4, 64]), op=ALU.mult),
                        [pk(b2), pk(b2 + 1), 'st1', QK], [QK])
                    S.op('dve', lambda e, qk_ps=qk_ps, Q=Q: e.tensor_tensor(
                        out=Q[:, 512:640].rearrange("p (h d) -> p h d", d=64), in0=qk_ps[:, 512:640].rearrange("p (h d) -> p h d", d=64),
                        in1=st1[:, 12:14].unsqueeze(2).to_broadcast([128, 2, 64]), op=ALU.mult),
                        [pk(b2), pk(b2 + 1), 'st1', QK], [QK])
                    S.op('act', lambda e, b2=b2, b=b: e.activation(
                        out=Vaug[:, l, b + 1, :, 0:64], in_=pb(b2 + 1, 128, 256).rearrange("p (g d) -> p g d", g=2),
                        func=AF.Copy), [pk(b2 + 1)], ['Vaug'])

                def att_A2(b):
                    Q = qnb[b % 2]; QK = QNK[b % 2]
                    tb = bank2()
                    for m in range(4):
                        S.op('pe', lambda e, tb=tb, m=m, Q=Q: e.transpose(
                            out=pb(tb, m * 128, (m + 1) * 128),
                            in_=Q[:, m * 128:(m + 1) * 128], identity=ident[:]),
                            [QK, 'ident'], [pk(tb)])
                    S.op('pe', lambda e, tb=tb, Q=Q: e.transpose(out=pb(tb + 1, 0, 128), in_=Q[:, 512:640], identity=ident[:]),
                         [QK, 'ident'], [pk(tb + 1)])
                    S.op('act', lambda e, tb=tb, b=b: e.activation(
                        out=qT[:, b, :], in_=pb(tb), func=AF.Identity,
                        scale=qsc[:, l:l + 1]), [pk(tb), 'qsc'], ['qT'])
                    S.op('act', lambda e, tb=tb, b=b: e.activation(
                        out=kT[:, l, 128 + b * 128:128 + (b + 1) * 128], in_=pb(tb + 1, 0, 128), func=AF.Identity,
                        scale=ksc[:, l:l + 1]), [pk(tb + 1), 'ksc'], [kTk])

                BST = {}

                def att_B1(b):
                    first = (ti == 0 and b == 0)
                    tiles = [1] if first else [0, 1]
                    BST[b] = tiles
                    for g in range(2):
                        gs = slice(64 * g, 64 * g + 64)
                        pt = PT[g]; ptk = f'PT{g}'
                        for tl in tiles:
                            sb_ = bank()
                            kcol = b * 128 + tl * 128
                            S.op('pe', lambda e, sb_=sb_, gs=gs, kcol=kcol, b=b: e.matmul(
                                pb(sb_), lhsT=kT[gs, l, kcol:kcol + 128], rhs=qT[gs, b, :],
                                start=True, stop=True), [kTk, 'qT'], [pk(sb_)])
                            ssb_ = S_sb[tl]; ssk = SSK[tl]
                            S.op('dve', lambda e, sb_=sb_, ssb_=ssb_, tl=tl, g=g: e.tensor_tensor(
                                out=ssb_[:].rearrange("p (m q) -> p m q", m=4), in0=pb(sb_).rearrange("p (m q) -> p m q", m=4),
                                in1=biasT[:, tl, 4 * g:4 * g + 4, :], op=ALU.add), [pk(sb_), 'biasT'], [ssk])
                            S.op('act', lambda e, ssb_=ssb_, pt=pt, tl=tl: e.activation(out=pt[:, tl, :], in_=ssb_[:], func=AF.Exp),
                                 [ssk], [ptk])

                def att_B2(b):
                    tiles = BST[b]
                    ob = bank2()
                    for g in range(2):
                        pt = PT[g]; ptk = f'PT{g}'
                        for m in range(4):
                            for ii, tl in enumerate(tiles):
                                S.op('pe', lambda e, ob=ob, g=g, m=m, tl=tl, ii=ii, pt=pt, b=b, n_=len(tiles): e.matmul(
                                    pb(ob + g, m * 65, m * 65 + 65), lhsT=pt[:, tl, m * 128:(m + 1) * 128],
                                    rhs=Vaug[:, l, b + tl, g, :], start=(ii == 0), stop=(ii == n_ - 1)),
                                    [ptk, 'Vaug'], [pk(ob + g)])
                    at = tok[b % 2]; atk = f'tok{b % 2}'
                    for g in range(2):
                        o3 = pb(ob + g, 0, 260).rearrange("p (m d) -> p m d", m=4)
                        S.op('dve', lambda e, o3=o3, g=g: e.tensor_tensor(out=st1[:, 0:4], in0=o3[:, :, 64], in1=esink[:, l, 4 * g:4 * g + 4],
                                                                          op=ALU.add), [pk(ob + g), 'esink', 'st1'], ['st1'])
                        S.op('dve', lambda e: e.reciprocal(out=st1[:, 0:4], in_=st1[:, 0:4]), ['st1'], ['st1'])
                        S.op('dve', lambda e, o3=o3, g=g, at=at: e.tensor_tensor(
                            out=at[:, 256 * g:256 * g + 256].rearrange("p (m d) -> p m d", m=4), in0=o3[:, :, 0:64],
                            in1=st1[:, 0:4].unsqueeze(2).to_broadcast([128, 4, 64]), op=ALU.mult),
                            [pk(ob + g), 'st1', atk], [atk])
                    S.op('act', lambda e, at=at, b=b: e.activation(out=qkv_sq[:, 0:512], in_=at[:, 0:512], func=AF.Square,
                                                                   accum_out=rs_a[:, b:b + 1]), [atk, 'qkv_sq'], ['qkv_sq', 'rs_a'])
                    rstd_pow(rs_a[:, b:b + 1], rs_a[:, b:b + 1], 1.0 / 512, 1, ['rs_a'], ['rs_a'])

                def att_B3(b):
                    at = tok[b % 2]; atk = f'tok{b % 2}'
                    tb = bank()
                    for m in range(4):
                        S.op('pe', lambda e, tb=tb, m=m, at=at: e.transpose(out=pb(tb, m * 128, (m + 1) * 128),
                                                                           in_=at[:, m * 128:(m + 1) * 128], identity=ident[:]),
                             [atk, 'ident'], [pk(tb)])
                    for m in range(4):
                        S.op('act', lambda e, tb=tb, m=m, b=b: e.activation(
                            out=attnT[:, m, b * 128:(b + 1) * 128], in_=pb(tb, m * 128, (m + 1) * 128), func=AF.Identity,
                            scale=ona[:, l, m:m + 1]), [pk(tb), 'ona'], ['attnT'])

                def Y_tile(t6):
                    n_ = nr6(t6)
                    b_ = bank()
                    for i in range(TC):
                        for j in range(i + 1):
                            S.op('pe', lambda e, b_=b_, i=i, j=j, t6=t6, n_=n_: e.matmul(
                                pb(b_, i * 64, i * 64 + 64)[0:n_, :], lhsT=KTt[0:n_, i - j, t6, 0:n_], rhs=uT[0:n_, t6, j, :],
                                start=(j == 0), stop=False, skip_group_check=True),
                                ['tab', 'uT'], [pk(b_)])
                        for q in range(np6(t6)):
                            pr = t6 * 3 + q
                            for part in range(2):
                                last = (q == np6(t6) - 1 and part == 1)
                                S.op('pe', lambda e, b_=b_, i=i, q=q, pr=pr, part=part, last=last: e.matmul(
                                    pb(b_, i * 64, i * 64 + 64)[32 * q:32 * q + 32, :], lhsT=CSt[:, i, part, pr, :],
                                    rhs=Sprev[:, part * 16 + pr, :], start=False, stop=last, skip_group_check=True),
                                    ['tab', 'Sprev'], [pk(b_)])
                    yb = ysb[t6 % 2]; yk = YSK[t6 % 2]; yt_ = ytmp[t6 % 2]; ytk = YTK[t6 % 2]
                    S.op('dve', lambda e, b_=b_, t6=t6, yb=yb, n_=n_: e.scalar_tensor_tensor(
                        out=yb[0:n_, :], in0=uT[0:n_, t6].rearrange("p j c -> p (j c)"), scalar=dT[0:n_, l, t6:t6 + 1], in1=pb(b_)[0:n_, :],
                        op0=ALU.mult, op1=ALU.add), ['uT', 'dT', pk(b_)], [yk])
                    S.op('act', lambda e, yb=yb, yt_=yt_, n_=n_: e.activation(out=yt_[0:n_, :], in_=yb[0:n_, :], func=AF.Square), [yk], [ytk])
                    S.op('dve', lambda e, yt_=yt_, n_=n_: e.tensor_scalar(out=yt_[0:n_, :], in0=yt_[0:n_, :], scalar1=0.044715, scalar2=1.0,
                                                                           op0=ALU.mult, op1=ALU.add), [ytk], [ytk])
                    S.op('dve', lambda e, yt_=yt_, yb=yb, n_=n_: e.tensor_tensor(out=yt_[0:n_, :], in0=yt_[0:n_, :], in1=yb[0:n_, :], op=ALU.mult),
                         [ytk, yk], [ytk])
                    S.op('act', lambda e, yt_=yt_, n_=n_: e.activation(out=yt_[0:n_, :], in_=yt_[0:n_, :], func=AF.Tanh, scale=0.7978845608),
                         [ytk], [ytk])
                    S.op('dve', lambda e, yt_=yt_, yb=yb, t6=t6, n_=n_: e.scalar_tensor_tensor(
                        out=zT[0:n_, t6, :].rearrange("p (c j) -> p j c", j=TC), in0=yt_[0:n_, :].rearrange("p (j c) -> p j c", j=TC),
                        scalar=1.0, in1=yb[0:n_, :].rearrange("p (j c) -> p j c", j=TC), op0=ALU.add, op1=ALU.mult), [ytk, yk], ['zT'])
                T2, T2K = chain_pre()
                att_A1(0)
                att_A1(1)
                att_A2(0)
                chain_scan()
                att_A1(2)
                att_A2(1)
                att_A1(3)
                att_A2(2)
                chain_post(T2, T2K)
                att_A2(3)
                att_B1(0); Y_tile(0); att_B2(0); Y_tile(1); att_B3(0)
                att_B1(1); Y_tile(2); att_B2(1); Y_tile(3); att_B3(1)
                att_B1(2); Y_tile(4); att_B2(2); Y_tile(5); att_B3(2)
                att_B1(3); att_B2(3); att_B3(3)
                S.op('pool', lambda e: e.tensor_copy(out=kT[:, l, 0:128], in_=kT[:, l, TT:TT + 128]), [kTk], [kTk])
                S.op('pool', lambda e: e.tensor_copy(out=Vaug[:, l, 0, :, :], in_=Vaug[:, l, NB, :, :]), ['Vaug'], ['Vaug'])
                if not (ti == nt - 1 and l == 1):
                    S.dma('sp', tab[:, 0:12288], bf_s[1 - l].rearrange("p a b c d -> p (a b c d)"), [], ['tab'], 'tab')
                ssb = bank()
                for m in range(NT6):
                    no = nr6(m)
                    b_ = bank()
                    if b_ == ssb:
                        b_ = bank()
                    for t6 in range(NT6):
                        n_ = nr6(t6)
                        S.op('pe', lambda e, b_=b_, m=m, t6=t6, n_=n_, no=no: e.matmul(
                            pb(b_)[0:no, :], lhsT=wgl[0:n_, t6, m * 96:m * 96 + no], rhs=zT[0:n_, t6, :],
                            start=(t6 == 0), stop=(t6 == NT6 - 1)), [wgk, 'zT'], [pk(b_)])
                    yb = ysb[m % 2]; yk = YSK[m % 2]; yt_ = ytmp[m % 2]; ytk = YTK[m % 2]
                    S.op('act', lambda e, b_=b_, m=m, yb=yb, no=no: e.activation(out=yb[0:no, :], in_=pb(b_)[0:no, :], func=AF.Tanh,
                                                                                bias=bglu[0:no, l, m:m + 1], scale=0.25),
                         [pk(b_), 'bglu'], [yk])
                    S.op('dve', lambda e, m=m, yb=yb, no=no: e.scalar_tensor_tensor(out=yb[0:no, :], in0=yb[0:no, :], scalar=1.0, in1=zT[0:no, m, :],
                                                                                    op0=ALU.add, op1=ALU.mult), [yk, 'zT'], [yk])
                    S.op('act', lambda e, yb=yb, yt_=yt_, no=no: e.activation(out=yt_[0:no, :], in_=yb[0:no, :], func=AF.Square), [yk], [ytk])
                    S.op('dve', lambda e, m=m, yb=yb, no=no: e.tensor_scalar(out=ssmT[0:no, m, :], in0=yb[0:no, :], scalar1=ons[0:no, l, m:m + 1],
                                                                              scalar2=None, op0=ALU.mult), [yk, 'ons'], ['ssmT'])
                    for b in range(NB):
                        S.op('pe', lambda e, m=m, b=b, yt_=yt_, ssb=ssb, no=no: e.matmul(
                            pb(ssb)[:, m * NB + b:m * NB + b + 1], lhsT=yt_[0:no, b * 128:(b + 1) * 128], rhs=ones_f[0:no, 0:1],
                            start=True, stop=True), [ytk, 'ones_f'], [pk(ssb)])
                S.op('dve', lambda e, ssb=ssb: e.tensor_reduce(out=rs_s[:], in_=pb(ssb)[:, 0:NT6 * NB].rearrange("p (m b) -> p b m", b=NB),
                                                              axis=AX.X, op=ALU.add), [pk(ssb)], ['rs_s'])
                rstd_pow(rs_s[:], rs_s[:], 1.0 / (16 * 512), NB, ['rs_s'], ['rs_s'])
                for half in range(2):
                    wo_t, wok = wslot()
                    woh = wo_t[:, 0:5120].rearrange("p (k c) -> p k c", k=10)
                    S.dma('sp', wo_t[:, 0:5120], wout_s[l, half], [], [wok], wok)
                    for b in range(NB):
                        ba = bank(); bb_ = bank()
                        for t6 in range(NT6):
                            n_ = nr6(t6)
                            S.op('pe', lambda e, ba=ba, t6=t6, b=b, n_=n_, woh=woh: e.matmul(pb(ba), lhsT=ssmT[0:n_, t6, b * 128:(b + 1) * 128],
                                                                                   rhs=woh[0:n_, t6, :], start=(t6 == 0), stop=(t6 == NT6 - 1)),
                                 ['ssmT', wok], [pk(ba)])
                        for ft in range(4):
                            S.op('pe', lambda e, bb_=bb_, ft=ft, b=b, woh=woh: e.matmul(pb(bb_), lhsT=attnT[:, ft, b * 128:(b + 1) * 128],
                                                                              rhs=woh[:, 6 + ft, :], start=(ft == 0), stop=(ft == 3)),
                                 ['attnT', wok], [pk(bb_)])
                        xs = xt[:, b, half * 512:(half + 1) * 512]
                        S.op('dve', lambda e, ba=ba, b=b, xs=xs: e.scalar_tensor_tensor(
                            out=xs, in0=pb(ba), scalar=rs_s[:, b:b + 1], in1=xs, op0=ALU.mult, op1=ALU.add),
                            [pk(ba), 'rs_s', XK[b]], [XK[b]])
                        S.op('dve', lambda e, bb_=bb_, b=b, xs=xs: e.scalar_tensor_tensor(
                            out=xs, in0=pb(bb_), scalar=rs_a[:, b:b + 1], in1=xs, op0=ALU.mult, op1=ALU.add),
                            [pk(bb_), 'rs_a', XK[b]], [XK[b]])
                rms_to_featT(l, gB, shB)
                ups_all = {}

                def stage_up(v):
                    dgv = dg[v % 2]; dgk = f'dg{v % 2}'
                    ups = []
                    for vg in range(2):
                        wc, wk = load_chunk(wup_s[l, vg * NV + v].rearrange("p (k c) -> p k c", k=8))
                        b_ = bank()
                        for kt in range(8):
                            S.op('pe', lambda e, b_=b_, kt=kt, wc=wc: e.matmul(pb(b_), lhsT=wc[:, kt, :], rhs=featT[:, kt, :],
                                                                               start=(kt == 0), stop=(kt == 7)),
                                 [wk, 'featT'], [pk(b_)])
                        ups.append(b_)
                    ups_all[v] = ups
                    S.op('pool', lambda e, dgv=dgv, v=v: e.tensor_tensor(
                        out=dgv[:, :, :].rearrange("p (g j) c -> p g j c", g=2),
                        in0=identb[:].unsqueeze(1).unsqueeze(1).to_broadcast([128, 2, 3, 128]),
                        in1=cw[:, l].rearrange("p (g v) j -> p g v j", g=2)[:, :, v, :].unsqueeze(3).to_broadcast([128, 2, 3, 128]),
                        op=ALU.mult), ['identb', 'cw', dgk], [dgk])

                def stage_mid(v):
                    Uv = U[v % 2]; Uk = f'U{v % 2}'; dgv = dg[v % 2]; dgk = f'dg{v % 2}'
                    sg = sgate[v % 2]; sgk = YSK[v % 2]
                    ups = ups_all.pop(v)
                    S.op('pool', lambda e, Uv=Uv, v=v: e.tensor_copy(out=Uv[:, :, 0:2], in_=HB[:, l, v, :, :]), [HBk, Uk], [Uk])
                    for vg in range(2):
                        S.op('act', lambda e, Uv=Uv, vg=vg, b_=ups[vg]: e.activation(out=Uv[:, vg, 2:TT + 2], in_=pb(b_), func=AF.Copy),
                             [pk(ups[vg]), Uk], [Uk])
                    S.op('pool', lambda e, Uv=Uv, v=v: e.tensor_copy(out=HB[:, l, v, :, :], in_=Uv[:, :, TT:TT + 2]), [Uk, HBk], [HBk])
                    cps = []
                    for vg in range(2):
                        b_ = bank()
                        for j in range(3):
                            S.op('pe', lambda e, b_=b_, vg=vg, j=j, dgv=dgv, Uv=Uv: e.matmul(
                                pb(b_), lhsT=dgv[:, vg * 3 + j, :], rhs=Uv[:, vg, j:j + TT], start=(j == 0), stop=(j == 2)),
                                [dgk, Uk], [pk(b_)])
                        cps.append(b_)
                    S.op('act', lambda e, sg=sg, b_=cps[1], v=v: e.activation(out=sg[:], in_=pb(b_), func=AF.Silu,
                                                                             bias=cb[:, l, NV + v:NV + v + 1], scale=1.0),
                         [pk(cps[1]), 'cb'], [sgk])
                    S.op('dve', lambda e, sg=sg, b_=cps[0], v=v: e.scalar_tensor_tensor(
                        out=actT[:, v, :], in0=pb(b_), scalar=cb[:, l, v:v + 1], in1=sg[:], op0=ALU.add, op1=ALU.mult),
                        [pk(cps[0]), 'cb', sgk], ['actT'])

                stage_up(0)
                for v in range(NV):
                    if v + 1 < NV:
                        stage_up(v + 1)
                    stage_mid(v)
                for qt in range(4):
                    wd_t, wdk = wslot()
                    wdh = wd_t[:, 0:5632].rearrange("p (k c) -> p k c", k=NV)
                    S.dma('sp', wd_t[:, 0:5632], wdn_s[l, qt], [], [wdk], wdk)
                    for b in range(NB):
                        b_ = bank()
                        for v in range(NV):
                            S.op('pe', lambda e, b_=b_, v=v, b=b, wdh=wdh: e.matmul(pb(b_, 0, 256), lhsT=actT[:, v, b * 128:(b + 1) * 128],
                                                                          rhs=wdh[:, v, :], start=(v == 0), stop=(v == NV - 1)),
                                 ['actT', wdk], [pk(b_)])
                        xs = xt[:, b, qt * 256:(qt + 1) * 256]
                        S.op('dve', lambda e, b_=b_, xs=xs: e.tensor_tensor(out=xs, in0=pb(b_, 0, 256), in1=xs, op=ALU.add),
                             [pk(b_), XK[b]], [XK[b]])
                        if l == 1 and qt == 3:
                            tk = tok[b % 2]; tkk = f'tok{b % 2}'
                            S.op('act', lambda e, tk=tk, b=b: e.activation(out=tk[:], in_=xt[:, b, :], func=AF.Copy), [XK[b], tkk], [tkk])
                            S.dma('sp', y_d[ti * TT + b * 128: ti * TT + (b + 1) * 128, :], tk[:], [tkk], ['y'], f'yst{b % 2}')
                            if ti + 1 < nt:
                                S.dma('sp', xt[:, b, :], x_d[(ti + 1) * TT + b * 128:(ti + 1) * TT + (b + 1) * 128, :], [], [XK[b]], f'xld{b}')

        S.dma('sp', xt[:], x_d[0:TT, :].rearrange("(b p) f -> p b f", p=128), [], XK, 'xt')
        for ti in range(nt):
            for l in range(2):
                tile_layer(ti, l)
        S.wait_all('sp')
        block = es.enter_context(nc.Block())
        S.emit(block)
    return nc


def _bucket_onehot():
    nb, md = 32, 128
    idx = np.arange(384)
    dist = idx - 127
    n = np.maximum(dist, 0)
    max_exact = nb // 2
    log_part = np.log(np.maximum(n, 1) / max_exact) / np.log(md / max_exact)
    large = max_exact + (log_part * (nb - max_exact)).astype(np.int32)
    large = np.minimum(large, nb - 1)
    bucket = np.where(n < max_exact, n, large).astype(np.int32)
    valid = (dist >= 0) & (dist < 128)
    oh = np.zeros((33, 384), np.float32)
    for i in range(384):
        if valid[i]:
            oh[bucket[i], i] = 1.0
        else:
            oh[32, i] = 1.0
    return oh


def prep_inputs(b, x, c, w_mod, b_mod, norm1_w, w_in, lam_re, lam_im, log_dt, ssm_b_re, ssm_b_im, ssm_c_re, ssm_c_im,
                ssm_d, w_glu, b_glu, q_norm_w, k_norm_w, rel_bias, sinks, out_norm_ssm, out_norm_attn, w_out, norm2_w,
                w_up, conv_w, conv_b, w_down):
    f = lambda a: np.ascontiguousarray(a, dtype=np.float32)

    def fm(a, n):
        return f(np.asarray(a).reshape(2, n, 128).transpose(0, 2, 1))

    def pairlay(a):
        a = np.asarray(a)
        sh = a.shape
        a = a.reshape(2, 16, 2, 64, *sh[3:])
        a = np.moveaxis(a, 1, 3)
        return f(a.reshape(2, 128, 16, *sh[3:]))
    m = {}
    m["x"] = f(x[b])
    m["cT"] = f(np.asarray(c[b]).reshape(8, 128).T)
    m["w_mod"] = f(w_mod); m["b_mod"] = f(b_mod)
    m["n1T"] = fm(norm1_w, 8); m["n2T"] = fm(norm2_w, 8)
    m["w_in"] = f(w_in); m["w_out"] = f(w_out); m["w_up"] = f(w_up); m["w_down"] = f(w_down); m["w_glu"] = f(w_glu)
    m["lamre"] = pairlay(lam_re); m["lamim"] = pairlay(lam_im)
    m["ldt"] = pairlay(np.repeat(np.asarray(log_dt)[:, :, None], 64, axis=2))
    m["bre"] = pairlay(ssm_b_re); m["bim"] = pairlay(ssm_b_im)
    m["cre"] = pairlay(np.asarray(ssm_c_re).transpose(0, 1, 3, 2)); m["cim"] = pairlay(np.asarray(ssm_c_im).transpose(0, 1, 3, 2))
    def fm96(a):
        a = np.asarray(a, dtype=np.float32)
        o = np.zeros((2, 6, 128), np.float32)
        pad = np.zeros((2, 576), np.float32)
        pad[:, :512] = a
        o[:, :, :96] = pad.reshape(2, 6, 96)
        return f(o.transpose(0, 2, 1))
    m["dT"] = fm96(ssm_d); m["bgluT"] = fm96(b_glu)
    m["qnw"] = f(np.tile(np.asarray(q_norm_w), (1, 2))[:, :, None]); m["knw"] = f(np.tile(np.asarray(k_norm_w), (1, 2))[:, :, None])
    m["relb"] = f(rel_bias); m["oneh"] = _bucket_onehot(); m["sinks"] = f(sinks)
    m["onsT"] = fm96(out_norm_ssm); m["onaT"] = fm(out_norm_attn, 4)
    m["cwT"] = f(np.asarray(conv_w).reshape(2, 3, 44, 128).transpose(0, 3, 2, 1))
    m["cbT"] = fm(conv_b, 44)
    m["ident"] = np.eye(128, dtype=np.float32)
    m["antiI"] = np.ascontiguousarray(np.eye(128, dtype=np.float32)[::-1])
    return m


def kernel(**inputs):
    x = np.asarray(inputs["x"])
    B, seq, _ = x.shape
    nc = build(seq)
    maps = [prep_inputs(ci, **inputs) for ci in range(B)]
    res = run_bass_kernel_spmd(nc, maps, core_ids=list(range(B)))
    out = np.stack([np.asarray(res.results[b]["y"]) for b in range(B)], axis=0)
    return out.astype(np.float32)
```

```python
import numpy as np
from contextlib import ExitStack
import concourse.bass as bass
import concourse.mybir as mybir
from concourse.bass_utils import run_bass_kernel_spmd

F32 = mybir.dt.float32
BF = mybir.dt.bfloat16
AF = mybir.ActivationFunctionType
ALU = mybir.AluOpType
AX = mybir.AxisListType

D = 1024
TT = 512
NB = 4
TC = 8
NCH = TT // TC
DFF = 2816
NV = 22
EPS = 1e-6
NEG = -30000.0
NT6 = 6
STAGE = 99


def nr6(t):
    return 96 if t < 5 else 32


def np6(t):
    return 3 if t < 5 else 1


class Sched:
    ROT = 30000

    def __init__(s, nc, es):
        s.nc = nc
        s.es = es
        s.prog = {e: [] for e in ('pe', 'act', 'dve', 'pool', 'sp')}
        s.cnt = {e: 0 for e in s.prog}
        s.sem = {}
        s.allsems = []
        s.nsem = 0
        for e in s.prog:
            s._newsem(e)
        s.waited = {e: {} for e in s.prog}
        s.res = {}
        s.dsem = {}

    def _newsem(s, e):
        s.nsem += 1
        sm = s.es.enter_context(s.nc.semaphore(f"s_{e}_{s.nsem}"))
        s.sem[e] = sm
        s.cnt[e] = 0
        s.allsems.append([sm, 0])
        s.cur = getattr(s, 'cur', {})
        s.cur[e] = s.allsems[-1]

    def _deps(s, reads, writes):
        deps = {}

        def add(tok):
            if tok is None:
                return
            sem, val = tok
            k = id(sem)
            if k not in deps or deps[k][1] < val:
                deps[k] = (sem, val)
        for k in reads:
            r = s.res.get(k)
            if r:
                add(r[0])
        for k in writes:
            r = s.res.get(k)
            if r:
                add(r[0])
                for t in r[1].values():
                    add(t)
        return deps

    def _emit_waits(s, e, deps):
        for k, (sem, val) in deps.items():
            if s.waited[e].get(k, 0) < val:
                s.waited[e][k] = val
                s.prog[e].append(('w', sem, val))

    def _record(s, tok, reads, writes):
        kk = id(tok[0])
        for k in reads:
            r = s.res.setdefault(k, [None, {}])
            if kk not in r[1] or r[1][kk][1] < tok[1]:
                r[1][kk] = tok
        for k in writes:
            s.res[k] = [tok, {}]

    def op(s, e, fn, reads=(), writes=()):
        deps = s._deps(reads, writes)
        if e == 'pe':
            deps.pop(id(s.sem['pe']), None)
        s._emit_waits(e, deps)
        if s.cnt[e] >= s.ROT:
            s._newsem(e)
        s.cnt[e] += 1
        s.cur[e][1] = s.cnt[e]
        tok = (s.sem[e], s.cnt[e])
        s.prog[e].append(('i', fn, tok[0]))
        s._record(tok, reads, writes)

    def dma(s, e, out, in_, reads, writes, semkey, **kw):
        d = s.dsem.get(semkey)
        if d is None:
            sm = s.es.enter_context(s.nc.semaphore(f"d_{len(s.dsem)}"))
            d = [sm, 0]
            s.dsem[semkey] = d
            s.allsems.append(d)
        deps = s._deps(reads, writes)
        s._emit_waits(e, deps)
        d[1] += 16
        tok = (d[0], d[1])
        s.prog[e].append(('d', out, in_, tok[0], kw))
        s._record(tok, reads, writes)

    def barrier(s):
        for e in s.prog:
            deps = {id(sm): (sm, v) for sm, v in s.allsems if v > 0}
            if e == 'pe':
                deps.pop(id(s.sem['pe']), None)
            s._emit_waits(e, deps)
        s.res = {}

    def wait_all(s, e):
        deps = {id(sm): (sm, v) for sm, v in s.allsems if v > 0}
        s._emit_waits(e, deps)

    def emit(s, block):
        def run(eng, lst):
            for it in lst:
                if it[0] == 'w':
                    eng.wait_ge(it[1], it[2])
                elif it[0] == 'i':
                    it[1](eng).then_inc(it[2], 1)
                else:
                    eng.dma_start(out=it[1], in_=it[2], allow_slow_non_contiguous=True, **it[4]).then_inc(it[3], 16)

        @block.tensor
        def _(e):
            run(e, s.prog['pe'])

        @block.scalar
        def _(e):
            run(e, s.prog['act'])

        @block.vector
        def _(e):
            run(e, s.prog['dve'])

        @block.gpsimd
        def _(e):
            run(e, s.prog['pool'])

        @block.sync
        def _(e):
            run(e, s.prog['sp'])


def build(seq, dbg=False):
    nt = seq // TT
    nc = bass.Bass("TRN2", target_bir_lowering=False)

    def din(name, shape, dt=F32):
        return nc.dram_tensor(name, list(shape), dt, kind="ExternalInput").ap()

    def dscr(name, shape, dt=BF):
        return nc.dram_tensor(name, list(shape), dt, kind="Internal").ap()

    x_d = din("x", [seq, D])
    cT_d = din("cT", [128, 8])
    wmod_d = din("w_mod", [2, D, 6 * D])
    bmod_d = din("b_mod", [2, 6 * D])
    n1T_d = din("n1T", [2, 128, 8])
    n2T_d = din("n2T", [2, 128, 8])
    win_d = din("w_in", [2, D, 1280])
    wout_d = din("w_out", [2, D, D])
    wup_d = din("w_up", [2, D, 2 * DFF])
    wdn_d = din("w_down", [2, DFF, D])
    wglu_d = din("w_glu", [2, 512, 512])
    lamre_d = din("lamre", [2, 128, 16])
    lamim_d = din("lamim", [2, 128, 16])
    ldt_d = din("ldt", [2, 128, 16])
    bre_d = din("bre", [2, 128, 16, 16])
    bim_d = din("bim", [2, 128, 16, 16])
    cre_d = din("cre", [2, 128, 16, 16])
    cim_d = din("cim", [2, 128, 16, 16])
    dT_d = din("dT", [2, 128, 6])
    bgluT_d = din("bgluT", [2, 128, 6])
    qnw_d = din("qnw", [2, 128, 1])
    knw_d = din("knw", [2, 128, 1])
    relb_d = din("relb", [32, 8])
    oneh_d = din("oneh", [33, 384])
    sinks_d = din("sinks", [2, 8])
    onsT_d = din("onsT", [2, 128, 6])
    onaT_d = din("onaT", [2, 128, 4])
    cwT_d = din("cwT", [2, 128, 44, 3])
    cbT_d = din("cbT", [2, 128, 44])
    ident_d = din("ident", [128, 128])
    anti_d = din("antiI", [128, 128])
    y_d = nc.dram_tensor("y", [seq, D], F32, kind="ExternalOutput").ap()

    winu_s = dscr("winu_s", [2, 6, 128, 8 * 96])
    winq_s = dscr("winq_s", [2, 128, 8, 768])
    wout_s = dscr("wout_s", [2, 2, 128, 10 * 512])
    wglu_s = dscr("wglu_s", [2, 128, 6, 512])
    wup_s = dscr("wup_s", [2, 44, 128, 8 * 128])
    wdn_s = dscr("wdn_s", [2, 4, 128, NV * 256])
    kt_s = dscr("kt_s", [2, 128, 8, 6, 128])
    bf_s = dscr("bf_s", [2, 128, 6, 8, 2, 128])
    cs_s = dscr("cs_s", [2, 128, 8, 2, 16, 32])
    bd_s = dscr("bd_s", [8, 384], F32)
    ph_s = dscr("ph_s", [2, 128, 2, NCH + 1, 16], F32)

    with ExitStack() as es:
        S = Sched(nc, es)

        def sb(name, shape, dt=F32):
            return es.enter_context(nc.sbuf_tensor(name, list(shape), dt))

        ident = sb("ident_sb", [128, 128])
        identb = sb("identb", [128, 128], BF)
        ones_f = sb("ones_f", [128, 1])
        nhalf = sb("nhalf", [128, 10])
        epsc = sb("epsc", [128, 1])
        gA = sb("gA", [128, 2, 8]); shA = sb("shA", [128, 2, 8])
        gB = sb("gB", [128, 2, 8]); shB = sb("shB", [128, 2, 8])
        dT = sb("dTt", [128, 2, 6]); bglu = sb("bglu", [128, 2, 6])
        qsc = sb("qsc", [128, 2]); ksc = sb("ksc", [128, 2])
        ons = sb("ons", [128, 2, 6]); ona = sb("ona", [128, 2, 4])
        cw = sb("cw", [128, 2, 44, 3]); cb = sb("cb", [128, 2, 44])
        esink = sb("esink", [128, 2, 8])
        AT1 = sb("AT1", [128, 2, 32]); AT2 = sb("AT2", [128, 2, 32])
        biasT = sb("biasT", [128, 2, 8, 128])
        R8t = sb("R8t", [128, 2, 16])
        SC = sb("SC", [128, 2, 32])
        kT = sb("kT", [128, 2, 128 + TT], BF)
        Vaug = sb("Vaug", [128, 2, NB + 1, 2, 65], BF)
        HB = sb("HB", [128, 2, NV, 2, 2], BF)

        ps_t = [es.enter_context(nc.psum_tensor(f"ps{i}", [128, 1024], F32)) for i in range(4)]
        state = {'bank': 0}

        def bank():
            b = state['bank']
            state['bank'] = (b + 1) % 8
            return b

        def bank2():
            b = state['bank']
            if b % 2:
                b = (b + 1) % 8
            state['bank'] = (b + 2) % 8
            return b

        def pb(b, lo=0, hi=512):
            return ps_t[b // 2][:, (b % 2) * 512 + lo:(b % 2) * 512 + hi]

        def pk(b):
            return ('ps', b)

        pes = ExitStack()

        def psb(name, shape, dt=F32):
            return pes.enter_context(nc.sbuf_tensor(name, list(shape), dt))

        S.dma('sp', ident[:], ident_d[:, :], [], ['ident'], 'c0')
        S.op('dve', lambda e: e.tensor_copy(out=identb[:], in_=ident[:]), ['ident'], ['identb'])
        S.op('dve', lambda e: e.memset(ones_f[:], 1.0), [], ['ones_f'])
        S.op('dve', lambda e: e.memset(nhalf[:], -0.5), [], ['nhalf'])
        S.op('dve', lambda e: e.memset(epsc[:], EPS), [], ['epsc'])
        S.op('pool', lambda e: e.memset(SC[:], 0.0), [], ['SC0', 'SC1'])
        S.op('pool', lambda e: e.memset(kT[:], 0.0), [], ['kT0', 'kT1'])
        S.op('pool', lambda e: e.memset(Vaug[:], 0.0), [], ['Vaug'])
        S.op('pool', lambda e: e.memset(Vaug[:, :, :, :, 64:65], 1.0), [], ['Vaug'])
        S.op('pool', lambda e: e.memset(HB[:], 0.0), [], ['HB0', 'HB1'])

        small = [(dT, dT_d, 'dT'), (bglu, bgluT_d, 'bglu'), (ons, onsT_d, 'ons'), (ona, onaT_d, 'ona'),
                 (cb, cbT_d, 'cb')]
        for i, (t, d_, k) in enumerate(small):
            S.dma('sp', t[:], d_.rearrange("l p a -> p l a"), [], [k], f'c{i + 1}')
        S.dma('sp', cw[:], cwT_d.rearrange("l p a b -> p l a b"), [], ['cw'], 'c6')
        S.op('dve', lambda e: e.tensor_scalar(out=bglu[:], in0=bglu[:], scalar1=0.5, scalar2=None, op0=ALU.mult), ['bglu'], ['bglu'])
        S.op('dve', lambda e: e.tensor_scalar(out=ons[:], in0=ons[:], scalar1=0.25, scalar2=None, op0=ALU.mult), ['ons'], ['ons'])
        qn_t = psb("qn_t", [128, 2, 1]); kn_t = psb("kn_t", [128, 2, 1])
        S.dma('sp', qn_t[:], qnw_d.rearrange("l p a -> p l a"), [], ['qn_t'], 'c7')
        S.dma('sp', kn_t[:], knw_d.rearrange("l p a -> p l a"), [], ['kn_t'], 'c8')
        S.op('dve', lambda e: e.tensor_scalar(out=qsc[:], in0=qn_t[:, :, 0], scalar1=0.125, scalar2=None, op0=ALU.mult),
             ['qn_t'], ['qsc'])
        S.op('dve', lambda e: e.tensor_copy(out=ksc[:], in_=kn_t[:, :, 0]), ['kn_t'], ['ksc'])
        sk_t = psb("sk_t", [128, 2, 8])
        S.dma('sp', sk_t[:].rearrange("p l h -> p (l h)"),
              sinks_d.rearrange("l h -> (l h)").unsqueeze(0).to_broadcast([128, 16]), [], ['sk_t'], 'c9')
        S.op('act', lambda e: e.activation(out=esink[:], in_=sk_t[:], func=AF.Exp), ['sk_t'], ['esink'])

        rb = psb("rb", [33, 8]); oh = psb("oh", [33, 384])
        S.op('dve', lambda e: e.memset(rb[:], NEG), [], ['rb'])
        S.dma('sp', rb[0:32, :], relb_d[:, :], [], ['rb'], 'c10')
        S.dma('sp', oh[:], oneh_d[:, :], [], ['oh'], 'c11')
        b0 = bank()
        S.op('pe', lambda e: e.matmul(pb(b0)[0:8, 0:384], lhsT=rb[:, :], rhs=oh[:, :], start=True, stop=True),
             ['rb', 'oh'], [pk(b0)])
        bd_sb = psb("bd_sb", [8, 384])
        S.op('dve', lambda e: e.tensor_copy(out=bd_sb[:], in_=pb(b0)[0:8, 0:384]), [pk(b0)], ['bd_sb'])
        S.dma('sp', bd_s[:, :], bd_sb[:], ['bd_sb'], ['bd_s'], 'c12')
        antiI = psb("antiI_sb", [128, 128])
        S.dma('sp', antiI[:], anti_d[:, :], [], ['antiI'], 'cai')
        tmpb = psb("tmpb", [128, 2, 8, 128])
        for tl in range(2):
            src = bass.AP(tensor=bd_s.tensor, offset=128 * (1 - tl), ap=[[1, 128], [384, 8], [1, 128]])
            S.dma('sp', tmpb[:, tl], src, ['bd_s'], ['tmpb'], f'cb{tl}')
            for hh in range(2):
                b_ = bank()
                S.op('pe', lambda e, b_=b_, tl=tl, hh=hh: e.matmul(
                    pb(b_), lhsT=antiI[:, :], rhs=tmpb[:, tl, 4 * hh:4 * hh + 4, :].rearrange("p h q -> p (h q)"),
                    start=True, stop=True), ['antiI', 'tmpb'], [pk(b_)])
                S.op('dve', lambda e, b_=b_, tl=tl, hh=hh: e.tensor_copy(
                    out=biasT[:, tl, 4 * hh:4 * hh + 4, :].rearrange("p h q -> p (h q)"), in_=pb(b_)), [pk(b_)], ['biasT'])

        cT = psb("cTt", [128, 8]); cact = psb("cact", [128, 8]); cbc = psb("cbc", [128, 8, 128])
        S.dma('sp', cT[:], cT_d[:, :], [], ['cT'], 'c13')
        S.op('act', lambda e: e.activation(out=cact[:], in_=cT[:], func=AF.Silu), ['cT'], ['cact'])
        S.op('dve', lambda e: e.tensor_copy(out=cbc[:], in_=cact[:].unsqueeze(2).to_broadcast([128, 8, 128])),
             ['cact'], ['cbc'])
        grow = psb("grow", [128, 2, 2, D])
        wm = [psb(f"wm{i}", [128, 8, 512]) for i in range(2)]
        bmr = [psb(f"bmr{i}", [128, 512]) for i in range(2)]
        modt = [psb(f"modt{i}", [128, 512]) for i in range(2)]
        n1 = psb("n1", [128, 2, 8]); n2 = psb("n2", [128, 2, 8])
        S.dma('sp', n1[:], n1T_d.rearrange("l p a -> p l a"), [], ['n1'], 'c15')
        S.dma('sp', n2[:], n2T_d.rearrange("l p a -> p l a"), [], ['n2'], 'c16')
        dtmp = psb("dtmp", [128, 128]); sct = psb("sct", [128, 2, 8])
        cnt = 0
        for l in range(2):
            for cc in range(12):
                i_ = cnt % 2
                w_ = wm[i_]; wk = f'wm{i_}'; bm_ = bmr[i_]; bk = f'bmr{i_}'; mt_ = modt[i_]; mk = f'modt{i_}'
                cnt += 1
                S.dma('sp', w_[:], wmod_d[l].rearrange("(kt p) n -> p kt n", p=128)[:, :, cc * 512:(cc + 1) * 512],
                      [], [wk], wk)
                S.dma('sp', bm_[:], bmod_d[l:l + 1, cc * 512:(cc + 1) * 512].to_broadcast([128, 512]), [], [bk], bk)
                b_ = bank()
                for kt in range(8):
                    S.op('pe', lambda e, b_=b_, kt=kt, w_=w_: e.matmul(pb(b_), lhsT=cbc[:, kt, :], rhs=w_[:, kt, :],
                                                                       start=(kt == 0), stop=(kt == 7)),
                         ['cbc', wk], [pk(b_)])
                slot, hf = cc // 2, cc % 2
                if slot in (2, 5):
                    S.op('dve', lambda e, b_=b_, l=l, slot=slot, hf=hf, bm_=bm_: e.tensor_tensor(
                        out=grow[:, l, 0 if slot == 2 else 1, hf * 512:(hf + 1) * 512], in0=pb(b_), in1=bm_[:], op=ALU.add),
                        [pk(b_), bk], ['grow'])
                else:
                    S.op('dve', lambda e, b_=b_, bm_=bm_, mt_=mt_: e.tensor_tensor(out=mt_[:], in0=pb(b_), in1=bm_[:], op=ALU.add),
                         [pk(b_), bk], [mk])
                    dst = {0: shA, 3: shB}.get(slot)
                    for k4 in range(4):
                        kt = hf * 4 + k4
                        tgt = (dst[:, l, kt:kt + 1] if dst is not None else sct[:, 0 if slot == 1 else 1, kt:kt + 1])
                        S.op('dve', lambda e, mt_=mt_, k4=k4: e.tensor_tensor(
                            out=dtmp[:], in0=mt_[:, k4 * 128:(k4 + 1) * 128], in1=ident[:], op=ALU.mult),
                            [mk, 'ident'], ['dtmp'])
                        S.op('dve', lambda e, tgt=tgt: e.tensor_reduce(out=tgt, in_=dtmp[:], axis=AX.X, op=ALU.add),
                             ['dtmp'], ['sct', 'shA', 'shB'])
            S.op('dve', lambda e, l=l: e.scalar_tensor_tensor(out=gA[:, l, :], in0=sct[:, 0, :], scalar=1.0, in1=n1[:, l, :],
                                                              op0=ALU.add, op1=ALU.mult), ['sct', 'n1'], ['gA'])
            S.op('dve', lambda e, l=l: e.scalar_tensor_tensor(out=gB[:, l, :], in0=sct[:, 1, :], scalar=1.0, in1=n2[:, l, :],
                                                              op0=ALU.add, op1=ALU.mult), ['sct', 'n2'], ['gB'])

        NWS = 4
        wst = [psb(f"wst{i}", [128, 2048]) for i in range(NWS)]
        wsb = [psb(f"wsb{i}", [128, 2048], BF) for i in range(NWS)]
        cnt = 0

        def cast_piece(src_ap, stores, n, gate=None, rows=128):
            nonlocal cnt
            i = cnt % NWS
            cnt += 1
            S.dma('sp', wst[i][0:rows, 0:n], src_ap, [], [f'wst{i}'], f'wst{i}')
            if gate is None:
                S.op('dve', lambda e: e.tensor_copy(out=wsb[i][0:rows, 0:n], in_=wst[i][0:rows, 0:n]), [f'wst{i}'], [f'wsb{i}'])
            else:
                S.op('dve', lambda e: e.tensor_tensor(out=wsb[i][0:rows, 0:n], in0=wst[i][0:rows, 0:n], in1=gate[0:rows], op=ALU.mult),
                     [f'wst{i}', 'grow'], [f'wsb{i}'])
            for dst_ap, vf in stores:
                S.dma('act', dst_ap, vf(wsb[i]), [f'wsb{i}'], ['wscr'], f'wsbst{i}')

        def cast_all():
            for l in range(2):
                for kt in range(8):
                    cast_piece(win_d[l, kt * 128:(kt + 1) * 128, :], [
                        (winu_s[l, 0:5].rearrange("t p c -> p t c")[:, :, kt * 96:(kt + 1) * 96],
                         lambda w: w[:, 0:480].rearrange("p (t c) -> p t c", c=96)),
                        (winu_s[l, 5, :, kt * 96:kt * 96 + 32], lambda w: w[:, 480:512]),
                        (winq_s[l, :, kt, :], lambda w: w[:, 512:1280])], 1280)
                    yield
                    for c3 in range(4):
                        cast_piece(wup_d[l, kt * 128:(kt + 1) * 128, c3 * 1408:(c3 + 1) * 1408], [
                            (wup_s[l, c3 * 11:(c3 + 1) * 11].rearrange("c p x -> p c x")[:, :, kt * 128:(kt + 1) * 128],
                             lambda w: w[:, 0:1408].rearrange("p (c x) -> p c x", x=128))], 1408)
                        yield
                for t6 in range(NT6):
                    n_ = nr6(t6)
                    cast_piece(wglu_d[l, t6 * 96:t6 * 96 + n_, :], [(wglu_s[l, 0:n_, t6, :], lambda w, n_=n_: w[0:n_, 0:512])], 512, rows=n_)
                    yield
                    cast_piece(wout_d[l, t6 * 96:t6 * 96 + n_, :], [
                        (wout_s[l, :, 0:n_, t6 * 512:(t6 + 1) * 512].rearrange("h p c -> p h c"),
                         lambda w, n_=n_: w[0:n_, 0:1024].rearrange("p (h c) -> p h c", h=2))], 1024, gate=grow[:, l, 0, :], rows=n_)
                    yield
                for kt in range(4):
                    cast_piece(wout_d[l, 512 + kt * 128:512 + (kt + 1) * 128, :], [
                        (wout_s[l, :, :, (6 + kt) * 512:(7 + kt) * 512].rearrange("h p c -> p h c"),
                         lambda w: w[:, 0:1024].rearrange("p (h c) -> p h c", h=2))], 1024, gate=grow[:, l, 0, :])
                    yield
                for v in range(NV):
                    cast_piece(wdn_d[l, v * 128:(v + 1) * 128, :], [
                        (wdn_s[l, :, :, v * 256:(v + 1) * 256].rearrange("q p c -> p q c"),
                         lambda w: w[:, 0:1024].rearrange("p (q c) -> p q c", q=4))], 1024, gate=grow[:, l, 1, :])
                    yield


        castgen = cast_all()
        dvc = {'n': 0}

        def V(name, shape=(128, 16)):
            return psb(name, list(shape))
        lre = V("lre", (128, 2, 16)); lim = V("lim", (128, 2, 16)); ldt = V("ldtt", (128, 2, 16))
        S.dma('sp', lre[:], lamre_d.rearrange("l p a -> p l a"), [], ['lre'], 'c17')
        S.dma('sp', lim[:], lamim_d.rearrange("l p a -> p l a"), [], ['lim'], 'c18')
        S.dma('sp', ldt[:], ldt_d.rearrange("l p a -> p l a"), [], ['ldt'], 'c19')
        Bre = V("Bre", (128, 2, 16, 16)); Bim = V("Bim", (128, 2, 16, 16))
        Cre = V("Cre", (128, 2, 16, 16)); Cim = V("Cim", (128, 2, 16, 16))
        for i, (t, d_) in enumerate(((Bre, bre_d), (Bim, bim_d), (Cre, cre_d), (Cim, cim_d))):
            S.dma('sp', t[:], d_.rearrange("l p a b -> p l a b"), [], [f'BC{i}'], f'c2{i}')
        tnames = ['dt', 'zr', 'th', 'mag', 'sn', 'cs', 't1', 't2', 't3', 'ar', 'ai', 'fr', 'fi', 'pr', 'pi', 'qr', 'qi']
        tv = {n: V("v_" + n) for n in tnames}
        Er = V("Er", (128, 16, 16)); Ei = V("Ei", (128, 16, 16)); Gr = V("Gr", (128, 16, 16)); Gi = V("Gi", (128, 16, 16))
        X1 = V("X1", (128, 16, 16)); X2 = V("X2", (128, 16, 16)); X3 = V("X3", (128, 16, 16))
        Eblk = psb("Eblk", [128, 16, 8, 2, 32], BF)
        Gblk = psb("Gblk", [128, 8, 2, 16, 32], BF)
        Cw = psb("Cw", [128, 16, 2, 128], BF)
        tabev = psb("tabev", [128, 768], BF)
        PHsb = psb("PHsb", [128, 2, NCH + 1, 16])
        phu = psb("phu", [128, 2, 16]); pht = psb("pht", [128, 4, 16])

        def dv(fn, r, w):
            S.op('dve', fn, r, w)
            dvc['n'] += 1
            if dvc['n'] % 5 == 0:
                next(castgen, None)

        def tt(o, a, b, op, r=('ssmv',), w=('ssmv',)):
            dv(lambda e: e.tensor_tensor(out=o, in0=a, in1=b, op=op), list(r), list(w))

        def ts(o, a, s1, s2, op0, op1=None, r=('ssmv',), w=('ssmv',)):
            if op1 is None:
                dv(lambda e: e.tensor_scalar(out=o, in0=a, scalar1=s1, scalar2=None, op0=op0), list(r), list(w))
            else:
                dv(lambda e: e.tensor_scalar(out=o, in0=a, scalar1=s1, scalar2=s2, op0=op0, op1=op1), list(r), list(w))

        def bc(a):
            return a.unsqueeze(2).to_broadcast([128, 16, 16])

        def cmul_b(orr, oi, sr, si, xr, xi):
            tt(X1[:], xr, bc(sr), ALU.mult); tt(X2[:], xi, bc(si), ALU.mult)
            tt(X3[:], X1[:], X2[:], ALU.subtract)
            tt(X1[:], xr, bc(si), ALU.mult); tt(X2[:], xi, bc(sr), ALU.mult)
            tt(oi, X1[:], X2[:], ALU.add)
            dv(lambda e: e.tensor_copy(out=orr, in_=X3[:]), ['ssmv'], ['ssmv'])

        for l in range(2):
            rk = ['ssmv', 'lre', 'lim', 'ldt', 'BC0', 'BC1', 'BC2', 'BC3']
            t = {k: v[:] for k, v in tv.items()}
            S.op('act', lambda e, l=l: e.activation(out=tv['dt'][:], in_=ldt[:, l, :], func=AF.Exp), ['ldt', 'ssmv'], ['ssmv'])
            ts(t['t1'], lre[:, l, :], -1e-4, None, ALU.min, r=rk)
            tt(t['zr'], t['t1'], t['dt'], ALU.mult)
            tt(t['th'], lim[:, l, :], t['dt'], ALU.mult, r=rk)
            ts(t['mag'], t['zr'], 1.0 / 720, 1.0 / 120, ALU.mult, ALU.add)
            for cf in (1.0 / 24, 1.0 / 6, 0.5, 1.0, 1.0):
                tt(t['mag'], t['mag'], t['zr'], ALU.mult)
                ts(t['mag'], t['mag'], cf, None, ALU.add)
            ts(t['t2'], t['th'], 1.0 / 32, None, ALU.mult)
            tt(t['t3'], t['t2'], t['t2'], ALU.mult)
            ts(t['sn'], t['t3'], 1.0 / 362880, -1.0 / 5040, ALU.mult, ALU.add)
            for cf in (1.0 / 120, -1.0 / 6, 1.0):
                tt(t['sn'], t['sn'], t['t3'], ALU.mult)
                ts(t['sn'], t['sn'], cf, None, ALU.add)
            tt(t['sn'], t['sn'], t['t2'], ALU.mult)
            ts(t['cs'], t['t3'], -1.0 / 3628800, 1.0 / 40320, ALU.mult, ALU.add)
            for cf in (-1.0 / 720, 1.0 / 24, -0.5, 1.0):
                tt(t['cs'], t['cs'], t['t3'], ALU.mult)
                ts(t['cs'], t['cs'], cf, None, ALU.add)
            for _ in range(5):
                tt(t['pr'], t['cs'], t['cs'], ALU.mult); tt(t['pi'], t['sn'], t['sn'], ALU.mult)
                tt(t['qr'], t['sn'], t['cs'], ALU.mult)
                tt(t['cs'], t['pr'], t['pi'], ALU.subtract)
                ts(t['sn'], t['qr'], 2.0, None, ALU.mult)
            tt(t['ar'], t['mag'], t['cs'], ALU.mult); tt(t['ai'], t['mag'], t['sn'], ALU.mult)
            tt(t['pr'], t['t1'], t['t1'], ALU.mult); tt(t['pi'], lim[:, l, :], lim[:, l, :], ALU.mult, r=rk)
            tt(t['pr'], t['pr'], t['pi'], ALU.add)
            dv(lambda e: e.reciprocal(out=tv['pr'][:], in_=tv['pr'][:]), ['ssmv'], ['ssmv'])
            ts(t['qr'], t['ar'], -1.0, None, ALU.add)
            tt(t['t2'], t['qr'], t['t1'], ALU.mult); tt(t['t3'], t['ai'], lim[:, l, :], ALU.mult, r=rk)
            tt(t['t2'], t['t2'], t['t3'], ALU.add); tt(t['fr'], t['t2'], t['pr'], ALU.mult)
            tt(t['t2'], t['ai'], t['t1'], ALU.mult); tt(t['t3'], t['qr'], lim[:, l, :], ALU.mult, r=rk)
            tt(t['t2'], t['t2'], t['t3'], ALU.subtract); tt(t['fi'], t['t2'], t['pr'], ALU.mult)
            cmul_b(Er[:], Ei[:], t['fr'], t['fi'], Bre[:, l], Bim[:, l])
            cmul_b(Gr[:], Gi[:], t['ar'], t['ai'], Cre[:, l], Cim[:, l])
            S.op('pool', lambda e: e.memset(Eblk[:], 0.0), ['Eblk'], ['Eblk'])
            S.op('pool', lambda e: e.memset(Gblk[:], 0.0), ['Gblk'], ['Gblk'])
            S.op('pool', lambda e: e.memset(Cw[:], 0.0), ['Cw'], ['Cw'])
            for two in range(2):
                ps_ = slice(64 * two, 64 * two + 64)
                for q in range(3):
                    prs = slice(q, 16, 3)
                    S.op('dve', lambda e, ps_=ps_, two=two, q=q, prs=prs, l=l: e.tensor_copy(
                        out=Cw[ps_, prs, 0, 32 * q + 16 * two:32 * q + 16 * two + 16], in_=Cre[ps_, l, prs, :]),
                        ['BC2', 'Cw'], ['Cw'])
                    S.op('dve', lambda e, ps_=ps_, two=two, q=q, prs=prs, l=l: e.tensor_scalar(
                        out=Cw[ps_, prs, 1, 32 * q + 16 * two:32 * q + 16 * two + 16], in0=Cim[ps_, l, prs, :],
                        scalar1=-1.0, scalar2=None, op0=ALU.mult), ['BC3', 'Cw'], ['Cw'])
            for d_ in range(8):
                if d_ > 0:
                    cmul_b(Er[:], Ei[:], t['ar'], t['ai'], Er[:], Ei[:])
                    cmul_b(Gr[:], Gi[:], t['ar'], t['ai'], Gr[:], Gi[:])
                for two in range(2):
                    ps_ = slice(64 * two, 64 * two + 64)
                    cs_ = slice(16 * two, 16 * two + 16)
                    for part, (E_, G_) in enumerate(((Er, Gr), (Ei, Gi))):
                        S.op('dve', lambda e, ps_=ps_, cs_=cs_, d_=d_, part=part, E_=E_: e.tensor_copy(
                            out=Eblk[ps_, :, d_, part, cs_], in_=E_[ps_, :, :]), ['ssmv', 'Eblk'], ['Eblk'])
                        if part == 0:
                            S.op('dve', lambda e, ps_=ps_, cs_=cs_, d_=d_, G_=G_: e.tensor_copy(
                                out=Gblk[ps_, d_, 0, :, cs_], in_=G_[ps_, :, :]), ['ssmv', 'Gblk'], ['Gblk'])
                        else:
                            S.op('dve', lambda e, ps_=ps_, cs_=cs_, d_=d_, G_=G_: e.tensor_scalar(
                                out=Gblk[ps_, d_, 1, :, cs_], in0=G_[ps_, :, :], scalar1=-1.0, scalar2=None, op0=ALU.mult),
                                ['ssmv', 'Gblk'], ['Gblk'])
            tt(t['t2'], t['mag'], t['mag'], ALU.mult); tt(t['t3'], t['t2'], t['t2'], ALU.mult)
            dv(lambda e, l=l: e.tensor_tensor(out=R8t[:, l, :], in0=tv['t3'][:], in1=tv['t3'][:], op=ALU.mult), ['ssmv'], ['R8t'])

            def csq(orr, oi, xr, xi):
                tt(t['t2'], xr, xr, ALU.mult); tt(t['t3'], xi, xi, ALU.mult)
                tt(t['mag'], xr, xi, ALU.mult)
                tt(orr, t['t2'], t['t3'], ALU.subtract)
                ts(oi, t['mag'], 2.0, None, ALU.mult)
            csq(t['pr'], t['pi'], t['ar'], t['ai'])
            csq(t['qr'], t['qi'], t['pr'], t['pi'])
            csq(t['pr'], t['pi'], t['qr'], t['qi'])
            dv(lambda e, l=l: e.tensor_copy(out=AT1[:, l, 0:16], in_=tv['pr'][:]), ['ssmv'], ['AT'])
            dv(lambda e, l=l: e.tensor_copy(out=AT1[:, l, 16:32], in_=tv['pr'][:]), ['ssmv'], ['AT'])
            dv(lambda e, l=l: e.tensor_copy(out=AT2[:, l, 16:32], in_=tv['pi'][:]), ['ssmv'], ['AT'])
            dv(lambda e, l=l: e.tensor_scalar(out=AT2[:, l, 0:16], in0=tv['pi'][:], scalar1=-1.0, scalar2=None, op0=ALU.mult),
               ['ssmv'], ['AT'])
            dv(lambda e, l=l: e.reciprocal(out=tv['t1'][:], in_=R8t[:, l, :]), ['R8t', 'ssmv'], ['ssmv'])
            tt(t['qr'], t['pr'], t['t1'], ALU.mult); tt(t['qi'], t['pi'], t['t1'], ALU.mult)
            dv(lambda e: e.tensor_copy(out=phu[:, 0, :], in_=tv['qr'][:]), ['ssmv', 'phu'], ['phu'])
            dv(lambda e: e.tensor_copy(out=phu[:, 1, :], in_=tv['qi'][:]), ['ssmv', 'phu'], ['phu'])
            S.op('pool', lambda e: e.memset(PHsb[:, 0, 0, :], 1.0), ['PHsb'], ['PHsb'])
            S.op('pool', lambda e: e.memset(PHsb[:, 1, 0, :], 0.0), ['PHsb'], ['PHsb'])

            def ptt(o, a_, b_, op, r, w):
                S.op('pool', lambda e: e.tensor_tensor(out=o, in0=a_, in1=b_, op=op), r, w)
            for c in range(NCH):
                cr, ci = PHsb[:, 0, c, :], PHsb[:, 1, c, :]
                ptt(pht[:, 0, :], cr, phu[:, 0, :], ALU.mult, ['phu', 'PHsb', 'pht0'], ['pht0'])
                ptt(pht[:, 1, :], ci, phu[:, 1, :], ALU.mult, ['phu', 'PHsb', 'pht1'], ['pht1'])
                ptt(PHsb[:, 0, c + 1, :], pht[:, 0, :], pht[:, 1, :], ALU.subtract, ['pht0', 'pht1'], ['PHsb'])
                ptt(pht[:, 2, :], cr, phu[:, 1, :], ALU.mult, ['phu', 'PHsb', 'pht2'], ['pht2'])
                ptt(pht[:, 3, :], ci, phu[:, 0, :], ALU.mult, ['phu', 'PHsb', 'pht3'], ['pht3'])
                ptt(PHsb[:, 1, c + 1, :], pht[:, 2, :], pht[:, 3, :], ALU.add, ['pht2', 'pht3'], ['PHsb'])
            S.dma('sp', ph_s[l], PHsb[:], ['PHsb'], ['ph_s'], 'tb3')
            S.dma('sp', cs_s[l], Gblk[:], ['Gblk'], ['cs_s'], 'tb0')
            for d_ in range(8):
                b_ = bank2()
                S.op('dve', lambda e, b_=b_: e.memset(ps_t[b_ // 2][:, 0:768], 0.0), [], [pk(b_), pk(b_ + 1)])
                for pr in range(16):
                    t6, q = pr // 3, pr % 3
                    for part in range(2):
                        S.op('pe', lambda e, b_=b_, t6=t6, q=q, pr=pr, part=part, d_=d_: e.matmul(
                            ps_t[b_ // 2][32 * q:32 * q + 32, t6 * 128:(t6 + 1) * 128], lhsT=Eblk[:, pr, d_, part, :],
                            rhs=Cw[:, pr, part, :], start=(part == 0), stop=(part == 1)),
                            ['Eblk', 'Cw'], [pk(b_), pk(b_ + 1)])
                S.op('dve', lambda e, b_=b_: e.tensor_copy(out=tabev[:], in_=ps_t[b_ // 2][:, 0:768]), [pk(b_), pk(b_ + 1)], ['tabev'])
                S.dma('sp', kt_s[l, :, d_].rearrange("p g c -> p (g c)"), tabev[:], ['tabev'], ['kt_s'], 'tb1')
            for j in range(8):
                for part in range(2):
                    b_ = bank2()
                    S.op('dve', lambda e, b_=b_: e.memset(ps_t[b_ // 2][:, 0:768], 0.0), [], [pk(b_), pk(b_ + 1)])
                    for pr in range(16):
                        t6, q = pr // 3, pr % 3
                        S.op('pe', lambda e, b_=b_, t6=t6, q=q, pr=pr, part=part, j=j: e.matmul(
                            ps_t[b_ // 2][32 * q:32 * q + 32, t6 * 128:(t6 + 1) * 128], lhsT=Eblk[:, pr, 7 - j, part, :],
                            rhs=identb[:, :], start=True, stop=True), ['Eblk', 'identb'], [pk(b_), pk(b_ + 1)])
                    S.op('dve', lambda e, b_=b_: e.tensor_copy(out=tabev[:], in_=ps_t[b_ // 2][:, 0:768]), [pk(b_), pk(b_ + 1)], ['tabev'])
                    S.dma('sp', bf_s[l, :, :, j, part, :], tabev[:].rearrange("p (g c) -> p g c", g=6), ['tabev'], ['bf_s'], 'tb2')

        for _ in castgen:
            pass
        S.barrier()
        pes.close()

        xt = sb("xt", [128, NB, D])
        tok = [sb(f"tok{i}", [128, D]) for i in range(2)]
        sqj = sb("sqj", [128, D], BF)
        rst = sb("rst", [128, 8])
        featT = sb("featT", [128, 8, TT], BF)
        uT = sb("uT", [128, NT6, TC, NCH], BF)
        qkv_sq = sb("qkv_sq", [128, 640])
        qn = sb("qn", [128, 640])
        qnb = [qn, qkv_sq]; QNK = ['qn', 'qkv_sq']
        qT = sb("qT", [128, NB, 512], BF)
        SAw = sb("SAw", [128, NCH + 1, 32])
        Sprev = sb("Sprev", [128, 32, NCH], BF)
        f32all = sb("f32all", [128, 6, TT])
        f32b = [f32all[:, i, :] for i in range(6)]
        PHt = sb("PHt", [128, 2, NCH + 1, 16])
        ysb = f32b[0:2]; ytmp = f32b[2:4]; S_sb = f32b[4:6]; sgate = f32b[0:2]
        YSK = ["f32b0", "f32b1"]; YTK = ["f32b2", "f32b3"]; SSK = ["f32b4", "f32b5"]
        zT = sb("zT", [128, NT6, TT], BF)
        ssmT = sb("ssmT", [128, NT6, TT], BF)
        attnT = sb("attnT", [128, 4, TT], BF)
        PT = [sb(f"PT{i}", [128, 2, 512], BF) for i in range(2)]
        actT = sb("actT", [128, NV, TT], BF)
        U = [sb(f"U{i}", [128, 2, TT + 2], BF) for i in range(2)]
        dg = [sb(f"dg{i}", [128, 6, 128], BF) for i in range(2)]
        NWCH = 4
        wch = [sb(f"wch{i}", [128, 8, 128], BF) for i in range(NWCH)]
        wsl = [sb(f"wsl{i}", [128, 6144], BF) for i in range(2)]
        wslc = {"n": 0}

        def wslot():
            i = wslc["n"] % 2
            wslc["n"] += 1
            return wsl[i], f"wsl{i}"
        tab = sb("tab", [128, 14336], BF)
        BFt = tab[:, 0:12288].rearrange("p (a b c d) -> p a b c d", a=6, b=8, c=2)
        KTt = tab[:, 0:6144].rearrange("p (a b c) -> p a b c", a=8, b=6)
        CSt = tab[:, 6144:14336].rearrange("p (a b c d) -> p a b c d", a=8, b=2, c=16)
        st1 = sb("st1", [128, 16])
        rs_s = sb("rs_s", [128, NB]); rs_a = sb("rs_a", [128, NB])
        ctmp = [sb(f"ctmp{i}", [128, 32]) for i in range(2)]
        wcnt = {'n': 0}

        def load_chunk(src):
            i = wcnt['n'] % NWCH
            wcnt['n'] += 1
            S.dma('sp', wch[i][:], src, [], [f'wch{i}'], f'wch{i}')
            return wch[i], f'wch{i}'

        XK = [f'xt{b}' for b in range(NB)]

        def rstd_pow(out_ap, in_ap, scale, n, rkeys, wkeys):
            S.op('pool', lambda e: e.tensor_scalar(out=out_ap, in0=in_ap, scalar1=scale, scalar2=EPS, op0=ALU.mult, op1=ALU.add),
                 list(rkeys), list(wkeys))
            S.op('pool', lambda e: e.tensor_tensor(out=out_ap, in0=out_ap, in1=nhalf[:, 0:n], op=ALU.pow),
                 list(wkeys) + ['nhalf'], list(wkeys))

        def rms_to_featT(l, g_t, sh_t):
            for b in range(NB):
                S.op('act', lambda e, b=b: e.activation(out=sqj[:], in_=xt[:, b, :], func=AF.Square,
                                                        accum_out=rst[:, b:b + 1]), [XK[b]], ['sqj', f'rst{b}'])
            for b in range(NB):
                rstd_pow(rst[:, 4 + b:5 + b], rst[:, b:b + 1], 1.0 / D, 1, [f'rst{b}'], [f'rstd{b}'])
            for b in range(NB):
                tk = tok[b % 2]
                tkk = f'tok{b % 2}'
                S.op('dve', lambda e, b=b, tk=tk: e.tensor_scalar(out=tk[:], in0=xt[:, b, :], scalar1=rst[:, 4 + b:5 + b], scalar2=None,
                                                                   op0=ALU.mult), [XK[b], f'rstd{b}', tkk], [tkk])
                for half in range(2):
                    b_ = bank()
                    for j in range(4):
                        kt = half * 4 + j
                        S.op('pe', lambda e, b_=b_, j=j, kt=kt, tk=tk: e.transpose(
                            out=pb(b_, j * 128, (j + 1) * 128), in_=tk[:, kt * 128:(kt + 1) * 128], identity=ident[:]),
                            [tkk, 'ident'], [pk(b_)])
                    for j in range(4):
                        kt = half * 4 + j
                        if kt < 3:
                            S.op('dve', lambda e, b_=b_, j=j, kt=kt, b=b: e.tensor_scalar(
                                out=featT[:, kt, b * 128:(b + 1) * 128], in0=pb(b_, j * 128, (j + 1) * 128),
                                scalar1=g_t[:, l, kt:kt + 1], scalar2=sh_t[:, l, kt:kt + 1], op0=ALU.mult, op1=ALU.add),
                                [pk(b_), 'gA', 'gB', 'shA', 'shB'], ['featT'])
                        else:
                            S.op('act', lambda e, b_=b_, j=j, kt=kt, b=b: e.activation(
                                out=featT[:, kt, b * 128:(b + 1) * 128], in_=pb(b_, j * 128, (j + 1) * 128), func=AF.Identity,
                                bias=sh_t[:, l, kt:kt + 1], scale=g_t[:, l, kt:kt + 1]), [pk(b_), 'gA', 'gB', 'shA', 'shB'], ['featT'])

        def tile_layer(ti, l):
            if True:
                SAk = 'SAw'; SCk = f'SC{l}'; kTk = f'kT{l}'; HBk = f'HB{l}'
                if ti == 0 and l == 0:
                    S.dma('sp', tab[:, 0:12288], bf_s[l].rearrange("p a b c d -> p (a b c d)"), [], ['tab'], 'tab')
                S.dma('sp', PHt[:], ph_s[l], [], ['PHt'], 'PHt')
                wq_t, wqk = wslot()
                wqkv = wq_t[:, :].rearrange("p (k c) -> p k c", k=8)
                S.dma('sp', wq_t[:, :], winq_s[l].rearrange("p k c -> p (k c)"), [], [wqk], wqk)
                wg_t, wgk = wslot()
                wgl = wg_t[:, 0:3072].rearrange("p (k c) -> p k c", k=NT6)
                S.dma('sp', wg_t[:, 0:3072], wglu_s[l].rearrange("p k c -> p (k c)"), [], [wgk], wgk)
                rms_to_featT(l, gA, shA)
                for t6 in range(NT6):
                    n_ = nr6(t6)
                    i_ = wcnt['n'] % NWCH
                    wcnt['n'] += 1
                    wc, wk = wch[i_], f'wch{i_}'
                    S.dma('sp', wc[:, :, 0:n_], winu_s[l, t6].rearrange("p (k c) -> p k c", c=96)[:, :, 0:n_], [], [wk], wk)
                    b_ = bank()
                    for kt in range(8):
                        S.op('pe', lambda e, b_=b_, kt=kt, wc=wc, n_=n_: e.matmul(pb(b_)[0:n_, :], lhsT=wc[:, kt, 0:n_], rhs=featT[:, kt, :],
                                                                                 start=(kt == 0), stop=(kt == 7)),
                             [wk, 'featT'], [pk(b_)])
                    S.op('act', lambda e, b_=b_, t6=t6, n_=n_: e.activation(
                        out=uT[0:n_, t6].rearrange("p j c -> p c j"), in_=pb(b_)[0:n_, :].rearrange("p (c j) -> p c j", j=TC),
                        func=AF.Copy), [pk(b_)], ['uT'])
                for q in range(3):
                    combos = [(part, pr) for part in range(2) for pr in range(16) if pr % 3 == q]
                    b2 = bank2()
                    for sl, (part, pr) in enumerate(combos):
                        t6 = pr // 3
                        bb = b2 + sl // 8
                        for j in range(TC):
                            S.op('pe', lambda e, bb=bb, sl=sl, t6=t6, q=q, j=j, part=part: e.matmul(
                                pb(bb, (sl % 8) * 64, (sl % 8) * 64 + 64), lhsT=BFt[32 * q:32 * q + 32, t6, j, part, :],
                                rhs=uT[32 * q:32 * q + 32, t6, j, :], start=(j == 0), stop=(j == TC - 1)),
                                ['tab', 'uT'], [pk(bb)])
                    for sl, (part, pr) in enumerate(combos):
                        bb = b2 + sl // 8
                        S.op('dve', lambda e, bb=bb, sl=sl, part=part, pr=pr: e.tensor_copy(
                            out=SAw[:, 1:NCH + 1, part * 16 + pr], in_=pb(bb, (sl % 8) * 64, (sl % 8) * 64 + 64)), [pk(bb)], [SAk])
                S.dma('sp', tab[:, 0:6144], kt_s[l].rearrange("p a b c -> p (a b c)"), [], ['tab'], 'tab')
                S.dma('sp', tab[:, 6144:14336], cs_s[l].rearrange("p a b c d -> p (a b c d)"), [], ['tab'], 'tab2')
                def chain_pre():
                    S.op('pool', lambda e: e.tensor_copy(out=SAw[:, 0, :], in_=SC[:, l, :]), [SCk, SAk], [SAk])
                    T2 = f32all[:, 0:4, :].rearrange("p a (c t q) -> p (a c) t q", t=2, q=16)
                    T2K = ['f32b0', 'f32b1', 'f32b2', 'f32b3']
                    Fv = SAw[:, 1:NCH + 1, :].rearrange("p c (t q) -> p c t q", t=2)
                    cosf, sinf = PHt[:, 0, 1:NCH + 1, :], PHt[:, 1, 1:NCH + 1, :]
                    S.op('pool', lambda e: e.tensor_tensor(out=T2[:, :, 0, :], in0=Fv[:, :, 1, :], in1=sinf, op=ALU.mult), [SAk, 'PHt'] + T2K, T2K)
                    S.op('pool', lambda e: e.tensor_tensor(out=T2[:, :, 1, :], in0=Fv[:, :, 0, :], in1=sinf, op=ALU.mult), [SAk, 'PHt'] + T2K, T2K)
                    S.op('pool', lambda e: e.tensor_tensor(out=Fv, in0=Fv, in1=cosf.unsqueeze(2).to_broadcast([128, NCH, 2, 16]), op=ALU.mult),
                         [SAk, 'PHt'], [SAk])
                    S.op('pool', lambda e: e.tensor_tensor(out=Fv[:, :, 0, :], in0=Fv[:, :, 0, :], in1=T2[:, :, 0, :], op=ALU.add), [SAk] + T2K, [SAk])
                    S.op('pool', lambda e: e.tensor_tensor(out=Fv[:, :, 1, :], in0=Fv[:, :, 1, :], in1=T2[:, :, 1, :], op=ALU.subtract), [SAk] + T2K, [SAk])
                    return T2, T2K

                def chain_scan():
                    for s_ in range(32):
                        q_ = s_ % 16
                        S.op('dve', lambda e, s_=s_, q_=q_: e.tensor_tensor_scan(
                            out=SAw[:, 1:NCH + 1, s_], data0=R8t[:, l, q_:q_ + 1].to_broadcast([128, NCH]), data1=SAw[:, 1:NCH + 1, s_],
                            initial=SAw[:, 0, s_:s_ + 1], op0=ALU.mult, op1=ALU.add), [SAk, 'R8t'], [SAk])

                def chain_post(T2, T2K):
                    Wv = SAw[:, 0:NCH, :].rearrange("p c (t q) -> p c t q", t=2)
                    cosb, sinb = PHt[:, 0, 0:NCH, :], PHt[:, 1, 0:NCH, :]
                    Spv = Sprev[:].rearrange("p (t q) c -> p c t q", t=2)
                    W64 = SAw[:, NCH, :]
                    S.op('pool', lambda e: e.tensor_tensor(out=ctmp[0][:, 0:16], in0=W64[:, 16:32], in1=PHt[:, 1, NCH, :], op=ALU.mult), [SAk, 'PHt'], ['ct0'])
                    S.op('pool', lambda e: e.tensor_tensor(out=ctmp[0][:, 16:32], in0=W64[:, 0:16], in1=PHt[:, 1, NCH, :], op=ALU.mult), [SAk, 'PHt', 'ct0'], ['ct0'])
                    S.op('pool', lambda e: e.tensor_tensor(out=ctmp[1][:, 0:16], in0=W64[:, 0:16], in1=PHt[:, 0, NCH, :], op=ALU.mult), [SAk, 'PHt'], ['ct1'])
                    S.op('pool', lambda e: e.tensor_tensor(out=ctmp[1][:, 16:32], in0=W64[:, 16:32], in1=PHt[:, 0, NCH, :], op=ALU.mult), [SAk, 'PHt', 'ct1'], ['ct1'])
                    S.op('pool', lambda e: e.tensor_tensor(out=SC[:, l, 0:16], in0=ctmp[1][:, 0:16], in1=ctmp[0][:, 0:16], op=ALU.subtract), ['ct0', 'ct1', SCk], [SCk])
                    S.op('pool', lambda e: e.tensor_tensor(out=SC[:, l, 16:32], in0=ctmp[1][:, 16:32], in1=ctmp[0][:, 16:32], op=ALU.add), ['ct0', 'ct1', SCk], [SCk])
                    S.op('pool', lambda e: e.tensor_tensor(out=T2[:, :, 0, :], in0=Wv[:, :, 1, :], in1=sinb, op=ALU.mult), [SAk, 'PHt'] + T2K, T2K)
                    S.op('pool', lambda e: e.tensor_tensor(out=T2[:, :, 1, :], in0=Wv[:, :, 0, :], in1=sinb, op=ALU.mult), [SAk, 'PHt'] + T2K, T2K)
                    S.op('pool', lambda e: e.tensor_tensor(out=Wv, in0=Wv, in1=cosb.unsqueeze(2).to_broadcast([128, NCH, 2, 16]), op=ALU.mult),
                         [SAk, 'PHt'], [SAk])
                    S.op('pool', lambda e: e.tensor_tensor(out=Spv[:, :, 0, :], in0=Wv[:, :, 0, :], in1=T2[:, :, 0, :], op=ALU.subtract), [SAk] + T2K, ['Sprev'])
                    S.op('pool', lambda e: e.tensor_tensor(out=Spv[:, :, 1, :], in0=Wv[:, :, 1, :], in1=T2[:, :, 1, :], op=ALU.add), [SAk] + T2K + ['Sprev'], ['Sprev'])

                def att_A1(b):
                    Q = qnb[b % 2]; QK = QNK[b % 2]
                    b2 = bank2()
                    for kt in range(8):
                        S.op('pe', lambda e, b2=b2, kt=kt, b=b: e.matmul(pb(b2), lhsT=featT[:, kt, b * 128:(b + 1) * 128],
                                                                        rhs=wqkv[:, kt, 0:512], start=(kt == 0), stop=(kt == 7)),
                             ['featT', wqk], [pk(b2)])
                    for kt in range(8):
                        S.op('pe', lambda e, b2=b2, kt=kt, b=b: e.matmul(pb(b2 + 1, 0, 256), lhsT=featT[:, kt, b * 128:(b + 1) * 128],
                                                                        rhs=wqkv[:, kt, 512:768], start=(kt == 0), stop=(kt == 7)),
                             ['featT', wqk], [pk(b2 + 1)])
                    qk_ps = ps_t[b2 // 2][:, 0:640]
                    S.op('act', lambda e, qk_ps=qk_ps, Q=Q: e.activation(out=Q[:], in_=qk_ps, func=AF.Square),
                         [pk(b2), pk(b2 + 1)], [QK])
                    S.op('dve', lambda e, Q=Q: e.tensor_reduce(out=st1[:, 4:14], in_=Q[:].rearrange("p (h d) -> p h d", d=64),
                                                          axis=AX.X, op=ALU.add), [QK], ['st1'])
                    S.op('act', lambda e: e.activation(out=st1[:, 4:14], in_=st1[:, 4:14], func=AF.Sqrt, bias=epsc[:, 0:1],
                                                       scale=1.0 / 64), ['st1', 'epsc'], ['st1'])
                    S.op('dve', lambda e: e.reciprocal(out=st1[:, 4:14], in_=st1[:, 4:14]), ['st1'], ['st1'])
                    S.op('dve', lambda e, qk_ps=qk_ps, Q=Q: e.tensor_tensor(
                        out=Q[:, 0:512].rearrange("p (m t d) -> p t m d", m=4, t=2),
                        in0=qk_ps[:, 0:512].rearrange("p (t m d) -> p t m d", t=2, m=4),
                        in1=st1[:, 4:12].rearrange("p (t m) -> p t m", t=2).unsqueeze(3).to_broadcast([128, 2, 4, 64]), op=ALU.mult),
                        [pk(b2), pk(b2 + 1), 'st1', QK], [QK])
                    S.op('dve', lambda e, qk_ps=qk_ps, Q=Q: e.tensor_tensor(
                        out=Q[:, 512:640].rearrange("p (h d) -> p h d", d=64), in0=qk_ps[:, 512:640].rearrange("p (h d) -> p h d", d=64),
                        in1=st1[:, 12:14].unsqueeze(2).to_broadcast([128, 2, 64]), op=ALU.mult),
                        [pk(b2), pk(b2 + 1), 'st1', QK], [QK])
                    S.op('act', lambda e, b2=b2, b=b: e.activation(
                        out=Vaug[:, l, b + 1, :, 0:64], in_=pb(b2 + 1, 128, 256).rearrange("p (g d) -> p g d", g=2),
                        func=AF.Copy), [pk(b2 + 1)], ['Vaug'])

                def att_A2(b):
                    Q = qnb[b % 2]; QK = QNK[b % 2]
                    tb = bank2()
                    for m in range(4):
                        S.op('pe', lambda e, tb=tb, m=m, Q=Q: e.transpose(
                            out=pb(tb, m * 128, (m + 1) * 128),
                            in_=Q[:, m * 128:(m + 1) * 128], identity=ident[:]),
                            [QK, 'ident'], [pk(tb)])
                    S.op('pe', lambda e, tb=tb, Q=Q: e.transpose(out=pb(tb + 1, 0, 128), in_=Q[:, 512:640], identity=ident[:]),
                         [QK, 'ident'], [pk(tb + 1)])
                    S.op('act', lambda e, tb=tb, b=b: e.activation(
                        out=qT[:, b, :], in_=pb(tb), func=AF.Identity,
                        scale=qsc[:, l:l + 1]), [pk(tb), 'qsc'], ['qT'])
                    S.op('act', lambda e, tb=tb, b=b: e.activation(
                        out=kT[:, l, 128 + b * 128:128 + (b + 1) * 128], in_=pb(tb + 1, 0, 128), func=AF.Identity,
                        scale=ksc[:, l:l + 1]), [pk(tb + 1), 'ksc'], [kTk])

                BST = {}

                def att_B1(b):
                    first = (ti == 0 and b == 0)
                    tiles = [1] if first else [0, 1]
                    BST[b] = tiles
                    for g in range(2):
                        gs = slice(64 * g, 64 * g + 64)
                        pt = PT[g]; ptk = f'PT{g}'
                        for tl in tiles:
                            sb_ = bank()
                            kcol = b * 128 + tl * 128
                            S.op('pe', lambda e, sb_=sb_, gs=gs, kcol=kcol, b=b: e.matmul(
                                pb(sb_), lhsT=kT[gs, l, kcol:kcol + 128], rhs=qT[gs, b, :],
                                start=True, stop=True), [kTk, 'qT'], [pk(sb_)])
                            ssb_ = S_sb[tl]; ssk = SSK[tl]
                            S.op('dve', lambda e, sb_=sb_, ssb_=ssb_, tl=tl, g=g: e.tensor_tensor(
                                out=ssb_[:].rearrange("p (m q) -> p m q", m=4), in0=pb(sb_).rearrange("p (m q) -> p m q", m=4),
                                in1=biasT[:, tl, 4 * g:4 * g + 4, :], op=ALU.add), [pk(sb_), 'biasT'], [ssk])
                            S.op('act', lambda e, ssb_=ssb_, pt=pt, tl=tl: e.activation(out=pt[:, tl, :], in_=ssb_[:], func=AF.Exp),
                                 [ssk], [ptk])

                def att_B2(b):
                    tiles = BST[b]
                    ob = bank2()
                    for g in range(2):
                        pt = PT[g]; ptk = f'PT{g}'
                        for m in range(4):
                            for ii, tl in enumerate(tiles):
                                S.op('pe', lambda e, ob=ob, g=g, m=m, tl=tl, ii=ii, pt=pt, b=b, n_=len(tiles): e.matmul(
                                    pb(ob + g, m * 65, m * 65 + 65), lhsT=pt[:, tl, m * 128:(m + 1) * 128],
                                    rhs=Vaug[:, l, b + tl, g, :], start=(ii == 0), stop=(ii == n_ - 1)),
                                    [ptk, 'Vaug'], [pk(ob + g)])
                    at = tok[b % 2]; atk = f'tok{b % 2}'
                    for g in range(2):
                        o3 = pb(ob + g, 0, 260).rearrange("p (m d) -> p m d", m=4)
                        S.op('dve', lambda e, o3=o3, g=g: e.tensor_tensor(out=st1[:, 0:4], in0=o3[:, :, 64], in1=esink[:, l, 4 * g:4 * g + 4],
                                                                          op=ALU.add), [pk(ob + g), 'esink', 'st1'], ['st1'])
                        S.op('dve', lambda e: e.reciprocal(out=st1[:, 0:4], in_=st1[:, 0:4]), ['st1'], ['st1'])
                        S.op('dve', lambda e, o3=o3, g=g, at=at: e.tensor_tensor(
                            out=at[:, 256 * g:256 * g + 256].rearrange("p (m d) -> p m d", m=4), in0=o3[:, :, 0:64],
                            in1=st1[:, 0:4].unsqueeze(2).to_broadcast([128, 4, 64]), op=ALU.mult),
                            [pk(ob + g), 'st1', atk], [atk])
                    S.op('act', lambda e, at=at, b=b: e.activation(out=qkv_sq[:, 0:512], in_=at[:, 0:512], func=AF.Square,
                                                                   accum_out=rs_a[:, b:b + 1]), [atk, 'qkv_sq'], ['qkv_sq', 'rs_a'])
                    rstd_pow(rs_a[:, b:b + 1], rs_a[:, b:b + 1], 1.0 / 512, 1, ['rs_a'], ['rs_a'])

                def att_B3(b):
                    at = tok[b % 2]; atk = f'tok{b % 2}'
                    tb = bank()
                    for m in range(4):
                        S.op('pe', lambda e, tb=tb, m=m, at=at: e.transpose(out=pb(tb, m * 128, (m + 1) * 128),
                                                                           in_=at[:, m * 128:(m + 1) * 128], identity=ident[:]),
                             [atk, 'ident'], [pk(tb)])
                    for m in range(4):
                        S.op('act', lambda e, tb=tb, m=m, b=b: e.activation(
                            out=attnT[:, m, b * 128:(b + 1) * 128], in_=pb(tb, m * 128, (m + 1) * 128), func=AF.Identity,
                            scale=ona[:, l, m:m + 1]), [pk(tb), 'ona'], ['attnT'])

                def Y_tile(t6):
                    n_ = nr6(t6)
                    b_ = bank()
                    for i in range(TC):
                        for j in range(i + 1):
                            S.op('pe', lambda e, b_=b_, i=i, j=j, t6=t6, n_=n_: e.matmul(
                                pb(b_, i * 64, i * 64 + 64)[0:n_, :], lhsT=KTt[0:n_, i - j, t6, 0:n_], rhs=uT[0:n_, t6, j, :],
                                start=(j == 0), stop=False, skip_group_check=True),
                                ['tab', 'uT'], [pk(b_)])
                        for q in range(np6(t6)):
                            pr = t6 * 3 + q
                            for part in range(2):
                                last = (q == np6(t6) - 1 and part == 1)
                                S.op('pe', lambda e, b_=b_, i=i, q=q, pr=pr, part=part, last=last: e.matmul(
                                    pb(b_, i * 64, i * 64 + 64)[32 * q:32 * q + 32, :], lhsT=CSt[:, i, part, pr, :],
                                    rhs=Sprev[:, part * 16 + pr, :], start=False, stop=last, skip_group_check=True),
                                    ['tab', 'Sprev'], [pk(b_)])
                    yb = ysb[t6 % 2]; yk = YSK[t6 % 2]; yt_ = ytmp[t6 % 2]; ytk = YTK[t6 % 2]
                    S.op('dve', lambda e, b_=b_, t6=t6, yb=yb, n_=n_: e.scalar_tensor_tensor(
                        out=yb[0:n_, :], in0=uT[0:n_, t6].rearrange("p j c -> p (j c)"), scalar=dT[0:n_, l, t6:t6 + 1], in1=pb(b_)[0:n_, :],
                        op0=ALU.mult, op1=ALU.add), ['uT', 'dT', pk(b_)], [yk])
                    S.op('act', lambda e, yb=yb, yt_=yt_, n_=n_: e.activation(out=yt_[0:n_, :], in_=yb[0:n_, :], func=AF.Square), [yk], [ytk])
                    S.op('dve', lambda e, yt_=yt_, n_=n_: e.tensor_scalar(out=yt_[0:n_, :], in0=yt_[0:n_, :], scalar1=0.044715, scalar2=1.0,
                                                                           op0=ALU.mult, op1=ALU.add), [ytk], [ytk])
                    S.op('dve', lambda e, yt_=yt_, yb=yb, n_=n_: e.tensor_tensor(out=yt_[0:n_, :], in0=yt_[0:n_, :], in1=yb[0:n_, :], op=ALU.mult),
                         [ytk, yk], [ytk])
                    S.op('act', lambda e, yt_=yt_, n_=n_: e.activation(out=yt_[0:n_, :], in_=yt_[0:n_, :], func=AF.Tanh, scale=0.7978845608),
                         [ytk], [ytk])
                    S.op('dve', lambda e, yt_=yt_, yb=yb, t6=t6, n_=n_: e.scalar_tensor_tensor(
                        out=zT[0:n_, t6, :].rearrange("p (c j) -> p j c", j=TC), in0=yt_[0:n_, :].rearrange("p (j c) -> p j c", j=TC),
                        scalar=1.0, in1=yb[0:n_, :].rearrange("p (j c) -> p j c", j=TC), op0=ALU.add, op1=ALU.mult), [ytk, yk], ['zT'])
                T2, T2K = chain_pre()
                att_A1(0)
                att_A1(1)
                att_A2(0)
                chain_scan()
                att_A1(2)
                att_A2(1)
                att_A1(3)
                att_A2(2)
                chain_post(T2, T2K)
                att_A2(3)
                att_B1(0); Y_tile(0); att_B2(0); Y_tile(1); att_B3(0)
                att_B1(1); Y_tile(2); att_B2(1); Y_tile(3); att_B3(1)
                att_B1(2); Y_tile(4); att_B2(2); Y_tile(5); att_B3(2)
                att_B1(3); att_B2(3); att_B3(3)
                S.op('pool', lambda e: e.tensor_copy(out=kT[:, l, 0:128], in_=kT[:, l, TT:TT + 128]), [kTk], [kTk])
                S.op('pool', lambda e: e.tensor_copy(out=Vaug[:, l, 0, :, :], in_=Vaug[:, l, NB, :, :]), ['Vaug'], ['Vaug'])
                if not (ti == nt - 1 and l == 1):
                    S.dma('sp', tab[:, 0:12288], bf_s[1 - l].rearrange("p a b c d -> p (a b c d)"), [], ['tab'], 'tab')
                ssb = bank()
                glu_pending = []
                for m in range(NT6):
                    no = nr6(m)
                    b_ = bank()
                    if b_ == ssb:
                        b_ = bank()
                    for t6 in range(NT6):
                        n_ = nr6(t6)
                        S.op('pe', lambda e, b_=b_, m=m, t6=t6, n_=n_, no=no: e.matmul(
                            pb(b_)[0:no, :], lhsT=wgl[0:n_, t6, m * 96:m * 96 + no], rhs=zT[0:n_, t6, :],
                            start=(t6 == 0), stop=(t6 == NT6 - 1)), [wgk, 'zT'], [pk(b_)])
                    while len(glu_pending) > 0:
                        glu_pending.pop(0)()
                    yb = ysb[m % 2]; yk = YSK[m % 2]; yt_ = ytmp[m % 2]; ytk = YTK[m % 2]
                    S.op('act', lambda e, b_=b_, m=m, yb=yb, no=no: e.activation(out=yb[0:no, :], in_=pb(b_)[0:no, :], func=AF.Tanh,
                                                                                bias=bglu[0:no, l, m:m + 1], scale=0.25),
                         [pk(b_), 'bglu'], [yk])
                    S.op('dve', lambda e, m=m, yb=yb, no=no: e.scalar_tensor_tensor(out=yb[0:no, :], in0=yb[0:no, :], scalar=1.0, in1=zT[0:no, m, :],
                                                                                    op0=ALU.add, op1=ALU.mult), [yk, 'zT'], [yk])
                    S.op('act', lambda e, yb=yb, yt_=yt_, no=no: e.activation(out=yt_[0:no, :], in_=yb[0:no, :], func=AF.Square), [yk], [ytk])
                    S.op('dve', lambda e, m=m, yb=yb, no=no: e.tensor_scalar(out=ssmT[0:no, m, :], in0=yb[0:no, :], scalar1=ons[0:no, l, m:m + 1],
                                                                              scalar2=None, op0=ALU.mult), [yk, 'ons'], ['ssmT'])
                    def ones_mm(m=m, yt_=yt_, ytk=ytk, no=no):
                        for b in range(NB):
                            S.op('pe', lambda e, m=m, b=b, yt_=yt_, ssb=ssb, no=no: e.matmul(
                                pb(ssb)[:, m * NB + b:m * NB + b + 1], lhsT=yt_[0:no, b * 128:(b + 1) * 128], rhs=ones_f[0:no, 0:1],
                                start=True, stop=True), [ytk, 'ones_f'], [pk(ssb)])
                    glu_pending.append(ones_mm)
                while len(glu_pending) > 0:
                    glu_pending.pop(0)()
                S.op('dve', lambda e, ssb=ssb: e.tensor_reduce(out=rs_s[:], in_=pb(ssb)[:, 0:NT6 * NB].rearrange("p (m b) -> p b m", b=NB),
                                                              axis=AX.X, op=ALU.add), [pk(ssb)], ['rs_s'])
                rstd_pow(rs_s[:], rs_s[:], 1.0 / (16 * 512), NB, ['rs_s'], ['rs_s'])
                for half in range(2):
                    wo_t, wok = wslot()
                    woh = wo_t[:, 0:5120].rearrange("p (k c) -> p k c", k=10)
                    S.dma('sp', wo_t[:, 0:5120], wout_s[l, half], [], [wok], wok)
                    for b in range(NB):
                        ba = bank(); bb_ = bank()
                        for t6 in range(NT6):
                            n_ = nr6(t6)
                            S.op('pe', lambda e, ba=ba, t6=t6, b=b, n_=n_, woh=woh: e.matmul(pb(ba), lhsT=ssmT[0:n_, t6, b * 128:(b + 1) * 128],
                                                                                   rhs=woh[0:n_, t6, :], start=(t6 == 0), stop=(t6 == NT6 - 1)),
                                 ['ssmT', wok], [pk(ba)])
                        for ft in range(4):
                            S.op('pe', lambda e, bb_=bb_, ft=ft, b=b, woh=woh: e.matmul(pb(bb_), lhsT=attnT[:, ft, b * 128:(b + 1) * 128],
                                                                              rhs=woh[:, 6 + ft, :], start=(ft == 0), stop=(ft == 3)),
                                 ['attnT', wok], [pk(bb_)])
                        xs = xt[:, b, half * 512:(half + 1) * 512]
                        S.op('dve', lambda e, ba=ba, b=b, xs=xs: e.scalar_tensor_tensor(
                            out=xs, in0=pb(ba), scalar=rs_s[:, b:b + 1], in1=xs, op0=ALU.mult, op1=ALU.add),
                            [pk(ba), 'rs_s', XK[b]], [XK[b]])
                        S.op('dve', lambda e, bb_=bb_, b=b, xs=xs: e.scalar_tensor_tensor(
                            out=xs, in0=pb(bb_), scalar=rs_a[:, b:b + 1], in1=xs, op0=ALU.mult, op1=ALU.add),
                            [pk(bb_), 'rs_a', XK[b]], [XK[b]])
                rms_to_featT(l, gB, shB)
                ups_all = {}

                def stage_up(v):
                    dgv = dg[v % 2]; dgk = f'dg{v % 2}'
                    ups = []
                    for vg in range(2):
                        wc, wk = load_chunk(wup_s[l, vg * NV + v].rearrange("p (k c) -> p k c", k=8))
                        b_ = bank()
                        for kt in range(8):
                            S.op('pe', lambda e, b_=b_, kt=kt, wc=wc: e.matmul(pb(b_), lhsT=wc[:, kt, :], rhs=featT[:, kt, :],
                                                                               start=(kt == 0), stop=(kt == 7)),
                                 [wk, 'featT'], [pk(b_)])
                        ups.append(b_)
                    ups_all[v] = ups
                    S.op('pool', lambda e, dgv=dgv, v=v: e.tensor_tensor(
                        out=dgv[:, :, :].rearrange("p (g j) c -> p g j c", g=2),
                        in0=identb[:].unsqueeze(1).unsqueeze(1).to_broadcast([128, 2, 3, 128]),
                        in1=cw[:, l].rearrange("p (g v) j -> p g v j", g=2)[:, :, v, :].unsqueeze(3).to_broadcast([128, 2, 3, 128]),
                        op=ALU.mult), ['identb', 'cw', dgk], [dgk])

                def stage_mid(v):
                    Uv = U[v % 2]; Uk = f'U{v % 2}'; dgv = dg[v % 2]; dgk = f'dg{v % 2}'
                    sg = sgate[v % 2]; sgk = YSK[v % 2]
                    ups = ups_all.pop(v)
                    S.op('pool', lambda e, Uv=Uv, v=v: e.tensor_copy(out=Uv[:, :, 0:2], in_=HB[:, l, v, :, :]), [HBk, Uk], [Uk])
                    for vg in range(2):
                        S.op('act', lambda e, Uv=Uv, vg=vg, b_=ups[vg]: e.activation(out=Uv[:, vg, 2:TT + 2], in_=pb(b_), func=AF.Copy),
                             [pk(ups[vg]), Uk], [Uk])
                    S.op('pool', lambda e, Uv=Uv, v=v: e.tensor_copy(out=HB[:, l, v, :, :], in_=Uv[:, :, TT:TT + 2]), [Uk, HBk], [HBk])
                    cps = []
                    for vg in range(2):
                        b_ = bank()
                        for j in range(3):
                            S.op('pe', lambda e, b_=b_, vg=vg, j=j, dgv=dgv, Uv=Uv: e.matmul(
                                pb(b_), lhsT=dgv[:, vg * 3 + j, :], rhs=Uv[:, vg, j:j + TT], start=(j == 0), stop=(j == 2)),
                                [dgk, Uk], [pk(b_)])
                        cps.append(b_)
                    S.op('act', lambda e, sg=sg, b_=cps[1], v=v: e.activation(out=sg[:], in_=pb(b_), func=AF.Silu,
                                                                             bias=cb[:, l, NV + v:NV + v + 1], scale=1.0),
                         [pk(cps[1]), 'cb'], [sgk])
                    S.op('dve', lambda e, sg=sg, b_=cps[0], v=v: e.scalar_tensor_tensor(
                        out=actT[:, v, :], in0=pb(b_), scalar=cb[:, l, v:v + 1], in1=sg[:], op0=ALU.add, op1=ALU.mult),
                        [pk(cps[0]), 'cb', sgk], ['actT'])

                stage_up(0)
                for v in range(NV):
                    if v + 1 < NV:
                        stage_up(v + 1)
                    stage_mid(v)
                for qt in range(4):
                    wd_t, wdk = wslot()
                    wdh = wd_t[:, 0:5632].rearrange("p (k c) -> p k c", k=NV)
                    S.dma('sp', wd_t[:, 0:5632], wdn_s[l, qt], [], [wdk], wdk)
                    for b in range(NB):
                        b_ = bank()
                        for v in range(NV):
                            S.op('pe', lambda e, b_=b_, v=v, b=b, wdh=wdh: e.matmul(pb(b_, 0, 256), lhsT=actT[:, v, b * 128:(b + 1) * 128],
                                                                          rhs=wdh[:, v, :], start=(v == 0), stop=(v == NV - 1)),
                                 ['actT', wdk], [pk(b_)])
                        xs = xt[:, b, qt * 256:(qt + 1) * 256]
                        S.op('dve', lambda e, b_=b_, xs=xs: e.tensor_tensor(out=xs, in0=pb(b_, 0, 256), in1=xs, op=ALU.add),
                             [pk(b_), XK[b]], [XK[b]])
                        if l == 1 and qt == 3:
                            tk = tok[b % 2]; tkk = f'tok{b % 2}'
                            S.op('act', lambda e, tk=tk, b=b: e.activation(out=tk[:], in_=xt[:, b, :], func=AF.Copy), [XK[b], tkk], [tkk])
                            S.dma('sp', y_d[ti * TT + b * 128: ti * TT + (b + 1) * 128, :], tk[:], [tkk], ['y'], f'yst{b % 2}')
                            if ti + 1 < nt:
                                S.dma('sp', xt[:, b, :], x_d[(ti + 1) * TT + b * 128:(ti + 1) * TT + (b + 1) * 128, :], [], [XK[b]], f'xld{b}')

        S.dma('sp', xt[:], x_d[0:TT, :].rearrange("(b p) f -> p b f", p=128), [], XK, 'xt')
        for ti in range(nt):
            for l in range(2):
                tile_layer(ti, l)
        S.wait_all('sp')
        block = es.enter_context(nc.Block())
        S.emit(block)
    return nc


def _bucket_onehot():
    nb, md = 32, 128
    idx = np.arange(384)
    dist = idx - 127
    n = np.maximum(dist, 0)
    max_exact = nb // 2
    log_part = np.log(np.maximum(n, 1) / max_exact) / np.log(md / max_exact)
    large = max_exact + (log_part * (nb - max_exact)).astype(np.int32)
    large = np.minimum(large, nb - 1)
    bucket = np.where(n < max_exact, n, large).astype(np.int32)
    valid = (dist >= 0) & (dist < 128)
    oh = np.zeros((33, 384), np.float32)
    for i in range(384):
        if valid[i]:
            oh[bucket[i], i] = 1.0
        else:
            oh[32, i] = 1.0
    return oh


def prep_inputs(b, x, c, w_mod, b_mod, norm1_w, w_in, lam_re, lam_im, log_dt, ssm_b_re, ssm_b_im, ssm_c_re, ssm_c_im,
                ssm_d, w_glu, b_glu, q_norm_w, k_norm_w, rel_bias, sinks, out_norm_ssm, out_norm_attn, w_out, norm2_w,
                w_up, conv_w, conv_b, w_down):
    f = lambda a: np.ascontiguousarray(a, dtype=np.float32)

    def fm(a, n):
        return f(np.asarray(a).reshape(2, n, 128).transpose(0, 2, 1))

    def pairlay(a):
        a = np.asarray(a)
        sh = a.shape
        a = a.reshape(2, 16, 2, 64, *sh[3:])
        a = np.moveaxis(a, 1, 3)
        return f(a.reshape(2, 128, 16, *sh[3:]))
    m = {}
    m["x"] = f(x[b])
    m["cT"] = f(np.asarray(c[b]).reshape(8, 128).T)
    m["w_mod"] = f(w_mod); m["b_mod"] = f(b_mod)
    m["n1T"] = fm(norm1_w, 8); m["n2T"] = fm(norm2_w, 8)
    m["w_in"] = f(w_in); m["w_out"] = f(w_out); m["w_up"] = f(w_up); m["w_down"] = f(w_down); m["w_glu"] = f(w_glu)
    m["lamre"] = pairlay(lam_re); m["lamim"] = pairlay(lam_im)
    m["ldt"] = pairlay(np.repeat(np.asarray(log_dt)[:, :, None], 64, axis=2))
    m["bre"] = pairlay(ssm_b_re); m["bim"] = pairlay(ssm_b_im)
    m["cre"] = pairlay(np.asarray(ssm_c_re).transpose(0, 1, 3, 2)); m["cim"] = pairlay(np.asarray(ssm_c_im).transpose(0, 1, 3, 2))
    def fm96(a):
        a = np.asarray(a, dtype=np.float32)
        o = np.zeros((2, 6, 128), np.float32)
        pad = np.zeros((2, 576), np.float32)
        pad[:, :512] = a
        o[:, :, :96] = pad.reshape(2, 6, 96)
        return f(o.transpose(0, 2, 1))
    m["dT"] = fm96(ssm_d); m["bgluT"] = fm96(b_glu)
    m["qnw"] = f(np.tile(np.asarray(q_norm_w), (1, 2))[:, :, None]); m["knw"] = f(np.tile(np.asarray(k_norm_w), (1, 2))[:, :, None])
    m["relb"] = f(rel_bias); m["oneh"] = _bucket_onehot(); m["sinks"] = f(sinks)
    m["onsT"] = fm96(out_norm_ssm); m["onaT"] = fm(out_norm_attn, 4)
    m["cwT"] = f(np.asarray(conv_w).reshape(2, 3, 44, 128).transpose(0, 3, 2, 1))
    m["cbT"] = fm(conv_b, 44)
    m["ident"] = np.eye(128, dtype=np.float32)
    m["antiI"] = np.ascontiguousarray(np.eye(128, dtype=np.float32)[::-1])
    return m


def kernel(**inputs):
    x = np.asarray(inputs["x"])
    B, seq, _ = x.shape
    nc = build(seq)
    maps = [prep_inputs(ci, **inputs) for ci in range(B)]
    res = run_bass_kernel_spmd(nc, maps, core_ids=list(range(B)))
    out = np.stack([np.asarray(res.results[b]["y"]) for b in range(B)], axis=0)
    return out.astype(np.float32)
```

```python
import numpy as np
from contextlib import ExitStack
import concourse.bass as bass
import concourse.mybir as mybir
from concourse.bass_utils import run_bass_kernel_spmd

F32 = mybir.dt.float32
BF = mybir.dt.bfloat16
AF = mybir.ActivationFunctionType
ALU = mybir.AluOpType
AX = mybir.AxisListType

D = 1024
TT = 512
NB = 4
TC = 8
NCH = TT // TC
DFF = 2816
NV = 22
EPS = 1e-6
NEG = -30000.0
NT6 = 6
STAGE = 99


def nr6(t):
    return 96 if t < 5 else 32


def np6(t):
    return 3 if t < 5 else 1


class Sched:
    ROT = 30000

    def __init__(s, nc, es):
        s.nc = nc
        s.es = es
        s.prog = {e: [] for e in ('pe', 'act', 'dve', 'pool', 'sp')}
        s.cnt = {e: 0 for e in s.prog}
        s.sem = {}
        s.allsems = []
        s.nsem = 0
        for e in s.prog:
            s._newsem(e)
        s.waited = {e: {} for e in s.prog}
        s.res = {}
        s.dsem = {}

    def _newsem(s, e):
        s.nsem += 1
        sm = s.es.enter_context(s.nc.semaphore(f"s_{e}_{s.nsem}"))
        s.sem[e] = sm
        s.cnt[e] = 0
        s.allsems.append([sm, 0])
        s.cur = getattr(s, 'cur', {})
        s.cur[e] = s.allsems[-1]

    def _deps(s, reads, writes):
        deps = {}

        def add(tok):
            if tok is None:
                return
            sem, val = tok
            k = id(sem)
            if k not in deps or deps[k][1] < val:
                deps[k] = (sem, val)
        for k in reads:
            r = s.res.get(k)
            if r:
                add(r[0])
        for k in writes:
            r = s.res.get(k)
            if r:
                add(r[0])
                for t in r[1].values():
                    add(t)
        return deps

    def _emit_waits(s, e, deps):
        for k, (sem, val) in deps.items():
            if s.waited[e].get(k, 0) < val:
                s.waited[e][k] = val
                s.prog[e].append(('w', sem, val))

    def _record(s, tok, reads, writes):
        kk = id(tok[0])
        for k in reads:
            r = s.res.setdefault(k, [None, {}])
            if kk not in r[1] or r[1][kk][1] < tok[1]:
                r[1][kk] = tok
        for k in writes:
            s.res[k] = [tok, {}]

    def op(s, e, fn, reads=(), writes=()):
        deps = s._deps(reads, writes)
        if e == 'pe':
            deps.pop(id(s.sem['pe']), None)
        s._emit_waits(e, deps)
        if s.cnt[e] >= s.ROT:
            s._newsem(e)
        s.cnt[e] += 1
        s.cur[e][1] = s.cnt[e]
        tok = (s.sem[e], s.cnt[e])
        s.prog[e].append(('i', fn, tok[0]))
        s._record(tok, reads, writes)

    def dma(s, e, out, in_, reads, writes, semkey, **kw):
        d = s.dsem.get(semkey)
        if d is None:
            sm = s.es.enter_context(s.nc.semaphore(f"d_{len(s.dsem)}"))
            d = [sm, 0]
            s.dsem[semkey] = d
            s.allsems.append(d)
        deps = s._deps(reads, writes)
        s._emit_waits(e, deps)
        d[1] += 16
        tok = (d[0], d[1])
        s.prog[e].append(('d', out, in_, tok[0], kw))
        s._record(tok, reads, writes)

    def barrier(s):
        for e in s.prog:
            deps = {id(sm): (sm, v) for sm, v in s.allsems if v > 0}
            if e == 'pe':
                deps.pop(id(s.sem['pe']), None)
            s._emit_waits(e, deps)
        s.res = {}

    def wait_all(s, e):
        deps = {id(sm): (sm, v) for sm, v in s.allsems if v > 0}
        s._emit_waits(e, deps)

    def emit(s, block):
        def run(eng, lst):
            for it in lst:
                if it[0] == 'w':
                    eng.wait_ge(it[1], it[2])
                elif it[0] == 'i':
                    it[1](eng).then_inc(it[2], 1)
                else:
                    eng.dma_start(out=it[1], in_=it[2], allow_slow_non_contiguous=True, **it[4]).then_inc(it[3], 16)

        @block.tensor
        def _(e):
            run(e, s.prog['pe'])

        @block.scalar
        def _(e):
            run(e, s.prog['act'])

        @block.vector
        def _(e):
            run(e, s.prog['dve'])

        @block.gpsimd
        def _(e):
            run(e, s.prog['pool'])

        @block.sync
        def _(e):
            run(e, s.prog['sp'])


def build(seq, dbg=False):
    nt = seq // TT
    nc = bass.Bass("TRN2", target_bir_lowering=False)

    def din(name, shape, dt=F32):
        return nc.dram_tensor(name, list(shape), dt, kind="ExternalInput").ap()

    def dscr(name, shape, dt=BF):
        return nc.dram_tensor(name, list(shape), dt, kind="Internal").ap()

    x_d = din("x", [seq, D])
    cT_d = din("cT", [128, 8])
    wmod_d = din("w_mod", [2, D, 6 * D])
    bmod_d = din("b_mod", [2, 6 * D])
    n1T_d = din("n1T", [2, 128, 8])
    n2T_d = din("n2T", [2, 128, 8])
    win_d = din("w_in", [2, D, 1280])
    wout_d = din("w_out", [2, D, D])
    wup_d = din("w_up", [2, D, 2 * DFF])
    wdn_d = din("w_down", [2, DFF, D])
    wglu_d = din("w_glu", [2, 512, 512])
    lamre_d = din("lamre", [2, 128, 16])
    lamim_d = din("lamim", [2, 128, 16])
    ldt_d = din("ldt", [2, 128, 16])
    bre_d = din("bre", [2, 128, 16, 16])
    bim_d = din("bim", [2, 128, 16, 16])
    cre_d = din("cre", [2, 128, 16, 16])
    cim_d = din("cim", [2, 128, 16, 16])
    dT_d = din("dT", [2, 128, 6])
    bgluT_d = din("bgluT", [2, 128, 6])
    qnw_d = din("qnw", [2, 128, 1])
    knw_d = din("knw", [2, 128, 1])
    relb_d = din("relb", [32, 8])
    oneh_d = din("oneh", [33, 384])
    sinks_d = din("sinks", [2, 8])
    onsT_d = din("onsT", [2, 128, 6])
    onaT_d = din("onaT", [2, 128, 4])
    cwT_d = din("cwT", [2, 128, 44, 3])
    cbT_d = din("cbT", [2, 128, 44])
    ident_d = din("ident", [128, 128])
    anti_d = din("antiI", [128, 128])
    y_d = nc.dram_tensor("y", [seq, D], F32, kind="ExternalOutput").ap()

    winu_s = dscr("winu_s", [2, 6, 128, 8 * 96])
    winq_s = dscr("winq_s", [2, 128, 8, 768])
    wout_s = dscr("wout_s", [2, 2, 128, 10 * 512])
    wglu_s = dscr("wglu_s", [2, 128, 6, 512])
    wup_s = dscr("wup_s", [2, 44, 128, 8 * 128])
    wdn_s = dscr("wdn_s", [2, 4, 128, NV * 256])
    kt_s = dscr("kt_s", [2, 128, 8, 6, 128])
    bf_s = dscr("bf_s", [2, 128, 6, 8, 2, 128])
    cs_s = dscr("cs_s", [2, 128, 8, 2, 16, 32])
    bd_s = dscr("bd_s", [8, 384], F32)
    ph_s = dscr("ph_s", [2, 128, 2, NCH + 1, 16], F32)

    with ExitStack() as es:
        S = Sched(nc, es)

        def sb(name, shape, dt=F32):
            return es.enter_context(nc.sbuf_tensor(name, list(shape), dt))

        ident = sb("ident_sb", [128, 128])
        identb = sb("identb", [128, 128], BF)
        ones_f = sb("ones_f", [128, 1])
        nhalf = sb("nhalf", [128, 10])
        epsc = sb("epsc", [128, 1])
        gA = sb("gA", [128, 2, 8]); shA = sb("shA", [128, 2, 8])
        gB = sb("gB", [128, 2, 8]); shB = sb("shB", [128, 2, 8])
        dT = sb("dTt", [128, 2, 6]); bglu = sb("bglu", [128, 2, 6])
        qsc = sb("qsc", [128, 2]); ksc = sb("ksc", [128, 2])
        ons = sb("ons", [128, 2, 6]); ona = sb("ona", [128, 2, 4])
        cw = sb("cw", [128, 2, 44, 3]); cb = sb("cb", [128, 2, 44])
        esink = sb("esink", [128, 2, 8])
        AT1 = sb("AT1", [128, 2, 32]); AT2 = sb("AT2", [128, 2, 32])
        biasT = sb("biasT", [128, 2, 8, 128])
        R8t = sb("R8t", [128, 2, 16])
        SC = sb("SC", [128, 2, 32])
        kT = sb("kT", [128, 2, 128 + TT], BF)
        Vaug = sb("Vaug", [128, 2, NB + 1, 2, 65], BF)
        HB = sb("HB", [128, 2, NV, 2, 2], BF)

        ps_t = [es.enter_context(nc.psum_tensor(f"ps{i}", [128, 1024], F32)) for i in range(4)]
        state = {'bank': 0}

        def bank():
            b = state['bank']
            state['bank'] = (b + 1) % 8
            return b

        def bank2():
            b = state['bank']
            if b % 2:
                b = (b + 1) % 8
            state['bank'] = (b + 2) % 8
            return b

        def pb(b, lo=0, hi=512):
            return ps_t[b // 2][:, (b % 2) * 512 + lo:(b % 2) * 512 + hi]

        def pk(b):
            return ('ps', b)

        pes = ExitStack()

        def psb(name, shape, dt=F32):
            return pes.enter_context(nc.sbuf_tensor(name, list(shape), dt))

        S.dma('sp', ident[:], ident_d[:, :], [], ['ident'], 'c0')
        S.op('dve', lambda e: e.tensor_copy(out=identb[:], in_=ident[:]), ['ident'], ['identb'])
        S.op('dve', lambda e: e.memset(ones_f[:], 1.0), [], ['ones_f'])
        S.op('dve', lambda e: e.memset(nhalf[:], -0.5), [], ['nhalf'])
        S.op('dve', lambda e: e.memset(epsc[:], EPS), [], ['epsc'])
        S.op('pool', lambda e: e.memset(SC[:], 0.0), [], ['SC0', 'SC1'])
        S.op('pool', lambda e: e.memset(kT[:], 0.0), [], ['kT0', 'kT1'])
        S.op('pool', lambda e: e.memset(Vaug[:], 0.0), [], ['Vaug'])
        S.op('pool', lambda e: e.memset(Vaug[:, :, :, :, 64:65], 1.0), [], ['Vaug'])
        S.op('pool', lambda e: e.memset(HB[:], 0.0), [], ['HB0', 'HB1'])

        small = [(dT, dT_d, 'dT'), (bglu, bgluT_d, 'bglu'), (ons, onsT_d, 'ons'), (ona, onaT_d, 'ona'),
                 (cb, cbT_d, 'cb')]
        for i, (t, d_, k) in enumerate(small):
            S.dma('sp', t[:], d_.rearrange("l p a -> p l a"), [], [k], f'c{i + 1}')
        S.dma('sp', cw[:], cwT_d.rearrange("l p a b -> p l a b"), [], ['cw'], 'c6')
        S.op('dve', lambda e: e.tensor_scalar(out=bglu[:], in0=bglu[:], scalar1=0.5, scalar2=None, op0=ALU.mult), ['bglu'], ['bglu'])
        S.op('dve', lambda e: e.tensor_scalar(out=ons[:], in0=ons[:], scalar1=0.25, scalar2=None, op0=ALU.mult), ['ons'], ['ons'])
        qn_t = psb("qn_t", [128, 2, 1]); kn_t = psb("kn_t", [128, 2, 1])
        S.dma('sp', qn_t[:], qnw_d.rearrange("l p a -> p l a"), [], ['qn_t'], 'c7')
        S.dma('sp', kn_t[:], knw_d.rearrange("l p a -> p l a"), [], ['kn_t'], 'c8')
        S.op('dve', lambda e: e.tensor_scalar(out=qsc[:], in0=qn_t[:, :, 0], scalar1=0.125, scalar2=None, op0=ALU.mult),
             ['qn_t'], ['qsc'])
        S.op('dve', lambda e: e.tensor_copy(out=ksc[:], in_=kn_t[:, :, 0]), ['kn_t'], ['ksc'])
        sk_t = psb("sk_t", [128, 2, 8])
        S.dma('sp', sk_t[:].rearrange("p l h -> p (l h)"),
              sinks_d.rearrange("l h -> (l h)").unsqueeze(0).to_broadcast([128, 16]), [], ['sk_t'], 'c9')
        S.op('act', lambda e: e.activation(out=esink[:], in_=sk_t[:], func=AF.Exp), ['sk_t'], ['esink'])

        rb = psb("rb", [33, 8]); oh = psb("oh", [33, 384])
        S.op('dve', lambda e: e.memset(rb[:], NEG), [], ['rb'])
        S.dma('sp', rb[0:32, :], relb_d[:, :], [], ['rb'], 'c10')
        S.dma('sp', oh[:], oneh_d[:, :], [], ['oh'], 'c11')
        b0 = bank()
        S.op('pe', lambda e: e.matmul(pb(b0)[0:8, 0:384], lhsT=rb[:, :], rhs=oh[:, :], start=True, stop=True),
             ['rb', 'oh'], [pk(b0)])
        bd_sb = psb("bd_sb", [8, 384])
        S.op('dve', lambda e: e.tensor_copy(out=bd_sb[:], in_=pb(b0)[0:8, 0:384]), [pk(b0)], ['bd_sb'])
        S.dma('sp', bd_s[:, :], bd_sb[:], ['bd_sb'], ['bd_s'], 'c12')
        antiI = psb("antiI_sb", [128, 128])
        S.dma('sp', antiI[:], anti_d[:, :], [], ['antiI'], 'cai')
        tmpb = psb("tmpb", [128, 2, 8, 128])
        for tl in range(2):
            src = bass.AP(tensor=bd_s.tensor, offset=128 * (1 - tl), ap=[[1, 128], [384, 8], [1, 128]])
            S.dma('sp', tmpb[:, tl], src, ['bd_s'], ['tmpb'], f'cb{tl}')
            for hh in range(2):
                b_ = bank()
                S.op('pe', lambda e, b_=b_, tl=tl, hh=hh: e.matmul(
                    pb(b_), lhsT=antiI[:, :], rhs=tmpb[:, tl, 4 * hh:4 * hh + 4, :].rearrange("p h q -> p (h q)"),
                    start=True, stop=True), ['antiI', 'tmpb'], [pk(b_)])
                S.op('dve', lambda e, b_=b_, tl=tl, hh=hh: e.tensor_copy(
                    out=biasT[:, tl, 4 * hh:4 * hh + 4, :].rearrange("p h q -> p (h q)"), in_=pb(b_)), [pk(b_)], ['biasT'])

        cT = psb("cTt", [128, 8]); cact = psb("cact", [128, 8]); cbc = psb("cbc", [128, 8, 128])
        S.dma('sp', cT[:], cT_d[:, :], [], ['cT'], 'c13')
        S.op('act', lambda e: e.activation(out=cact[:], in_=cT[:], func=AF.Silu), ['cT'], ['cact'])
        S.op('dve', lambda e: e.tensor_copy(out=cbc[:], in_=cact[:].unsqueeze(2).to_broadcast([128, 8, 128])),
             ['cact'], ['cbc'])
        grow = psb("grow", [128, 2, 2, D])
        wm = [psb(f"wm{i}", [128, 8, 512]) for i in range(2)]
        bmr = [psb(f"bmr{i}", [128, 512]) for i in range(2)]
        modt = [psb(f"modt{i}", [128, 512]) for i in range(2)]
        n1 = psb("n1", [128, 2, 8]); n2 = psb("n2", [128, 2, 8])
        S.dma('sp', n1[:], n1T_d.rearrange("l p a -> p l a"), [], ['n1'], 'c15')
        S.dma('sp', n2[:], n2T_d.rearrange("l p a -> p l a"), [], ['n2'], 'c16')
        dtmp = psb("dtmp", [128, 128]); sct = psb("sct", [128, 2, 8])
        cnt = 0
        for l in range(2):
            for cc in range(12):
                i_ = cnt % 2
                w_ = wm[i_]; wk = f'wm{i_}'; bm_ = bmr[i_]; bk = f'bmr{i_}'; mt_ = modt[i_]; mk = f'modt{i_}'
                cnt += 1
                S.dma('sp', w_[:], wmod_d[l].rearrange("(kt p) n -> p kt n", p=128)[:, :, cc * 512:(cc + 1) * 512],
                      [], [wk], wk)
                S.dma('sp', bm_[:], bmod_d[l:l + 1, cc * 512:(cc + 1) * 512].to_broadcast([128, 512]), [], [bk], bk)
                b_ = bank()
                for kt in range(8):
                    S.op('pe', lambda e, b_=b_, kt=kt, w_=w_: e.matmul(pb(b_), lhsT=cbc[:, kt, :], rhs=w_[:, kt, :],
                                                                       start=(kt == 0), stop=(kt == 7)),
                         ['cbc', wk], [pk(b_)])
                slot, hf = cc // 2, cc % 2
                if slot in (2, 5):
                    S.op('dve', lambda e, b_=b_, l=l, slot=slot, hf=hf, bm_=bm_: e.tensor_tensor(
                        out=grow[:, l, 0 if slot == 2 else 1, hf * 512:(hf + 1) * 512], in0=pb(b_), in1=bm_[:], op=ALU.add),
                        [pk(b_), bk], ['grow'])
                else:
                    S.op('dve', lambda e, b_=b_, bm_=bm_, mt_=mt_: e.tensor_tensor(out=mt_[:], in0=pb(b_), in1=bm_[:], op=ALU.add),
                         [pk(b_), bk], [mk])
                    dst = {0: shA, 3: shB}.get(slot)
                    for k4 in range(4):
                        kt = hf * 4 + k4
                        tgt = (dst[:, l, kt:kt + 1] if dst is not None else sct[:, 0 if slot == 1 else 1, kt:kt + 1])
                        S.op('dve', lambda e, mt_=mt_, k4=k4: e.tensor_tensor(
                            out=dtmp[:], in0=mt_[:, k4 * 128:(k4 + 1) * 128], in1=ident[:], op=ALU.mult),
                            [mk, 'ident'], ['dtmp'])
                        S.op('dve', lambda e, tgt=tgt: e.tensor_reduce(out=tgt, in_=dtmp[:], axis=AX.X, op=ALU.add),
                             ['dtmp'], ['sct', 'shA', 'shB'])
            S.op('dve', lambda e, l=l: e.scalar_tensor_tensor(out=gA[:, l, :], in0=sct[:, 0, :], scalar=1.0, in1=n1[:, l, :],
                                                              op0=ALU.add, op1=ALU.mult), ['sct', 'n1'], ['gA'])
            S.op('dve', lambda e, l=l: e.scalar_tensor_tensor(out=gB[:, l, :], in0=sct[:, 1, :], scalar=1.0, in1=n2[:, l, :],
                                                              op0=ALU.add, op1=ALU.mult), ['sct', 'n2'], ['gB'])

        NWS = 4
        wst = [psb(f"wst{i}", [128, 2048]) for i in range(NWS)]
        wsb = [psb(f"wsb{i}", [128, 2048], BF) for i in range(NWS)]
        cnt = 0

        def cast_piece(src_ap, stores, n, gate=None, rows=128):
            nonlocal cnt
            i = cnt % NWS
            cnt += 1
            S.dma('sp', wst[i][0:rows, 0:n], src_ap, [], [f'wst{i}'], f'wst{i}')
            if gate is None:
                S.op('dve', lambda e: e.tensor_copy(out=wsb[i][0:rows, 0:n], in_=wst[i][0:rows, 0:n]), [f'wst{i}'], [f'wsb{i}'])
            else:
                S.op('dve', lambda e: e.tensor_tensor(out=wsb[i][0:rows, 0:n], in0=wst[i][0:rows, 0:n], in1=gate[0:rows], op=ALU.mult),
                     [f'wst{i}', 'grow'], [f'wsb{i}'])
            for dst_ap, vf in stores:
                S.dma('act', dst_ap, vf(wsb[i]), [f'wsb{i}'], ['wscr'], f'wsbst{i}')

        def cast_all():
            for l in range(2):
                for kt in range(8):
                    cast_piece(win_d[l, kt * 128:(kt + 1) * 128, :], [
                        (winu_s[l, 0:5].rearrange("t p c -> p t c")[:, :, kt * 96:(kt + 1) * 96],
                         lambda w: w[:, 0:480].rearrange("p (t c) -> p t c", c=96)),
                        (winu_s[l, 5, :, kt * 96:kt * 96 + 32], lambda w: w[:, 480:512]),
                        (winq_s[l, :, kt, :], lambda w: w[:, 512:1280])], 1280)
                    yield
                    for c3 in range(4):
                        cast_piece(wup_d[l, kt * 128:(kt + 1) * 128, c3 * 1408:(c3 + 1) * 1408], [
                            (wup_s[l, c3 * 11:(c3 + 1) * 11].rearrange("c p x -> p c x")[:, :, kt * 128:(kt + 1) * 128],
                             lambda w: w[:, 0:1408].rearrange("p (c x) -> p c x", x=128))], 1408)
                        yield
                for t6 in range(NT6):
                    n_ = nr6(t6)
                    cast_piece(wglu_d[l, t6 * 96:t6 * 96 + n_, :], [(wglu_s[l, 0:n_, t6, :], lambda w, n_=n_: w[0:n_, 0:512])], 512, rows=n_)
                    yield
                    cast_piece(wout_d[l, t6 * 96:t6 * 96 + n_, :], [
                        (wout_s[l, :, 0:n_, t6 * 512:(t6 + 1) * 512].rearrange("h p c -> p h c"),
                         lambda w, n_=n_: w[0:n_, 0:1024].rearrange("p (h c) -> p h c", h=2))], 1024, gate=grow[:, l, 0, :], rows=n_)
                    yield
                for kt in range(4):
                    cast_piece(wout_d[l, 512 + kt * 128:512 + (kt + 1) * 128, :], [
                        (wout_s[l, :, :, (6 + kt) * 512:(7 + kt) * 512].rearrange("h p c -> p h c"),
                         lambda w: w[:, 0:1024].rearrange("p (h c) -> p h c", h=2))], 1024, gate=grow[:, l, 0, :])
                    yield
                for v in range(NV):
                    cast_piece(wdn_d[l, v * 128:(v + 1) * 128, :], [
                        (wdn_s[l, :, :, v * 256:(v + 1) * 256].rearrange("q p c -> p q c"),
                         lambda w: w[:, 0:1024].rearrange("p (q c) -> p q c", q=4))], 1024, gate=grow[:, l, 1, :])
                    yield


        castgen = cast_all()
        dvc = {'n': 0}

        def V(name, shape=(128, 16)):
            return psb(name, list(shape))
        lre = V("lre", (128, 2, 16)); lim = V("lim", (128, 2, 16)); ldt = V("ldtt", (128, 2, 16))
        S.dma('sp', lre[:], lamre_d.rearrange("l p a -> p l a"), [], ['lre'], 'c17')
        S.dma('sp', lim[:], lamim_d.rearrange("l p a -> p l a"), [], ['lim'], 'c18')
        S.dma('sp', ldt[:], ldt_d.rearrange("l p a -> p l a"), [], ['ldt'], 'c19')
        Bre = V("Bre", (128, 2, 16, 16)); Bim = V("Bim", (128, 2, 16, 16))
        Cre = V("Cre", (128, 2, 16, 16)); Cim = V("Cim", (128, 2, 16, 16))
        for i, (t, d_) in enumerate(((Bre, bre_d), (Bim, bim_d), (Cre, cre_d), (Cim, cim_d))):
            S.dma('sp', t[:], d_.rearrange("l p a b -> p l a b"), [], [f'BC{i}'], f'c2{i}')
        tnames = ['dt', 'zr', 'th', 'mag', 'sn', 'cs', 't1', 't2', 't3', 'ar', 'ai', 'fr', 'fi', 'pr', 'pi', 'qr', 'qi']
        tv = {n: V("v_" + n) for n in tnames}
        Er = V("Er", (128, 16, 16)); Ei = V("Ei", (128, 16, 16)); Gr = V("Gr", (128, 16, 16)); Gi = V("Gi", (128, 16, 16))
        X1 = V("X1", (128, 16, 16)); X2 = V("X2", (128, 16, 16)); X3 = V("X3", (128, 16, 16))
        Eblk = psb("Eblk", [128, 16, 8, 2, 32], BF)
        Gblk = psb("Gblk", [128, 8, 2, 16, 32], BF)
        Cw = psb("Cw", [128, 16, 2, 128], BF)
        tabev = psb("tabev", [128, 768], BF)
        PHsb = psb("PHsb", [128, 2, NCH + 1, 16])
        phu = psb("phu", [128, 2, 16]); pht = psb("pht", [128, 4, 16])

        def dv(fn, r, w):
            S.op('dve', fn, r, w)
            dvc['n'] += 1
            if dvc['n'] % 5 == 0:
                next(castgen, None)

        def tt(o, a, b, op, r=('ssmv',), w=('ssmv',)):
            dv(lambda e: e.tensor_tensor(out=o, in0=a, in1=b, op=op), list(r), list(w))

        def ts(o, a, s1, s2, op0, op1=None, r=('ssmv',), w=('ssmv',)):
            if op1 is None:
                dv(lambda e: e.tensor_scalar(out=o, in0=a, scalar1=s1, scalar2=None, op0=op0), list(r), list(w))
            else:
                dv(lambda e: e.tensor_scalar(out=o, in0=a, scalar1=s1, scalar2=s2, op0=op0, op1=op1), list(r), list(w))

        def bc(a):
            return a.unsqueeze(2).to_broadcast([128, 16, 16])

        def cmul_b(orr, oi, sr, si, xr, xi):
            tt(X1[:], xr, bc(sr), ALU.mult); tt(X2[:], xi, bc(si), ALU.mult)
            tt(X3[:], X1[:], X2[:], ALU.subtract)
            tt(X1[:], xr, bc(si), ALU.mult); tt(X2[:], xi, bc(sr), ALU.mult)
            tt(oi, X1[:], X2[:], ALU.add)
            dv(lambda e: e.tensor_copy(out=orr, in_=X3[:]), ['ssmv'], ['ssmv'])

        for l in range(2):
            rk = ['ssmv', 'lre', 'lim', 'ldt', 'BC0', 'BC1', 'BC2', 'BC3']
            t = {k: v[:] for k, v in tv.items()}
            S.op('act', lambda e, l=l: e.activation(out=tv['dt'][:], in_=ldt[:, l, :], func=AF.Exp), ['ldt', 'ssmv'], ['ssmv'])
            ts(t['t1'], lre[:, l, :], -1e-4, None, ALU.min, r=rk)
            tt(t['zr'], t['t1'], t['dt'], ALU.mult)
            tt(t['th'], lim[:, l, :], t['dt'], ALU.mult, r=rk)
            ts(t['mag'], t['zr'], 1.0 / 720, 1.0 / 120, ALU.mult, ALU.add)
            for cf in (1.0 / 24, 1.0 / 6, 0.5, 1.0, 1.0):
                tt(t['mag'], t['mag'], t['zr'], ALU.mult)
                ts(t['mag'], t['mag'], cf, None, ALU.add)
            ts(t['t2'], t['th'], 1.0 / 32, None, ALU.mult)
            tt(t['t3'], t['t2'], t['t2'], ALU.mult)
            ts(t['sn'], t['t3'], 1.0 / 362880, -1.0 / 5040, ALU.mult, ALU.add)
            for cf in (1.0 / 120, -1.0 / 6, 1.0):
                tt(t['sn'], t['sn'], t['t3'], ALU.mult)
                ts(t['sn'], t['sn'], cf, None, ALU.add)
            tt(t['sn'], t['sn'], t['t2'], ALU.mult)
            ts(t['cs'], t['t3'], -1.0 / 3628800, 1.0 / 40320, ALU.mult, ALU.add)
            for cf in (-1.0 / 720, 1.0 / 24, -0.5, 1.0):
                tt(t['cs'], t['cs'], t['t3'], ALU.mult)
                ts(t['cs'], t['cs'], cf, None, ALU.add)
            for _ in range(5):
                tt(t['pr'], t['cs'], t['cs'], ALU.mult); tt(t['pi'], t['sn'], t['sn'], ALU.mult)
                tt(t['qr'], t['sn'], t['cs'], ALU.mult)
                tt(t['cs'], t['pr'], t['pi'], ALU.subtract)
                ts(t['sn'], t['qr'], 2.0, None, ALU.mult)
            tt(t['ar'], t['mag'], t['cs'], ALU.mult); tt(t['ai'], t['mag'], t['sn'], ALU.mult)
            tt(t['pr'], t['t1'], t['t1'], ALU.mult); tt(t['pi'], lim[:, l, :], lim[:, l, :], ALU.mult, r=rk)
            tt(t['pr'], t['pr'], t['pi'], ALU.add)
            dv(lambda e: e.reciprocal(out=tv['pr'][:], in_=tv['pr'][:]), ['ssmv'], ['ssmv'])
            ts(t['qr'], t['ar'], -1.0, None, ALU.add)
            tt(t['t2'], t['qr'], t['t1'], ALU.mult); tt(t['t3'], t['ai'], lim[:, l, :], ALU.mult, r=rk)
            tt(t['t2'], t['t2'], t['t3'], ALU.add); tt(t['fr'], t['t2'], t['pr'], ALU.mult)
            tt(t['t2'], t['ai'], t['t1'], ALU.mult); tt(t['t3'], t['qr'], lim[:, l, :], ALU.mult, r=rk)
            tt(t['t2'], t['t2'], t['t3'], ALU.subtract); tt(t['fi'], t['t2'], t['pr'], ALU.mult)
            cmul_b(Er[:], Ei[:], t['fr'], t['fi'], Bre[:, l], Bim[:, l])
            cmul_b(Gr[:], Gi[:], t['ar'], t['ai'], Cre[:, l], Cim[:, l])
            S.op('pool', lambda e: e.memset(Eblk[:], 0.0), ['Eblk'], ['Eblk'])
            S.op('pool', lambda e: e.memset(Gblk[:], 0.0), ['Gblk'], ['Gblk'])
            S.op('pool', lambda e: e.memset(Cw[:], 0.0), ['Cw'], ['Cw'])
            for two in range(2):
                ps_ = slice(64 * two, 64 * two + 64)
                for q in range(3):
                    prs = slice(q, 16, 3)
                    S.op('dve', lambda e, ps_=ps_, two=two, q=q, prs=prs, l=l: e.tensor_copy(
                        out=Cw[ps_, prs, 0, 32 * q + 16 * two:32 * q + 16 * two + 16], in_=Cre[ps_, l, prs, :]),
                        ['BC2', 'Cw'], ['Cw'])
                    S.op('dve', lambda e, ps_=ps_, two=two, q=q, prs=prs, l=l: e.tensor_scalar(
                        out=Cw[ps_, prs, 1, 32 * q + 16 * two:32 * q + 16 * two + 16], in0=Cim[ps_, l, prs, :],
                        scalar1=-1.0, scalar2=None, op0=ALU.mult), ['BC3', 'Cw'], ['Cw'])
            for d_ in range(8):
                if d_ > 0:
                    cmul_b(Er[:], Ei[:], t['ar'], t['ai'], Er[:], Ei[:])
                    cmul_b(Gr[:], Gi[:], t['ar'], t['ai'], Gr[:], Gi[:])
                for two in range(2):
                    ps_ = slice(64 * two, 64 * two + 64)
                    cs_ = slice(16 * two, 16 * two + 16)
                    for part, (E_, G_) in enumerate(((Er, Gr), (Ei, Gi))):
                        S.op('dve', lambda e, ps_=ps_, cs_=cs_, d_=d_, part=part, E_=E_: e.tensor_copy(
                            out=Eblk[ps_, :, d_, part, cs_], in_=E_[ps_, :, :]), ['ssmv', 'Eblk'], ['Eblk'])
                        if part == 0:
                            S.op('dve', lambda e, ps_=ps_, cs_=cs_, d_=d_, G_=G_: e.tensor_copy(
                                out=Gblk[ps_, d_, 0, :, cs_], in_=G_[ps_, :, :]), ['ssmv', 'Gblk'], ['Gblk'])
                        else:
                            S.op('dve', lambda e, ps_=ps_, cs_=cs_, d_=d_, G_=G_: e.tensor_scalar(
                                out=Gblk[ps_, d_, 1, :, cs_], in0=G_[ps_, :, :], scalar1=-1.0, scalar2=None, op0=ALU.mult),
                                ['ssmv', 'Gblk'], ['Gblk'])
            tt(t['t2'], t['mag'], t['mag'], ALU.mult); tt(t['t3'], t['t2'], t['t2'], ALU.mult)
            dv(lambda e, l=l: e.tensor_tensor(out=R8t[:, l, :], in0=tv['t3'][:], in1=tv['t3'][:], op=ALU.mult), ['ssmv'], ['R8t'])

            def csq(orr, oi, xr, xi):
                tt(t['t2'], xr, xr, ALU.mult); tt(t['t3'], xi, xi, ALU.mult)
                tt(t['mag'], xr, xi, ALU.mult)
                tt(orr, t['t2'], t['t3'], ALU.subtract)
                ts(oi, t['mag'], 2.0, None, ALU.mult)
            csq(t['pr'], t['pi'], t['ar'], t['ai'])
            csq(t['qr'], t['qi'], t['pr'], t['pi'])
            csq(t['pr'], t['pi'], t['qr'], t['qi'])
            dv(lambda e, l=l: e.tensor_copy(out=AT1[:, l, 0:16], in_=tv['pr'][:]), ['ssmv'], ['AT'])
            dv(lambda e, l=l: e.tensor_copy(out=AT1[:, l, 16:32], in_=tv['pr'][:]), ['ssmv'], ['AT'])
            dv(lambda e, l=l: e.tensor_copy(out=AT2[:, l, 16:32], in_=tv['pi'][:]), ['ssmv'], ['AT'])
            dv(lambda e, l=l: e.tensor_scalar(out=AT2[:, l, 0:16], in0=tv['pi'][:], scalar1=-1.0, scalar2=None, op0=ALU.mult),
               ['ssmv'], ['AT'])
            dv(lambda e, l=l: e.reciprocal(out=tv['t1'][:], in_=R8t[:, l, :]), ['R8t', 'ssmv'], ['ssmv'])
            tt(t['qr'], t['pr'], t['t1'], ALU.mult); tt(t['qi'], t['pi'], t['t1'], ALU.mult)
            dv(lambda e: e.tensor_copy(out=phu[:, 0, :], in_=tv['qr'][:]), ['ssmv', 'phu'], ['phu'])
            dv(lambda e: e.tensor_copy(out=phu[:, 1, :], in_=tv['qi'][:]), ['ssmv', 'phu'], ['phu'])
            S.op('pool', lambda e: e.memset(PHsb[:, 0, 0, :], 1.0), ['PHsb'], ['PHsb'])
            S.op('pool', lambda e: e.memset(PHsb[:, 1, 0, :], 0.0), ['PHsb'], ['PHsb'])

            def ptt(o, a_, b_, op, r, w):
                S.op('pool', lambda e: e.tensor_tensor(out=o, in0=a_, in1=b_, op=op), r, w)
            for c in range(NCH):
                cr, ci = PHsb[:, 0, c, :], PHsb[:, 1, c, :]
                ptt(pht[:, 0, :], cr, phu[:, 0, :], ALU.mult, ['phu', 'PHsb', 'pht0'], ['pht0'])
                ptt(pht[:, 1, :], ci, phu[:, 1, :], ALU.mult, ['phu', 'PHsb', 'pht1'], ['pht1'])
                ptt(PHsb[:, 0, c + 1, :], pht[:, 0, :], pht[:, 1, :], ALU.subtract, ['pht0', 'pht1'], ['PHsb'])
                ptt(pht[:, 2, :], cr, phu[:, 1, :], ALU.mult, ['phu', 'PHsb', 'pht2'], ['pht2'])
                ptt(pht[:, 3, :], ci, phu[:, 0, :], ALU.mult, ['phu', 'PHsb', 'pht3'], ['pht3'])
                ptt(PHsb[:, 1, c + 1, :], pht[:, 2, :], pht[:, 3, :], ALU.add, ['pht2', 'pht3'], ['PHsb'])
            S.dma('sp', ph_s[l], PHsb[:], ['PHsb'], ['ph_s'], 'tb3')
            S.dma('sp', cs_s[l], Gblk[:], ['Gblk'], ['cs_s'], 'tb0')
            for d_ in range(8):
                b_ = bank2()
                S.op('dve', lambda e, b_=b_: e.memset(ps_t[b_ // 2][:, 0:768], 0.0), [], [pk(b_), pk(b_ + 1)])
                for pr in range(16):
                    t6, q = pr // 3, pr % 3
                    for part in range(2):
                        S.op('pe', lambda e, b_=b_, t6=t6, q=q, pr=pr, part=part, d_=d_: e.matmul(
                            ps_t[b_ // 2][32 * q:32 * q + 32, t6 * 128:(t6 + 1) * 128], lhsT=Eblk[:, pr, d_, part, :],
                            rhs=Cw[:, pr, part, :], start=(part == 0), stop=(part == 1)),
                            ['Eblk', 'Cw'], [pk(b_), pk(b_ + 1)])
                S.op('dve', lambda e, b_=b_: e.tensor_copy(out=tabev[:], in_=ps_t[b_ // 2][:, 0:768]), [pk(b_), pk(b_ + 1)], ['tabev'])
                S.dma('sp', kt_s[l, :, d_].rearrange("p g c -> p (g c)"), tabev[:], ['tabev'], ['kt_s'], 'tb1')
            for j in range(8):
                for part in range(2):
                    b_ = bank2()
                    S.op('dve', lambda e, b_=b_: e.memset(ps_t[b_ // 2][:, 0:768], 0.0), [], [pk(b_), pk(b_ + 1)])
                    for pr in range(16):
                        t6, q = pr // 3, pr % 3
                        S.op('pe', lambda e, b_=b_, t6=t6, q=q, pr=pr, part=part, j=j: e.matmul(
                            ps_t[b_ // 2][32 * q:32 * q + 32, t6 * 128:(t6 + 1) * 128], lhsT=Eblk[:, pr, 7 - j, part, :],
                            rhs=identb[:, :], start=True, stop=True), ['Eblk', 'identb'], [pk(b_), pk(b_ + 1)])
                    S.op('dve', lambda e, b_=b_: e.tensor_copy(out=tabev[:], in_=ps_t[b_ // 2][:, 0:768]), [pk(b_), pk(b_ + 1)], ['tabev'])
                    S.dma('sp', bf_s[l, :, :, j, part, :], tabev[:].rearrange("p (g c) -> p g c", g=6), ['tabev'], ['bf_s'], 'tb2')

        for _ in castgen:
            pass
        S.barrier()
        pes.close()

        xt = sb("xt", [128, NB, D])
        tok = [sb(f"tok{i}", [128, D]) for i in range(2)]
        sqj = sb("sqj", [128, D], BF)
        rst = sb("rst", [128, 8])
        featT = sb("featT", [128, 8, TT], BF)
        uT = sb("uT", [128, NT6, TC, NCH], BF)
        qkv_sq = sb("qkv_sq", [128, 640])
        qn = sb("qn", [128, 640])
        qnb = [qn, qkv_sq]; QNK = ['qn', 'qkv_sq']
        qT = sb("qT", [128, NB, 512], BF)
        SAw = sb("SAw", [128, NCH + 1, 32])
        Sprev = sb("Sprev", [128, 32, NCH], BF)
        f32all = sb("f32all", [128, 6, TT])
        f32b = [f32all[:, i, :] for i in range(6)]
        PHt = sb("PHt", [128, 2, NCH + 1, 16])
        ysb = f32b[0:2]; ytmp = f32b[2:4]; S_sb = f32b[4:6]; sgate = f32b[0:2]
        YSK = ["f32b0", "f32b1"]; YTK = ["f32b2", "f32b3"]; SSK = ["f32b4", "f32b5"]
        zT = sb("zT", [128, NT6, TT], BF)
        ssmT = sb("ssmT", [128, NT6, TT], BF)
        attnT = sb("attnT", [128, 4, TT], BF)
        PT = [sb(f"PT{i}", [128, 2, 512], BF) for i in range(2)]
        actT = sb("actT", [128, NV, TT], BF)
        U = [sb(f"U{i}", [128, 2, TT + 2], BF) for i in range(2)]
        dg = [sb(f"dg{i}", [128, 6, 128], BF) for i in range(2)]
        NWCH = 4
        wch = [sb(f"wch{i}", [128, 8, 128], BF) for i in range(NWCH)]
        wsl = [sb(f"wsl{i}", [128, 6144], BF) for i in range(2)]
        wslc = {"n": 0}

        def wslot():
            i = wslc["n"] % 2
            wslc["n"] += 1
            return wsl[i], f"wsl{i}"
        tab = sb("tab", [128, 14336], BF)
        BFt = tab[:, 0:12288].rearrange("p (a b c d) -> p a b c d", a=6, b=8, c=2)
        KTt = tab[:, 0:6144].rearrange("p (a b c) -> p a b c", a=8, b=6)
        CSt = tab[:, 6144:14336].rearrange("p (a b c d) -> p a b c d", a=8, b=2, c=16)
        st1 = sb("st1", [128, 16])
        rs_s = sb("rs_s", [128, NB]); rs_a = sb("rs_a", [128, NB])
        ctmp = [sb(f"ctmp{i}", [128, 32]) for i in range(2)]
        wcnt = {'n': 0}

        def load_chunk(src):
            i = wcnt['n'] % NWCH
            wcnt['n'] += 1
            S.dma('sp', wch[i][:], src, [], [f'wch{i}'], f'wch{i}')
            return wch[i], f'wch{i}'

        XK = [f'xt{b}' for b in range(NB)]

        def rstd_pow(out_ap, in_ap, scale, n, rkeys, wkeys):
            S.op('pool', lambda e: e.tensor_scalar(out=out_ap, in0=in_ap, scalar1=scale, scalar2=EPS, op0=ALU.mult, op1=ALU.add),
                 list(rkeys), list(wkeys))
            S.op('pool', lambda e: e.tensor_tensor(out=out_ap, in0=out_ap, in1=nhalf[:, 0:n], op=ALU.pow),
                 list(wkeys) + ['nhalf'], list(wkeys))

        def rms_to_featT(l, g_t, sh_t):
            for b in range(NB):
                S.op('act', lambda e, b=b: e.activation(out=sqj[:], in_=xt[:, b, :], func=AF.Square,
                                                        accum_out=rst[:, b:b + 1]), [XK[b]], ['sqj', f'rst{b}'])
            for b in range(NB):
                rstd_pow(rst[:, 4 + b:5 + b], rst[:, b:b + 1], 1.0 / D, 1, [f'rst{b}'], [f'rstd{b}'])
            for b in range(NB):
                tk = tok[b % 2]
                tkk = f'tok{b % 2}'
                S.op('dve', lambda e, b=b, tk=tk: e.tensor_scalar(out=tk[:], in0=xt[:, b, :], scalar1=rst[:, 4 + b:5 + b], scalar2=None,
                                                                   op0=ALU.mult), [XK[b], f'rstd{b}', tkk], [tkk])
                for half in range(2):
                    b_ = bank()
                    for j in range(4):
                        kt = half * 4 + j
                        S.op('pe', lambda e, b_=b_, j=j, kt=kt, tk=tk: e.transpose(
                            out=pb(b_, j * 128, (j + 1) * 128), in_=tk[:, kt * 128:(kt + 1) * 128], identity=ident[:]),
                            [tkk, 'ident'], [pk(b_)])
                    for j in range(4):
                        kt = half * 4 + j
                        if kt < 3:
                            S.op('dve', lambda e, b_=b_, j=j, kt=kt, b=b: e.tensor_scalar(
                                out=featT[:, kt, b * 128:(b + 1) * 128], in0=pb(b_, j * 128, (j + 1) * 128),
                                scalar1=g_t[:, l, kt:kt + 1], scalar2=sh_t[:, l, kt:kt + 1], op0=ALU.mult, op1=ALU.add),
                                [pk(b_), 'gA', 'gB', 'shA', 'shB'], ['featT'])
                        else:
                            S.op('act', lambda e, b_=b_, j=j, kt=kt, b=b: e.activation(
                                out=featT[:, kt, b * 128:(b + 1) * 128], in_=pb(b_, j * 128, (j + 1) * 128), func=AF.Identity,
                                bias=sh_t[:, l, kt:kt + 1], scale=g_t[:, l, kt:kt + 1]), [pk(b_), 'gA', 'gB', 'shA', 'shB'], ['featT'])

        def tile_layer(ti, l):
            if True:
                SAk = 'SAw'; SCk = f'SC{l}'; kTk = f'kT{l}'; HBk = f'HB{l}'
                if ti == 0 and l == 0:
                    S.dma('sp', tab[:, 0:12288], bf_s[l].rearrange("p a b c d -> p (a b c d)"), [], ['tab'], 'tab')
                S.dma('sp', PHt[:], ph_s[l], [], ['PHt'], 'PHt')
                wq_t, wqk = wslot()
                wqkv = wq_t[:, :].rearrange("p (k c) -> p k c", k=8)
                S.dma('sp', wq_t[:, :], winq_s[l].rearrange("p k c -> p (k c)"), [], [wqk], wqk)
                wg_t, wgk = wslot()
                wgl = wg_t[:, 0:3072].rearrange("p (k c) -> p k c", k=NT6)
                S.dma('sp', wg_t[:, 0:3072], wglu_s[l].rearrange("p k c -> p (k c)"), [], [wgk], wgk)
                rms_to_featT(l, gA, shA)
                for t6 in range(NT6):
                    n_ = nr6(t6)
                    i_ = wcnt['n'] % NWCH
                    wcnt['n'] += 1
                    wc, wk = wch[i_], f'wch{i_}'
                    S.dma('sp', wc[:, :, 0:n_], winu_s[l, t6].rearrange("p (k c) -> p k c", c=96)[:, :, 0:n_], [], [wk], wk)
                    b_ = bank()
                    for kt in range(8):
                        S.op('pe', lambda e, b_=b_, kt=kt, wc=wc, n_=n_: e.matmul(pb(b_)[0:n_, :], lhsT=wc[:, kt, 0:n_], rhs=featT[:, kt, :],
                                                                                 start=(kt == 0), stop=(kt == 7)),
                             [wk, 'featT'], [pk(b_)])
                    S.op('act', lambda e, b_=b_, t6=t6, n_=n_: e.activation(
                        out=uT[0:n_, t6].rearrange("p j c -> p c j"), in_=pb(b_)[0:n_, :].rearrange("p (c j) -> p c j", j=TC),
                        func=AF.Copy), [pk(b_)], ['uT'])
                for q in range(3):
                    combos = [(part, pr) for part in range(2) for pr in range(16) if pr % 3 == q]
                    b2 = bank2()
                    for sl, (part, pr) in enumerate(combos):
                        t6 = pr // 3
                        bb = b2 + sl // 8
                        for j in range(TC):
                            S.op('pe', lambda e, bb=bb, sl=sl, t6=t6, q=q, j=j, part=part: e.matmul(
                                pb(bb, (sl % 8) * 64, (sl % 8) * 64 + 64), lhsT=BFt[32 * q:32 * q + 32, t6, j, part, :],
                                rhs=uT[32 * q:32 * q + 32, t6, j, :], start=(j == 0), stop=(j == TC - 1)),
                                ['tab', 'uT'], [pk(bb)])
                    runs = []
                    for sl, (part, pr) in enumerate(combos):
                        if runs and runs[-1][0] == sl // 8 and runs[-1][1] == part:
                            runs[-1][3] = sl; runs[-1][5] = pr
                        else:
                            runs.append([sl // 8, part, sl, sl, pr, pr])
                    for (bk_, part, s0, s1, p0, p1) in runs:
                        bb = b2 + bk_
                        ns = s1 - s0 + 1
                        S.op('dve', lambda e, bb=bb, s0=s0, s1=s1, part=part, p0=p0, p1=p1, ns=ns: e.tensor_copy(
                            out=SAw[:, 1:NCH + 1, part * 16 + p0:part * 16 + p1 + 1:3].rearrange("p c s -> p s c"),
                            in_=pb(bb, (s0 % 8) * 64, (s1 % 8 + 1) * 64).rearrange("p (s c) -> p s c", s=ns)), [pk(bb)], [SAk])
                S.dma('sp', tab[:, 0:6144], kt_s[l].rearrange("p a b c -> p (a b c)"), [], ['tab'], 'tab')
                S.dma('sp', tab[:, 6144:14336], cs_s[l].rearrange("p a b c d -> p (a b c d)"), [], ['tab'], 'tab2')
                def chain_pre():
                    S.op('pool', lambda e: e.tensor_copy(out=SAw[:, 0, :], in_=SC[:, l, :]), [SCk, SAk], [SAk])
                    T2 = f32all[:, 0:4, :].rearrange("p a (c t q) -> p (a c) t q", t=2, q=16)
                    T2K = ['f32b0', 'f32b1', 'f32b2', 'f32b3']
                    Fv = SAw[:, 1:NCH + 1, :].rearrange("p c (t q) -> p c t q", t=2)
                    cosf, sinf = PHt[:, 0, 1:NCH + 1, :], PHt[:, 1, 1:NCH + 1, :]
                    S.op('pool', lambda e: e.tensor_tensor(out=T2[:, :, 0, :], in0=Fv[:, :, 1, :], in1=sinf, op=ALU.mult), [SAk, 'PHt'] + T2K, T2K)
                    S.op('pool', lambda e: e.tensor_tensor(out=T2[:, :, 1, :], in0=Fv[:, :, 0, :], in1=sinf, op=ALU.mult), [SAk, 'PHt'] + T2K, T2K)
                    S.op('pool', lambda e: e.tensor_tensor(out=Fv, in0=Fv, in1=cosf.unsqueeze(2).to_broadcast([128, NCH, 2, 16]), op=ALU.mult),
                         [SAk, 'PHt'], [SAk])
                    S.op('pool', lambda e: e.tensor_tensor(out=Fv[:, :, 0, :], in0=Fv[:, :, 0, :], in1=T2[:, :, 0, :], op=ALU.add), [SAk] + T2K, [SAk])
                    S.op('pool', lambda e: e.tensor_tensor(out=Fv[:, :, 1, :], in0=Fv[:, :, 1, :], in1=T2[:, :, 1, :], op=ALU.subtract), [SAk] + T2K, [SAk])
                    return T2, T2K

                def chain_scan():
                    for s_ in range(32):
                        q_ = s_ % 16
                        S.op('dve', lambda e, s_=s_, q_=q_: e.tensor_tensor_scan(
                            out=SAw[:, 1:NCH + 1, s_], data0=R8t[:, l, q_:q_ + 1].to_broadcast([128, NCH]), data1=SAw[:, 1:NCH + 1, s_],
                            initial=SAw[:, 0, s_:s_ + 1], op0=ALU.mult, op1=ALU.add), [SAk, 'R8t'], [SAk])

                def chain_post(T2, T2K):
                    Wv = SAw[:, 0:NCH, :].rearrange("p c (t q) -> p c t q", t=2)
                    cosb, sinb = PHt[:, 0, 0:NCH, :], PHt[:, 1, 0:NCH, :]
                    Spv = Sprev[:].rearrange("p (t q) c -> p c t q", t=2)
                    W64 = SAw[:, NCH, :]
                    S.op('pool', lambda e: e.tensor_tensor(out=ctmp[0][:, 0:16], in0=W64[:, 16:32], in1=PHt[:, 1, NCH, :], op=ALU.mult), [SAk, 'PHt'], ['ct0'])
                    S.op('pool', lambda e: e.tensor_tensor(out=ctmp[0][:, 16:32], in0=W64[:, 0:16], in1=PHt[:, 1, NCH, :], op=ALU.mult), [SAk, 'PHt', 'ct0'], ['ct0'])
                    S.op('pool', lambda e: e.tensor_tensor(out=ctmp[1][:, 0:16], in0=W64[:, 0:16], in1=PHt[:, 0, NCH, :], op=ALU.mult), [SAk, 'PHt'], ['ct1'])
                    S.op('pool', lambda e: e.tensor_tensor(out=ctmp[1][:, 16:32], in0=W64[:, 16:32], in1=PHt[:, 0, NCH, :], op=ALU.mult), [SAk, 'PHt', 'ct1'], ['ct1'])
                    S.op('pool', lambda e: e.tensor_tensor(out=SC[:, l, 0:16], in0=ctmp[1][:, 0:16], in1=ctmp[0][:, 0:16], op=ALU.subtract), ['ct0', 'ct1', SCk], [SCk])
                    S.op('pool', lambda e: e.tensor_tensor(out=SC[:, l, 16:32], in0=ctmp[1][:, 16:32], in1=ctmp[0][:, 16:32], op=ALU.add), ['ct0', 'ct1', SCk], [SCk])
                    S.op('pool', lambda e: e.tensor_tensor(out=T2[:, :, 0, :], in0=Wv[:, :, 1, :], in1=sinb, op=ALU.mult), [SAk, 'PHt'] + T2K, T2K)
                    S.op('pool', lambda e: e.tensor_tensor(out=T2[:, :, 1, :], in0=Wv[:, :, 0, :], in1=sinb, op=ALU.mult), [SAk, 'PHt'] + T2K, T2K)
                    S.op('pool', lambda e: e.tensor_tensor(out=Wv, in0=Wv, in1=cosb.unsqueeze(2).to_broadcast([128, NCH, 2, 16]), op=ALU.mult),
                         [SAk, 'PHt'], [SAk])
                    S.op('pool', lambda e: e.tensor_tensor(out=Spv[:, :, 0, :], in0=Wv[:, :, 0, :], in1=T2[:, :, 0, :], op=ALU.subtract), [SAk] + T2K, ['Sprev'])
                    S.op('pool', lambda e: e.tensor_tensor(out=Spv[:, :, 1, :], in0=Wv[:, :, 1, :], in1=T2[:, :, 1, :], op=ALU.add), [SAk] + T2K + ['Sprev'], ['Sprev'])

                def att_A1(b):
                    Q = qnb[b % 2]; QK = QNK[b % 2]
                    b2 = bank2()
                    for kt in range(8):
                        S.op('pe', lambda e, b2=b2, kt=kt, b=b: e.matmul(pb(b2), lhsT=featT[:, kt, b * 128:(b + 1) * 128],
                                                                        rhs=wqkv[:, kt, 0:512], start=(kt == 0), stop=(kt == 7)),
                             ['featT', wqk], [pk(b2)])
                    for kt in range(8):
                        S.op('pe', lambda e, b2=b2, kt=kt, b=b: e.matmul(pb(b2 + 1, 0, 256), lhsT=featT[:, kt, b * 128:(b + 1) * 128],
                                                                        rhs=wqkv[:, kt, 512:768], start=(kt == 0), stop=(kt == 7)),
                             ['featT', wqk], [pk(b2 + 1)])
                    qk_ps = ps_t[b2 // 2][:, 0:640]
                    S.op('act', lambda e, qk_ps=qk_ps, Q=Q: e.activation(out=Q[:], in_=qk_ps, func=AF.Square),
                         [pk(b2), pk(b2 + 1)], [QK])
                    S.op('dve', lambda e, Q=Q: e.tensor_reduce(out=st1[:, 4:14], in_=Q[:].rearrange("p (h d) -> p h d", d=64),
                                                          axis=AX.X, op=ALU.add), [QK], ['st1'])
                    S.op('act', lambda e: e.activation(out=st1[:, 4:14], in_=st1[:, 4:14], func=AF.Sqrt, bias=epsc[:, 0:1],
                                                       scale=1.0 / 64), ['st1', 'epsc'], ['st1'])
                    S.op('dve', lambda e: e.reciprocal(out=st1[:, 4:14], in_=st1[:, 4:14]), ['st1'], ['st1'])
                    S.op('dve', lambda e, qk_ps=qk_ps, Q=Q: e.tensor_tensor(
                        out=Q[:, 0:512].rearrange("p (m t d) -> p t m d", m=4, t=2),
                        in0=qk_ps[:, 0:512].rearrange("p (t m d) -> p t m d", t=2, m=4),
                        in1=st1[:, 4:12].rearrange("p (t m) -> p t m", t=2).unsqueeze(3).to_broadcast([128, 2, 4, 64]), op=ALU.mult),
                        [pk(b2), pk(b2 + 1), 'st1', QK], [QK])
                    S.op('dve', lambda e, qk_ps=qk_ps, Q=Q: e.tensor_tensor(
                        out=Q[:, 512:640].rearrange("p (h d) -> p h d", d=64), in0=qk_ps[:, 512:640].rearrange("p (h d) -> p h d", d=64),
                        in1=st1[:, 12:14].unsqueeze(2).to_broadcast([128, 2, 64]), op=ALU.mult),
                        [pk(b2), pk(b2 + 1), 'st1', QK], [QK])
                    S.op('act', lambda e, b2=b2, b=b: e.activation(
                        out=Vaug[:, l, b + 1, :, 0:64], in_=pb(b2 + 1, 128, 256).rearrange("p (g d) -> p g d", g=2),
                        func=AF.Copy), [pk(b2 + 1)], ['Vaug'])

                def att_A2(b):
                    Q = qnb[b % 2]; QK = QNK[b % 2]
                    tb = bank2()
                    for m in range(4):
                        S.op('pe', lambda e, tb=tb, m=m, Q=Q: e.transpose(
                            out=pb(tb, m * 128, (m + 1) * 128),
                            in_=Q[:, m * 128:(m + 1) * 128], identity=ident[:]),
                            [QK, 'ident'], [pk(tb)])
                    S.op('pe', lambda e, tb=tb, Q=Q: e.transpose(out=pb(tb + 1, 0, 128), in_=Q[:, 512:640], identity=ident[:]),
                         [QK, 'ident'], [pk(tb + 1)])
                    S.op('act', lambda e, tb=tb, b=b: e.activation(
                        out=qT[:, b, :], in_=pb(tb), func=AF.Identity,
                        scale=qsc[:, l:l + 1]), [pk(tb), 'qsc'], ['qT'])
                    S.op('act', lambda e, tb=tb, b=b: e.activation(
                        out=kT[:, l, 128 + b * 128:128 + (b + 1) * 128], in_=pb(tb + 1, 0, 128), func=AF.Identity,
                        scale=ksc[:, l:l + 1]), [pk(tb + 1), 'ksc'], [kTk])

                BST = {}

                def att_B1(b):
                    first = (ti == 0 and b == 0)
                    tiles = [1] if first else [0, 1]
                    BST[b] = tiles
                    for g in range(2):
                        gs = slice(64 * g, 64 * g + 64)
                        pt = PT[g]; ptk = f'PT{g}'
                        for tl in tiles:
                            sb_ = bank()
                            kcol = b * 128 + tl * 128
                            S.op('pe', lambda e, sb_=sb_, gs=gs, kcol=kcol, b=b: e.matmul(
                                pb(sb_), lhsT=kT[gs, l, kcol:kcol + 128], rhs=qT[gs, b, :],
                                start=True, stop=True), [kTk, 'qT'], [pk(sb_)])
                            ssb_ = S_sb[tl]; ssk = SSK[tl]
                            S.op('dve', lambda e, sb_=sb_, ssb_=ssb_, tl=tl, g=g: e.tensor_tensor(
                                out=ssb_[:].rearrange("p (m q) -> p m q", m=4), in0=pb(sb_).rearrange("p (m q) -> p m q", m=4),
                                in1=biasT[:, tl, 4 * g:4 * g + 4, :], op=ALU.add), [pk(sb_), 'biasT'], [ssk])
                            S.op('act', lambda e, ssb_=ssb_, pt=pt, tl=tl: e.activation(out=pt[:, tl, :], in_=ssb_[:], func=AF.Exp),
                                 [ssk], [ptk])

                def att_B2(b):
                    tiles = BST[b]
                    ob = bank2()
                    for g in range(2):
                        pt = PT[g]; ptk = f'PT{g}'
                        for m in range(4):
                            for ii, tl in enumerate(tiles):
                                S.op('pe', lambda e, ob=ob, g=g, m=m, tl=tl, ii=ii, pt=pt, b=b, n_=len(tiles): e.matmul(
                                    pb(ob + g, m * 65, m * 65 + 65), lhsT=pt[:, tl, m * 128:(m + 1) * 128],
                                    rhs=Vaug[:, l, b + tl, g, :], start=(ii == 0), stop=(ii == n_ - 1)),
                                    [ptk, 'Vaug'], [pk(ob + g)])
                    at = tok[b % 2]; atk = f'tok{b % 2}'
                    for g in range(2):
                        o3 = pb(ob + g, 0, 260).rearrange("p (m d) -> p m d", m=4)
                        S.op('dve', lambda e, o3=o3, g=g: e.tensor_tensor(out=st1[:, 0:4], in0=o3[:, :, 64], in1=esink[:, l, 4 * g:4 * g + 4],
                                                                          op=ALU.add), [pk(ob + g), 'esink', 'st1'], ['st1'])
                        S.op('dve', lambda e: e.reciprocal(out=st1[:, 0:4], in_=st1[:, 0:4]), ['st1'], ['st1'])
                        S.op('dve', lambda e, o3=o3, g=g, at=at: e.tensor_tensor(
                            out=at[:, 256 * g:256 * g + 256].rearrange("p (m d) -> p m d", m=4), in0=o3[:, :, 0:64],
                            in1=st1[:, 0:4].unsqueeze(2).to_broadcast([128, 4, 64]), op=ALU.mult),
                            [pk(ob + g), 'st1', atk], [atk])
                    S.op('act', lambda e, at=at, b=b: e.activation(out=qkv_sq[:, 0:512], in_=at[:, 0:512], func=AF.Square,
                                                                   accum_out=rs_a[:, b:b + 1]), [atk, 'qkv_sq'], ['qkv_sq', 'rs_a'])
                    rstd_pow(rs_a[:, b:b + 1], rs_a[:, b:b + 1], 1.0 / 512, 1, ['rs_a'], ['rs_a'])

                def att_B3(b):
                    at = tok[b % 2]; atk = f'tok{b % 2}'
                    tb = bank()
                    for m in range(4):
                        S.op('pe', lambda e, tb=tb, m=m, at=at: e.transpose(out=pb(tb, m * 128, (m + 1) * 128),
                                                                           in_=at[:, m * 128:(m + 1) * 128], identity=ident[:]),
                             [atk, 'ident'], [pk(tb)])
                    for m in range(4):
                        S.op('act', lambda e, tb=tb, m=m, b=b: e.activation(
                            out=attnT[:, m, b * 128:(b + 1) * 128], in_=pb(tb, m * 128, (m + 1) * 128), func=AF.Identity,
                            scale=ona[:, l, m:m + 1]), [pk(tb), 'ona'], ['attnT'])

                def Y_tile(t6):
                    n_ = nr6(t6)
                    b_ = bank()
                    for i in range(TC):
                        for j in range(i + 1):
                            S.op('pe', lambda e, b_=b_, i=i, j=j, t6=t6, n_=n_: e.matmul(
                                pb(b_, i * 64, i * 64 + 64)[0:n_, :], lhsT=KTt[0:n_, i - j, t6, 0:n_], rhs=uT[0:n_, t6, j, :],
                                start=(j == 0), stop=False, skip_group_check=True),
                                ['tab', 'uT'], [pk(b_)])
                        for q in range(np6(t6)):
                            pr = t6 * 3 + q
                            for part in range(2):
                                last = (q == np6(t6) - 1 and part == 1)
                                S.op('pe', lambda e, b_=b_, i=i, q=q, pr=pr, part=part, last=last: e.matmul(
                                    pb(b_, i * 64, i * 64 + 64)[32 * q:32 * q + 32, :], lhsT=CSt[:, i, part, pr, :],
                                    rhs=Sprev[:, part * 16 + pr, :], start=False, stop=last, skip_group_check=True),
                                    ['tab', 'Sprev'], [pk(b_)])
                    yb = ysb[t6 % 2]; yk = YSK[t6 % 2]; yt_ = ytmp[t6 % 2]; ytk = YTK[t6 % 2]
                    S.op('dve', lambda e, b_=b_, t6=t6, yb=yb, n_=n_: e.scalar_tensor_tensor(
                        out=yb[0:n_, :], in0=uT[0:n_, t6].rearrange("p j c -> p (j c)"), scalar=dT[0:n_, l, t6:t6 + 1], in1=pb(b_)[0:n_, :],
                        op0=ALU.mult, op1=ALU.add), ['uT', 'dT', pk(b_)], [yk])
                    S.op('act', lambda e, yb=yb, yt_=yt_, n_=n_: e.activation(out=yt_[0:n_, :], in_=yb[0:n_, :], func=AF.Square), [yk], [ytk])
                    S.op('dve', lambda e, yt_=yt_, n_=n_: e.tensor_scalar(out=yt_[0:n_, :], in0=yt_[0:n_, :], scalar1=0.044715, scalar2=1.0,
                                                                           op0=ALU.mult, op1=ALU.add), [ytk], [ytk])
                    S.op('dve', lambda e, yt_=yt_, yb=yb, n_=n_: e.tensor_tensor(out=yt_[0:n_, :], in0=yt_[0:n_, :], in1=yb[0:n_, :], op=ALU.mult),
                         [ytk, yk], [ytk])
                    S.op('act', lambda e, yt_=yt_, n_=n_: e.activation(out=yt_[0:n_, :], in_=yt_[0:n_, :], func=AF.Tanh, scale=0.7978845608),
                         [ytk], [ytk])
                    S.op('dve', lambda e, yt_=yt_, yb=yb, t6=t6, n_=n_: e.scalar_tensor_tensor(
                        out=zT[0:n_, t6, :].rearrange("p (c j) -> p j c", j=TC), in0=yt_[0:n_, :].rearrange("p (j c) -> p j c", j=TC),
                        scalar=1.0, in1=yb[0:n_, :].rearrange("p (j c) -> p j c", j=TC), op0=ALU.add, op1=ALU.mult), [ytk, yk], ['zT'])
                T2, T2K = chain_pre()
                att_A1(0)
                att_A1(1)
                att_A2(0)
                chain_scan()
                att_A1(2)
                att_A2(1)
                att_A1(3)
                att_A2(2)
                chain_post(T2, T2K)
                att_A2(3)
                att_B1(0); Y_tile(0); att_B2(0); Y_tile(1); att_B3(0)
                att_B1(1); Y_tile(2); att_B2(1); Y_tile(3); att_B3(1)
                att_B1(2); Y_tile(4); att_B2(2); Y_tile(5); att_B3(2)
                att_B1(3); att_B2(3); att_B3(3)
                S.op('pool', lambda e: e.tensor_copy(out=kT[:, l, 0:128], in_=kT[:, l, TT:TT + 128]), [kTk], [kTk])
                S.op('pool', lambda e: e.tensor_copy(out=Vaug[:, l, 0, :, :], in_=Vaug[:, l, NB, :, :]), ['Vaug'], ['Vaug'])
                if not (ti == nt - 1 and l == 1):
                    S.dma('sp', tab[:, 0:12288], bf_s[1 - l].rearrange("p a b c d -> p (a b c d)"), [], ['tab'], 'tab')
                ssb = bank()
                glu_pending = []
                for m in range(NT6):
                    no = nr6(m)
                    b_ = bank()
                    if b_ == ssb:
                        b_ = bank()
                    for t6 in range(NT6):
                        n_ = nr6(t6)
                        S.op('pe', lambda e, b_=b_, m=m, t6=t6, n_=n_, no=no: e.matmul(
                            pb(b_)[0:no, :], lhsT=wgl[0:n_, t6, m * 96:m * 96 + no], rhs=zT[0:n_, t6, :],
                            start=(t6 == 0), stop=(t6 == NT6 - 1)), [wgk, 'zT'], [pk(b_)])
                    while len(glu_pending) > 0:
                        glu_pending.pop(0)()
                    yb = ysb[m % 2]; yk = YSK[m % 2]; yt_ = ytmp[m % 2]; ytk = YTK[m % 2]
                    S.op('act', lambda e, b_=b_, m=m, yb=yb, no=no: e.activation(out=yb[0:no, :], in_=pb(b_)[0:no, :], func=AF.Tanh,
                                                                                bias=bglu[0:no, l, m:m + 1], scale=0.25),
                         [pk(b_), 'bglu'], [yk])
                    S.op('dve', lambda e, m=m, yb=yb, no=no: e.scalar_tensor_tensor(out=yb[0:no, :], in0=yb[0:no, :], scalar=1.0, in1=zT[0:no, m, :],
                                                                                    op0=ALU.add, op1=ALU.mult), [yk, 'zT'], [yk])
                    S.op('act', lambda e, yb=yb, yt_=yt_, no=no: e.activation(out=yt_[0:no, :], in_=yb[0:no, :], func=AF.Square), [yk], [ytk])
                    S.op('dve', lambda e, m=m, yb=yb, no=no: e.tensor_scalar(out=ssmT[0:no, m, :], in0=yb[0:no, :], scalar1=ons[0:no, l, m:m + 1],
                                                                              scalar2=None, op0=ALU.mult), [yk, 'ons'], ['ssmT'])
                    def ones_mm(m=m, yt_=yt_, ytk=ytk, no=no):
                        for b in range(NB):
                            S.op('pe', lambda e, m=m, b=b, yt_=yt_, ssb=ssb, no=no: e.matmul(
                                pb(ssb)[:, m * NB + b:m * NB + b + 1], lhsT=yt_[0:no, b * 128:(b + 1) * 128], rhs=ones_f[0:no, 0:1],
                                start=True, stop=True), [ytk, 'ones_f'], [pk(ssb)])
                    glu_pending.append(ones_mm)
                while len(glu_pending) > 0:
                    glu_pending.pop(0)()
                S.op('dve', lambda e, ssb=ssb: e.tensor_reduce(out=rs_s[:], in_=pb(ssb)[:, 0:NT6 * NB].rearrange("p (m b) -> p b m", b=NB),
                                                              axis=AX.X, op=ALU.add), [pk(ssb)], ['rs_s'])
                rstd_pow(rs_s[:], rs_s[:], 1.0 / (16 * 512), NB, ['rs_s'], ['rs_s'])
                for half in range(2):
                    wo_t, wok = wslot()
                    woh = wo_t[:, 0:5120].rearrange("p (k c) -> p k c", k=10)
                    S.dma('sp', wo_t[:, 0:5120], wout_s[l, half], [], [wok], wok)
                    for b in range(NB):
                        ba = bank(); bb_ = bank()
                        for t6 in range(NT6):
                            n_ = nr6(t6)
                            S.op('pe', lambda e, ba=ba, t6=t6, b=b, n_=n_, woh=woh: e.matmul(pb(ba), lhsT=ssmT[0:n_, t6, b * 128:(b + 1) * 128],
                                                                                   rhs=woh[0:n_, t6, :], start=(t6 == 0), stop=(t6 == NT6 - 1)),
                                 ['ssmT', wok], [pk(ba)])
                        for ft in range(4):
                            S.op('pe', lambda e, bb_=bb_, ft=ft, b=b, woh=woh: e.matmul(pb(bb_), lhsT=attnT[:, ft, b * 128:(b + 1) * 128],
                                                                              rhs=woh[:, 6 + ft, :], start=(ft == 0), stop=(ft == 3)),
                                 ['attnT', wok], [pk(bb_)])
                        xs = xt[:, b, half * 512:(half + 1) * 512]
                        S.op('dve', lambda e, ba=ba, b=b, xs=xs: e.scalar_tensor_tensor(
                            out=xs, in0=pb(ba), scalar=rs_s[:, b:b + 1], in1=xs, op0=ALU.mult, op1=ALU.add),
                            [pk(ba), 'rs_s', XK[b]], [XK[b]])
                        S.op('dve', lambda e, bb_=bb_, b=b, xs=xs: e.scalar_tensor_tensor(
                            out=xs, in0=pb(bb_), scalar=rs_a[:, b:b + 1], in1=xs, op0=ALU.mult, op1=ALU.add),
                            [pk(bb_), 'rs_a', XK[b]], [XK[b]])
                rms_to_featT(l, gB, shB)
                ups_all = {}

                def stage_up(v):
                    dgv = dg[v % 2]; dgk = f'dg{v % 2}'
                    ups = []
                    for vg in range(2):
                        wc, wk = load_chunk(wup_s[l, vg * NV + v].rearrange("p (k c) -> p k c", k=8))
                        b_ = bank()
                        for kt in range(8):
                            S.op('pe', lambda e, b_=b_, kt=kt, wc=wc: e.matmul(pb(b_), lhsT=wc[:, kt, :], rhs=featT[:, kt, :],
                                                                               start=(kt == 0), stop=(kt == 7)),
                                 [wk, 'featT'], [pk(b_)])
                        ups.append(b_)
                    ups_all[v] = ups
                    S.op('pool', lambda e, dgv=dgv, v=v: e.tensor_tensor(
                        out=dgv[:, :, :].rearrange("p (g j) c -> p g j c", g=2),
                        in0=identb[:].unsqueeze(1).unsqueeze(1).to_broadcast([128, 2, 3, 128]),
                        in1=cw[:, l].rearrange("p (g v) j -> p g v j", g=2)[:, :, v, :].unsqueeze(3).to_broadcast([128, 2, 3, 128]),
                        op=ALU.mult), ['identb', 'cw', dgk], [dgk])

                def stage_mid(v):
                    Uv = U[v % 2]; Uk = f'U{v % 2}'; dgv = dg[v % 2]; dgk = f'dg{v % 2}'
                    sg = sgate[v % 2]; sgk = YSK[v % 2]
                    ups = ups_all.pop(v)
                    S.op('pool', lambda e, Uv=Uv, v=v: e.tensor_copy(out=Uv[:, :, 0:2], in_=HB[:, l, v, :, :]), [HBk, Uk], [Uk])
                    for vg in range(2):
                        S.op('act', lambda e, Uv=Uv, vg=vg, b_=ups[vg]: e.activation(out=Uv[:, vg, 2:TT + 2], in_=pb(b_), func=AF.Copy),
                             [pk(ups[vg]), Uk], [Uk])
                    S.op('pool', lambda e, Uv=Uv, v=v: e.tensor_copy(out=HB[:, l, v, :, :], in_=Uv[:, :, TT:TT + 2]), [Uk, HBk], [HBk])
                    cps = []
                    for vg in range(2):
                        b_ = bank()
                        for j in range(3):
                            S.op('pe', lambda e, b_=b_, vg=vg, j=j, dgv=dgv, Uv=Uv: e.matmul(
                                pb(b_), lhsT=dgv[:, vg * 3 + j, :], rhs=Uv[:, vg, j:j + TT], start=(j == 0), stop=(j == 2)),
                                [dgk, Uk], [pk(b_)])
                        cps.append(b_)
                    S.op('act', lambda e, sg=sg, b_=cps[1], v=v: e.activation(out=sg[:], in_=pb(b_), func=AF.Silu,
                                                                             bias=cb[:, l, NV + v:NV + v + 1], scale=1.0),
                         [pk(cps[1]), 'cb'], [sgk])
                    S.op('dve', lambda e, sg=sg, b_=cps[0], v=v: e.scalar_tensor_tensor(
                        out=actT[:, v, :], in0=pb(b_), scalar=cb[:, l, v:v + 1], in1=sg[:], op0=ALU.add, op1=ALU.mult),
                        [pk(cps[0]), 'cb', sgk], ['actT'])

                stage_up(0)
                for v in range(NV):
                    if v + 1 < NV:
                        stage_up(v + 1)
                    stage_mid(v)
                for qt in range(4):
                    wd_t, wdk = wslot()
                    wdh = wd_t[:, 0:5632].rearrange("p (k c) -> p k c", k=NV)
                    S.dma('sp', wd_t[:, 0:5632], wdn_s[l, qt], [], [wdk], wdk)
                    for b in range(NB):
                        b_ = bank()
                        for v in range(NV):
                            S.op('pe', lambda e, b_=b_, v=v, b=b, wdh=wdh: e.matmul(pb(b_, 0, 256), lhsT=actT[:, v, b * 128:(b + 1) * 128],
                                                                          rhs=wdh[:, v, :], start=(v == 0), stop=(v == NV - 1)),
                                 ['actT', wdk], [pk(b_)])
                        xs = xt[:, b, qt * 256:(qt + 1) * 256]
                        S.op('dve', lambda e, b_=b_, xs=xs: e.tensor_tensor(out=xs, in0=pb(b_, 0, 256), in1=xs, op=ALU.add),
                             [pk(b_), XK[b]], [XK[b]])
                        if l == 1 and qt == 3:
                            tk = tok[b % 2]; tkk = f'tok{b % 2}'
                            S.op('act', lambda e, tk=tk, b=b: e.activation(out=tk[:], in_=xt[:, b, :], func=AF.Copy), [XK[b], tkk], [tkk])
                            S.dma('sp', y_d[ti * TT + b * 128: ti * TT + (b + 1) * 128, :], tk[:], [tkk], ['y'], f'yst{b % 2}')
                            if ti + 1 < nt:
                                S.dma('sp', xt[:, b, :], x_d[(ti + 1) * TT + b * 128:(ti + 1) * TT + (b + 1) * 128, :], [], [XK[b]], f'xld{b}')

        S.dma('sp', xt[:], x_d[0:TT, :].rearrange("(b p) f -> p b f", p=128), [], XK, 'xt')
        for ti in range(nt):
            for l in range(2):
                tile_layer(ti, l)
        S.wait_all('sp')
        block = es.enter_context(nc.Block())
        S.emit(block)
    return nc


def _bucket_onehot():
    nb, md = 32, 128
    idx = np.arange(384)
    dist = idx - 127
    n = np.maximum(dist, 0)
    max_exact = nb // 2
    log_part = np.log(np.maximum(n, 1) / max_exact) / np.log(md / max_exact)
    large = max_exact + (log_part * (nb - max_exact)).astype(np.int32)
    large = np.minimum(large, nb - 1)
    bucket = np.where(n < max_exact, n, large).astype(np.int32)
    valid = (dist >= 0) & (dist < 128)
    oh = np.zeros((33, 384), np.float32)
    for i in range(384):
        if valid[i]:
            oh[bucket[i], i] = 1.0
        else:
            oh[32, i] = 1.0
    return oh


def prep_inputs(b, x, c, w_mod, b_mod, norm1_w, w_in, lam_re, lam_im, log_dt, ssm_b_re, ssm_b_im, ssm_c_re, ssm_c_im,
                ssm_d, w_glu, b_glu, q_norm_w, k_norm_w, rel_bias, sinks, out_norm_ssm, out_norm_attn, w_out, norm2_w,
                w_up, conv_w, conv_b, w_down):
    f = lambda a: np.ascontiguousarray(a, dtype=np.float32)

    def fm(a, n):
        return f(np.asarray(a).reshape(2, n, 128).transpose(0, 2, 1))

    def pairlay(a):
        a = np.asarray(a)
        sh = a.shape
        a = a.reshape(2, 16, 2, 64, *sh[3:])
        a = np.moveaxis(a, 1, 3)
        return f(a.reshape(2, 128, 16, *sh[3:]))
    m = {}
    m["x"] = f(x[b])
    m["cT"] = f(np.asarray(c[b]).reshape(8, 128).T)
    m["w_mod"] = f(w_mod); m["b_mod"] = f(b_mod)
    m["n1T"] = fm(norm1_w, 8); m["n2T"] = fm(norm2_w, 8)
    m["w_in"] = f(w_in); m["w_out"] = f(w_out); m["w_up"] = f(w_up); m["w_down"] = f(w_down); m["w_glu"] = f(w_glu)
    m["lamre"] = pairlay(lam_re); m["lamim"] = pairlay(lam_im)
    m["ldt"] = pairlay(np.repeat(np.asarray(log_dt)[:, :, None], 64, axis=2))
    m["bre"] = pairlay(ssm_b_re); m["bim"] = pairlay(ssm_b_im)
    m["cre"] = pairlay(np.asarray(ssm_c_re).transpose(0, 1, 3, 2)); m["cim"] = pairlay(np.asarray(ssm_c_im).transpose(0, 1, 3, 2))
    def fm96(a):
        a = np.asarray(a, dtype=np.float32)
        o = np.zeros((2, 6, 128), np.float32)
        pad = np.zeros((2, 576), np.float32)
        pad[:, :512] = a
        o[:, :, :96] = pad.reshape(2, 6, 96)
        return f(o.transpose(0, 2, 1))
    m["dT"] = fm96(ssm_d); m["bgluT"] = fm96(b_glu)
    m["qnw"] = f(np.tile(np.asarray(q_norm_w), (1, 2))[:, :, None]); m["knw"] = f(np.tile(np.asarray(k_norm_w), (1, 2))[:, :, None])
    m["relb"] = f(rel_bias); m["oneh"] = _bucket_onehot(); m["sinks"] = f(sinks)
    m["onsT"] = fm96(out_norm_ssm); m["onaT"] = fm(out_norm_attn, 4)
    m["cwT"] = f(np.asarray(conv_w).reshape(2, 3, 44, 128).transpose(0, 3, 2, 1))
    m["cbT"] = fm(conv_b, 44)
    m["ident"] = np.eye(128, dtype=np.float32)
    m["antiI"] = np.ascontiguousarray(np.eye(128, dtype=np.float32)[::-1])
    return m


def kernel(**inputs):
    x = np.asarray(inputs["x"])
    B, seq, _ = x.shape
    nc = build(seq)
    maps = [prep_inputs(ci, **inputs) for ci in range(B)]
    res = run_bass_kernel_spmd(nc, maps, core_ids=list(range(B)))
    out = np.stack([np.asarray(res.results[b]["y"]) for b in range(B)], axis=0)
    return out.astype(np.float32)
```
